# Optimizing a Trainium2 kernel written in Bass

```python
import math
import jax, jax.numpy as jnp
from jax import lax
import numpy as np

D_MODEL = 1024
BATCH = 16
SEQ = 4096
DEPTH = 1

SSM_WIDTH = D_MODEL // 2
SSM_GROUP = 16
SSM_GROUPS = SSM_WIDTH // SSM_GROUP
SSM_STATE = 64
DT_MIN = 0.001
DT_MAX = 0.1

HEAD_DIM = 64
HEADS_PER_GROUP = 4
ATTN_PATTERNS = ((128, 1), (512, 4), (2048, 16))
N_HEADS = HEADS_PER_GROUP * len(ATTN_PATTERNS)
ATTN_WIDTH = N_HEADS * HEAD_DIM
ATTN_OUT_WIDTH = HEADS_PER_GROUP * HEAD_DIM
ROPE_THETA = 10000.0

N_BRANCHES = 2
IN_WIDTH = SSM_WIDTH + 3 * ATTN_WIDTH + N_BRANCHES * D_MODEL

D_FF = -(-8 * D_MODEL // (3 * 256)) * 256
CONV_WIDTH = 3

LN_EPS = 1e-5
NEG_INF = -1e30
DEEPNORM_ALPHA = (2 * DEPTH) ** 0.25
DEEPNORM_BETA = (8 * DEPTH) ** -0.25

kernel_name = "hybrid_s5_dilated_attn_convffn_deepnorm"


def layer_norm(x, g, b):
    xf = x.astype(jnp.float32)
    mu = jnp.mean(xf, axis=-1, keepdims=True)
    var = jnp.mean(jnp.square(xf - mu), axis=-1, keepdims=True)
    y = (xf - mu) * lax.rsqrt(var + LN_EPS)
    return (y * g.astype(jnp.float32) + b.astype(jnp.float32)).astype(x.dtype)


def rotary(t, positions):
    half = HEAD_DIM // 2
    inv_freq = jnp.power(ROPE_THETA, -jnp.arange(half, dtype=jnp.float32) * 2.0 / HEAD_DIM)
    ang = positions.astype(jnp.float32)[..., None] * inv_freq
    cos = jnp.cos(ang)[:, :, None, :]
    sin = jnp.sin(ang)[:, :, None, :]
    tf = t.astype(jnp.float32)
    t1, t2 = tf[..., :half], tf[..., half:]
    return jnp.concatenate([t1 * cos - t2 * sin, t2 * cos + t1 * sin], axis=-1).astype(t.dtype)


def _cmul(ar, ai, br, bi):
    return ar * br - ai * bi, ar * bi + ai * br


def _linear_recurrence(e1, e2):
    a1r, a1i, b1r, b1i = e1
    a2r, a2i, b2r, b2i = e2
    ar, ai = _cmul(a2r, a2i, a1r, a1i)
    br, bi = _cmul(a2r, a2i, b1r, b1i)
    return (ar, ai, br + b2r, bi + b2i)


def bidirectional_s5(u, lam_re, lam_im, log_dt, b_re, b_im, c_re, c_im, d_skip):
    bsz, s, _ = u.shape
    uf = u.astype(jnp.float32)
    ug = uf.reshape(bsz, s, SSM_GROUPS, SSM_GROUP)
    y = uf * d_skip.astype(jnp.float32)
    for direction in range(2):
        lr = lam_re[direction].astype(jnp.float32)
        li = lam_im[direction].astype(jnp.float32)
        dt = jnp.exp(log_dt[direction].astype(jnp.float32))[:, None]
        xr, xi = lr * dt, li * dt
        abar_m1_r = jnp.expm1(xr) * jnp.cos(xi) - 2.0 * jnp.square(jnp.sin(0.5 * xi))
        abar_i = jnp.exp(xr) * jnp.sin(xi)
        abar_r = abar_m1_r + 1.0
        den = lr * lr + li * li
        kr = (abar_m1_r * lr + abar_i * li) / den
        ki = (abar_i * lr - abar_m1_r * li) / den
        bbr, bbi = _cmul(kr[..., None], ki[..., None],
                         b_re[direction].astype(jnp.float32), b_im[direction].astype(jnp.float32))
        bu_r = jnp.einsum('bsgc,gpc->bsgp', ug, bbr)
        bu_i = jnp.einsum('bsgc,gpc->bsgp', ug, bbi)
        a_r = jnp.broadcast_to(abar_r, (1, s) + abar_r.shape)
        a_i = jnp.broadcast_to(abar_i, (1, s) + abar_i.shape)
        _, _, s_r, s_i = lax.associative_scan(
            _linear_recurrence, (a_r, a_i, bu_r, bu_i), reverse=(direction == 1), axis=1)
        y_dir = (jnp.einsum('bsgp,gcp->bsgc', s_r, c_re[direction].astype(jnp.float32))
                 - jnp.einsum('bsgp,gcp->bsgc', s_i, c_im[direction].astype(jnp.float32)))
        y = y + y_dir.reshape(bsz, s, SSM_WIDTH)
    return y


def dilated_window_attention(q, k, v, dilation, half):
    bsz, s, h, dh = q.shape
    n_sub = s // dilation
    blk = half
    nb = -(-n_sub // blk)
    lp = nb * blk

    def to_blocks(t, extra):
        t = t.reshape(bsz, n_sub, dilation, h, dh).transpose(0, 2, 3, 1, 4)
        t = jnp.pad(t, ((0, 0), (0, 0), (0, 0), (extra, lp - n_sub + extra), (0, 0)))
        return t.reshape(bsz, dilation, h, nb + 2 * (extra // blk), blk, dh)

    def neighbourhood(t):
        tb = to_blocks(t, blk)
        return jnp.concatenate([tb[:, :, :, :-2], tb[:, :, :, 1:-1], tb[:, :, :, 2:]], axis=4)

    qb = to_blocks(q, 0)
    kb = neighbourhood(k)
    vb = neighbourhood(v)
    qi = jnp.arange(blk)[:, None]
    kj = jnp.arange(3 * blk)[None, :]
    step = kj - blk - qi
    key_pos = jnp.arange(nb)[:, None, None] * blk + kj[None] - blk
    mask = (jnp.abs(step) <= half)[None] & (key_pos >= 0) & (key_pos < n_sub)
    scores = jnp.einsum('brhnqd,brhnkd->brhnqk', qb, kb).astype(jnp.float32) * (dh ** -0.5)
    scores = jnp.where(mask, scores, NEG_INF)
    m = jnp.max(scores, axis=-1, keepdims=True)
    p = jnp.exp(scores - m)
    den = jnp.sum(p, axis=-1)
    out = jnp.einsum('brhnqk,brhnkd->brhnqd', p, vb.astype(jnp.float32)) / den[..., None]
    lse = m[..., 0] + jnp.log(den)
    out = out.reshape(bsz, dilation, h, lp, dh)[:, :, :, :n_sub]
    out = out.transpose(0, 3, 1, 2, 4).reshape(bsz, s, h, dh)
    lse = lse.reshape(bsz, dilation, h, lp)[..., :n_sub].transpose(0, 3, 1, 2).reshape(bsz, s, h)
    return out, lse


def depthwise_conv(t, w, b):
    c = t.shape[-1]
    y = lax.conv_general_dilated(
        t, w[:, None, :].astype(t.dtype), window_strides=(1,),
        padding=((CONV_WIDTH // 2, CONV_WIDTH // 2),),
        dimension_numbers=('NWC', 'WIO', 'NWC'), feature_group_count=c)
    return y + b


def hybrid_layer(x, positions, w_in, b_in, ssm_lam_re, ssm_lam_im, ssm_log_dt, ssm_b_re, ssm_b_im,
                 ssm_c_re, ssm_c_im, ssm_d, w_glu_v, w_glu_g, w_attn_br, w_out, ln1_g, ln1_b,
                 w_up, conv_w, conv_b, w_down, ln2_g, ln2_b):
    bsz, s, _ = x.shape
    proj = x @ w_in + b_in
    u_ssm, q, k, v, gates = jnp.split(
        proj, [SSM_WIDTH, SSM_WIDTH + ATTN_WIDTH, SSM_WIDTH + 2 * ATTN_WIDTH,
               SSM_WIDTH + 3 * ATTN_WIDTH], axis=-1)

    y_ssm = bidirectional_s5(u_ssm, ssm_lam_re, ssm_lam_im, ssm_log_dt, ssm_b_re, ssm_b_im,
                             ssm_c_re, ssm_c_im, ssm_d)
    z = jax.nn.gelu(y_ssm, approximate=False).astype(x.dtype)
    ssm_out = (z @ w_glu_v) * jax.nn.sigmoid(z @ w_glu_g)

    q = rotary(q.reshape(bsz, s, N_HEADS, HEAD_DIM), positions)
    k = rotary(k.reshape(bsz, s, N_HEADS, HEAD_DIM), positions)
    v = v.reshape(bsz, s, N_HEADS, HEAD_DIM)
    outs, lses = [], []
    for gi, (window, dilation) in enumerate(ATTN_PATTERNS):
        hs = slice(gi * HEADS_PER_GROUP, (gi + 1) * HEADS_PER_GROUP)
        o, l = dilated_window_attention(q[:, :, hs], k[:, :, hs], v[:, :, hs],
                                        dilation, window // (2 * dilation))
        outs.append(o)
        lses.append(l)
    wts = jax.nn.softmax(jnp.stack(lses, axis=0), axis=0)
    attn = jnp.sum(wts[..., None] * jnp.stack(outs, axis=0), axis=0)
    attn_out = attn.reshape(bsz, s, ATTN_OUT_WIDTH).astype(x.dtype) @ w_attn_br

    g_ssm, g_attn = jnp.split(jax.nn.sigmoid(gates), N_BRANCHES, axis=-1)
    mixed = (g_ssm * ssm_out + g_attn * attn_out) @ w_out
    h = layer_norm(DEEPNORM_ALPHA * x + mixed, ln1_g, ln1_b)

    up = depthwise_conv(h @ w_up, conv_w, conv_b)
    a, val = jnp.split(up, 2, axis=-1)
    ffn = (jax.nn.gelu(a, approximate=False) * val) @ w_down
    return layer_norm(DEEPNORM_ALPHA * h + ffn, ln2_g, ln2_b)


def setup_inputs(seed: int = 0) -> dict:
    key = jax.random.key(seed)
    ks = jax.random.split(key, 26)
    f32 = jnp.float32
    L, G, P, HG = DEPTH, SSM_GROUPS, SSM_STATE, SSM_GROUP

    def normal(k, shape, scale):
        return jax.random.normal(k, shape, f32) * scale

    x = normal(ks[0], (BATCH, SEQ, D_MODEL), 1.0)
    positions = jnp.tile(jnp.arange(SEQ, dtype=jnp.int32)[None, :], (BATCH, 1))
    w_in = normal(ks[1], (L, D_MODEL, IN_WIDTH), D_MODEL ** -0.5)
    b_in = normal(ks[2], (L, IN_WIDTH), 0.02)
    ssm_lam_re = -0.5 + normal(ks[3], (L, 2, G, P), 0.01)
    ssm_lam_im = math.pi * jnp.arange(P, dtype=f32) + normal(ks[4], (L, 2, G, P), 0.01)
    ssm_log_dt = jax.random.uniform(ks[5], (L, 2, G), f32, math.log(DT_MIN), math.log(DT_MAX))
    ssm_b_re = normal(ks[6], (L, 2, G, P, HG), (2 * HG) ** -0.5)
    ssm_b_im = normal(ks[7], (L, 2, G, P, HG), (2 * HG) ** -0.5)
    ssm_c_re = normal(ks[8], (L, 2, G, HG, P), (2 * P) ** -0.5)
    ssm_c_im = normal(ks[9], (L, 2, G, HG, P), (2 * P) ** -0.5)
    ssm_d = normal(ks[10], (L, SSM_WIDTH), 0.5)
    w_glu_v = normal(ks[11], (L, SSM_WIDTH, D_MODEL), SSM_WIDTH ** -0.5)
    w_glu_g = normal(ks[12], (L, SSM_WIDTH, D_MODEL), SSM_WIDTH ** -0.5)
    w_attn_br = normal(ks[13], (L, ATTN_OUT_WIDTH, D_MODEL), ATTN_OUT_WIDTH ** -0.5)
    w_out = normal(ks[14], (L, D_MODEL, D_MODEL), D_MODEL ** -0.5 * DEEPNORM_BETA)
    ln1_g = 1.0 + normal(ks[15], (L, D_MODEL), 0.02)
    ln1_b = normal(ks[16], (L, D_MODEL), 0.02)
    w_up = normal(ks[17], (L, D_MODEL, 2 * D_FF), D_MODEL ** -0.5)
    conv_w = normal(ks[18], (L, CONV_WIDTH, 2 * D_FF), CONV_WIDTH ** -0.5)
    conv_b = normal(ks[19], (L, 2 * D_FF), 0.02)
    w_down = normal(ks[20], (L, D_FF, D_MODEL), D_FF ** -0.5 * DEEPNORM_BETA)
    ln2_g = 1.0 + normal(ks[21], (L, D_MODEL), 0.02)
    ln2_b = normal(ks[22], (L, D_MODEL), 0.02)
    return {"x": x, "positions": positions, "w_in": w_in, "b_in": b_in,
            "ssm_lam_re": ssm_lam_re, "ssm_lam_im": ssm_lam_im, "ssm_log_dt": ssm_log_dt,
            "ssm_b_re": ssm_b_re, "ssm_b_im": ssm_b_im, "ssm_c_re": ssm_c_re, "ssm_c_im": ssm_c_im,
            "ssm_d": ssm_d, "w_glu_v": w_glu_v, "w_glu_g": w_glu_g, "w_attn_br": w_attn_br,
            "w_out": w_out, "ln1_g": ln1_g, "ln1_b": ln1_b, "w_up": w_up, "conv_w": conv_w,
            "conv_b": conv_b, "w_down": w_down, "ln2_g": ln2_g, "ln2_b": ln2_b}


def reference(x, positions, w_in, b_in, ssm_lam_re, ssm_lam_im, ssm_log_dt, ssm_b_re, ssm_b_im,
              ssm_c_re, ssm_c_im, ssm_d, w_glu_v, w_glu_g, w_attn_br, w_out, ln1_g, ln1_b,
              w_up, conv_w, conv_b, w_down, ln2_g, ln2_b):
    h = x
    for layer in range(DEPTH):
        h = hybrid_layer(h, positions, w_in[layer], b_in[layer], ssm_lam_re[layer], ssm_lam_im[layer],
                         ssm_log_dt[layer], ssm_b_re[layer], ssm_b_im[layer], ssm_c_re[layer],
                         ssm_c_im[layer], ssm_d[layer], w_glu_v[layer], w_glu_g[layer],
                         w_attn_br[layer], w_out[layer], ln1_g[layer], ln1_b[layer], w_up[layer],
                         conv_w[layer], conv_b[layer], w_down[layer], ln2_g[layer], ln2_b[layer])
    return h
```

```python
import math
from contextlib import ExitStack

import numpy as np
import concourse.bass as bass
import concourse.mybir as mybir
from concourse.bass_utils import run_bass_kernel_spmd

F32 = mybir.dt.float32
BF16 = mybir.dt.bfloat16
I32 = mybir.dt.int32
AF = mybir.ActivationFunctionType
ALU = mybir.AluOpType
AX = mybir.AxisListType

S = 4096
D = 1024
NCORES = 8
NSEQ = 2
SSMW = 512
AW = 768
DFF = 2816
NFM = 5632
ALPHA = 2.0 ** 0.25
LN_EPS = 1e-5
TWO_PI = 2.0 * math.pi
DIL = (1, 4, 16)
KPAD = 1024


class Buf:
    __slots__ = ("t", "w", "r", "dsem", "const")

    def __init__(self, t, dsem=None, const=False):
        self.t = t
        self.w = None
        self.r = {}
        self.dsem = dsem
        self.const = const

    def __getitem__(self, k):
        return self.t[k]


class Prog:
    ENG = ("pe", "act", "dve", "pool", "sp")

    def __init__(self, nc, es, n_dsem=72):
        self.nc = nc
        self.engobj = {'pe': nc.tensor, 'act': nc.scalar, 'dve': nc.vector, 'pool': nc.gpsimd, 'sp': nc.sync}
        self.ninst = 0
        self.stopped = False
        self.esem = {e: es.enter_context(nc.semaphore("es_" + e)) for e in ("pe", "act", "dve", "pool")}
        self.ecount = {e: 0 for e in self.esem}
        self.dsems = [es.enter_context(nc.semaphore(f"ds{i}")) for i in range(n_dsem)]
        self.dcount = {id(s): 0 for s in self.dsems}
        self.dnext = 0
        self.waited = {e: {} for e in self.ENG}
        self.semobj = {}
        for s in list(self.esem.values()) + self.dsems:
            self.semobj[id(s)] = s

    def buf(self, t, dma=False, const=False):
        ds = None
        if dma:
            assert self.dnext < len(self.dsems), "out of DMA semaphores in this phase"
            ds = self.dsems[self.dnext]
            self.dnext += 1
        return Buf(t, ds, const)

    def _deps(self, reads, writes):
        deps = {}

        def add(ev):
            if ev is None:
                return
            k, v = ev
            if deps.get(k, 0) < v:
                deps[k] = v
        for b in reads:
            add(b.w)
        for b in writes:
            add(b.w)
            for k, v in b.r.items():
                add((k, v))
        return deps

    def _record(self, ev, reads, writes):
        for b in writes:
            b.w = ev
            b.r = {}
        for b in reads:
            if b.const:
                continue
            if b.r.get(ev[0], 0) < ev[1]:
                b.r[ev[0]] = ev[1]

    def _emit(self, eng, deps, fn, inc):
        e = self.engobj[eng]
        wd = self.waited[eng]
        own = id(self.esem[eng]) if eng in self.esem else None
        for k, v in deps.items():
            if eng == "pe" and k == own:
                continue
            if wd.get(k, 0) >= v:
                continue
            wd[k] = v
            e.wait_ge(self.semobj[k], v)
        if fn is None:
            return
        ins = fn(e)
        if inc is not None:
            ins.then_inc(inc[0], inc[1])
        self.ninst += 1

    def op(self, eng, fn, reads=(), writes=()):
        if self.stopped:
            return None
        deps = self._deps(reads, writes)
        self.ecount[eng] += 1
        sem = self.esem[eng]
        ev = (id(sem), self.ecount[eng])
        self._emit(eng, deps, fn, (sem, 1))
        self._record(ev, reads, writes)
        return ev

    def ops(self, eng, fns, reads=(), writes=()):
        assert eng == "pe"
        if self.stopped:
            return None
        deps = self._deps(reads, writes)
        for fn in fns[:-1]:
            self._emit(eng, deps, fn, None)
            deps = {}
        self.ecount[eng] += 1
        sem = self.esem[eng]
        ev = (id(sem), self.ecount[eng])
        self._emit(eng, deps, fns[-1], (sem, 1))
        self._record(ev, reads, writes)
        return ev

    def dma(self, fn, sb, load, reads=(), writes=(), q="sp"):
        if self.stopped:
            return None
        reads = list(reads)
        writes = list(writes)
        if load:
            writes.append(sb)
        else:
            reads.append(sb)
        deps = self._deps(reads, writes)
        sem = sb.dsem
        assert sem is not None
        self.dcount[id(sem)] += 16
        ev = (id(sem), self.dcount[id(sem)])
        self._emit(q, deps, fn, (sem, 16))
        self._record(ev, reads, writes)
        return ev

    def barrier(self):
        allev = {}
        for e, s in self.esem.items():
            if self.ecount[e]:
                allev[id(s)] = self.ecount[e]
        for s in self.dsems:
            if self.dcount[id(s)]:
                allev[id(s)] = self.dcount[id(s)]
        for eng in self.ENG:
            self._emit(eng, allev, None, None)
        self.dnext = 0

    def emit(self):
        pass


class StopBuild(Exception):
    pass


class Ring:
    def __init__(self, bufs):
        self.bufs = bufs
        self.i = 0

    def next(self):
        b = self.bufs[self.i % len(self.bufs)]
        self.i += 1
        return b


def build(nseq=NSEQ, debug=False, stop_after=None):
    nc = bass.Bass("TRN2", target_bir_lowering=False)

    def din(name, shape, dt=F32):
        return nc.dram_tensor(name, list(shape), dt, kind="ExternalInput").ap()

    dbg_kind = "ExternalOutput" if debug else "Internal"

    def dscr(name, shape, dt):
        return nc.dram_tensor(name, list(shape), dt, kind=dbg_kind).ap()

    x_d = din("x", [nseq, S, D])
    pos_d = din("pos", [nseq, S], I32)
    ident_d = din("ident", [128, 128])
    invf_d = din("invf", [128, 2])
    w_in_d = din("w_in_fm", [D, NFM])
    b_fm_d = din("b_fm", [128, NFM // 128])
    w_v_d = din("w_v", [D, AW])
    b_v_d = din("b_v", [1, AW])
    out_d = nc.dram_tensor("out", [nseq, S, D], F32, kind="ExternalOutput").ap()
    ssm_d = dict(
        lre=din("lre_h", [128, 32]), lim=din("lim_h", [128, 32]), ldt=din("ldt_h", [128, 32]),
        bzr=din("bzr_h", [128, 32, 128]), bzi=din("bzi_h", [128, 32, 128]),
        cbr=din("cbr_h", [32, 32, 128]), cbi=din("cbi_h", [32, 32, 128]),
        dsk=din("dsk_h", [128, 4]), iota=din("iota_h", [1, S]))
    zT_s = dscr("zT_s", [nseq, SSMW, S], BF16)
    aT_s = dscr("aT_s", [nseq, 256, S], BF16)
    h_s = dscr("h_s", [nseq, S, D], F32)
    hT_s = dscr("hT_s", [nseq, D, S], BF16)
    md = dict(maskb=din("maskb_h", [128, 256]), ones3=din("ones3_h", [128, 3, 64]),
              wgv=din("wgv_h", [512, D]), wgg=din("wgg_h", [512, D]), wab=din("wab_h", [256, D]), wo=din("wo_h", [D, D]),
              ln1g=din("ln1g_h", [1, D]), ln1b=din("ln1b_h", [1, D]), ln2g=din("ln2g_h", [1, D]), ln2b=din("ln2b_h", [1, D]),
              wup=din("wup_h", [D, 2 * DFF]), wdn=din("wdn_h", [DFF, D]), cw=din("cw_h", [128, 44, 3]), cbias=din("cb_h", [128, 44]),
              aT_s=aT_s, h_s=h_s, hT_s=hT_s)

    xT_s = dscr("xT_s", [nseq, D, S], BF16)
    uT_s = dscr("uT_s", [nseq, SSMW, S], BF16)
    qT_s = dscr("qT_s", [nseq, AW, S], BF16)
    kT_s = dscr("kT_s", [nseq, AW, S], BF16)
    gT_s = dscr("gT_s", [nseq, 2 * D, S], BF16)
    NBLK = [d * (S // d // 128 + 1) for d in DIL]
    v_s = [dscr(f"v_s{g}", [nseq, 128, NBLK[g], 256], BF16) for g in range(3)]

    with ExitStack() as es0:
        p = Prog(nc, es0)
        psum = [p.buf(es0.enter_context(nc.psum_tensor(f"ps{i}", [128, 512], F32))) for i in range(8)]
        ident = p.buf(es0.enter_context(nc.sbuf_tensor("ident_sb", [128, 128], F32)), dma=True, const=True)
        p.dma(lambda e: e.dma_start(out=ident[:], in_=ident_d), ident, True)
        p.dnext = 1

        def new_phase():
            p.barrier()
            p.dnext = 1

        def stop(tag):
            if stop_after == tag:
                p.stopped = True

        try:
            _phases(nc, p, psum, ident, nseq, locals_d=dict(x_d=x_d, pos_d=pos_d, invf_d=invf_d, w_in_d=w_in_d, b_fm_d=b_fm_d, w_v_d=w_v_d, b_v_d=b_v_d, out_d=out_d, xT_s=xT_s, uT_s=uT_s, qT_s=qT_s, kT_s=kT_s, gT_s=gT_s, v_s=v_s, NBLK=NBLK, ssm_d=ssm_d, zT_s=zT_s, md=md), new_phase=new_phase, stop=stop)
        except StopBuild:
            pass
        p.stopped = False
        p.barrier()
    print('instructions', p.ninst)
    return nc


def _phases(nc, p, psum, ident, nseq, locals_d, new_phase, stop):
    globals_ = locals_d
    x_d = globals_['x_d']; pos_d = globals_['pos_d']; invf_d = globals_['invf_d']; w_in_d = globals_['w_in_d']; b_fm_d = globals_['b_fm_d']
    w_v_d = globals_['w_v_d']; b_v_d = globals_['b_v_d']; out_d = globals_['out_d']; xT_s = globals_['xT_s']; uT_s = globals_['uT_s']
    qT_s = globals_['qT_s']; kT_s = globals_['kT_s']; gT_s = globals_['gT_s']; v_s = globals_['v_s']; NBLK = globals_['NBLK']
    ssm_d = globals_['ssm_d']; zT_s = globals_['zT_s']; md = globals_['md']
    aT_s = md['aT_s']; h_s = md['h_s']; hT_s = md['hT_s']
    if True:

        with ExitStack() as es:
            def sb(name, shape, dt, dma=False, const=False):
                return p.buf(es.enter_context(nc.sbuf_tensor(name, list(shape), dt)), dma=dma, const=const)

            wA = sb("wA", [128, 8, NFM], BF16)
            bfm = sb("bfm", [128, NFM // 128], F32, dma=True)
            invf = sb("invf_sb", [128, 2], F32, dma=True)
            p.dma(lambda e: e.dma_start(out=bfm[:], in_=b_fm_d), bfm, True)
            p.dma(lambda e: e.dma_start(out=invf[:], in_=invf_d), invf, True)
            WP = 1408
            wst = Ring([sb(f"wst{i}", [128, WP], F32, dma=True) for i in range(3)])
            cast_engs = ("dve", "act", "pool")
            ci = 0
            for k in range(8):
                for c in range(NFM // WP):
                    st = wst.next()
                    p.dma(lambda e, st=st, k=k, c=c: e.dma_start(
                        out=st[:], in_=w_in_d[k * 128:(k + 1) * 128, c * WP:(c + 1) * WP]), st, True)
                    eng = cast_engs[ci % 3]
                    ci += 1
                    if eng == "act":
                        p.op("act", lambda e, st=st, k=k, c=c: e.copy(out=wA[:, k, c * WP:(c + 1) * WP], in_=st[:]),
                             reads=[st], writes=[wA])
                    else:
                        p.op(eng, lambda e, st=st, k=k, c=c: e.tensor_copy(out=wA[:, k, c * WP:(c + 1) * WP], in_=st[:]),
                             reads=[st], writes=[wA])
            wA.const = True
            stop('A0')

            cosT = sb("cosT", [128, S], F32)
            sinT = sb("sinT", [128, S], F32)
            posi = sb("posi", [128, 1024], I32, dma=True)
            tur = sb("tur", [128, 1024], F32)
            turi = sb("turi", [128, 1024], I32)
            xs = [sb(f"xs{i}", [128, D], F32, dma=True) for i in range(4)]
            xT = Ring([sb(f"xT{j}", [128, 8, 512], BF16, dma=True) for j in range(2)])
            ev_bf = Ring([sb(f"evbf{j}", [128, 512], BF16, dma=True) for j in range(8)])
            rt = Ring([sb(f"rt{j}", [128, 512], F32) for j in range(4)])
            psr = Ring(psum)

            for sq in range(nseq):
                for c in range(S // 1024):
                    cs = slice(c * 1024, (c + 1) * 1024)
                    p.dma(lambda e, cs=cs: e.dma_start(out=posi[:], in_=pos_d[sq:sq + 1, cs].partition_broadcast(128)), posi, True)
                    for (tab, addc, scol) in ((sinT, 0.0, 1), (cosT, 0.25, None)):
                        p.op("dve", lambda e: e.tensor_copy(out=tur[:], in_=posi[:]), reads=[posi], writes=[tur])
                        p.op("dve", lambda e, addc=addc: e.tensor_scalar(out=tur[:], in0=tur[:], scalar1=invf[:, 0:1], scalar2=addc,
                                                                          op0=ALU.mult, op1=ALU.add), reads=[tur, invf], writes=[tur])
                        p.op("dve", lambda e: e.tensor_copy(out=turi[:], in_=tur[:]), reads=[tur], writes=[turi])
                        p.op("dve", lambda e: e.tensor_tensor(out=tur[:], in0=tur[:], in1=turi[:], op=ALU.subtract),
                             reads=[tur, turi], writes=[tur])
                        if scol is not None:
                            p.op("act", lambda e, tab=tab, cs=cs: e.activation(out=tab[:, cs], in_=tur[:], func=AF.Sin, scale=invf[:, 1:2]),
                                 reads=[tur, invf], writes=[tab])
                        else:
                            p.op("act", lambda e, tab=tab, cs=cs: e.activation(out=tab[:, cs], in_=tur[:], func=AF.Sin, scale=TWO_PI),
                                 reads=[tur], writes=[tab])
                stop('A1')
                def load_x(tb_):
                    for i in range(4):
                        p.dma(lambda e, i=i: e.dma_start(out=xs[i][:], in_=x_d[sq, tb_ * 512 + i * 128:tb_ * 512 + (i + 1) * 128, :]), xs[i], True)
                load_x(0)
                for tb in range(S // 512):
                    t0 = tb * 512
                    ts = slice(t0, t0 + 512)
                    xtile = xs
                    xTb = xT.next()
                    for k in range(8):
                        ps = psr.next()
                        p.ops("pe", [lambda e, ps=ps, i=i, k=k: e.transpose(out=ps[:, i * 128:(i + 1) * 128],
                                                                           in_=xtile[i][:, k * 128:(k + 1) * 128], identity=ident[:])
                                     for i in range(4)], reads=xtile + [ident], writes=[ps])
                        if k % 2 == 0:
                            p.op("act", lambda e, ps=ps, k=k: e.copy(out=xTb[:, k, :], in_=ps[:]), reads=[ps], writes=[xTb])
                        else:
                            p.op("dve", lambda e, ps=ps, k=k: e.tensor_copy(out=xTb[:, k, :], in_=ps[:]), reads=[ps], writes=[xTb])
                    if tb + 1 < S // 512:
                        load_x(tb + 1)
                    p.dma(lambda e: e.dma_start(out=xT_s[sq].rearrange("(k q) t -> q k t", q=128)[:, :, ts], in_=xTb[:]), xTb, False)

                    def proj(fo):
                        ps = psr.next()
                        p.ops("pe", [lambda e, ps=ps, k=k: e.matmul(ps[:], lhsT=wA[:, k, fo * 128:(fo + 1) * 128], rhs=xTb[:, k, :],
                                                                      start=(k == 0), stop=(k == 7)) for k in range(8)],
                              reads=[wA, xTb], writes=[ps])
                        return ps

                    for fo in range(4):
                        ps = proj(fo)
                        o = ev_bf.next()
                        p.op("act", lambda e, ps=ps, o=o, fo=fo: e.activation(out=o[:], in_=ps[:], func=AF.Identity, bias=bfm[:, fo:fo + 1]),
                             reads=[ps, bfm], writes=[o])
                        p.dma(lambda e, o=o, fo=fo: e.dma_start(out=uT_s[sq, fo * 128:(fo + 1) * 128, ts], in_=o[:]), o, False)
                    for which, dst in ((0, qT_s), (1, kT_s)):
                        for c in range(6):
                            fo = 4 + which * 6 + c
                            psa = proj(fo)
                            psb = proj(fo + 12)
                            t1 = rt.next()
                            t2 = rt.next()
                            p.op("dve", lambda e, psa=psa, t1=t1, fo=fo: e.scalar_tensor_tensor(
                                out=t1[:], in0=psa[:], scalar=bfm[:, fo:fo + 1], in1=cosT[:, ts], op0=ALU.add, op1=ALU.mult),
                                reads=[psa, bfm, cosT], writes=[t1])
                            p.op("dve", lambda e, psb=psb, t2=t2, fo=fo: e.scalar_tensor_tensor(
                                out=t2[:], in0=psb[:], scalar=bfm[:, fo + 12:fo + 13], in1=sinT[:, ts], op0=ALU.add, op1=ALU.mult),
                                reads=[psb, bfm, sinT], writes=[t2])
                            o = ev_bf.next()
                            p.op("pool", lambda e, o=o, t1=t1, t2=t2: e.tensor_tensor(out=o[:], in0=t1[:], in1=t2[:], op=ALU.add),
                                 reads=[t1, t2], writes=[o])
                            p.dma(lambda e, o=o, c=c, dst=dst: e.dma_start(out=dst[sq, c * 128:(c + 1) * 128, ts], in_=o[:]), o, False)
                    for c in range(16):
                        fo = 28 + c
                        ps = proj(fo)
                        o = ev_bf.next()
                        p.op("act", lambda e, ps=ps, o=o, fo=fo: e.activation(out=o[:], in_=ps[:], func=AF.Sigmoid, bias=bfm[:, fo:fo + 1]),
                             reads=[ps, bfm], writes=[o])
                        p.dma(lambda e, o=o, c=c: e.dma_start(out=gT_s[sq, c * 128:(c + 1) * 128, ts], in_=o[:]), o, False)
                    stop(f'A2_{tb}')
        new_phase()
        stop('A')

        with ExitStack() as es:
            def sb(name, shape, dt, dma=False, const=False):
                return p.buf(es.enter_context(nc.sbuf_tensor(name, list(shape), dt)), dma=dma, const=const)

            wV = sb("wV", [128, 8, AW], BF16)
            wvst = Ring([sb(f"wvst{i}", [128, AW], F32, dma=True) for i in range(2)])
            for k in range(8):
                st = wvst.next()
                p.dma(lambda e, st=st, k=k: e.dma_start(out=st[:], in_=w_v_d[k * 128:(k + 1) * 128, :]), st, True)
                p.op("dve", lambda e, st=st, k=k: e.tensor_copy(out=wV[:, k, :], in_=st[:]), reads=[st], writes=[wV])
            wV.const = True
            bv = sb("bv", [128, AW], F32, dma=True, const=True)
            p.dma(lambda e: e.dma_start(out=bv[:], in_=b_v_d.partition_broadcast(128)), bv, True)
            xTf = sb("xTf", [128, 8, S], BF16, dma=True)
            VCH = 12
            vring = Ring([sb(f"vstg{j}", [128, VCH, 256], BF16, dma=True) for j in range(2)])
            psr = Ring(psum)
            for sq in range(nseq):
                p.dma(lambda e: e.dma_start(out=xTf[:], in_=xT_s[sq].rearrange("(k q) t -> q k t", q=128)), xTf, True)
                for g in range(3):
                    d = DIL[g]
                    L = S // d
                    nb = L // 128 + 1
                    blocks = [(r, m) for r in range(d) for m in range(nb)]
                    for c0 in range(0, len(blocks), VCH):
                        chunk = blocks[c0:c0 + VCH]
                        stg = vring.next()
                        p.op("pool", lambda e, stg=stg: e.memset(stg[:], 0.0), writes=[stg])
                        for j, (r, m) in enumerate(chunk):
                            lo = 64 + 128 * (m - 1)
                            i0 = max(0, -lo)
                            i1 = min(128, L - lo)
                            M = i1 - i0
                            tok0 = r + d * (lo + i0)
                            ps = psr.next()
                            p.ops("pe", [lambda e, ps=ps, k=k, tok0=tok0, M=M, i0=i0, d=d, g=g: e.matmul(
                                ps[i0:i0 + M, 0:256], lhsT=xTf[:, k, tok0:tok0 + d * (M - 1) + 1:d], rhs=wV[:, k, g * 256:(g + 1) * 256],
                                start=(k == 0), stop=(k == 7)) for k in range(8)], reads=[xTf, wV], writes=[ps])
                            p.op("dve", lambda e, ps=ps, stg=stg, j=j, i0=i0, M=M, g=g: e.tensor_tensor(
                                out=stg[i0:i0 + M, j, :], in0=ps[i0:i0 + M, 0:256], in1=bv[i0:i0 + M, g * 256:(g + 1) * 256], op=ALU.add),
                                reads=[ps, bv], writes=[stg])
                        p.dma(lambda e, stg=stg, c0=c0, n=len(chunk), g=g: e.dma_start(out=v_s[g][sq, :, c0:c0 + n, :], in_=stg[:, 0:n, :]), stg, False)
                        stop(f'V{g}_{c0}')
                    stop(f'V{g}')
        new_phase()

        with ExitStack() as es:
            def sb(name, shape, dt, dma=False, const=False):
                return p.buf(es.enter_context(nc.sbuf_tensor(name, list(shape), dt)), dma=dma, const=const)

            NT = 32
            lre = sb("lre", [128, NT], F32, dma=True); lim = sb("lim", [128, NT], F32, dma=True); ldt = sb("ldt", [128, NT], F32, dma=True)
            p.dma(lambda e: e.dma_start(out=lre[:], in_=ssm_d["lre"]), lre, True)
            p.dma(lambda e: e.dma_start(out=lim[:], in_=ssm_d["lim"]), lim, True)
            p.dma(lambda e: e.dma_start(out=ldt[:], in_=ssm_d["ldt"]), ldt, True)
            dsk = sb("dsk", [128, 4], F32, dma=True)
            p.dma(lambda e: e.dma_start(out=dsk[:], in_=ssm_d["dsk"]), dsk, True)
            tI = sb("tI", [128, S], F32, dma=True, const=True)
            p.dma(lambda e: e.dma_start(out=tI[:], in_=ssm_d["iota"].partition_broadcast(128)), tI, True)
            sm = {n: sb("sm_" + n, [128, NT], F32) for n in
                  ("dt", "xr", "xi", "rho", "th", "t0", "t1", "f", "sinx", "cosx", "sinh", "em1", "am1", "abi", "den", "kr", "ki", "u0", "u1")}
            smi = sb("smi", [128, NT], I32)

            def V(fn, reads, writes):
                return p.op("dve", fn, reads=reads, writes=writes)

            def A(fn, reads, writes):
                return p.op("act", fn, reads=reads, writes=writes)

            def tt(o, a, b, op):
                V(lambda e: e.tensor_tensor(out=o[:], in0=a[:], in1=b[:], op=op), [a, b], [o])

            def tsc(o, a, s1, op0, s2=None, op1=None):
                if op1 is None:
                    V(lambda e: e.tensor_scalar(out=o[:], in0=a[:], scalar1=s1, scalar2=None, op0=op0), [a], [o])
                else:
                    V(lambda e: e.tensor_scalar(out=o[:], in0=a[:], scalar1=s1, scalar2=s2, op0=op0, op1=op1), [a], [o])

            def frac_sin(o, turns_src, mul, add):
                tsc(sm["t0"], turns_src, mul, ALU.mult, add, ALU.add)
                V(lambda e: e.tensor_copy(out=smi[:], in_=sm["t0"][:]), [sm["t0"]], [smi])
                tt(sm["f"], sm["t0"], smi, ALU.subtract)
                A(lambda e: e.activation(out=o[:], in_=sm["f"][:], func=AF.Sin, scale=TWO_PI), [sm["f"]], [o])

            A(lambda e: e.activation(out=sm["dt"][:], in_=ldt[:], func=AF.Exp), [ldt], [sm["dt"]])
            tt(sm["xr"], lre, sm["dt"], ALU.mult)
            tt(sm["xi"], lim, sm["dt"], ALU.mult)
            A(lambda e: e.activation(out=sm["rho"][:], in_=sm["xr"][:], func=AF.Exp), [sm["xr"]], [sm["rho"]])
            tsc(sm["th"], sm["xi"], 1.0 / TWO_PI, ALU.mult)
            frac_sin(sm["sinx"], sm["th"], 1.0, 0.0)
            frac_sin(sm["cosx"], sm["th"], 1.0, 0.25)
            frac_sin(sm["sinh"], sm["th"], 0.5, 0.0)
            tsc(sm["em1"], sm["xr"], 0.2, ALU.mult, 1.0, ALU.add)
            for cdiv in (0.25, 1.0 / 3.0, 0.5):
                tt(sm["em1"], sm["em1"], sm["xr"], ALU.mult)
                tsc(sm["em1"], sm["em1"], cdiv, ALU.mult, 1.0, ALU.add)
            tt(sm["em1"], sm["em1"], sm["xr"], ALU.mult)
            tt(sm["am1"], sm["em1"], sm["cosx"], ALU.mult)
            tt(sm["u0"], sm["sinh"], sm["sinh"], ALU.mult)
            V(lambda e: e.scalar_tensor_tensor(out=sm["am1"][:], in0=sm["u0"][:], scalar=-2.0, in1=sm["am1"][:], op0=ALU.mult, op1=ALU.add),
              [sm["u0"], sm["am1"]], [sm["am1"]])
            tt(sm["abi"], sm["rho"], sm["sinx"], ALU.mult)
            tt(sm["den"], lre, lre, ALU.mult)
            tt(sm["u0"], lim, lim, ALU.mult)
            tt(sm["den"], sm["den"], sm["u0"], ALU.add)
            V(lambda e: e.reciprocal(out=sm["den"][:], in_=sm["den"][:]), [sm["den"]], [sm["den"]])
            tt(sm["u0"], sm["am1"], lre, ALU.mult)
            tt(sm["u1"], sm["abi"], lim, ALU.mult)
            tt(sm["u0"], sm["u0"], sm["u1"], ALU.add)
            tt(sm["kr"], sm["u0"], sm["den"], ALU.mult)
            tt(sm["u0"], sm["abi"], lre, ALU.mult)
            tt(sm["u1"], sm["am1"], lim, ALU.mult)
            tt(sm["u0"], sm["u0"], sm["u1"], ALU.subtract)
            tt(sm["ki"], sm["u0"], sm["den"], ALU.mult)
            tsc(sm["t1"], sm["ki"], -1.0, ALU.mult)

            ZTr = sb("ZTr", [128, NT, 128], BF16); ZTi = sb("ZTi", [128, NT, 128], BF16)
            LCr = sb("LCr", [128, NT, 64], BF16); LCi = sb("LCi", [128, NT, 64], BF16)
            p.op("pool", lambda e: e.memset(LCr[:], 0.0), writes=[LCr])
            p.op("pool", lambda e: e.memset(LCi[:], 0.0), writes=[LCi])
            Dd = sb("Dd", [128, 4, 128], BF16)
            for q in range(4):
                V(lambda e, q=q: e.tensor_scalar(out=Dd[:, q, :], in0=ident[:], scalar1=dsk[:, q:q + 1], scalar2=None, op0=ALU.mult),
                  [ident, dsk], [Dd])
            psr = Ring(psum)
            with ExitStack() as es2:
                bzr = p.buf(es2.enter_context(nc.sbuf_tensor("bzr", [128, NT, 128], F32)), dma=True)
                bzi = p.buf(es2.enter_context(nc.sbuf_tensor("bzi", [128, NT, 128], F32)), dma=True)
                cbr = p.buf(es2.enter_context(nc.sbuf_tensor("cbr", [32, NT, 128], F32)), dma=True)
                cbi = p.buf(es2.enter_context(nc.sbuf_tensor("cbi", [32, NT, 128], F32)), dma=True)
                zt1 = p.buf(es2.enter_context(nc.sbuf_tensor("zt1", [128, 128], F32)))
                zt2 = p.buf(es2.enter_context(nc.sbuf_tensor("zt2", [128, 128], F32)))
                p.dma(lambda e: e.dma_start(out=bzr[:], in_=ssm_d["bzr"]), bzr, True)
                p.dma(lambda e: e.dma_start(out=bzi[:], in_=ssm_d["bzi"]), bzi, True)
                p.dma(lambda e: e.dma_start(out=cbr[:], in_=ssm_d["cbr"]), cbr, True)
                p.dma(lambda e: e.dma_start(out=cbi[:], in_=ssm_d["cbi"]), cbi, True)
                for j in range(NT):
                    V(lambda e, j=j: e.tensor_scalar(out=zt1[:], in0=bzr[:, j, :], scalar1=sm["kr"][:, j:j + 1], scalar2=None, op0=ALU.mult),
                      [bzr, sm["kr"]], [zt1])
                    V(lambda e, j=j: e.scalar_tensor_tensor(out=zt1[:], in0=bzi[:, j, :], scalar=sm["t1"][:, j:j + 1], in1=zt1[:], op0=ALU.mult, op1=ALU.add),
                      [bzi, sm["t1"], zt1], [zt1])
                    V(lambda e, j=j: e.tensor_scalar(out=zt2[:], in0=bzi[:, j, :], scalar1=sm["kr"][:, j:j + 1], scalar2=None, op0=ALU.mult),
                      [bzi, sm["kr"]], [zt2])
                    V(lambda e, j=j: e.scalar_tensor_tensor(out=zt2[:], in0=bzr[:, j, :], scalar=sm["ki"][:, j:j + 1], in1=zt2[:], op0=ALU.mult, op1=ALU.add),
                      [bzr, sm["ki"], zt2], [zt2])
                    ps = psr.next()
                    p.ops("pe", [lambda e, ps=ps: e.transpose(out=ps[:, 0:128], in_=zt1[:], identity=ident[:]),
                                 lambda e, ps=ps: e.transpose(out=ps[:, 128:256], in_=zt2[:], identity=ident[:]),
                                 lambda e, ps=ps, j=j: e.transpose(out=ps[:, 256:288], in_=cbr[:, j, :], identity=ident[0:32, 0:32]),
                                 lambda e, ps=ps, j=j: e.transpose(out=ps[:, 288:320], in_=cbi[:, j, :], identity=ident[0:32, 0:32])],
                          reads=[zt1, zt2, cbr, cbi, ident], writes=[ps])
                    A(lambda e, ps=ps, j=j: e.copy(out=ZTr[:, j, :], in_=ps[:, 0:128]), [ps], [ZTr])
                    A(lambda e, ps=ps, j=j: e.copy(out=ZTi[:, j, :], in_=ps[:, 128:256]), [ps], [ZTi])
                    A(lambda e, ps=ps, j=j: e.copy(out=LCr[:, j, 32:64], in_=ps[:, 256:288]), [ps], [LCr])
                    A(lambda e, ps=ps, j=j: e.mul(out=LCi[:, j, 32:64], in_=ps[:, 288:320], mul=-1.0), [ps], [LCi])
            p.barrier()
            stop('S0')

            PC = 1024
            cosS = [sb(f"cosS{d_}", [128, S], F32) for d_ in range(2)]
            sinS = [sb(f"sinS{d_}", [128, S], F32) for d_ in range(2)]
            tur = sb("turS", [128, PC], F32); turi = sb("turiS", [128, PC], I32)
            rhoT = [sb(f"rhoT{d_}", [128, PC], F32) for d_ in range(2)]
            uT = sb("uTc", [128, S], BF16, dma=True)
            sR = [sb(f"sR{d_}", [128, S], BF16) for d_ in range(2)]
            sI = [sb(f"sI{d_}", [128, S], BF16) for d_ in range(2)]
            tmp = [sb(f"tmpS{i}", [128, PC], F32) for i in range(4)]
            wr = sb("wrS", [128, PC], BF16); wi = sb("wiS", [128, PC], BF16)
            Rr = Ring([sb(f"RrS{i}", [128, PC], F32) for i in range(2)])
            Ri = Ring([sb(f"RiS{i}", [128, PC], F32) for i in range(2)])
            zst = Ring([sb(f"zst{i}", [128, 512], BF16, dma=True) for i in range(2)])
            psB = [psum[0:2], psum[2:4]]
            psY = Ring(psum[4:8])

            def rev(a, b):
                return slice(a, None, -1) if b < 0 else slice(a, b, -1)

            for gp in range(16):
                q = gp // 4
                for d_ in range(2):
                    j = d_ * 16 + gp
                    for c in range(S // PC):
                        cs = slice(c * PC, (c + 1) * PC)
                        for (tab, addc) in ((sinS[d_], 0.0), (cosS[d_], 0.25)):
                            V(lambda e, cs=cs, j=j, addc=addc: e.tensor_scalar(out=tur[:], in0=tI[:, cs], scalar1=sm["th"][:, j:j + 1], scalar2=addc,
                                                                                 op0=ALU.mult, op1=ALU.add), [tI, sm["th"]], [tur])
                            V(lambda e: e.tensor_copy(out=turi[:], in_=tur[:]), [tur], [turi])
                            V(lambda e: e.tensor_tensor(out=tur[:], in0=tur[:], in1=turi[:], op=ALU.subtract), [tur, turi], [tur])
                            A(lambda e, tab=tab, cs=cs: e.activation(out=tab[:, cs], in_=tur[:], func=AF.Sin, scale=TWO_PI), [tur], [tab])
                    p.op("pool", lambda e, d_=d_: e.memset(rhoT[d_][:], 1.0), writes=[rhoT[d_]])
                    p.op("pool", lambda e, d_=d_, j=j: e.tensor_scalar(out=rhoT[d_][:], in0=rhoT[d_][:], scalar1=sm["rho"][:, j:j + 1], scalar2=None, op0=ALU.mult),
                         reads=[rhoT[d_], sm["rho"]], writes=[rhoT[d_]])
                for sq in range(nseq):
                    p.dma(lambda e: e.dma_start(out=uT[:], in_=uT_s[sq, q * 128:(q + 1) * 128, :]), uT, True)
                    for d_ in range(2):
                        j = d_ * 16 + gp
                        prevR = None
                        for c in range(S // PC):
                            cs = slice(c * PC, (c + 1) * PC)
                            for h in range(2):
                                if d_ == 0:
                                    usl = slice(c * PC + h * 512, c * PC + (h + 1) * 512)
                                else:
                                    a0 = S - 1 - (c * PC + h * 512)
                                    usl = rev(a0, a0 - 512)
                                for (ZT, bank) in ((ZTr, psB[0][h]), (ZTi, psB[1][h])):
                                    p.ops("pe", [lambda e, ZT=ZT, bank=bank, usl=usl, j=j: e.matmul(bank[:], lhsT=ZT[:, j, :], rhs=uT[:, usl], start=True, stop=True)],
                                          reads=[ZT, uT], writes=[bank])
                            rr = Rr.next(); ri = Ri.next()
                            for h in range(2):
                                hs = slice(h * 512, (h + 1) * 512)
                                gs = slice(c * PC + h * 512, c * PC + (h + 1) * 512)
                                br = psB[0][h]; bi = psB[1][h]
                                V(lambda e, hs=hs, gs=gs, br=br: e.tensor_tensor(out=tmp[0][:, hs], in0=br[:], in1=cosS[d_][:, gs], op=ALU.mult), [br, cosS[d_]], [tmp[0]])
                                V(lambda e, hs=hs, gs=gs, bi=bi: e.tensor_tensor(out=tmp[1][:, hs], in0=bi[:], in1=sinS[d_][:, gs], op=ALU.mult), [bi, sinS[d_]], [tmp[1]])
                                V(lambda e, hs=hs, gs=gs, bi=bi: e.tensor_tensor(out=tmp[2][:, hs], in0=bi[:], in1=cosS[d_][:, gs], op=ALU.mult), [bi, cosS[d_]], [tmp[2]])
                                V(lambda e, hs=hs, gs=gs, br=br: e.tensor_tensor(out=tmp[3][:, hs], in0=br[:], in1=sinS[d_][:, gs], op=ALU.mult), [br, sinS[d_]], [tmp[3]])
                            p.op("pool", lambda e: e.tensor_tensor(out=wr[:], in0=tmp[0][:], in1=tmp[1][:], op=ALU.add), reads=[tmp[0], tmp[1]], writes=[wr])
                            p.op("pool", lambda e: e.tensor_tensor(out=wi[:], in0=tmp[2][:], in1=tmp[3][:], op=ALU.subtract), reads=[tmp[2], tmp[3]], writes=[wi])
                            for (ro, wsrc, idx) in ((rr, wr, 0), (ri, wi, 1)):
                                ini = 0.0 if prevR is None else prevR[idx][:, PC - 1:PC]
                                rd = [rhoT[d_], wsrc] + ([] if prevR is None else [prevR[idx]])
                                V(lambda e, ro=ro, wsrc=wsrc, ini=ini: e.tensor_tensor_scan(out=ro[:], data0=rhoT[d_][:], data1=wsrc[:], initial=ini,
                                                                                            op0=ALU.mult, op1=ALU.add), rd, [ro])
                            prevR = (rr, ri)
                            G = lambda fn, rd_, wr_: p.op("pool", fn, reads=rd_, writes=wr_)
                            G(lambda e, cs=cs: e.tensor_tensor(out=tmp[0][:], in0=rr[:], in1=cosS[d_][:, cs], op=ALU.mult), [rr, cosS[d_]], [tmp[0]])
                            G(lambda e, cs=cs: e.tensor_tensor(out=tmp[1][:], in0=ri[:], in1=sinS[d_][:, cs], op=ALU.mult), [ri, sinS[d_]], [tmp[1]])
                            G(lambda e, cs=cs: e.tensor_tensor(out=sR[d_][:, cs], in0=tmp[0][:], in1=tmp[1][:], op=ALU.subtract), [tmp[0], tmp[1]], [sR[d_]])
                            G(lambda e, cs=cs: e.tensor_tensor(out=tmp[2][:], in0=ri[:], in1=cosS[d_][:, cs], op=ALU.mult), [ri, cosS[d_]], [tmp[2]])
                            G(lambda e, cs=cs: e.tensor_tensor(out=tmp[3][:], in0=rr[:], in1=sinS[d_][:, cs], op=ALU.mult), [rr, sinS[d_]], [tmp[3]])
                            G(lambda e, cs=cs: e.tensor_tensor(out=sI[d_][:, cs], in0=tmp[2][:], in1=tmp[3][:], op=ALU.add), [tmp[2], tmp[3]], [sI[d_]])
                    qq = gp % 4
                    if qq < 3:
                        rows = slice(32 * qq, 32 * qq + 32); lcs = slice(32, 64); dds = slice(32 * qq, 32 * qq + 32)
                    else:
                        rows = slice(64, 128); lcs = slice(0, 64); dds = slice(64, 128)
                    orow = slice(32 * qq, 32 * qq + 32)
                    for tb in range(S // 512):
                        fs = slice(tb * 512, (tb + 1) * 512)
                        a0 = S - 1 - tb * 512
                        bs = rev(a0, a0 - 512)
                        py = psY.next()
                        p.ops("pe", [
                            lambda e: e.matmul(py[rows, :], lhsT=LCr[:, gp, lcs], rhs=sR[0][:, fs], start=True, stop=False),
                            lambda e: e.matmul(py[rows, :], lhsT=LCi[:, gp, lcs], rhs=sI[0][:, fs], start=False, stop=False),
                            lambda e: e.matmul(py[rows, :], lhsT=LCr[:, 16 + gp, lcs], rhs=sR[1][:, bs], start=False, stop=False),
                            lambda e: e.matmul(py[rows, :], lhsT=LCi[:, 16 + gp, lcs], rhs=sI[1][:, bs], start=False, stop=False),
                            lambda e: e.matmul(py[rows, :], lhsT=Dd[:, q, dds], rhs=uT[:, fs], start=False, stop=True),
                        ], reads=[LCr, LCi, Dd, sR[0], sI[0], sR[1], sI[1], uT], writes=[py])
                        zo = zst.next()
                        A(lambda e: e.activation(out=zo[rows, :], in_=py[rows, :], func=AF.Gelu), [py], [zo])
                        p.dma(lambda e: e.dma_start(out=zT_s[sq, gp * 32:(gp + 1) * 32, fs], in_=zo[orow, :]), zo, False)
                stop(f'S_gp{gp}')
        new_phase()
        stop('S')

        with ExitStack() as es:
            def sb(name, shape, dt, dma=False, const=False):
                return p.buf(es.enter_context(nc.sbuf_tensor(name, list(shape), dt)), dma=dma, const=const)

            mstage = sb("mstage", [128, 256], F32, dma=True)
            ostage = sb("ostage", [128, 3, 64], F32, dma=True)
            maskB = sb("maskB", [128, 256], BF16); ones3 = sb("ones3", [128, 3, 64], BF16); identb = sb("identb", [128, 128], BF16)
            p.dma(lambda e: e.dma_start(out=mstage[:], in_=md["maskb"]), mstage, True)
            p.dma(lambda e: e.dma_start(out=ostage[:], in_=md["ones3"]), ostage, True)
            p.op("dve", lambda e: e.tensor_copy(out=maskB[:], in_=mstage[:]), reads=[mstage], writes=[maskB])
            p.op("dve", lambda e: e.tensor_copy(out=ones3[:], in_=ostage[:]), reads=[ostage], writes=[ones3])
            p.op("dve", lambda e: e.tensor_copy(out=identb[:], in_=ident[:]), reads=[ident], writes=[identb])
            maskB.const = True; ones3.const = True; identb.const = True
            qTr = Ring([sb(f"qTa{i}", [128, S], BF16, dma=True) for i in range(2)])
            kTr = Ring([sb(f"kTa{i}", [128, S + 2 * KPAD], BF16, dma=True) for i in range(2)])
            for b in kTr.bufs:
                p.op("pool", lambda e, b=b: e.memset(b[:], 0.0), writes=[b])
            vTr = Ring([sb(f"vTa{i}", [128, 48, 128], BF16, dma=True) for i in range(2)])
            acc = sb("acc", [128, 2, S], F32)
            rden = sb("rden", [128, S], F32)
            aTo = sb("aTo", [128, S], BF16, dma=True)
            PTr = Ring([sb(f"PT{i}", [128, 256], BF16) for i in range(4)])
            psS = Ring(psum[0:4]); psO = Ring(psum[4:8])
            SCALE = 64.0 ** -0.5
            for sq in range(nseq):
                for c in range(2):
                    for g in range(3):
                        d = DIL[g]; L = S // d; nb = L // 128 + 1
                        qT = qTr.next(); kT = kTr.next(); vT = vTr.next()
                        ch = 2 * g + c
                        p.dma(lambda e: e.dma_start(out=qT[:], in_=qT_s[sq, ch * 128:(ch + 1) * 128, :]), qT, True)
                        p.dma(lambda e: e.dma_start(out=kT[:, KPAD:KPAD + S], in_=kT_s[sq, ch * 128:(ch + 1) * 128, :]), kT, True)
                        p.dma(lambda e: e.dma_start(out=vT[:, 0:NBLK[g], :], in_=v_s[g][sq, :, :, c * 128:(c + 1) * 128]), vT, True)
                        for r in range(d):
                            for a in range(L // 128):
                                qsl = slice(r + d * 128 * a, r + d * 128 * a + d * 127 + 1, d)
                                pO = psO.next()
                                fns = []
                                pts = []
                                for hp in range(2):
                                    pb = 64 * hp
                                    pS = psS.next()
                                    ks = []
                                    for m in (a, a + 1):
                                        st = KPAD + r + d * (128 * m - 64)
                                        ks.append(slice(st, st + d * 127 + 1, d))
                                    p.ops("pe", [
                                        lambda e, pS=pS, pb=pb, ks=ks: e.matmul(pS[:, 0:128], lhsT=kT[pb:pb + 64, ks[0]], rhs=qT[pb:pb + 64, qsl], start=True, stop=False),
                                        lambda e, pS=pS, pb=pb, ks=ks: e.matmul(pS[:, 128:256], lhsT=kT[pb:pb + 64, ks[1]], rhs=qT[pb:pb + 64, qsl], start=False, stop=False),
                                        lambda e, pS=pS: e.matmul(pS[:, 0:256], lhsT=identb[:], rhs=maskB[:], start=False, stop=True),
                                    ], reads=[kT, qT, identb, maskB], writes=[pS])
                                    PT = PTr.next()
                                    p.op("act", lambda e, pS=pS, PT=PT: e.activation(out=PT[:], in_=pS[:, 0:256], func=AF.Exp, scale=SCALE), reads=[pS], writes=[PT])
                                    pts.append(PT)
                                    o1 = 1 if a == 0 else 0
                                    o2 = 2 if a + 1 == nb - 1 else 0
                                    b1 = r * nb + a; b2 = r * nb + a + 1
                                    fns += [
                                        lambda e, PT=PT, pb=pb, b1=b1, hp=hp: e.matmul(pO[pb:pb + 64, 0:128], lhsT=vT[:, b1, hp * 64:(hp + 1) * 64], rhs=PT[:, 0:128], start=True, stop=False),
                                        lambda e, PT=PT, pb=pb, b2=b2, hp=hp: e.matmul(pO[pb:pb + 64, 0:128], lhsT=vT[:, b2, hp * 64:(hp + 1) * 64], rhs=PT[:, 128:256], start=False, stop=False),
                                        lambda e, PT=PT, pb=pb, o1=o1: e.matmul(pO[pb:pb + 64, 128:256], lhsT=ones3[:, o1, :], rhs=PT[:, 0:128], start=False, stop=False),
                                        lambda e, PT=PT, pb=pb, o2=o2: e.matmul(pO[pb:pb + 64, 128:256], lhsT=ones3[:, o2, :], rhs=PT[:, 128:256], start=False, stop=True),
                                    ]
                                p.ops("pe", fns, reads=pts + [vT, ones3], writes=[pO])
                                pov = pO[:, 0:256].rearrange("p (n i) -> p n i", n=2)
                                if g == 0:
                                    p.op("dve", lambda e, pov=pov: e.tensor_copy(out=acc[:, :, qsl], in_=pov), reads=[pO], writes=[acc])
                                else:
                                    p.op("dve", lambda e, pov=pov: e.tensor_tensor(out=acc[:, :, qsl], in0=pov, in1=acc[:, :, qsl], op=ALU.add), reads=[pO, acc], writes=[acc])
                    p.op("dve", lambda e: e.reciprocal(out=rden[:], in_=acc[:, 1, :]), reads=[acc], writes=[rden])
                    p.op("dve", lambda e: e.tensor_tensor(out=aTo[:], in0=acc[:, 0, :], in1=rden[:], op=ALU.mult), reads=[acc, rden], writes=[aTo])
                    p.dma(lambda e: e.dma_start(out=aT_s[sq, c * 128:(c + 1) * 128, :], in_=aTo[:]), aTo, False)
        new_phase()
        stop('T')

        def load_w_bf16(sbf, wdst, src, K, N, tag, stg=None):
            piece = 1024 if N >= 1024 else N
            if stg is None:
                stg = Ring([sbf(f"wl_{tag}{i}", [128, piece], F32, dma=True) for i in range(2)])
            engs = ("dve", "pool", "act")
            n = 0
            for k in range(K):
                for c0 in range(0, N, piece):
                    w = min(piece, N - c0)
                    st = stg.next()
                    p.dma(lambda e, st=st, k=k, c0=c0, w=w: e.dma_start(out=st[:, 0:w], in_=src[k * 128:(k + 1) * 128, c0:c0 + w]), st, True)
                    eng = engs[n % 3]; n += 1
                    if eng == "act":
                        p.op("act", lambda e, st=st, k=k, c0=c0, w=w: e.copy(out=wdst[:, k, c0:c0 + w], in_=st[:, 0:w]), reads=[st], writes=[wdst])
                    else:
                        p.op(eng, lambda e, st=st, k=k, c0=c0, w=w: e.tensor_copy(out=wdst[:, k, c0:c0 + w], in_=st[:, 0:w]), reads=[st], writes=[wdst])
            wdst.const = True

        def layer_norm(sbufs, hpre, gB, bB, outt):
            stats, mv, rstd, hn = sbufs
            for n in range(2):
                p.op("dve", lambda e, n=n: e.bn_stats(out=stats[:, n, :], in_=hpre[:, n * 512:(n + 1) * 512]), reads=[hpre], writes=[stats])
            p.op("dve", lambda e: e.bn_aggr(out=mv[:], in_=stats[:].rearrange("p n s -> p (n s)")), reads=[stats], writes=[mv])
            p.op("act", lambda e: e.activation(out=rstd[:], in_=mv[:, 1:2], func=AF.Sqrt, bias=epsb[:, 0:1]), reads=[mv, epsb], writes=[rstd])
            p.op("dve", lambda e: e.reciprocal(out=rstd[:], in_=rstd[:]), reads=[rstd], writes=[rstd])
            p.op("dve", lambda e: e.tensor_scalar(out=hn[:], in0=hpre[:], scalar1=mv[:, 0:1], scalar2=rstd[:, 0:1], op0=ALU.subtract, op1=ALU.mult),
                 reads=[hpre, mv, rstd], writes=[hn])
            p.op("pool", lambda e: e.tensor_tensor(out=hn[:], in0=hn[:], in1=gB[:], op=ALU.mult), reads=[hn, gB], writes=[hn])
            p.op("pool", lambda e: e.tensor_tensor(out=outt[:], in0=hn[:], in1=bB[:], op=ALU.add), reads=[hn, bB], writes=[outt])

        with ExitStack() as es:
            def sb(name, shape, dt, dma=False, const=False):
                return p.buf(es.enter_context(nc.sbuf_tensor(name, list(shape), dt)), dma=dma, const=const)
            wgv = sb("wgv", [128, 4, D], BF16); wgg = sb("wgg", [128, 4, D], BF16); wab = sb("wab", [128, 2, D], BF16); wo = sb("wo", [128, 8, D], BF16)
            load_w_bf16(sb, wgv, md["wgv"], 4, D, "a"); load_w_bf16(sb, wgg, md["wgg"], 4, D, "b")
            load_w_bf16(sb, wab, md["wab"], 2, D, "c"); load_w_bf16(sb, wo, md["wo"], 8, D, "d")
            gB = sb("ln1gB", [128, D], F32, dma=True, const=True); bB = sb("ln1bB", [128, D], F32, dma=True, const=True)
            p.dma(lambda e: e.dma_start(out=gB[:], in_=md["ln1g"].partition_broadcast(128)), gB, True)
            p.dma(lambda e: e.dma_start(out=bB[:], in_=md["ln1b"].partition_broadcast(128)), bB, True)
            epsb = sb("epsb", [128, 1], F32)
            p.op("pool", lambda e: e.memset(epsb[:], LN_EPS), writes=[epsb])
            zT = sb("zTm", [128, 4, 512], BF16, dma=True); aT = sb("aTm", [128, 2, 512], BF16, dma=True); gT = sb("gTm", [128, 16, 512], BF16, dma=True)
            xs = Ring([sb(f"xm{i}", [128, D], F32, dma=True) for i in range(2)])
            mixT = sb("mixT", [128, 8, 512], BF16)
            sg = Ring([sb(f"sg{i}", [128, 512], F32) for i in range(2)])
            t1r = Ring([sb(f"t1m{i}", [128, 512], F32) for i in range(2)])
            t2r = Ring([sb(f"t2m{i}", [128, 512], F32) for i in range(2)])
            hpre = Ring([sb(f"hpre{i}", [128, D], F32) for i in range(2)])
            hout = Ring([sb(f"hout{i}", [128, D], F32, dma=True) for i in range(2)])
            hTt = Ring([sb(f"hTt{i}", [128, 8, 128], BF16, dma=True) for i in range(2)])
            lnb = (sb("st1", [128, 2, 6], F32), sb("mv1", [128, 2], F32), sb("rstd1", [128, 1], F32), sb("hn1", [128, D], F32))
            psr = Ring(psum)
            for sq in range(nseq):
                for tb in range(S // 512):
                    ts = slice(tb * 512, (tb + 1) * 512)
                    p.dma(lambda e: e.dma_start(out=zT[:], in_=zT_s[sq].rearrange("(k q) t -> q k t", q=128)[:, :, ts]), zT, True)
                    p.dma(lambda e: e.dma_start(out=aT[:], in_=aT_s[sq].rearrange("(k q) t -> q k t", q=128)[:, :, ts]), aT, True)
                    p.dma(lambda e: e.dma_start(out=gT[:], in_=gT_s[sq].rearrange("(k q) t -> q k t", q=128)[:, :, ts]), gT, True)
                    for do in range(8):
                        ds_ = slice(do * 128, (do + 1) * 128)
                        pA = psr.next(); pG = psr.next(); pB = psr.next()
                        p.ops("pe", [lambda e, k=k: e.matmul(pA[:], lhsT=wgv[:, k, ds_], rhs=zT[:, k, :], start=(k == 0), stop=(k == 3)) for k in range(4)], reads=[wgv, zT], writes=[pA])
                        p.ops("pe", [lambda e, k=k: e.matmul(pG[:], lhsT=wgg[:, k, ds_], rhs=zT[:, k, :], start=(k == 0), stop=(k == 3)) for k in range(4)], reads=[wgg, zT], writes=[pG])
                        p.ops("pe", [lambda e, k=k: e.matmul(pB[:], lhsT=wab[:, k, ds_], rhs=aT[:, k, :], start=(k == 0), stop=(k == 1)) for k in range(2)], reads=[wab, aT], writes=[pB])
                        sgt = sg.next(); t1 = t1r.next(); t2 = t2r.next()
                        p.op("act", lambda e: e.activation(out=sgt[:], in_=pG[:], func=AF.Sigmoid), reads=[pG], writes=[sgt])
                        p.op("dve", lambda e: e.tensor_tensor(out=t1[:], in0=pA[:], in1=sgt[:], op=ALU.mult), reads=[pA, sgt], writes=[t1])
                        p.op("dve", lambda e: e.tensor_tensor(out=t2[:], in0=pB[:], in1=gT[:, 8 + do, :], op=ALU.mult), reads=[pB, gT], writes=[t2])
                        p.op("pool", lambda e: e.tensor_tensor(out=t1[:], in0=t1[:], in1=gT[:, do, :], op=ALU.mult), reads=[t1, gT], writes=[t1])
                        p.op("pool", lambda e: e.tensor_tensor(out=mixT[:, do, :], in0=t1[:], in1=t2[:], op=ALU.add), reads=[t1, t2], writes=[mixT])
                    for i in range(4):
                        tok = slice(tb * 512 + i * 128, tb * 512 + (i + 1) * 128)
                        xt = xs.next()
                        p.dma(lambda e: e.dma_start(out=xt[:], in_=x_d[sq, tok, :]), xt, True)
                        hp_ = hpre.next()
                        for n in range(2):
                            ns = slice(n * 512, (n + 1) * 512)
                            po = psr.next()
                            p.ops("pe", [lambda e, k=k: e.matmul(po[:], lhsT=mixT[:, k, i * 128:(i + 1) * 128], rhs=wo[:, k, ns], start=(k == 0), stop=(k == 7)) for k in range(8)],
                                  reads=[mixT, wo], writes=[po])
                            p.op("dve", lambda e: e.scalar_tensor_tensor(out=hp_[:, ns], in0=xt[:, ns], scalar=ALPHA, in1=po[:], op0=ALU.mult, op1=ALU.add),
                                 reads=[xt, po], writes=[hp_])
                        ho = hout.next()
                        layer_norm(lnb, hp_, gB, bB, ho)
                        p.dma(lambda e: e.dma_start(out=h_s[sq, tok, :], in_=ho[:]), ho, False)
                        hT = hTt.next()
                        for kk in range(2):
                            pt = psr.next()
                            p.ops("pe", [lambda e, k4=k4: e.transpose(out=pt[:, k4 * 128:(k4 + 1) * 128], in_=ho[:, (kk * 4 + k4) * 128:(kk * 4 + k4 + 1) * 128], identity=ident[:])
                                         for k4 in range(4)], reads=[ho, ident], writes=[pt])
                            p.op("act", lambda e: e.copy(out=hT[:, kk * 4:(kk + 1) * 4, :], in_=pt[:].rearrange("p (k t) -> p k t", k=4)), reads=[pt], writes=[hT])
                        p.dma(lambda e: e.dma_start(out=hT_s[sq].rearrange("(k q) t -> q k t", q=128)[:, :, tok], in_=hT[:]), hT, False)
        new_phase()
        stop('M1')

        with ExitStack() as es:
            def sb(name, shape, dt, dma=False, const=False):
                return p.buf(es.enter_context(nc.sbuf_tensor(name, list(shape), dt)), dma=dma, const=const)
            wup = sb("wup", [128, 8, 2 * DFF], BF16); wdn = sb("wdn", [128, 22, D], BF16)
            stg_ = Ring([sb(f"wl_u{i}", [128, 1024], F32, dma=True) for i in range(2)])
            load_w_bf16(sb, wup, md["wup"], 8, 2 * DFF, "u", stg_); load_w_bf16(sb, wdn, md["wdn"], 22, D, "v", stg_)
            gB = sb("ln2gB", [128, D], F32, dma=True, const=True); bB = sb("ln2bB", [128, D], F32, dma=True, const=True)
            p.dma(lambda e: e.dma_start(out=gB[:], in_=md["ln2g"].partition_broadcast(128)), gB, True)
            p.dma(lambda e: e.dma_start(out=bB[:], in_=md["ln2b"].partition_broadcast(128)), bB, True)
            cw = sb("cw", [128, 44, 3], F32, dma=True, const=True); cbias = sb("cbias", [128, 44], F32, dma=True, const=True)
            p.dma(lambda e: e.dma_start(out=cw[:], in_=md["cw"]), cw, True)
            p.dma(lambda e: e.dma_start(out=cbias[:], in_=md["cbias"]), cbias, True)
            epsb = sb("epsb2", [128, 1], F32)
            p.op("pool", lambda e: e.memset(epsb[:], LN_EPS), writes=[epsb])
            hT = sb("hTf", [128, 8, 514], BF16, dma=True)
            hres = Ring([sb(f"hres{i}", [128, D], F32, dma=True) for i in range(1)])
            upE = Ring([sb(f"upE{i}", [128, 514], F32) for i in range(2)])
            cvr = Ring([sb(f"cv{i}", [128, 512], F32) for i in range(3)])
            actT = sb("actT", [128, 22, 512], BF16)
            ctmp = sb("ctmp", [128, 512], F32)
            opre = sb("opre", [128, D], F32)
            oout = Ring([sb(f"oout{i}", [128, D], F32, dma=True) for i in range(1)])
            lnb = (sb("st2", [128, 2, 6], F32), sb("mv2", [128, 2], F32), sb("rstd2", [128, 1], F32), sb("hn2", [128, D], F32))
            psm = Ring(psum[0:6]); psh = Ring(psum[6:8])
            for sq in range(nseq):
                for tb in range(S // 512):
                    t0 = tb * 512
                    lo = max(t0 - 1, 0); hi = min(t0 + 513, S)
                    if t0 == 0:
                        p.op("pool", lambda e: e.memset(hT[:, :, 0:1], 0.0), writes=[hT])
                    if t0 + 512 == S:
                        p.op("pool", lambda e: e.memset(hT[:, :, 513:514], 0.0), writes=[hT])
                    p.dma(lambda e: e.dma_start(out=hT[:, :, lo - (t0 - 1):hi - (t0 - 1)], in_=hT_s[sq].rearrange("(k q) t -> q k t", q=128)[:, :, lo:hi]), hT, True)
                    for c in range(22):
                        cvs = []
                        for (ch, eng) in ((c, "dve"), (22 + c, "pool")):
                            cs_ = slice(ch * 128, (ch + 1) * 128)
                            pm = psm.next(); ph = psh.next()
                            p.ops("pe", [lambda e, k=k: e.matmul(pm[:], lhsT=wup[:, k, cs_], rhs=hT[:, k, 1:513], start=(k == 0), stop=(k == 7)) for k in range(8)]
                                  + [lambda e, k=k: e.matmul(ph[:, 0:2], lhsT=wup[:, k, cs_], rhs=hT[:, k, 0:514:513], start=(k == 0), stop=(k == 7)) for k in range(8)],
                                  reads=[wup, hT], writes=[pm, ph])
                            ue = upE.next()
                            p.op("act", lambda e: e.copy(out=ue[:, 1:513], in_=pm[:]), reads=[pm], writes=[ue])
                            p.op("act", lambda e: e.copy(out=ue[:, 0:514:513], in_=ph[:, 0:2]), reads=[ph], writes=[ue])
                            cv = cvr.next()
                            p.op(eng, lambda e: e.tensor_scalar(out=cv[:], in0=ue[:, 1:513], scalar1=cw[:, ch, 1:2], scalar2=cbias[:, ch:ch + 1], op0=ALU.mult, op1=ALU.add),
                                 reads=[ue, cw, cbias], writes=[cv])
                            if eng == "dve":
                                p.op("dve", lambda e: e.scalar_tensor_tensor(out=cv[:], in0=ue[:, 0:512], scalar=cw[:, ch, 0:1], in1=cv[:], op0=ALU.mult, op1=ALU.add), reads=[ue, cw, cv], writes=[cv])
                                p.op("dve", lambda e: e.scalar_tensor_tensor(out=cv[:], in0=ue[:, 2:514], scalar=cw[:, ch, 2:3], in1=cv[:], op0=ALU.mult, op1=ALU.add), reads=[ue, cw, cv], writes=[cv])
                            else:
                                p.op("pool", lambda e: e.tensor_scalar(out=ctmp[:], in0=ue[:, 0:512], scalar1=cw[:, ch, 0:1], scalar2=None, op0=ALU.mult), reads=[ue, cw], writes=[ctmp])
                                p.op("pool", lambda e: e.tensor_tensor(out=cv[:], in0=cv[:], in1=ctmp[:], op=ALU.add), reads=[ctmp, cv], writes=[cv])
                                p.op("pool", lambda e: e.tensor_scalar(out=ctmp[:], in0=ue[:, 2:514], scalar1=cw[:, ch, 2:3], scalar2=None, op0=ALU.mult), reads=[ue, cw], writes=[ctmp])
                                p.op("pool", lambda e: e.tensor_tensor(out=cv[:], in0=cv[:], in1=ctmp[:], op=ALU.add), reads=[ctmp, cv], writes=[cv])
                            cvs.append(cv)
                        p.op("act", lambda e: e.activation(out=cvs[0][:], in_=cvs[0][:], func=AF.Gelu), reads=[cvs[0]], writes=[cvs[0]])
                        p.op("dve", lambda e: e.tensor_tensor(out=actT[:, c, :], in0=cvs[0][:], in1=cvs[1][:], op=ALU.mult), reads=cvs, writes=[actT])
                    for i in range(4):
                        tok = slice(t0 + i * 128, t0 + (i + 1) * 128)
                        hr = hres.next()
                        p.dma(lambda e: e.dma_start(out=hr[:], in_=h_s[sq, tok, :]), hr, True)
                        for n in range(2):
                            ns = slice(n * 512, (n + 1) * 512)
                            po = psm.next()
                            p.ops("pe", [lambda e, k=k: e.matmul(po[:], lhsT=actT[:, k, i * 128:(i + 1) * 128], rhs=wdn[:, k, ns], start=(k == 0), stop=(k == 21)) for k in range(22)],
                                  reads=[actT, wdn], writes=[po])
                            p.op("dve", lambda e: e.scalar_tensor_tensor(out=opre[:, ns], in0=hr[:, ns], scalar=ALPHA, in1=po[:], op0=ALU.mult, op1=ALU.add),
                                 reads=[hr, po], writes=[opre])
                        oo = oout.next()
                        layer_norm(lnb, opre, gB, bB, oo)
                        p.dma(lambda e: e.dma_start(out=out_d[sq, tok, :], in_=oo[:]), oo, False)
        new_phase()


def _host_inputs(inputs, core, nseq=NSEQ):
    f32 = np.float32
    x = np.ascontiguousarray(inputs["x"][core * nseq:(core + 1) * nseq]).astype(f32)
    pos = np.ascontiguousarray(inputs["positions"][core * nseq:(core + 1) * nseq]).astype(np.int32)
    w_in = np.asarray(inputs["w_in"][0], f32)
    b_in = np.asarray(inputs["b_in"][0], f32)
    sw = np.arange(AW).reshape(-1, 2, 32)[:, ::-1, :].reshape(-1)
    q0, k0, v0, g0 = SSMW, SSMW + AW, SSMW + 2 * AW, SSMW + 3 * AW
    cols = np.concatenate([np.arange(0, SSMW), np.arange(q0, q0 + AW), np.arange(k0, k0 + AW),
                           q0 + sw, k0 + sw, np.arange(g0, g0 + 2 * D)])
    w_fm = np.ascontiguousarray(w_in[:, cols])
    b_fm = np.ascontiguousarray(b_in[cols].reshape(NFM // 128, 128).T)
    w_v = np.ascontiguousarray(w_in[:, v0:v0 + AW])
    b_v = np.ascontiguousarray(b_in[v0:v0 + AW].reshape(1, AW))
    half = 32
    inv_freq = (10000.0 ** (-np.arange(half, dtype=np.float64) * 2.0 / 64)).astype(f32)
    invf = np.zeros((128, 2), f32)
    for pp in range(128):
        invf[pp, 0] = inv_freq[pp % 32] / TWO_PI
        invf[pp, 1] = -TWO_PI if (pp % 64) < 32 else TWO_PI
    def tile_layout(a):
        return np.ascontiguousarray(a.reshape(2, 16, 2, 64).transpose(2, 3, 0, 1).reshape(128, 32)).astype(f32)
    lre_h = tile_layout(np.asarray(inputs["ssm_lam_re"][0], f32))
    lim_h = tile_layout(np.asarray(inputs["ssm_lam_im"][0], f32))
    ldt_h = tile_layout(np.broadcast_to(np.asarray(inputs["ssm_log_dt"][0], f32)[:, :, None], (2, 32, 64)).copy())

    def bz(b):
        o = np.zeros((128, 32, 128), f32)
        b = np.asarray(b, f32)
        for dr in range(2):
            for gp in range(16):
                for gl in range(2):
                    c0 = (gp % 4) * 32 + gl * 16
                    o[gl * 64:(gl + 1) * 64, dr * 16 + gp, c0:c0 + 16] = b[dr, 2 * gp + gl]
        return o

    def cb(c):
        o = np.zeros((32, 32, 128), f32)
        c = np.asarray(c, f32)
        for dr in range(2):
            for gp in range(16):
                for gl in range(2):
                    o[gl * 16:(gl + 1) * 16, dr * 16 + gp, gl * 64:(gl + 1) * 64] = c[dr, 2 * gp + gl]
        return o
    ssm = {"lre_h": lre_h, "lim_h": lim_h, "ldt_h": ldt_h,
           "bzr_h": bz(inputs["ssm_b_re"][0]), "bzi_h": bz(inputs["ssm_b_im"][0]),
           "cbr_h": cb(inputs["ssm_c_re"][0]), "cbi_h": cb(inputs["ssm_c_im"][0]),
           "dsk_h": np.ascontiguousarray(np.asarray(inputs["ssm_d"][0], f32).reshape(4, 128).T),
           "iota_h": np.arange(S, dtype=f32).reshape(1, S)}
    d = {"x": x, "pos": pos, "ident": np.eye(128, dtype=f32), "invf": invf,
         "w_in_fm": w_fm, "b_fm": b_fm, "w_v": w_v, "b_v": b_v}
    d.update(ssm)
    ii = np.arange(128)[:, None]; jj = np.arange(128)[None, :]
    maskb = np.concatenate([np.where(ii >= jj, 0.0, -30000.0), np.where(ii <= jj, 0.0, -30000.0)], axis=1).astype(f32)
    ones3 = np.zeros((128, 3, 64), f32)
    ones3[:, 0, :] = 1.0; ones3[64:, 1, :] = 1.0; ones3[:64, 2, :] = 1.0
    g = lambda n: np.ascontiguousarray(np.asarray(inputs[n][0], f32))
    cwh = np.ascontiguousarray(g("conv_w").reshape(3, 44, 128).transpose(2, 1, 0))
    cbh = np.ascontiguousarray(g("conv_b").reshape(44, 128).T)
    d.update({"maskb_h": maskb, "ones3_h": ones3, "wgv_h": g("w_glu_v"), "wgg_h": g("w_glu_g"), "wab_h": g("w_attn_br"), "wo_h": g("w_out"),
              "ln1g_h": g("ln1_g").reshape(1, D), "ln1b_h": g("ln1_b").reshape(1, D), "ln2g_h": g("ln2_g").reshape(1, D), "ln2b_h": g("ln2_b").reshape(1, D),
              "wup_h": g("w_up"), "wdn_h": g("w_down"), "cw_h": cwh, "cb_h": cbh})
    return d


def kernel(**inputs):
    nc = build()
    in_maps = [_host_inputs(inputs, c) for c in range(NCORES)]
    res = run_bass_kernel_spmd(nc, in_maps, core_ids=list(range(NCORES)))
    out = np.concatenate([r["out"] for r in res.results], axis=0)
    return out.astype(np.float32)
```

```python
import math
from contextlib import ExitStack

import numpy as np
import concourse.bass as bass
import concourse.mybir as mybir
from concourse.bass_utils import run_bass_kernel_spmd

F32 = mybir.dt.float32
BF16 = mybir.dt.bfloat16
I32 = mybir.dt.int32
AF = mybir.ActivationFunctionType
ALU = mybir.AluOpType
AX = mybir.AxisListType

S = 4096
D = 1024
NCORES = 8
NSEQ = 2
SSMW = 512
AW = 768
DFF = 2816
NFM = 5632
ALPHA = 2.0 ** 0.25
LN_EPS = 1e-5
TWO_PI = 2.0 * math.pi
DIL = (1, 4, 16)
KPAD = 1024


class Buf:
    __slots__ = ("t", "w", "r", "dsem", "const")

    def __init__(self, t, dsem=None, const=False):
        self.t = t
        self.w = None
        self.r = {}
        self.dsem = dsem
        self.const = const

    def __getitem__(self, k):
        return self.t[k]


class Prog:
    ENG = ("pe", "act", "dve", "pool", "sp")

    def __init__(self, nc, es, n_dsem=72):
        self.nc = nc
        self.engobj = {'pe': nc.tensor, 'act': nc.scalar, 'dve': nc.vector, 'pool': nc.gpsimd, 'sp': nc.sync}
        self.ninst = 0
        self.stopped = False
        self.esem = {e: es.enter_context(nc.semaphore("es_" + e)) for e in ("pe", "act", "dve", "pool")}
        self.ecount = {e: 0 for e in self.esem}
        self.dsems = [es.enter_context(nc.semaphore(f"ds{i}")) for i in range(n_dsem)]
        self.dcount = {id(s): 0 for s in self.dsems}
        self.dnext = 0
        self.waited = {e: {} for e in self.ENG}
        self.semobj = {}
        for s in list(self.esem.values()) + self.dsems:
            self.semobj[id(s)] = s

    def buf(self, t, dma=False, const=False):
        ds = None
        if dma:
            assert self.dnext < len(self.dsems), "out of DMA semaphores in this phase"
            ds = self.dsems[self.dnext]
            self.dnext += 1
        return Buf(t, ds, const)

    def _deps(self, reads, writes):
        deps = {}

        def add(ev):
            if ev is None:
                return
            k, v = ev
            if deps.get(k, 0) < v:
                deps[k] = v
        for b in reads:
            add(b.w)
        for b in writes:
            add(b.w)
            for k, v in b.r.items():
                add((k, v))
        return deps

    def _record(self, ev, reads, writes):
        for b in writes:
            b.w = ev
            b.r = {}
        for b in reads:
            if b.const:
                continue
            if b.r.get(ev[0], 0) < ev[1]:
                b.r[ev[0]] = ev[1]

    def _emit(self, eng, deps, fn, inc):
        e = self.engobj[eng]
        wd = self.waited[eng]
        own = id(self.esem[eng]) if eng in self.esem else None
        for k, v in deps.items():
            if eng == "pe" and k == own:
                continue
            if wd.get(k, 0) >= v:
                continue
            wd[k] = v
            e.wait_ge(self.semobj[k], v)
        if fn is None:
            return
        ins = fn(e)
        if inc is not None:
            ins.then_inc(inc[0], inc[1])
        self.ninst += 1

    def op(self, eng, fn, reads=(), writes=()):
        if self.stopped:
            return None
        deps = self._deps(reads, writes)
        self.ecount[eng] += 1
        sem = self.esem[eng]
        ev = (id(sem), self.ecount[eng])
        self._emit(eng, deps, fn, (sem, 1))
        self._record(ev, reads, writes)
        return ev

    def ops(self, eng, fns, reads=(), writes=()):
        assert eng == "pe"
        if self.stopped:
            return None
        deps = self._deps(reads, writes)
        for fn in fns[:-1]:
            self._emit(eng, deps, fn, None)
            deps = {}
        self.ecount[eng] += 1
        sem = self.esem[eng]
        ev = (id(sem), self.ecount[eng])
        self._emit(eng, deps, fns[-1], (sem, 1))
        self._record(ev, reads, writes)
        return ev

    def dma(self, fn, sb, load, reads=(), writes=(), q="sp"):
        if self.stopped:
            return None
        reads = list(reads)
        writes = list(writes)
        if load:
            writes.append(sb)
        else:
            reads.append(sb)
        deps = self._deps(reads, writes)
        sem = sb.dsem
        assert sem is not None
        self.dcount[id(sem)] += 16
        ev = (id(sem), self.dcount[id(sem)])
        self._emit(q, deps, fn, (sem, 16))
        self._record(ev, reads, writes)
        return ev

    def barrier(self):
        allev = {}
        for e, s in self.esem.items():
            if self.ecount[e]:
                allev[id(s)] = self.ecount[e]
        for s in self.dsems:
            if self.dcount[id(s)]:
                allev[id(s)] = self.dcount[id(s)]
        for eng in self.ENG:
            self._emit(eng, allev, None, None)
        self.dnext = 0

    def emit(self):
        pass


class StopBuild(Exception):
    pass


class Ring:
    def __init__(self, bufs):
        self.bufs = bufs
        self.i = 0

    def next(self):
        b = self.bufs[self.i % len(self.bufs)]
        self.i += 1
        return b


def build(nseq=NSEQ, debug=False, stop_after=None):
    nc = bass.Bass("TRN2", target_bir_lowering=False)

    def din(name, shape, dt=F32):
        return nc.dram_tensor(name, list(shape), dt, kind="ExternalInput").ap()

    dbg_kind = "ExternalOutput" if debug else "Internal"

    def dscr(name, shape, dt):
        return nc.dram_tensor(name, list(shape), dt, kind=dbg_kind).ap()

    x_d = din("x", [nseq, S, D])
    pos_d = din("pos", [nseq, S], I32)
    ident_d = din("ident", [128, 128])
    invf_d = din("invf", [128, 2])
    w_in_d = din("w_in_fm", [D, NFM])
    b_fm_d = din("b_fm", [128, NFM // 128])
    w_v_d = din("w_v", [D, AW])
    b_v_d = din("b_v", [1, AW])
    out_d = nc.dram_tensor("out", [nseq, S, D], F32, kind="ExternalOutput").ap()
    ssm_d = dict(
        lre=din("lre_h", [128, 32]), lim=din("lim_h", [128, 32]), ldt=din("ldt_h", [128, 32]),
        bzr=din("bzr_h", [128, 32, 128]), bzi=din("bzi_h", [128, 32, 128]),
        cbr=din("cbr_h", [32, 32, 128]), cbi=din("cbi_h", [32, 32, 128]),
        dsk=din("dsk_h", [128, 4]), iota=din("iota_h", [1, S]))
    zT_s = dscr("zT_s", [nseq, SSMW, S], BF16)
    aT_s = dscr("aT_s", [nseq, 256, S], BF16)
    h_s = dscr("h_s", [nseq, S, D], F32)
    hT_s = dscr("hT_s", [nseq, D, S], BF16)
    md = dict(maskb=din("maskb_h", [128, 256]), ones3=din("ones3_h", [128, 3, 64]),
              wgv=din("wgv_h", [512, D]), wgg=din("wgg_h", [512, D]), wab=din("wab_h", [256, D]), wo=din("wo_h", [D, D]),
              ln1g=din("ln1g_h", [1, D]), ln1b=din("ln1b_h", [1, D]), ln2g=din("ln2g_h", [1, D]), ln2b=din("ln2b_h", [1, D]),
              wup=din("wup_h", [D, 2 * DFF]), wdn=din("wdn_h", [DFF, D]), cw=din("cw_h", [128, 44, 3]), cbias=din("cb_h", [128, 44]),
              aT_s=aT_s, h_s=h_s, hT_s=hT_s)

    xT_s = dscr("xT_s", [nseq, D, S], BF16)
    uT_s = dscr("uT_s", [nseq, SSMW, S], BF16)
    qT_s = dscr("qT_s", [nseq, AW, S], BF16)
    kT_s = dscr("kT_s", [nseq, AW, S], BF16)
    gT_s = dscr("gT_s", [nseq, 2 * D, S], BF16)
    NBLK = [d * (S // d // 128 + 1) for d in DIL]
    v_s = [dscr(f"v_s{g}", [nseq, 128, NBLK[g], 256], BF16) for g in range(3)]

    with ExitStack() as es0:
        p = Prog(nc, es0)
        psum = [p.buf(es0.enter_context(nc.psum_tensor(f"ps{i}", [128, 512], F32))) for i in range(8)]
        ident = p.buf(es0.enter_context(nc.sbuf_tensor("ident_sb", [128, 128], F32)), dma=True, const=True)
        p.dma(lambda e: e.dma_start(out=ident[:], in_=ident_d), ident, True)
        p.dnext = 1

        def new_phase():
            p.barrier()
            p.dnext = 1

        def stop(tag):
            if stop_after == tag:
                p.stopped = True

        try:
            _phases(nc, p, psum, ident, nseq, locals_d=dict(x_d=x_d, pos_d=pos_d, invf_d=invf_d, w_in_d=w_in_d, b_fm_d=b_fm_d, w_v_d=w_v_d, b_v_d=b_v_d, out_d=out_d, xT_s=xT_s, uT_s=uT_s, qT_s=qT_s, kT_s=kT_s, gT_s=gT_s, v_s=v_s, NBLK=NBLK, ssm_d=ssm_d, zT_s=zT_s, md=md), new_phase=new_phase, stop=stop)
        except StopBuild:
            pass
        p.stopped = False
        p.barrier()
    print('instructions', p.ninst)
    return nc


def _phases(nc, p, psum, ident, nseq, locals_d, new_phase, stop):
    globals_ = locals_d
    x_d = globals_['x_d']; pos_d = globals_['pos_d']; invf_d = globals_['invf_d']; w_in_d = globals_['w_in_d']; b_fm_d = globals_['b_fm_d']
    w_v_d = globals_['w_v_d']; b_v_d = globals_['b_v_d']; out_d = globals_['out_d']; xT_s = globals_['xT_s']; uT_s = globals_['uT_s']
    qT_s = globals_['qT_s']; kT_s = globals_['kT_s']; gT_s = globals_['gT_s']; v_s = globals_['v_s']; NBLK = globals_['NBLK']
    ssm_d = globals_['ssm_d']; zT_s = globals_['zT_s']; md = globals_['md']
    aT_s = md['aT_s']; h_s = md['h_s']; hT_s = md['hT_s']
    if True:

        with ExitStack() as es:
            def sb(name, shape, dt, dma=False, const=False):
                return p.buf(es.enter_context(nc.sbuf_tensor(name, list(shape), dt)), dma=dma, const=const)

            wA = sb("wA", [128, 8, NFM], BF16)
            bfm = sb("bfm", [128, NFM // 128], F32, dma=True)
            invf = sb("invf_sb", [128, 2], F32, dma=True)
            p.dma(lambda e: e.dma_start(out=bfm[:], in_=b_fm_d), bfm, True)
            p.dma(lambda e: e.dma_start(out=invf[:], in_=invf_d), invf, True)
            WP = 1408
            wst = Ring([sb(f"wst{i}", [128, WP], F32, dma=True) for i in range(3)])
            cast_engs = ("dve", "act", "pool")
            ci = 0
            for k in range(8):
                for c in range(NFM // WP):
                    st = wst.next()
                    p.dma(lambda e, st=st, k=k, c=c: e.dma_start(
                        out=st[:], in_=w_in_d[k * 128:(k + 1) * 128, c * WP:(c + 1) * WP]), st, True)
                    eng = cast_engs[ci % 3]
                    ci += 1
                    if eng == "act":
                        p.op("act", lambda e, st=st, k=k, c=c: e.copy(out=wA[:, k, c * WP:(c + 1) * WP], in_=st[:]),
                             reads=[st], writes=[wA])
                    else:
                        p.op(eng, lambda e, st=st, k=k, c=c: e.tensor_copy(out=wA[:, k, c * WP:(c + 1) * WP], in_=st[:]),
                             reads=[st], writes=[wA])
            wA.const = True
            stop('A0')

            cosT = sb("cosT", [128, S], F32)
            sinT = sb("sinT", [128, S], F32)
            posi = sb("posi", [128, 1024], I32, dma=True)
            tur = sb("tur", [128, 1024], F32)
            turi = sb("turi", [128, 1024], I32)
            xs = [sb(f"xs{i}", [128, D], F32, dma=True) for i in range(4)]
            xT = Ring([sb(f"xT{j}", [128, 8, 512], BF16, dma=True) for j in range(2)])
            ev_bf = Ring([sb(f"evbf{j}", [128, 512], BF16, dma=True) for j in range(8)])
            rt = Ring([sb(f"rt{j}", [128, 512], F32) for j in range(4)])
            psr = Ring(psum)

            for sq in range(nseq):
                for c in range(S // 1024):
                    cs = slice(c * 1024, (c + 1) * 1024)
                    p.dma(lambda e, cs=cs: e.dma_start(out=posi[:], in_=pos_d[sq:sq + 1, cs].partition_broadcast(128)), posi, True)
                    for (tab, addc, scol) in ((sinT, 0.0, 1), (cosT, 0.25, None)):
                        p.op("dve", lambda e: e.tensor_copy(out=tur[:], in_=posi[:]), reads=[posi], writes=[tur])
                        p.op("dve", lambda e, addc=addc: e.tensor_scalar(out=tur[:], in0=tur[:], scalar1=invf[:, 0:1], scalar2=addc,
                                                                          op0=ALU.mult, op1=ALU.add), reads=[tur, invf], writes=[tur])
                        p.op("dve", lambda e: e.tensor_copy(out=turi[:], in_=tur[:]), reads=[tur], writes=[turi])
                        p.op("dve", lambda e: e.tensor_tensor(out=tur[:], in0=tur[:], in1=turi[:], op=ALU.subtract),
                             reads=[tur, turi], writes=[tur])
                        if scol is not None:
                            p.op("act", lambda e, tab=tab, cs=cs: e.activation(out=tab[:, cs], in_=tur[:], func=AF.Sin, scale=invf[:, 1:2]),
                                 reads=[tur, invf], writes=[tab])
                        else:
                            p.op("act", lambda e, tab=tab, cs=cs: e.activation(out=tab[:, cs], in_=tur[:], func=AF.Sin, scale=TWO_PI),
                                 reads=[tur], writes=[tab])
                stop('A1')
                def load_x(tb_):
                    for i in range(4):
                        p.dma(lambda e, i=i: e.dma_start(out=xs[i][:], in_=x_d[sq, tb_ * 512 + i * 128:tb_ * 512 + (i + 1) * 128, :]), xs[i], True)
                load_x(0)
                for tb in range(S // 512):
                    t0 = tb * 512
                    ts = slice(t0, t0 + 512)
                    xtile = xs
                    xTb = xT.next()
                    for k in range(8):
                        ps = psr.next()
                        p.ops("pe", [lambda e, ps=ps, i=i, k=k: e.transpose(out=ps[:, i * 128:(i + 1) * 128],
                                                                           in_=xtile[i][:, k * 128:(k + 1) * 128], identity=ident[:])
                                     for i in range(4)], reads=xtile + [ident], writes=[ps])
                        if k % 2 == 0:
                            p.op("act", lambda e, ps=ps, k=k: e.copy(out=xTb[:, k, :], in_=ps[:]), reads=[ps], writes=[xTb])
                        else:
                            p.op("dve", lambda e, ps=ps, k=k: e.tensor_copy(out=xTb[:, k, :], in_=ps[:]), reads=[ps], writes=[xTb])
                    if tb + 1 < S // 512:
                        load_x(tb + 1)
                    p.dma(lambda e: e.dma_start(out=xT_s[sq].rearrange("(k q) t -> q k t", q=128)[:, :, ts], in_=xTb[:]), xTb, False)

                    def proj(fo):
                        ps = psr.next()
                        p.ops("pe", [lambda e, ps=ps, k=k: e.matmul(ps[:], lhsT=wA[:, k, fo * 128:(fo + 1) * 128], rhs=xTb[:, k, :],
                                                                      start=(k == 0), stop=(k == 7)) for k in range(8)],
                              reads=[wA, xTb], writes=[ps])
                        return ps

                    for fo in range(4):
                        ps = proj(fo)
                        o = ev_bf.next()
                        p.op("act", lambda e, ps=ps, o=o, fo=fo: e.activation(out=o[:], in_=ps[:], func=AF.Identity, bias=bfm[:, fo:fo + 1]),
                             reads=[ps, bfm], writes=[o])
                        p.dma(lambda e, o=o, fo=fo: e.dma_start(out=uT_s[sq, fo * 128:(fo + 1) * 128, ts], in_=o[:]), o, False)
                    for which, dst in ((0, qT_s), (1, kT_s)):
                        for c in range(6):
                            fo = 4 + which * 6 + c
                            psa = proj(fo)
                            psb = proj(fo + 12)
                            t1 = rt.next()
                            t2 = rt.next()
                            p.op("dve", lambda e, psa=psa, t1=t1, fo=fo: e.scalar_tensor_tensor(
                                out=t1[:], in0=psa[:], scalar=bfm[:, fo:fo + 1], in1=cosT[:, ts], op0=ALU.add, op1=ALU.mult),
                                reads=[psa, bfm, cosT], writes=[t1])
                            p.op("dve", lambda e, psb=psb, t2=t2, fo=fo: e.scalar_tensor_tensor(
                                out=t2[:], in0=psb[:], scalar=bfm[:, fo + 12:fo + 13], in1=sinT[:, ts], op0=ALU.add, op1=ALU.mult),
                                reads=[psb, bfm, sinT], writes=[t2])
                            o = ev_bf.next()
                            p.op("pool", lambda e, o=o, t1=t1, t2=t2: e.tensor_tensor(out=o[:], in0=t1[:], in1=t2[:], op=ALU.add),
                                 reads=[t1, t2], writes=[o])
                            p.dma(lambda e, o=o, c=c, dst=dst: e.dma_start(out=dst[sq, c * 128:(c + 1) * 128, ts], in_=o[:]), o, False)
                    for c in range(16):
                        fo = 28 + c
                        ps = proj(fo)
                        o = ev_bf.next()
                        p.op("act", lambda e, ps=ps, o=o, fo=fo: e.activation(out=o[:], in_=ps[:], func=AF.Sigmoid, bias=bfm[:, fo:fo + 1]),
                             reads=[ps, bfm], writes=[o])
                        p.dma(lambda e, o=o, c=c: e.dma_start(out=gT_s[sq, c * 128:(c + 1) * 128, ts], in_=o[:]), o, False)
                    stop(f'A2_{tb}')
        new_phase()
        stop('A')

        with ExitStack() as es:
            def sb(name, shape, dt, dma=False, const=False):
                return p.buf(es.enter_context(nc.sbuf_tensor(name, list(shape), dt)), dma=dma, const=const)

            wV = sb("wV", [128, 8, AW], BF16)
            wvst = Ring([sb(f"wvst{i}", [128, AW], F32, dma=True) for i in range(2)])
            for k in range(8):
                st = wvst.next()
                p.dma(lambda e, st=st, k=k: e.dma_start(out=st[:], in_=w_v_d[k * 128:(k + 1) * 128, :]), st, True)
                p.op("dve", lambda e, st=st, k=k: e.tensor_copy(out=wV[:, k, :], in_=st[:]), reads=[st], writes=[wV])
            wV.const = True
            bv = sb("bv", [128, AW], F32, dma=True, const=True)
            p.dma(lambda e: e.dma_start(out=bv[:], in_=b_v_d.partition_broadcast(128)), bv, True)
            xTf = sb("xTf", [128, 8, S], BF16, dma=True)
            VCH = 12
            vring = Ring([sb(f"vstg{j}", [128, VCH, 256], BF16, dma=True) for j in range(2)])
            psr = Ring(psum)
            for sq in range(nseq):
                p.dma(lambda e: e.dma_start(out=xTf[:], in_=xT_s[sq].rearrange("(k q) t -> q k t", q=128)), xTf, True)
                for g in range(3):
                    d = DIL[g]
                    L = S // d
                    nb = L // 128 + 1
                    blocks = [(r, m) for r in range(d) for m in range(nb)]
                    for c0 in range(0, len(blocks), VCH):
                        chunk = blocks[c0:c0 + VCH]
                        stg = vring.next()
                        p.op("pool", lambda e, stg=stg: e.memset(stg[:], 0.0), writes=[stg])
                        for j, (r, m) in enumerate(chunk):
                            lo = 64 + 128 * (m - 1)
                            i0 = max(0, -lo)
                            i1 = min(128, L - lo)
                            M = i1 - i0
                            tok0 = r + d * (lo + i0)
                            ps = psr.next()
                            p.ops("pe", [lambda e, ps=ps, k=k, tok0=tok0, M=M, i0=i0, d=d, g=g: e.matmul(
                                ps[i0:i0 + M, 0:256], lhsT=xTf[:, k, tok0:tok0 + d * (M - 1) + 1:d], rhs=wV[:, k, g * 256:(g + 1) * 256],
                                start=(k == 0), stop=(k == 7)) for k in range(8)], reads=[xTf, wV], writes=[ps])
                            p.op("dve", lambda e, ps=ps, stg=stg, j=j, i0=i0, M=M, g=g: e.tensor_tensor(
                                out=stg[i0:i0 + M, j, :], in0=ps[i0:i0 + M, 0:256], in1=bv[i0:i0 + M, g * 256:(g + 1) * 256], op=ALU.add),
                                reads=[ps, bv], writes=[stg])
                        p.dma(lambda e, stg=stg, c0=c0, n=len(chunk), g=g: e.dma_start(out=v_s[g][sq, :, c0:c0 + n, :], in_=stg[:, 0:n, :]), stg, False)
                        stop(f'V{g}_{c0}')
                    stop(f'V{g}')
        new_phase()

        with ExitStack() as es:
            def sb(name, shape, dt, dma=False, const=False):
                return p.buf(es.enter_context(nc.sbuf_tensor(name, list(shape), dt)), dma=dma, const=const)

            NT = 32
            lre = sb("lre", [128, NT], F32, dma=True); lim = sb("lim", [128, NT], F32, dma=True); ldt = sb("ldt", [128, NT], F32, dma=True)
            p.dma(lambda e: e.dma_start(out=lre[:], in_=ssm_d["lre"]), lre, True)
            p.dma(lambda e: e.dma_start(out=lim[:], in_=ssm_d["lim"]), lim, True)
            p.dma(lambda e: e.dma_start(out=ldt[:], in_=ssm_d["ldt"]), ldt, True)
            dsk = sb("dsk", [128, 4], F32, dma=True)
            p.dma(lambda e: e.dma_start(out=dsk[:], in_=ssm_d["dsk"]), dsk, True)
            tI = sb("tI", [128, S], F32, dma=True, const=True)
            p.dma(lambda e: e.dma_start(out=tI[:], in_=ssm_d["iota"].partition_broadcast(128)), tI, True)
            sm = {n: sb("sm_" + n, [128, NT], F32) for n in
                  ("dt", "xr", "xi", "rho", "th", "t0", "t1", "f", "sinx", "cosx", "sinh", "em1", "am1", "abi", "den", "kr", "ki", "u0", "u1")}
            smi = sb("smi", [128, NT], I32)

            def V(fn, reads, writes):
                return p.op("dve", fn, reads=reads, writes=writes)

            def A(fn, reads, writes):
                return p.op("act", fn, reads=reads, writes=writes)

            def tt(o, a, b, op):
                V(lambda e: e.tensor_tensor(out=o[:], in0=a[:], in1=b[:], op=op), [a, b], [o])

            def tsc(o, a, s1, op0, s2=None, op1=None):
                if op1 is None:
                    V(lambda e: e.tensor_scalar(out=o[:], in0=a[:], scalar1=s1, scalar2=None, op0=op0), [a], [o])
                else:
                    V(lambda e: e.tensor_scalar(out=o[:], in0=a[:], scalar1=s1, scalar2=s2, op0=op0, op1=op1), [a], [o])

            def frac_sin(o, turns_src, mul, add):
                tsc(sm["t0"], turns_src, mul, ALU.mult, add, ALU.add)
                V(lambda e: e.tensor_copy(out=smi[:], in_=sm["t0"][:]), [sm["t0"]], [smi])
                tt(sm["f"], sm["t0"], smi, ALU.subtract)
                A(lambda e: e.activation(out=o[:], in_=sm["f"][:], func=AF.Sin, scale=TWO_PI), [sm["f"]], [o])

            A(lambda e: e.activation(out=sm["dt"][:], in_=ldt[:], func=AF.Exp), [ldt], [sm["dt"]])
            tt(sm["xr"], lre, sm["dt"], ALU.mult)
            tt(sm["xi"], lim, sm["dt"], ALU.mult)
            A(lambda e: e.activation(out=sm["rho"][:], in_=sm["xr"][:], func=AF.Exp), [sm["xr"]], [sm["rho"]])
            tsc(sm["th"], sm["xi"], 1.0 / TWO_PI, ALU.mult)
            frac_sin(sm["sinx"], sm["th"], 1.0, 0.0)
            frac_sin(sm["cosx"], sm["th"], 1.0, 0.25)
            frac_sin(sm["sinh"], sm["th"], 0.5, 0.0)
            tsc(sm["em1"], sm["xr"], 0.2, ALU.mult, 1.0, ALU.add)
            for cdiv in (0.25, 1.0 / 3.0, 0.5):
                tt(sm["em1"], sm["em1"], sm["xr"], ALU.mult)
                tsc(sm["em1"], sm["em1"], cdiv, ALU.mult, 1.0, ALU.add)
            tt(sm["em1"], sm["em1"], sm["xr"], ALU.mult)
            tt(sm["am1"], sm["em1"], sm["cosx"], ALU.mult)
            tt(sm["u0"], sm["sinh"], sm["sinh"], ALU.mult)
            V(lambda e: e.scalar_tensor_tensor(out=sm["am1"][:], in0=sm["u0"][:], scalar=-2.0, in1=sm["am1"][:], op0=ALU.mult, op1=ALU.add),
              [sm["u0"], sm["am1"]], [sm["am1"]])
            tt(sm["abi"], sm["rho"], sm["sinx"], ALU.mult)
            tt(sm["den"], lre, lre, ALU.mult)
            tt(sm["u0"], lim, lim, ALU.mult)
            tt(sm["den"], sm["den"], sm["u0"], ALU.add)
            V(lambda e: e.reciprocal(out=sm["den"][:], in_=sm["den"][:]), [sm["den"]], [sm["den"]])
            tt(sm["u0"], sm["am1"], lre, ALU.mult)
            tt(sm["u1"], sm["abi"], lim, ALU.mult)
            tt(sm["u0"], sm["u0"], sm["u1"], ALU.add)
            tt(sm["kr"], sm["u0"], sm["den"], ALU.mult)
            tt(sm["u0"], sm["abi"], lre, ALU.mult)
            tt(sm["u1"], sm["am1"], lim, ALU.mult)
            tt(sm["u0"], sm["u0"], sm["u1"], ALU.subtract)
            tt(sm["ki"], sm["u0"], sm["den"], ALU.mult)
            tsc(sm["t1"], sm["ki"], -1.0, ALU.mult)

            ZTr = sb("ZTr", [128, NT, 128], BF16); ZTi = sb("ZTi", [128, NT, 128], BF16)
            LCr = sb("LCr", [128, NT, 64], BF16); LCi = sb("LCi", [128, NT, 64], BF16)
            p.op("pool", lambda e: e.memset(LCr[:], 0.0), writes=[LCr])
            p.op("pool", lambda e: e.memset(LCi[:], 0.0), writes=[LCi])
            Dd = sb("Dd", [128, 4, 128], BF16)
            for q in range(4):
                V(lambda e, q=q: e.tensor_scalar(out=Dd[:, q, :], in0=ident[:], scalar1=dsk[:, q:q + 1], scalar2=None, op0=ALU.mult),
                  [ident, dsk], [Dd])
            psr = Ring(psum)
            with ExitStack() as es2:
                bzr = p.buf(es2.enter_context(nc.sbuf_tensor("bzr", [128, NT, 128], F32)), dma=True)
                bzi = p.buf(es2.enter_context(nc.sbuf_tensor("bzi", [128, NT, 128], F32)), dma=True)
                cbr = p.buf(es2.enter_context(nc.sbuf_tensor("cbr", [32, NT, 128], F32)), dma=True)
                cbi = p.buf(es2.enter_context(nc.sbuf_tensor("cbi", [32, NT, 128], F32)), dma=True)
                zt1 = p.buf(es2.enter_context(nc.sbuf_tensor("zt1", [128, 128], F32)))
                zt2 = p.buf(es2.enter_context(nc.sbuf_tensor("zt2", [128, 128], F32)))
                p.dma(lambda e: e.dma_start(out=bzr[:], in_=ssm_d["bzr"]), bzr, True)
                p.dma(lambda e: e.dma_start(out=bzi[:], in_=ssm_d["bzi"]), bzi, True)
                p.dma(lambda e: e.dma_start(out=cbr[:], in_=ssm_d["cbr"]), cbr, True)
                p.dma(lambda e: e.dma_start(out=cbi[:], in_=ssm_d["cbi"]), cbi, True)
                for j in range(NT):
                    V(lambda e, j=j: e.tensor_scalar(out=zt1[:], in0=bzr[:, j, :], scalar1=sm["kr"][:, j:j + 1], scalar2=None, op0=ALU.mult),
                      [bzr, sm["kr"]], [zt1])
                    V(lambda e, j=j: e.scalar_tensor_tensor(out=zt1[:], in0=bzi[:, j, :], scalar=sm["t1"][:, j:j + 1], in1=zt1[:], op0=ALU.mult, op1=ALU.add),
                      [bzi, sm["t1"], zt1], [zt1])
                    V(lambda e, j=j: e.tensor_scalar(out=zt2[:], in0=bzi[:, j, :], scalar1=sm["kr"][:, j:j + 1], scalar2=None, op0=ALU.mult),
                      [bzi, sm["kr"]], [zt2])
                    V(lambda e, j=j: e.scalar_tensor_tensor(out=zt2[:], in0=bzr[:, j, :], scalar=sm["ki"][:, j:j + 1], in1=zt2[:], op0=ALU.mult, op1=ALU.add),
                      [bzr, sm["ki"], zt2], [zt2])
                    ps = psr.next()
                    p.ops("pe", [lambda e, ps=ps: e.transpose(out=ps[:, 0:128], in_=zt1[:], identity=ident[:]),
                                 lambda e, ps=ps: e.transpose(out=ps[:, 128:256], in_=zt2[:], identity=ident[:]),
                                 lambda e, ps=ps, j=j: e.transpose(out=ps[:, 256:288], in_=cbr[:, j, :], identity=ident[0:32, 0:32]),
                                 lambda e, ps=ps, j=j: e.transpose(out=ps[:, 288:320], in_=cbi[:, j, :], identity=ident[0:32, 0:32])],
                          reads=[zt1, zt2, cbr, cbi, ident], writes=[ps])
                    A(lambda e, ps=ps, j=j: e.copy(out=ZTr[:, j, :], in_=ps[:, 0:128]), [ps], [ZTr])
                    A(lambda e, ps=ps, j=j: e.copy(out=ZTi[:, j, :], in_=ps[:, 128:256]), [ps], [ZTi])
                    A(lambda e, ps=ps, j=j: e.copy(out=LCr[:, j, 32:64], in_=ps[:, 256:288]), [ps], [LCr])
                    A(lambda e, ps=ps, j=j: e.mul(out=LCi[:, j, 32:64], in_=ps[:, 288:320], mul=-1.0), [ps], [LCi])
            p.barrier()
            stop('S0')

            PC = 1024
            cosS = [sb(f"cosS{d_}", [128, S], F32) for d_ in range(2)]
            sinS = [sb(f"sinS{d_}", [128, S], F32) for d_ in range(2)]
            tur = sb("turS", [128, PC], F32); turi = sb("turiS", [128, PC], I32)
            rhoT = [sb(f"rhoT{d_}", [128, PC], F32) for d_ in range(2)]
            uT = sb("uTc", [128, S], BF16, dma=True)
            sR = [sb(f"sR{d_}", [128, S], BF16) for d_ in range(2)]
            sI = [sb(f"sI{d_}", [128, S], BF16) for d_ in range(2)]
            tmp = [sb(f"tmpS{i}", [128, PC], F32) for i in range(4)]
            wr = sb("wrS", [128, PC], BF16); wi = sb("wiS", [128, PC], BF16)
            Rr = Ring([sb(f"RrS{i}", [128, PC], F32) for i in range(2)])
            Ri = Ring([sb(f"RiS{i}", [128, PC], F32) for i in range(2)])
            zst = Ring([sb(f"zst{i}", [128, 512], BF16, dma=True) for i in range(2)])
            psB = [psum[0:2], psum[2:4]]
            psY = Ring(psum[4:8])

            def rev(a, b):
                return slice(a, None, -1) if b < 0 else slice(a, b, -1)

            for gp in range(16):
                q = gp // 4
                for d_ in range(2):
                    j = d_ * 16 + gp
                    for c in range(S // PC):
                        cs = slice(c * PC, (c + 1) * PC)
                        for (tab, addc) in ((sinS[d_], 0.0), (cosS[d_], 0.25)):
                            V(lambda e, cs=cs, j=j, addc=addc: e.tensor_scalar(out=tur[:], in0=tI[:, cs], scalar1=sm["th"][:, j:j + 1], scalar2=addc,
                                                                                 op0=ALU.mult, op1=ALU.add), [tI, sm["th"]], [tur])
                            V(lambda e: e.tensor_copy(out=turi[:], in_=tur[:]), [tur], [turi])
                            V(lambda e: e.tensor_tensor(out=tur[:], in0=tur[:], in1=turi[:], op=ALU.subtract), [tur, turi], [tur])
                            A(lambda e, tab=tab, cs=cs: e.activation(out=tab[:, cs], in_=tur[:], func=AF.Sin, scale=TWO_PI), [tur], [tab])
                    p.op("pool", lambda e, d_=d_: e.memset(rhoT[d_][:], 1.0), writes=[rhoT[d_]])
                    p.op("pool", lambda e, d_=d_, j=j: e.tensor_scalar(out=rhoT[d_][:], in0=rhoT[d_][:], scalar1=sm["rho"][:, j:j + 1], scalar2=None, op0=ALU.mult),
                         reads=[rhoT[d_], sm["rho"]], writes=[rhoT[d_]])
                for sq in range(nseq):
                    p.dma(lambda e: e.dma_start(out=uT[:], in_=uT_s[sq, q * 128:(q + 1) * 128, :]), uT, True)
                    for d_ in range(2):
                        j = d_ * 16 + gp
                        prevR = None
                        for c in range(S // PC):
                            cs = slice(c * PC, (c + 1) * PC)
                            for h in range(2):
                                if d_ == 0:
                                    usl = slice(c * PC + h * 512, c * PC + (h + 1) * 512)
                                else:
                                    a0 = S - 1 - (c * PC + h * 512)
                                    usl = rev(a0, a0 - 512)
                                for (ZT, bank) in ((ZTr, psB[0][h]), (ZTi, psB[1][h])):
                                    p.ops("pe", [lambda e, ZT=ZT, bank=bank, usl=usl, j=j: e.matmul(bank[:], lhsT=ZT[:, j, :], rhs=uT[:, usl], start=True, stop=True)],
                                          reads=[ZT, uT], writes=[bank])
                            rr = Rr.next(); ri = Ri.next()
                            for h in range(2):
                                hs = slice(h * 512, (h + 1) * 512)
                                gs = slice(c * PC + h * 512, c * PC + (h + 1) * 512)
                                br = psB[0][h]; bi = psB[1][h]
                                V(lambda e, hs=hs, gs=gs, br=br: e.tensor_tensor(out=tmp[0][:, hs], in0=br[:], in1=cosS[d_][:, gs], op=ALU.mult), [br, cosS[d_]], [tmp[0]])
                                V(lambda e, hs=hs, gs=gs, bi=bi: e.tensor_tensor(out=tmp[1][:, hs], in0=bi[:], in1=sinS[d_][:, gs], op=ALU.mult), [bi, sinS[d_]], [tmp[1]])
                                V(lambda e, hs=hs, gs=gs, bi=bi: e.tensor_tensor(out=tmp[2][:, hs], in0=bi[:], in1=cosS[d_][:, gs], op=ALU.mult), [bi, cosS[d_]], [tmp[2]])
                                V(lambda e, hs=hs, gs=gs, br=br: e.tensor_tensor(out=tmp[3][:, hs], in0=br[:], in1=sinS[d_][:, gs], op=ALU.mult), [br, sinS[d_]], [tmp[3]])
                            p.op("pool", lambda e: e.tensor_tensor(out=wr[:], in0=tmp[0][:], in1=tmp[1][:], op=ALU.add), reads=[tmp[0], tmp[1]], writes=[wr])
                            p.op("pool", lambda e: e.tensor_tensor(out=wi[:], in0=tmp[2][:], in1=tmp[3][:], op=ALU.subtract), reads=[tmp[2], tmp[3]], writes=[wi])
                            for (ro, wsrc, idx) in ((rr, wr, 0), (ri, wi, 1)):
                                ini = 0.0 if prevR is None else prevR[idx][:, PC - 1:PC]
                                rd = [rhoT[d_], wsrc] + ([] if prevR is None else [prevR[idx]])
                                V(lambda e, ro=ro, wsrc=wsrc, ini=ini: e.tensor_tensor_scan(out=ro[:], data0=rhoT[d_][:], data1=wsrc[:], initial=ini,
                                                                                            op0=ALU.mult, op1=ALU.add), rd, [ro])
                            prevR = (rr, ri)
                            G = lambda fn, rd_, wr_: p.op("pool", fn, reads=rd_, writes=wr_)
                            G(lambda e, cs=cs: e.tensor_tensor(out=tmp[0][:], in0=rr[:], in1=cosS[d_][:, cs], op=ALU.mult), [rr, cosS[d_]], [tmp[0]])
                            G(lambda e, cs=cs: e.tensor_tensor(out=tmp[1][:], in0=ri[:], in1=sinS[d_][:, cs], op=ALU.mult), [ri, sinS[d_]], [tmp[1]])
                            G(lambda e, cs=cs: e.tensor_tensor(out=sR[d_][:, cs], in0=tmp[0][:], in1=tmp[1][:], op=ALU.subtract), [tmp[0], tmp[1]], [sR[d_]])
                            G(lambda e, cs=cs: e.tensor_tensor(out=tmp[2][:], in0=ri[:], in1=cosS[d_][:, cs], op=ALU.mult), [ri, cosS[d_]], [tmp[2]])
                            G(lambda e, cs=cs: e.tensor_tensor(out=tmp[3][:], in0=rr[:], in1=sinS[d_][:, cs], op=ALU.mult), [rr, sinS[d_]], [tmp[3]])
                            G(lambda e, cs=cs: e.tensor_tensor(out=sI[d_][:, cs], in0=tmp[2][:], in1=tmp[3][:], op=ALU.add), [tmp[2], tmp[3]], [sI[d_]])
                    qq = gp % 4
                    if qq < 3:
                        rows = slice(32 * qq, 32 * qq + 32); lcs = slice(32, 64); dds = slice(32 * qq, 32 * qq + 32)
                    else:
                        rows = slice(64, 128); lcs = slice(0, 64); dds = slice(64, 128)
                    orow = slice(32 * qq, 32 * qq + 32)
                    for tb in range(S // 512):
                        fs = slice(tb * 512, (tb + 1) * 512)
                        a0 = S - 1 - tb * 512
                        bs = rev(a0, a0 - 512)
                        py = psY.next()
                        p.ops("pe", [
                            lambda e: e.matmul(py[rows, :], lhsT=LCr[:, gp, lcs], rhs=sR[0][:, fs], start=True, stop=False),
                            lambda e: e.matmul(py[rows, :], lhsT=LCi[:, gp, lcs], rhs=sI[0][:, fs], start=False, stop=False),
                            lambda e: e.matmul(py[rows, :], lhsT=LCr[:, 16 + gp, lcs], rhs=sR[1][:, bs], start=False, stop=False),
                            lambda e: e.matmul(py[rows, :], lhsT=LCi[:, 16 + gp, lcs], rhs=sI[1][:, bs], start=False, stop=False),
                            lambda e: e.matmul(py[rows, :], lhsT=Dd[:, q, dds], rhs=uT[:, fs], start=False, stop=True),
                        ], reads=[LCr, LCi, Dd, sR[0], sI[0], sR[1], sI[1], uT], writes=[py])
                        zo = zst.next()
                        A(lambda e: e.activation(out=zo[rows, :], in_=py[rows, :], func=AF.Gelu), [py], [zo])
                        p.dma(lambda e: e.dma_start(out=zT_s[sq, gp * 32:(gp + 1) * 32, fs], in_=zo[orow, :]), zo, False)
                stop(f'S_gp{gp}')
        new_phase()
        stop('S')

        with ExitStack() as es:
            def sb(name, shape, dt, dma=False, const=False):
                return p.buf(es.enter_context(nc.sbuf_tensor(name, list(shape), dt)), dma=dma, const=const)

            mstage = sb("mstage", [128, 256], F32, dma=True)
            ostage = sb("ostage", [128, 3, 64], F32, dma=True)
            maskB = sb("maskB", [128, 256], BF16); ones3 = sb("ones3", [128, 3, 64], BF16); identb = sb("identb", [128, 128], BF16)
            p.dma(lambda e: e.dma_start(out=mstage[:], in_=md["maskb"]), mstage, True)
            p.dma(lambda e: e.dma_start(out=ostage[:], in_=md["ones3"]), ostage, True)
            p.op("dve", lambda e: e.tensor_copy(out=maskB[:], in_=mstage[:]), reads=[mstage], writes=[maskB])
            p.op("dve", lambda e: e.tensor_copy(out=ones3[:], in_=ostage[:]), reads=[ostage], writes=[ones3])
            p.op("dve", lambda e: e.tensor_copy(out=identb[:], in_=ident[:]), reads=[ident], writes=[identb])
            maskB.const = True; ones3.const = True; identb.const = True
            qTr = Ring([sb(f"qTa{i}", [128, S], BF16, dma=True) for i in range(2)])
            kTr = Ring([sb(f"kTa{i}", [128, S + 2 * KPAD], BF16, dma=True) for i in range(2)])
            for b in kTr.bufs:
                p.op("pool", lambda e, b=b: e.memset(b[:], 0.0), writes=[b])
            vTr = Ring([sb(f"vTa{i}", [128, 48, 128], BF16, dma=True) for i in range(2)])
            acc = sb("acc", [128, 2, S], F32)
            rden = sb("rden", [128, S], F32)
            aTo = sb("aTo", [128, S], BF16, dma=True)
            PTr = Ring([sb(f"PT{i}", [128, 256], BF16) for i in range(4)])
            psS = Ring(psum[0:4]); psO = Ring(psum[4:8])
            SCALE = 64.0 ** -0.5
            for sq in range(nseq):
                for c in range(2):
                    for g in range(3):
                        d = DIL[g]; L = S // d; nb = L // 128 + 1
                        qT = qTr.next(); kT = kTr.next(); vT = vTr.next()
                        ch = 2 * g + c
                        p.dma(lambda e: e.dma_start(out=qT[:], in_=qT_s[sq, ch * 128:(ch + 1) * 128, :]), qT, True)
                        p.dma(lambda e: e.dma_start(out=kT[:, KPAD:KPAD + S], in_=kT_s[sq, ch * 128:(ch + 1) * 128, :]), kT, True)
                        p.dma(lambda e: e.dma_start(out=vT[:, 0:NBLK[g], :], in_=v_s[g][sq, :, :, c * 128:(c + 1) * 128]), vT, True)
                        for r in range(d):
                            for a in range(L // 128):
                                qsl = slice(r + d * 128 * a, r + d * 128 * a + d * 127 + 1, d)
                                pO = psO.next()
                                fns = []
                                pts = []
                                for hp in range(2):
                                    pb = 64 * hp
                                    pS = psS.next()
                                    ks = []
                                    for m in (a, a + 1):
                                        st = KPAD + r + d * (128 * m - 64)
                                        ks.append(slice(st, st + d * 127 + 1, d))
                                    p.ops("pe", [
                                        lambda e, pS=pS, pb=pb, ks=ks: e.matmul(pS[:, 0:128], lhsT=kT[pb:pb + 64, ks[0]], rhs=qT[pb:pb + 64, qsl], start=True, stop=False),
                                        lambda e, pS=pS, pb=pb, ks=ks: e.matmul(pS[:, 128:256], lhsT=kT[pb:pb + 64, ks[1]], rhs=qT[pb:pb + 64, qsl], start=False, stop=False),
                                        lambda e, pS=pS: e.matmul(pS[:, 0:256], lhsT=identb[:], rhs=maskB[:], start=False, stop=True),
                                    ], reads=[kT, qT, identb, maskB], writes=[pS])
                                    PT = PTr.next()
                                    p.op("act", lambda e, pS=pS, PT=PT: e.activation(out=PT[:], in_=pS[:, 0:256], func=AF.Exp, scale=SCALE), reads=[pS], writes=[PT])
                                    pts.append(PT)
                                    o1 = 1 if a == 0 else 0
                                    o2 = 2 if a + 1 == nb - 1 else 0
                                    b1 = r * nb + a; b2 = r * nb + a + 1
                                    fns += [
                                        lambda e, PT=PT, pb=pb, b1=b1, hp=hp: e.matmul(pO[pb:pb + 64, 0:128], lhsT=vT[:, b1, hp * 64:(hp + 1) * 64], rhs=PT[:, 0:128], start=True, stop=False),
                                        lambda e, PT=PT, pb=pb, b2=b2, hp=hp: e.matmul(pO[pb:pb + 64, 0:128], lhsT=vT[:, b2, hp * 64:(hp + 1) * 64], rhs=PT[:, 128:256], start=False, stop=False),
                                        lambda e, PT=PT, pb=pb, o1=o1: e.matmul(pO[pb:pb + 64, 128:256], lhsT=ones3[:, o1, :], rhs=PT[:, 0:128], start=False, stop=False),
                                        lambda e, PT=PT, pb=pb, o2=o2: e.matmul(pO[pb:pb + 64, 128:256], lhsT=ones3[:, o2, :], rhs=PT[:, 128:256], start=False, stop=True),
                                    ]
                                p.ops("pe", fns, reads=pts + [vT, ones3], writes=[pO])
                                pov = pO[:, 0:256].rearrange("p (n i) -> p n i", n=2)
                                if g == 0:
                                    p.op("dve", lambda e, pov=pov: e.tensor_copy(out=acc[:, :, qsl], in_=pov), reads=[pO], writes=[acc])
                                else:
                                    p.op("dve", lambda e, pov=pov: e.tensor_tensor(out=acc[:, :, qsl], in0=pov, in1=acc[:, :, qsl], op=ALU.add), reads=[pO, acc], writes=[acc])
                    p.op("dve", lambda e: e.reciprocal(out=rden[:], in_=acc[:, 1, :]), reads=[acc], writes=[rden])
                    p.op("dve", lambda e: e.tensor_tensor(out=aTo[:], in0=acc[:, 0, :], in1=rden[:], op=ALU.mult), reads=[acc, rden], writes=[aTo])
                    p.dma(lambda e: e.dma_start(out=aT_s[sq, c * 128:(c + 1) * 128, :], in_=aTo[:]), aTo, False)
        new_phase()
        stop('T')

        def load_w_bf16(sbf, wdst, src, K, N, tag, stg=None):
            piece = 1024 if N >= 1024 else N
            if stg is None:
                stg = Ring([sbf(f"wl_{tag}{i}", [128, piece], F32, dma=True) for i in range(2)])
            engs = ("dve", "pool", "act")
            n = 0
            for k in range(K):
                for c0 in range(0, N, piece):
                    w = min(piece, N - c0)
                    st = stg.next()
                    p.dma(lambda e, st=st, k=k, c0=c0, w=w: e.dma_start(out=st[:, 0:w], in_=src[k * 128:(k + 1) * 128, c0:c0 + w]), st, True)
                    eng = engs[n % 3]; n += 1
                    if eng == "act":
                        p.op("act", lambda e, st=st, k=k, c0=c0, w=w: e.copy(out=wdst[:, k, c0:c0 + w], in_=st[:, 0:w]), reads=[st], writes=[wdst])
                    else:
                        p.op(eng, lambda e, st=st, k=k, c0=c0, w=w: e.tensor_copy(out=wdst[:, k, c0:c0 + w], in_=st[:, 0:w]), reads=[st], writes=[wdst])
            wdst.const = True

        def layer_norm(sbufs, hpre, gB, bB, outt):
            stats, mv, rstd, hn = sbufs
            for n in range(2):
                p.op("dve", lambda e, n=n: e.bn_stats(out=stats[:, n, :], in_=hpre[:, n * 512:(n + 1) * 512]), reads=[hpre], writes=[stats])
            p.op("dve", lambda e: e.bn_aggr(out=mv[:], in_=stats[:].rearrange("p n s -> p (n s)")), reads=[stats], writes=[mv])
            p.op("act", lambda e: e.activation(out=rstd[:], in_=mv[:, 1:2], func=AF.Sqrt, bias=epsb[:, 0:1]), reads=[mv, epsb], writes=[rstd])
            p.op("dve", lambda e: e.reciprocal(out=rstd[:], in_=rstd[:]), reads=[rstd], writes=[rstd])
            p.op("dve", lambda e: e.tensor_scalar(out=hn[:], in0=hpre[:], scalar1=mv[:, 0:1], scalar2=rstd[:, 0:1], op0=ALU.subtract, op1=ALU.mult),
                 reads=[hpre, mv, rstd], writes=[hn])
            p.op("dve", lambda e: e.tensor_tensor(out=hn[:], in0=hn[:], in1=gB[:], op=ALU.mult), reads=[hn, gB], writes=[hn])
            p.op("dve", lambda e: e.tensor_tensor(out=outt[:], in0=hn[:], in1=bB[:], op=ALU.add), reads=[hn, bB], writes=[outt])

        with ExitStack() as es:
            def sb(name, shape, dt, dma=False, const=False):
                return p.buf(es.enter_context(nc.sbuf_tensor(name, list(shape), dt)), dma=dma, const=const)
            wgv = sb("wgv", [128, 4, D], BF16); wgg = sb("wgg", [128, 4, D], BF16); wab = sb("wab", [128, 2, D], BF16); wo = sb("wo", [128, 8, D], BF16)
            load_w_bf16(sb, wgv, md["wgv"], 4, D, "a"); load_w_bf16(sb, wgg, md["wgg"], 4, D, "b")
            load_w_bf16(sb, wab, md["wab"], 2, D, "c"); load_w_bf16(sb, wo, md["wo"], 8, D, "d")
            gB = sb("ln1gB", [128, D], F32, dma=True, const=True); bB = sb("ln1bB", [128, D], F32, dma=True, const=True)
            p.dma(lambda e: e.dma_start(out=gB[:], in_=md["ln1g"].partition_broadcast(128)), gB, True)
            p.dma(lambda e: e.dma_start(out=bB[:], in_=md["ln1b"].partition_broadcast(128)), bB, True)
            epsb = sb("epsb", [128, 1], F32)
            p.op("pool", lambda e: e.memset(epsb[:], LN_EPS), writes=[epsb])
            zT = sb("zTm", [128, 4, 512], BF16, dma=True); aT = sb("aTm", [128, 2, 512], BF16, dma=True); gT = sb("gTm", [128, 16, 512], BF16, dma=True)
            xs = Ring([sb(f"xm{i}", [128, D], F32, dma=True) for i in range(2)])
            mixT = sb("mixT", [128, 8, 512], BF16)
            sg = Ring([sb(f"sg{i}", [128, 512], F32) for i in range(2)])
            t1r = Ring([sb(f"t1m{i}", [128, 512], F32) for i in range(2)])
            t2r = Ring([sb(f"t2m{i}", [128, 512], F32) for i in range(2)])
            hpre = Ring([sb(f"hpre{i}", [128, D], F32) for i in range(2)])
            hout = Ring([sb(f"hout{i}", [128, D], F32, dma=True) for i in range(2)])
            hTt = Ring([sb(f"hTt{i}", [128, 8, 128], BF16, dma=True) for i in range(2)])
            lnb = (sb("st1", [128, 2, 6], F32), sb("mv1", [128, 2], F32), sb("rstd1", [128, 1], F32), sb("hn1", [128, D], F32))
            psr = Ring(psum)
            for sq in range(nseq):
                for tb in range(S // 512):
                    ts = slice(tb * 512, (tb + 1) * 512)
                    p.dma(lambda e: e.dma_start(out=zT[:], in_=zT_s[sq].rearrange("(k q) t -> q k t", q=128)[:, :, ts]), zT, True)
                    p.dma(lambda e: e.dma_start(out=aT[:], in_=aT_s[sq].rearrange("(k q) t -> q k t", q=128)[:, :, ts]), aT, True)
                    p.dma(lambda e: e.dma_start(out=gT[:], in_=gT_s[sq].rearrange("(k q) t -> q k t", q=128)[:, :, ts]), gT, True)
                    for do in range(8):
                        ds_ = slice(do * 128, (do + 1) * 128)
                        pA = psr.next(); pG = psr.next(); pB = psr.next()
                        p.ops("pe", [lambda e, k=k: e.matmul(pA[:], lhsT=wgv[:, k, ds_], rhs=zT[:, k, :], start=(k == 0), stop=(k == 3)) for k in range(4)], reads=[wgv, zT], writes=[pA])
                        p.ops("pe", [lambda e, k=k: e.matmul(pG[:], lhsT=wgg[:, k, ds_], rhs=zT[:, k, :], start=(k == 0), stop=(k == 3)) for k in range(4)], reads=[wgg, zT], writes=[pG])
                        p.ops("pe", [lambda e, k=k: e.matmul(pB[:], lhsT=wab[:, k, ds_], rhs=aT[:, k, :], start=(k == 0), stop=(k == 1)) for k in range(2)], reads=[wab, aT], writes=[pB])
                        sgt = sg.next(); t1 = t1r.next(); t2 = t2r.next()
                        p.op("act", lambda e: e.activation(out=sgt[:], in_=pG[:], func=AF.Sigmoid), reads=[pG], writes=[sgt])
                        p.op("dve", lambda e: e.tensor_tensor(out=t1[:], in0=pA[:], in1=sgt[:], op=ALU.mult), reads=[pA, sgt], writes=[t1])
                        p.op("dve", lambda e: e.tensor_tensor(out=t2[:], in0=pB[:], in1=gT[:, 8 + do, :], op=ALU.mult), reads=[pB, gT], writes=[t2])
                        p.op("dve", lambda e: e.tensor_tensor(out=t1[:], in0=t1[:], in1=gT[:, do, :], op=ALU.mult), reads=[t1, gT], writes=[t1])
                        p.op("dve", lambda e: e.tensor_tensor(out=mixT[:, do, :], in0=t1[:], in1=t2[:], op=ALU.add), reads=[t1, t2], writes=[mixT])
                    for i in range(4):
                        tok = slice(tb * 512 + i * 128, tb * 512 + (i + 1) * 128)
                        xt = xs.next()
                        p.dma(lambda e: e.dma_start(out=xt[:], in_=x_d[sq, tok, :]), xt, True)
                        hp_ = hpre.next()
                        for n in range(2):
                            ns = slice(n * 512, (n + 1) * 512)
                            po = psr.next()
                            p.ops("pe", [lambda e, k=k: e.matmul(po[:], lhsT=mixT[:, k, i * 128:(i + 1) * 128], rhs=wo[:, k, ns], start=(k == 0), stop=(k == 7)) for k in range(8)],
                                  reads=[mixT, wo], writes=[po])
                            p.op("dve", lambda e: e.scalar_tensor_tensor(out=hp_[:, ns], in0=xt[:, ns], scalar=ALPHA, in1=po[:], op0=ALU.mult, op1=ALU.add),
                                 reads=[xt, po], writes=[hp_])
                        ho = hout.next()
                        layer_norm(lnb, hp_, gB, bB, ho)
                        p.dma(lambda e: e.dma_start(out=h_s[sq, tok, :], in_=ho[:]), ho, False)
                        hT = hTt.next()
                        for kk in range(2):
                            pt = psr.next()
                            p.ops("pe", [lambda e, k4=k4: e.transpose(out=pt[:, k4 * 128:(k4 + 1) * 128], in_=ho[:, (kk * 4 + k4) * 128:(kk * 4 + k4 + 1) * 128], identity=ident[:])
                                         for k4 in range(4)], reads=[ho, ident], writes=[pt])
                            p.op("act", lambda e: e.copy(out=hT[:, kk * 4:(kk + 1) * 4, :], in_=pt[:].rearrange("p (k t) -> p k t", k=4)), reads=[pt], writes=[hT])
                        p.dma(lambda e: e.dma_start(out=hT_s[sq].rearrange("(k q) t -> q k t", q=128)[:, :, tok], in_=hT[:]), hT, False)
        new_phase()
        stop('M1')

        with ExitStack() as es:
            def sb(name, shape, dt, dma=False, const=False):
                return p.buf(es.enter_context(nc.sbuf_tensor(name, list(shape), dt)), dma=dma, const=const)
            wup = sb("wup", [128, 8, 2 * DFF], BF16); wdn = sb("wdn", [128, 22, D], BF16)
            stg_ = Ring([sb(f"wl_u{i}", [128, 1024], F32, dma=True) for i in range(2)])
            load_w_bf16(sb, wup, md["wup"], 8, 2 * DFF, "u", stg_); load_w_bf16(sb, wdn, md["wdn"], 22, D, "v", stg_)
            gB = sb("ln2gB", [128, D], F32, dma=True, const=True); bB = sb("ln2bB", [128, D], F32, dma=True, const=True)
            p.dma(lambda e: e.dma_start(out=gB[:], in_=md["ln2g"].partition_broadcast(128)), gB, True)
            p.dma(lambda e: e.dma_start(out=bB[:], in_=md["ln2b"].partition_broadcast(128)), bB, True)
            cw = sb("cw", [128, 44, 3], F32, dma=True, const=True); cbias = sb("cbias", [128, 44], F32, dma=True, const=True)
            p.dma(lambda e: e.dma_start(out=cw[:], in_=md["cw"]), cw, True)
            p.dma(lambda e: e.dma_start(out=cbias[:], in_=md["cbias"]), cbias, True)
            epsb = sb("epsb2", [128, 1], F32)
            p.op("pool", lambda e: e.memset(epsb[:], LN_EPS), writes=[epsb])
            hT = sb("hTf", [128, 8, 514], BF16, dma=True)
            hres = Ring([sb(f"hres{i}", [128, D], F32, dma=True) for i in range(1)])
            cvr = Ring([sb(f"cv{i}", [128, 512], F32) for i in range(4)])
            actT = sb("actT", [128, 22, 512], BF16)
            opre = sb("opre", [128, D], F32)
            oout = Ring([sb(f"oout{i}", [128, D], F32, dma=True) for i in range(1)])
            lnb = (sb("st2", [128, 2, 6], F32), sb("mv2", [128, 2], F32), sb("rstd2", [128, 1], F32), sb("hn2", [128, D], F32))
            psm = Ring(psum[0:6]); psh = Ring(psum[6:8])
            for sq in range(nseq):
                for tb in range(S // 512):
                    t0 = tb * 512
                    lo = max(t0 - 1, 0); hi = min(t0 + 513, S)
                    if t0 == 0:
                        p.op("pool", lambda e: e.memset(hT[:, :, 0:1], 0.0), writes=[hT])
                    if t0 + 512 == S:
                        p.op("pool", lambda e: e.memset(hT[:, :, 513:514], 0.0), writes=[hT])
                    p.dma(lambda e: e.dma_start(out=hT[:, :, lo - (t0 - 1):hi - (t0 - 1)], in_=hT_s[sq].rearrange("(k q) t -> q k t", q=128)[:, :, lo:hi]), hT, True)
                    for c in range(22):
                        cvs = []
                        for ch in (c, 22 + c):
                            cs_ = slice(ch * 128, (ch + 1) * 128)
                            pm = psm.next(); ph = psh.next()
                            p.ops("pe", [lambda e, k=k: e.matmul(pm[:], lhsT=wup[:, k, cs_], rhs=hT[:, k, 1:513], start=(k == 0), stop=(k == 7)) for k in range(8)]
                                  + [lambda e, k=k: e.matmul(ph[:, 0:2], lhsT=wup[:, k, cs_], rhs=hT[:, k, 0:514:513], start=(k == 0), stop=(k == 7)) for k in range(8)],
                                  reads=[wup, hT], writes=[pm, ph])
                            cv = cvr.next()
                            p.op("act", lambda e: e.activation(out=cv[:], in_=pm[:], func=AF.Identity, scale=cw[:, ch, 1:2], bias=cbias[:, ch:ch + 1]),
                                 reads=[pm, cw, cbias], writes=[cv])
                            p.op("dve", lambda e: e.scalar_tensor_tensor(out=cv[:, 1:512], in0=pm[:, 0:511], scalar=cw[:, ch, 0:1], in1=cv[:, 1:512], op0=ALU.mult, op1=ALU.add), reads=[pm, cw, cv], writes=[cv])
                            p.op("dve", lambda e: e.scalar_tensor_tensor(out=cv[:, 0:511], in0=pm[:, 1:512], scalar=cw[:, ch, 2:3], in1=cv[:, 0:511], op0=ALU.mult, op1=ALU.add), reads=[pm, cw, cv], writes=[cv])
                            p.op("dve", lambda e: e.scalar_tensor_tensor(out=cv[:, 0:1], in0=ph[:, 0:1], scalar=cw[:, ch, 0:1], in1=cv[:, 0:1], op0=ALU.mult, op1=ALU.add), reads=[ph, cw, cv], writes=[cv])
                            p.op("dve", lambda e: e.scalar_tensor_tensor(out=cv[:, 511:512], in0=ph[:, 1:2], scalar=cw[:, ch, 2:3], in1=cv[:, 511:512], op0=ALU.mult, op1=ALU.add), reads=[ph, cw, cv], writes=[cv])
                            cvs.append(cv)
                        p.op("act", lambda e: e.activation(out=cvs[0][:], in_=cvs[0][:], func=AF.Gelu), reads=[cvs[0]], writes=[cvs[0]])
                        p.op("dve", lambda e: e.tensor_tensor(out=actT[:, c, :], in0=cvs[0][:], in1=cvs[1][:], op=ALU.mult), reads=cvs, writes=[actT])
                    for i in range(4):
                        tok = slice(t0 + i * 128, t0 + (i + 1) * 128)
                        hr = hres.next()
                        p.dma(lambda e: e.dma_start(out=hr[:], in_=h_s[sq, tok, :]), hr, True)
                        for n in range(2):
                            ns = slice(n * 512, (n + 1) * 512)
                            po = psm.next()
                            p.ops("pe", [lambda e, k=k: e.matmul(po[:], lhsT=actT[:, k, i * 128:(i + 1) * 128], rhs=wdn[:, k, ns], start=(k == 0), stop=(k == 21)) for k in range(22)],
                                  reads=[actT, wdn], writes=[po])
                            p.op("dve", lambda e: e.scalar_tensor_tensor(out=opre[:, ns], in0=hr[:, ns], scalar=ALPHA, in1=po[:], op0=ALU.mult, op1=ALU.add),
                                 reads=[hr, po], writes=[opre])
                        oo = oout.next()
                        layer_norm(lnb, opre, gB, bB, oo)
                        p.dma(lambda e: e.dma_start(out=out_d[sq, tok, :], in_=oo[:]), oo, False)
        new_phase()


def _host_inputs(inputs, core, nseq=NSEQ):
    f32 = np.float32
    x = np.ascontiguousarray(inputs["x"][core * nseq:(core + 1) * nseq]).astype(f32)
    pos = np.ascontiguousarray(inputs["positions"][core * nseq:(core + 1) * nseq]).astype(np.int32)
    w_in = np.asarray(inputs["w_in"][0], f32)
    b_in = np.asarray(inputs["b_in"][0], f32)
    sw = np.arange(AW).reshape(-1, 2, 32)[:, ::-1, :].reshape(-1)
    q0, k0, v0, g0 = SSMW, SSMW + AW, SSMW + 2 * AW, SSMW + 3 * AW
    cols = np.concatenate([np.arange(0, SSMW), np.arange(q0, q0 + AW), np.arange(k0, k0 + AW),
                           q0 + sw, k0 + sw, np.arange(g0, g0 + 2 * D)])
    w_fm = np.ascontiguousarray(w_in[:, cols])
    b_fm = np.ascontiguousarray(b_in[cols].reshape(NFM // 128, 128).T)
    w_v = np.ascontiguousarray(w_in[:, v0:v0 + AW])
    b_v = np.ascontiguousarray(b_in[v0:v0 + AW].reshape(1, AW))
    half = 32
    inv_freq = (10000.0 ** (-np.arange(half, dtype=np.float64) * 2.0 / 64)).astype(f32)
    invf = np.zeros((128, 2), f32)
    for pp in range(128):
        invf[pp, 0] = inv_freq[pp % 32] / TWO_PI
        invf[pp, 1] = -TWO_PI if (pp % 64) < 32 else TWO_PI
    def tile_layout(a):
        return np.ascontiguousarray(a.reshape(2, 16, 2, 64).transpose(2, 3, 0, 1).reshape(128, 32)).astype(f32)
    lre_h = tile_layout(np.asarray(inputs["ssm_lam_re"][0], f32))
    lim_h = tile_layout(np.asarray(inputs["ssm_lam_im"][0], f32))
    ldt_h = tile_layout(np.broadcast_to(np.asarray(inputs["ssm_log_dt"][0], f32)[:, :, None], (2, 32, 64)).copy())

    def bz(b):
        o = np.zeros((128, 32, 128), f32)
        b = np.asarray(b, f32)
        for dr in range(2):
            for gp in range(16):
                for gl in range(2):
                    c0 = (gp % 4) * 32 + gl * 16
                    o[gl * 64:(gl + 1) * 64, dr * 16 + gp, c0:c0 + 16] = b[dr, 2 * gp + gl]
        return o

    def cb(c):
        o = np.zeros((32, 32, 128), f32)
        c = np.asarray(c, f32)
        for dr in range(2):
            for gp in range(16):
                for gl in range(2):
                    o[gl * 16:(gl + 1) * 16, dr * 16 + gp, gl * 64:(gl + 1) * 64] = c[dr, 2 * gp + gl]
        return o
    ssm = {"lre_h": lre_h, "lim_h": lim_h, "ldt_h": ldt_h,
           "bzr_h": bz(inputs["ssm_b_re"][0]), "bzi_h": bz(inputs["ssm_b_im"][0]),
           "cbr_h": cb(inputs["ssm_c_re"][0]), "cbi_h": cb(inputs["ssm_c_im"][0]),
           "dsk_h": np.ascontiguousarray(np.asarray(inputs["ssm_d"][0], f32).reshape(4, 128).T),
           "iota_h": np.arange(S, dtype=f32).reshape(1, S)}
    d = {"x": x, "pos": pos, "ident": np.eye(128, dtype=f32), "invf": invf,
         "w_in_fm": w_fm, "b_fm": b_fm, "w_v": w_v, "b_v": b_v}
    d.update(ssm)
    ii = np.arange(128)[:, None]; jj = np.arange(128)[None, :]
    maskb = np.concatenate([np.where(ii >= jj, 0.0, -30000.0), np.where(ii <= jj, 0.0, -30000.0)], axis=1).astype(f32)
    ones3 = np.zeros((128, 3, 64), f32)
    ones3[:, 0, :] = 1.0; ones3[64:, 1, :] = 1.0; ones3[:64, 2, :] = 1.0
    g = lambda n: np.ascontiguousarray(np.asarray(inputs[n][0], f32))
    cwh = np.ascontiguousarray(g("conv_w").reshape(3, 44, 128).transpose(2, 1, 0))
    cbh = np.ascontiguousarray(g("conv_b").reshape(44, 128).T)
    d.update({"maskb_h": maskb, "ones3_h": ones3, "wgv_h": g("w_glu_v"), "wgg_h": g("w_glu_g"), "wab_h": g("w_attn_br"), "wo_h": g("w_out"),
              "ln1g_h": g("ln1_g").reshape(1, D), "ln1b_h": g("ln1_b").reshape(1, D), "ln2g_h": g("ln2_g").reshape(1, D), "ln2b_h": g("ln2_b").reshape(1, D),
              "wup_h": g("w_up"), "wdn_h": g("w_down"), "cw_h": cwh, "cb_h": cbh})
    return d


def kernel(**inputs):
    nc = build()
    in_maps = [_host_inputs(inputs, c) for c in range(NCORES)]
    res = run_bass_kernel_spmd(nc, in_maps, core_ids=list(range(NCORES)))
    out = np.concatenate([r["out"] for r in res.results], axis=0)
    return out.astype(np.float32)
```

```python
import math
from contextlib import ExitStack

import numpy as np
import concourse.bass as bass
import concourse.mybir as mybir
from concourse.bass_utils import run_bass_kernel_spmd

F32 = mybir.dt.float32
BF16 = mybir.dt.bfloat16
I32 = mybir.dt.int32
AF = mybir.ActivationFunctionType
ALU = mybir.AluOpType
AX = mybir.AxisListType

S = 4096
D = 1024
NCORES = 8
NSEQ = 2
SSMW = 512
AW = 768
DFF = 2816
NFM = 5632
ALPHA = 2.0 ** 0.25
LN_EPS = 1e-5
TWO_PI = 2.0 * math.pi
DIL = (1, 4, 16)
KPAD = 1024


class Buf:
    __slots__ = ("t", "w", "r", "dsem", "const")

    def __init__(self, t, dsem=None, const=False):
        self.t = t
        self.w = None
        self.r = {}
        self.dsem = dsem
        self.const = const

    def __getitem__(self, k):
        return self.t[k]


class Prog:
    ENG = ("pe", "act", "dve", "pool", "sp")

    def __init__(self, nc, es, n_dsem=72):
        self.nc = nc
        self.engobj = {'pe': nc.tensor, 'act': nc.scalar, 'dve': nc.vector, 'pool': nc.gpsimd, 'sp': nc.sync}
        self.ninst = 0
        self.stopped = False
        self.esem = {e: es.enter_context(nc.semaphore("es_" + e)) for e in ("pe", "act", "dve", "pool")}
        self.ecount = {e: 0 for e in self.esem}
        self.dsems = [es.enter_context(nc.semaphore(f"ds{i}")) for i in range(n_dsem)]
        self.dcount = {id(s): 0 for s in self.dsems}
        self.dnext = 0
        self.waited = {e: {} for e in self.ENG}
        self.semobj = {}
        for s in list(self.esem.values()) + self.dsems:
            self.semobj[id(s)] = s

    def buf(self, t, dma=False, const=False):
        ds = None
        if dma:
            assert self.dnext < len(self.dsems), "out of DMA semaphores in this phase"
            ds = self.dsems[self.dnext]
            self.dnext += 1
        return Buf(t, ds, const)

    def _deps(self, reads, writes):
        deps = {}

        def add(ev):
            if ev is None:
                return
            k, v = ev
            if deps.get(k, 0) < v:
                deps[k] = v
        for b in reads:
            add(b.w)
        for b in writes:
            add(b.w)
            for k, v in b.r.items():
                add((k, v))
        return deps

    def _record(self, ev, reads, writes):
        for b in writes:
            b.w = ev
            b.r = {}
        for b in reads:
            if b.const:
                continue
            if b.r.get(ev[0], 0) < ev[1]:
                b.r[ev[0]] = ev[1]

    def _emit(self, eng, deps, fn, inc):
        e = self.engobj[eng]
        wd = self.waited[eng]
        own = id(self.esem[eng]) if eng in self.esem else None
        for k, v in deps.items():
            if eng == "pe" and k == own:
                continue
            if wd.get(k, 0) >= v:
                continue
            wd[k] = v
            e.wait_ge(self.semobj[k], v)
        if fn is None:
            return
        ins = fn(e)
        if inc is not None:
            ins.then_inc(inc[0], inc[1])
        self.ninst += 1

    def op(self, eng, fn, reads=(), writes=()):
        if self.stopped:
            return None
        deps = self._deps(reads, writes)
        self.ecount[eng] += 1
        sem = self.esem[eng]
        ev = (id(sem), self.ecount[eng])
        self._emit(eng, deps, fn, (sem, 1))
        self._record(ev, reads, writes)
        return ev

    def ops(self, eng, fns, reads=(), writes=()):
        assert eng == "pe"
        if self.stopped:
            return None
        deps = self._deps(reads, writes)
        for fn in fns[:-1]:
            self._emit(eng, deps, fn, None)
            deps = {}
        self.ecount[eng] += 1
        sem = self.esem[eng]
        ev = (id(sem), self.ecount[eng])
        self._emit(eng, deps, fns[-1], (sem, 1))
        self._record(ev, reads, writes)
        return ev

    def dma(self, fn, sb, load, reads=(), writes=(), q="sp"):
        if self.stopped:
            return None
        reads = list(reads)
        writes = list(writes)
        if load:
            writes.append(sb)
        else:
            reads.append(sb)
        deps = self._deps(reads, writes)
        sem = sb.dsem
        assert sem is not None
        self.dcount[id(sem)] += 16
        ev = (id(sem), self.dcount[id(sem)])
        self._emit(q, deps, fn, (sem, 16))
        self._record(ev, reads, writes)
        return ev

    def barrier(self):
        allev = {}
        for e, s in self.esem.items():
            if self.ecount[e]:
                allev[id(s)] = self.ecount[e]
        for s in self.dsems:
            if self.dcount[id(s)]:
                allev[id(s)] = self.dcount[id(s)]
        for eng in self.ENG:
            self._emit(eng, allev, None, None)
        self.dnext = 0

    def emit(self):
        pass


class StopBuild(Exception):
    pass


class Ring:
    def __init__(self, bufs):
        self.bufs = bufs
        self.i = 0

    def next(self):
        b = self.bufs[self.i % len(self.bufs)]
        self.i += 1
        return b


def build(nseq=NSEQ, debug=False, stop_after=None):
    nc = bass.Bass("TRN2", target_bir_lowering=False)

    def din(name, shape, dt=F32):
        return nc.dram_tensor(name, list(shape), dt, kind="ExternalInput").ap()

    dbg_kind = "ExternalOutput" if debug else "Internal"

    def dscr(name, shape, dt):
        return nc.dram_tensor(name, list(shape), dt, kind=dbg_kind).ap()

    x_d = din("x", [nseq, S, D])
    pos_d = din("pos", [nseq, S], I32)
    ident_d = din("ident", [128, 128])
    invf_d = din("invf", [128, 2])
    w_in_d = din("w_in_fm", [D, NFM])
    b_fm_d = din("b_fm", [128, NFM // 128])
    w_v_d = din("w_v", [D, AW])
    b_v_d = din("b_v", [1, AW])
    out_d = nc.dram_tensor("out", [nseq, S, D], F32, kind="ExternalOutput").ap()
    ssm_d = dict(
        lre=din("lre_h", [128, 32]), lim=din("lim_h", [128, 32]), ldt=din("ldt_h", [128, 32]),
        bzr=din("bzr_h", [128, 32, 128]), bzi=din("bzi_h", [128, 32, 128]),
        cbr=din("cbr_h", [32, 32, 128]), cbi=din("cbi_h", [32, 32, 128]),
        dsk=din("dsk_h", [128, 4]), iota=din("iota_h", [1, S]))
    zT_s = dscr("zT_s", [nseq, SSMW, S], BF16)
    aT_s = dscr("aT_s", [nseq, 256, S], BF16)
    h_s = dscr("h_s", [nseq, S, D], F32)
    hT_s = dscr("hT_s", [nseq, D, S], BF16)
    md = dict(maskb=din("maskb_h", [128, 256]), ones3=din("ones3_h", [128, 3, 64]),
              wgv=din("wgv_h", [512, D]), wgg=din("wgg_h", [512, D]), wab=din("wab_h", [256, D]), wo=din("wo_h", [D, D]),
              ln1g=din("ln1g_h", [1, D]), ln1b=din("ln1b_h", [1, D]), ln2g=din("ln2g_h", [1, D]), ln2b=din("ln2b_h", [1, D]),
              wup=din("wup_h", [D, 2 * DFF]), wdn=din("wdn_h", [DFF, D]), cw=din("cw_h", [128, 44, 3]), cbias=din("cb_h", [128, 44]),
              aT_s=aT_s, h_s=h_s, hT_s=hT_s)

    xT_s = dscr("xT_s", [nseq, D, S], BF16)
    uT_s = dscr("uT_s", [nseq, SSMW, S], BF16)
    qT_s = dscr("qT_s", [nseq, AW, S], BF16)
    kT_s = dscr("kT_s", [nseq, AW, S], BF16)
    gT_s = dscr("gT_s", [nseq, 2 * D, S], BF16)
    NBLK = [d * (S // d // 128 + 1) for d in DIL]
    v_s = [dscr(f"v_s{g}", [nseq, 128, NBLK[g], 256], BF16) for g in range(3)]

    with ExitStack() as es0:
        p = Prog(nc, es0)
        psum = [p.buf(es0.enter_context(nc.psum_tensor(f"ps{i}", [128, 512], F32))) for i in range(8)]
        ident = p.buf(es0.enter_context(nc.sbuf_tensor("ident_sb", [128, 128], F32)), dma=True, const=True)
        p.dma(lambda e: e.dma_start(out=ident[:], in_=ident_d), ident, True)
        p.dnext = 1

        def new_phase():
            p.barrier()
            p.dnext = 1

        def stop(tag):
            if stop_after == tag:
                p.stopped = True

        try:
            _phases(nc, p, psum, ident, nseq, locals_d=dict(x_d=x_d, pos_d=pos_d, invf_d=invf_d, w_in_d=w_in_d, b_fm_d=b_fm_d, w_v_d=w_v_d, b_v_d=b_v_d, out_d=out_d, xT_s=xT_s, uT_s=uT_s, qT_s=qT_s, kT_s=kT_s, gT_s=gT_s, v_s=v_s, NBLK=NBLK, ssm_d=ssm_d, zT_s=zT_s, md=md), new_phase=new_phase, stop=stop)
        except StopBuild:
            pass
        p.stopped = False
        p.barrier()
    print('instructions', p.ninst)
    return nc


def _phases(nc, p, psum, ident, nseq, locals_d, new_phase, stop):
    globals_ = locals_d
    x_d = globals_['x_d']; pos_d = globals_['pos_d']; invf_d = globals_['invf_d']; w_in_d = globals_['w_in_d']; b_fm_d = globals_['b_fm_d']
    w_v_d = globals_['w_v_d']; b_v_d = globals_['b_v_d']; out_d = globals_['out_d']; xT_s = globals_['xT_s']; uT_s = globals_['uT_s']
    qT_s = globals_['qT_s']; kT_s = globals_['kT_s']; gT_s = globals_['gT_s']; v_s = globals_['v_s']; NBLK = globals_['NBLK']
    ssm_d = globals_['ssm_d']; zT_s = globals_['zT_s']; md = globals_['md']
    aT_s = md['aT_s']; h_s = md['h_s']; hT_s = md['hT_s']
    if True:

        with ExitStack() as es:
            def sb(name, shape, dt, dma=False, const=False):
                return p.buf(es.enter_context(nc.sbuf_tensor(name, list(shape), dt)), dma=dma, const=const)

            wA = sb("wA", [128, 8, NFM], BF16)
            bfm = sb("bfm", [128, NFM // 128], F32, dma=True)
            invf = sb("invf_sb", [128, 2], F32, dma=True)
            p.dma(lambda e: e.dma_start(out=bfm[:], in_=b_fm_d), bfm, True)
            p.dma(lambda e: e.dma_start(out=invf[:], in_=invf_d), invf, True)
            WP = 1408
            wst = Ring([sb(f"wst{i}", [128, WP], F32, dma=True) for i in range(3)])
            cast_engs = ("dve", "act", "pool")
            ci = 0
            for k in range(8):
                for c in range(NFM // WP):
                    st = wst.next()
                    p.dma(lambda e, st=st, k=k, c=c: e.dma_start(
                        out=st[:], in_=w_in_d[k * 128:(k + 1) * 128, c * WP:(c + 1) * WP]), st, True)
                    eng = cast_engs[ci % 3]
                    ci += 1
                    if eng == "act":
                        p.op("act", lambda e, st=st, k=k, c=c: e.copy(out=wA[:, k, c * WP:(c + 1) * WP], in_=st[:]),
                             reads=[st], writes=[wA])
                    else:
                        p.op(eng, lambda e, st=st, k=k, c=c: e.tensor_copy(out=wA[:, k, c * WP:(c + 1) * WP], in_=st[:]),
                             reads=[st], writes=[wA])
            wA.const = True
            stop('A0')

            cosT = sb("cosT", [128, S], F32)
            sinT = sb("sinT", [128, S], F32)
            posi = sb("posi", [128, 1024], I32, dma=True)
            tur = sb("tur", [128, 1024], F32)
            turi = sb("turi", [128, 1024], I32)
            xs = [sb(f"xs{i}", [128, D], F32, dma=True) for i in range(4)]
            xT = Ring([sb(f"xT{j}", [128, 8, 512], BF16, dma=True) for j in range(2)])
            ev_bf = Ring([sb(f"evbf{j}", [128, 512], BF16, dma=True) for j in range(8)])
            rt = Ring([sb(f"rt{j}", [128, 512], F32) for j in range(4)])
            psr = Ring(psum)

            for sq in range(nseq):
                for c in range(S // 1024):
                    cs = slice(c * 1024, (c + 1) * 1024)
                    p.dma(lambda e, cs=cs: e.dma_start(out=posi[:], in_=pos_d[sq:sq + 1, cs].partition_broadcast(128)), posi, True)
                    for (tab, addc, scol) in ((sinT, 0.0, 1), (cosT, 0.25, None)):
                        p.op("dve", lambda e: e.tensor_copy(out=tur[:], in_=posi[:]), reads=[posi], writes=[tur])
                        p.op("dve", lambda e, addc=addc: e.tensor_scalar(out=tur[:], in0=tur[:], scalar1=invf[:, 0:1], scalar2=addc,
                                                                          op0=ALU.mult, op1=ALU.add), reads=[tur, invf], writes=[tur])
                        p.op("dve", lambda e: e.tensor_copy(out=turi[:], in_=tur[:]), reads=[tur], writes=[turi])
                        p.op("dve", lambda e: e.tensor_tensor(out=tur[:], in0=tur[:], in1=turi[:], op=ALU.subtract),
                             reads=[tur, turi], writes=[tur])
                        if scol is not None:
                            p.op("act", lambda e, tab=tab, cs=cs: e.activation(out=tab[:, cs], in_=tur[:], func=AF.Sin, scale=invf[:, 1:2]),
                                 reads=[tur, invf], writes=[tab])
                        else:
                            p.op("act", lambda e, tab=tab, cs=cs: e.activation(out=tab[:, cs], in_=tur[:], func=AF.Sin, scale=TWO_PI),
                                 reads=[tur], writes=[tab])
                stop('A1')
                def load_x(tb_):
                    for i in range(4):
                        p.dma(lambda e, i=i: e.dma_start(out=xs[i][:], in_=x_d[sq, tb_ * 512 + i * 128:tb_ * 512 + (i + 1) * 128, :]), xs[i], True)
                load_x(0)
                for tb in range(S // 512):
                    t0 = tb * 512
                    ts = slice(t0, t0 + 512)
                    xtile = xs
                    xTb = xT.next()
                    for k in range(8):
                        ps = psr.next()
                        p.ops("pe", [lambda e, ps=ps, i=i, k=k: e.transpose(out=ps[:, i * 128:(i + 1) * 128],
                                                                           in_=xtile[i][:, k * 128:(k + 1) * 128], identity=ident[:])
                                     for i in range(4)], reads=xtile + [ident], writes=[ps])
                        if k % 2 == 0:
                            p.op("act", lambda e, ps=ps, k=k: e.copy(out=xTb[:, k, :], in_=ps[:]), reads=[ps], writes=[xTb])
                        else:
                            p.op("dve", lambda e, ps=ps, k=k: e.tensor_copy(out=xTb[:, k, :], in_=ps[:]), reads=[ps], writes=[xTb])
                    if tb + 1 < S // 512:
                        load_x(tb + 1)
                    p.dma(lambda e: e.dma_start(out=xT_s[sq].rearrange("(k q) t -> q k t", q=128)[:, :, ts], in_=xTb[:]), xTb, False)

                    def proj(fo):
                        ps = psr.next()
                        p.ops("pe", [lambda e, ps=ps, k=k: e.matmul(ps[:], lhsT=wA[:, k, fo * 128:(fo + 1) * 128], rhs=xTb[:, k, :],
                                                                      start=(k == 0), stop=(k == 7)) for k in range(8)],
                              reads=[wA, xTb], writes=[ps])
                        return ps

                    for fo in range(4):
                        ps = proj(fo)
                        o = ev_bf.next()
                        p.op("act", lambda e, ps=ps, o=o, fo=fo: e.activation(out=o[:], in_=ps[:], func=AF.Identity, bias=bfm[:, fo:fo + 1]),
                             reads=[ps, bfm], writes=[o])
                        p.dma(lambda e, o=o, fo=fo: e.dma_start(out=uT_s[sq, fo * 128:(fo + 1) * 128, ts], in_=o[:]), o, False)
                    for which, dst in ((0, qT_s), (1, kT_s)):
                        for c in range(6):
                            fo = 4 + which * 6 + c
                            psa = proj(fo)
                            psb = proj(fo + 12)
                            t1 = rt.next()
                            t2 = rt.next()
                            p.op("dve", lambda e, psa=psa, t1=t1, fo=fo: e.scalar_tensor_tensor(
                                out=t1[:], in0=psa[:], scalar=bfm[:, fo:fo + 1], in1=cosT[:, ts], op0=ALU.add, op1=ALU.mult),
                                reads=[psa, bfm, cosT], writes=[t1])
                            p.op("dve", lambda e, psb=psb, t2=t2, fo=fo: e.scalar_tensor_tensor(
                                out=t2[:], in0=psb[:], scalar=bfm[:, fo + 12:fo + 13], in1=sinT[:, ts], op0=ALU.add, op1=ALU.mult),
                                reads=[psb, bfm, sinT], writes=[t2])
                            o = ev_bf.next()
                            p.op("pool", lambda e, o=o, t1=t1, t2=t2: e.tensor_tensor(out=o[:], in0=t1[:], in1=t2[:], op=ALU.add),
                                 reads=[t1, t2], writes=[o])
                            p.dma(lambda e, o=o, c=c, dst=dst: e.dma_start(out=dst[sq, c * 128:(c + 1) * 128, ts], in_=o[:]), o, False)
                    for c in range(16):
                        fo = 28 + c
                        ps = proj(fo)
                        o = ev_bf.next()
                        p.op("act", lambda e, ps=ps, o=o, fo=fo: e.activation(out=o[:], in_=ps[:], func=AF.Sigmoid, bias=bfm[:, fo:fo + 1]),
                             reads=[ps, bfm], writes=[o])
                        p.dma(lambda e, o=o, c=c: e.dma_start(out=gT_s[sq, c * 128:(c + 1) * 128, ts], in_=o[:]), o, False)
                    stop(f'A2_{tb}')
        new_phase()
        stop('A')

        with ExitStack() as es:
            def sb(name, shape, dt, dma=False, const=False):
                return p.buf(es.enter_context(nc.sbuf_tensor(name, list(shape), dt)), dma=dma, const=const)

            wV = sb("wV", [128, 8, AW], BF16)
            wvst = Ring([sb(f"wvst{i}", [128, AW], F32, dma=True) for i in range(2)])
            for k in range(8):
                st = wvst.next()
                p.dma(lambda e, st=st, k=k: e.dma_start(out=st[:], in_=w_v_d[k * 128:(k + 1) * 128, :]), st, True)
                p.op("dve", lambda e, st=st, k=k: e.tensor_copy(out=wV[:, k, :], in_=st[:]), reads=[st], writes=[wV])
            wV.const = True
            bv = sb("bv", [128, AW], F32, dma=True, const=True)
            p.dma(lambda e: e.dma_start(out=bv[:], in_=b_v_d.partition_broadcast(128)), bv, True)
            xTf = sb("xTf", [128, 8, S], BF16, dma=True)
            VCH = 12
            vring = Ring([sb(f"vstg{j}", [128, VCH, 256], BF16, dma=True) for j in range(2)])
            psr = Ring(psum)
            for sq in range(nseq):
                p.dma(lambda e: e.dma_start(out=xTf[:], in_=xT_s[sq].rearrange("(k q) t -> q k t", q=128)), xTf, True)
                for g in range(3):
                    d = DIL[g]
                    L = S // d
                    nb = L // 128 + 1
                    blocks = [(r, m) for r in range(d) for m in range(nb)]
                    for c0 in range(0, len(blocks), VCH):
                        chunk = blocks[c0:c0 + VCH]
                        stg = vring.next()
                        p.op("pool", lambda e, stg=stg: e.memset(stg[:], 0.0), writes=[stg])
                        for j, (r, m) in enumerate(chunk):
                            lo = 64 + 128 * (m - 1)
                            i0 = max(0, -lo)
                            i1 = min(128, L - lo)
                            M = i1 - i0
                            tok0 = r + d * (lo + i0)
                            ps = psr.next()
                            p.ops("pe", [lambda e, ps=ps, k=k, tok0=tok0, M=M, i0=i0, d=d, g=g: e.matmul(
                                ps[i0:i0 + M, 0:256], lhsT=xTf[:, k, tok0:tok0 + d * (M - 1) + 1:d], rhs=wV[:, k, g * 256:(g + 1) * 256],
                                start=(k == 0), stop=(k == 7)) for k in range(8)], reads=[xTf, wV], writes=[ps])
                            p.op("dve", lambda e, ps=ps, stg=stg, j=j, i0=i0, M=M, g=g: e.tensor_tensor(
                                out=stg[i0:i0 + M, j, :], in0=ps[i0:i0 + M, 0:256], in1=bv[i0:i0 + M, g * 256:(g + 1) * 256], op=ALU.add),
                                reads=[ps, bv], writes=[stg])
                        p.dma(lambda e, stg=stg, c0=c0, n=len(chunk), g=g: e.dma_start(out=v_s[g][sq, :, c0:c0 + n, :], in_=stg[:, 0:n, :]), stg, False)
                        stop(f'V{g}_{c0}')
                    stop(f'V{g}')
        new_phase()

        with ExitStack() as es:
            def sb(name, shape, dt, dma=False, const=False):
                return p.buf(es.enter_context(nc.sbuf_tensor(name, list(shape), dt)), dma=dma, const=const)

            NT = 32
            NCH = S // 8
            lre = sb("lre", [128, NT], F32, dma=True); lim = sb("lim", [128, NT], F32, dma=True); ldt = sb("ldt", [128, NT], F32, dma=True)
            p.dma(lambda e: e.dma_start(out=lre[:], in_=ssm_d["lre"]), lre, True)
            p.dma(lambda e: e.dma_start(out=lim[:], in_=ssm_d["lim"]), lim, True)
            p.dma(lambda e: e.dma_start(out=ldt[:], in_=ssm_d["ldt"]), ldt, True)
            dsk = sb("dsk", [128, 4], F32, dma=True)
            p.dma(lambda e: e.dma_start(out=dsk[:], in_=ssm_d["dsk"]), dsk, True)
            tI = sb("tI", [128, NCH], F32, dma=True, const=True)
            p.dma(lambda e: e.dma_start(out=tI[:], in_=ssm_d["iota"][:, 0:NCH].partition_broadcast(128)), tI, True)
            sm = {n: sb("sm_" + n, [128, NT], F32) for n in
                  ("dt", "xr", "xi", "rho", "th", "t0", "t1", "f", "sinx", "cosx", "sinh", "em1", "am1", "abi", "den", "kr", "ki", "u0", "u1",
                   "rho8", "th8", "pm", "pc", "ps")}
            smi = sb("smi", [128, NT], I32)
            pwr = sb("pwr", [128, 16, NT], F32); pwi = sb("pwi", [128, 16, NT], F32); npwi = sb("npwi", [128, 16, NT], F32)

            def V(fn, reads, writes):
                return p.op("dve", fn, reads=reads, writes=writes)

            def A(fn, reads, writes):
                return p.op("act", fn, reads=reads, writes=writes)

            def tt(o, a, b, op):
                V(lambda e: e.tensor_tensor(out=o[:], in0=a[:], in1=b[:], op=op), [a, b], [o])

            def tsc(o, a, s1, op0, s2=None, op1=None):
                if op1 is None:
                    V(lambda e: e.tensor_scalar(out=o[:], in0=a[:], scalar1=s1, scalar2=None, op0=op0), [a], [o])
                else:
                    V(lambda e: e.tensor_scalar(out=o[:], in0=a[:], scalar1=s1, scalar2=s2, op0=op0, op1=op1), [a], [o])

            def frac_sin(o, turns_src, mul, add):
                tsc(sm["t0"], turns_src, mul, ALU.mult, add, ALU.add)
                V(lambda e: e.tensor_copy(out=smi[:], in_=sm["t0"][:]), [sm["t0"]], [smi])
                tt(sm["f"], sm["t0"], smi, ALU.subtract)
                A(lambda e: e.activation(out=o[:], in_=sm["f"][:], func=AF.Sin, scale=TWO_PI), [sm["f"]], [o])

            A(lambda e: e.activation(out=sm["dt"][:], in_=ldt[:], func=AF.Exp), [ldt], [sm["dt"]])
            tt(sm["xr"], lre, sm["dt"], ALU.mult)
            tt(sm["xi"], lim, sm["dt"], ALU.mult)
            A(lambda e: e.activation(out=sm["rho"][:], in_=sm["xr"][:], func=AF.Exp), [sm["xr"]], [sm["rho"]])
            A(lambda e: e.activation(out=sm["rho8"][:], in_=sm["xr"][:], func=AF.Exp, scale=8.0), [sm["xr"]], [sm["rho8"]])
            tsc(sm["th"], sm["xi"], 1.0 / TWO_PI, ALU.mult)
            tsc(sm["th8"], sm["th"], 8.0, ALU.mult)
            frac_sin(sm["sinx"], sm["th"], 1.0, 0.0)
            frac_sin(sm["cosx"], sm["th"], 1.0, 0.25)
            frac_sin(sm["sinh"], sm["th"], 0.5, 0.0)
            tsc(sm["em1"], sm["xr"], 0.2, ALU.mult, 1.0, ALU.add)
            for cdiv in (0.25, 1.0 / 3.0, 0.5):
                tt(sm["em1"], sm["em1"], sm["xr"], ALU.mult)
                tsc(sm["em1"], sm["em1"], cdiv, ALU.mult, 1.0, ALU.add)
            tt(sm["em1"], sm["em1"], sm["xr"], ALU.mult)
            tt(sm["am1"], sm["em1"], sm["cosx"], ALU.mult)
            tt(sm["u0"], sm["sinh"], sm["sinh"], ALU.mult)
            V(lambda e: e.scalar_tensor_tensor(out=sm["am1"][:], in0=sm["u0"][:], scalar=-2.0, in1=sm["am1"][:], op0=ALU.mult, op1=ALU.add),
              [sm["u0"], sm["am1"]], [sm["am1"]])
            tt(sm["abi"], sm["rho"], sm["sinx"], ALU.mult)
            tt(sm["den"], lre, lre, ALU.mult)
            tt(sm["u0"], lim, lim, ALU.mult)
            tt(sm["den"], sm["den"], sm["u0"], ALU.add)
            V(lambda e: e.reciprocal(out=sm["den"][:], in_=sm["den"][:]), [sm["den"]], [sm["den"]])
            tt(sm["u0"], sm["am1"], lre, ALU.mult)
            tt(sm["u1"], sm["abi"], lim, ALU.mult)
            tt(sm["u0"], sm["u0"], sm["u1"], ALU.add)
            tt(sm["kr"], sm["u0"], sm["den"], ALU.mult)
            tt(sm["u0"], sm["abi"], lre, ALU.mult)
            tt(sm["u1"], sm["am1"], lim, ALU.mult)
            tt(sm["u0"], sm["u0"], sm["u1"], ALU.subtract)
            tt(sm["ki"], sm["u0"], sm["den"], ALU.mult)
            tsc(sm["t1"], sm["ki"], -1.0, ALU.mult)
            nki = sb("nki", [128, NT], F32)
            V(lambda e: e.tensor_copy(out=nki[:], in_=sm["t1"][:]), [sm["t1"]], [nki])
            for jj in range(16):
                jv = float(jj - 7)
                A(lambda e, jv=jv: e.activation(out=sm["pm"][:], in_=sm["xr"][:], func=AF.Exp, scale=jv), [sm["xr"]], [sm["pm"]])
                frac_sin(sm["ps"], sm["th"], jv, 0.0)
                frac_sin(sm["pc"], sm["th"], jv, 0.25)
                V(lambda e, jj=jj: e.tensor_tensor(out=pwr[:, jj, :], in0=sm["pm"][:], in1=sm["pc"][:], op=ALU.mult), [sm["pm"], sm["pc"]], [pwr])
                V(lambda e, jj=jj: e.tensor_tensor(out=pwi[:, jj, :], in0=sm["pm"][:], in1=sm["ps"][:], op=ALU.mult), [sm["pm"], sm["ps"]], [pwi])
            V(lambda e: e.tensor_scalar(out=npwi[:], in0=pwi[:], scalar1=-1.0, scalar2=None, op0=ALU.mult), [pwi], [npwi])
            for b_ in (pwr, pwi, npwi, sm["kr"], sm["ki"], nki, sm["rho8"], sm["th8"]):
                b_.const = True

            Dd = sb("Dd", [128, 4, 128], BF16)
            for q in range(4):
                V(lambda e, q=q: e.tensor_scalar(out=Dd[:, q, :], in0=ident[:], scalar1=dsk[:, q:q + 1], scalar2=None, op0=ALU.mult),
                  [ident, dsk], [Dd])
            Dd.const = True

            NSET = 2
            bz = [[sb(f"bz{i}_{k}", [128, 2, 128], F32, dma=True) for k in range(2)] for i in range(NSET)]
            cbt = [[sb(f"cbt{i}_{k}", [32, 2, 128], F32, dma=True) for k in range(2)] for i in range(NSET)]
            Bz = [[sb(f"Bz{i}_{k}", [128, 2, 128], F32) for k in range(2)] for i in range(NSET)]
            CT = [[sb(f"CT{i}_{k}", [128, 2, 64], F32) for k in range(2)] for i in range(NSET)]
            XT = [[sb(f"XT{i}_{k}", [128, 8, 2, 128], BF16) for k in range(2)] for i in range(NSET)]
            KT = [[sb(f"KT{i}_{k}", [128, 8, 64], BF16) for k in range(2)] for i in range(NSET)]
            LY = [[sb(f"LY{i}_{k}", [128, 8, 2, 64], BF16) for k in range(2)] for i in range(NSET)]
            cosN = [[sb(f"cosN{i}_{k}", [128, NCH], F32) for k in range(2)] for i in range(NSET)]
            sinN = [[sb(f"sinN{i}_{k}", [128, NCH], F32) for k in range(2)] for i in range(NSET)]
            rho8T = [[sb(f"rho8T{i}_{k}", [128, NCH], F32) for k in range(2)] for i in range(NSET)]
            for i in range(NSET):
                for k in range(2):
                    p.op("pool", lambda e, i=i, k=k: e.memset(CT[i][k][:], 0.0), writes=[CT[i][k]])
            xtmp = Ring([sb(f"xtmp{i}", [128, 2, 128], F32) for i in range(3)])
            lyf = Ring([sb(f"lyf{i}", [128, 2, 64], F32) for i in range(3)])
            turN = sb("turN", [128, NCH], F32); turNi = sb("turNi", [128, NCH], I32)
            uTr = Ring([sb(f"uTc{i}", [128, S], BF16, dma=True) for i in range(2)])
            tmpr = Ring([sb(f"tmpS{i}", [128, NCH], F32) for i in range(8)])
            wrr = Ring([sb(f"wS{i}", [128, NCH], F32) for i in range(4)])
            Rrr = Ring([sb(f"RS{i}", [128, NCH], F32) for i in range(4)])
            Vrr = Ring([sb(f"VS{i}", [128, NCH], F32) for i in range(4)])
            Zr_ = [Ring([sb(f"ZS{k}_{i}", [128, 2, NCH], BF16) for i in range(2)]) for k in range(2)]
            zor = Ring([sb(f"zo{i}", [128, S], BF16, dma=True) for i in range(2)])
            psT = Ring(psum[4:8])
            psSt = [psum[0:2], psum[2:4]]

            def cmul(o, orow, oi_row, src, sr, si, nsi):
                V(lambda e: e.tensor_scalar(out=o[:, 0, :], in0=src[:, 0, :], scalar1=sr, scalar2=None, op0=ALU.mult), [src], [o])
                V(lambda e: e.scalar_tensor_tensor(out=o[:, 0, :], in0=src[:, 1, :], scalar=nsi, in1=o[:, 0, :], op0=ALU.mult, op1=ALU.add), [src, o], [o])
                V(lambda e: e.tensor_scalar(out=o[:, 1, :], in0=src[:, 1, :], scalar1=sr, scalar2=None, op0=ALU.mult), [src], [o])
                V(lambda e: e.scalar_tensor_tensor(out=o[:, 1, :], in0=src[:, 0, :], scalar=si, in1=o[:, 1, :], op0=ALU.mult, op1=ALU.add), [src, o], [o])

            def prep(gp, st):
                for k in range(2):
                    j = k * 16 + gp
                    p.dma(lambda e: e.dma_start(out=bz[st][k][:, 0, :], in_=ssm_d["bzr"][:, j, :]), bz[st][k], True)
                    p.dma(lambda e: e.dma_start(out=bz[st][k][:, 1, :], in_=ssm_d["bzi"][:, j, :]), bz[st][k], True)
                    p.dma(lambda e: e.dma_start(out=cbt[st][k][:, 0, :], in_=ssm_d["cbr"][:, j, :]), cbt[st][k], True)
                    p.dma(lambda e: e.dma_start(out=cbt[st][k][:, 1, :], in_=ssm_d["cbi"][:, j, :]), cbt[st][k], True)
                    cmul(Bz[st][k], None, None, bz[st][k], sm["kr"][:, j:j + 1], sm["ki"][:, j:j + 1], nki[:, j:j + 1])
                    ps = psT.next()
                    p.ops("pe", [lambda e: e.transpose(out=ps[:, 0:32], in_=cbt[st][k][:, 0, :], identity=ident[0:32, 0:32]),
                                 lambda e: e.transpose(out=ps[:, 32:64], in_=cbt[st][k][:, 1, :], identity=ident[0:32, 0:32])],
                          reads=[cbt[st][k], ident], writes=[ps])
                    A(lambda e: e.copy(out=CT[st][k][:, 0, 32:64], in_=ps[:, 0:32]), [ps], [CT[st][k]])
                    A(lambda e: e.mul(out=CT[st][k][:, 1, 32:64], in_=ps[:, 32:64], mul=-1.0), [ps], [CT[st][k]])
                    for s_ in range(8):
                        if s_ == 0:
                            xs_ = Bz[st][k]
                        else:
                            xs_ = xtmp.next()
                            jj = 7 - s_
                            cmul(xs_, None, None, Bz[st][k], pwr[:, jj, j:j + 1], pwi[:, jj, j:j + 1], npwi[:, jj, j:j + 1])
                        ps = psT.next()
                        p.ops("pe", [lambda e: e.transpose(out=ps[:, 0:128], in_=xs_[:, 0, :], identity=ident[:]),
                                     lambda e: e.transpose(out=ps[:, 128:256], in_=xs_[:, 1, :], identity=ident[:])],
                              reads=[xs_, ident], writes=[ps])
                        A(lambda e: e.copy(out=XT[st][k][:, s_, :, :], in_=ps[:, 0:256].rearrange("p (r c) -> p r c", r=2)), [ps], [XT[st][k]])
                    for tau in range(8):
                        ly = lyf.next()
                        jj = 7 + tau
                        ctr = CT[st][k]
                        V(lambda e: e.tensor_scalar(out=ly[:, 0, :], in0=ctr[:, 0, :], scalar1=pwr[:, jj, j:j + 1], scalar2=None, op0=ALU.mult), [ctr], [ly])
                        V(lambda e: e.scalar_tensor_tensor(out=ly[:, 0, :], in0=ctr[:, 1, :], scalar=pwi[:, jj, j:j + 1], in1=ly[:, 0, :], op0=ALU.mult, op1=ALU.add), [ctr, ly], [ly])
                        V(lambda e: e.tensor_scalar(out=ly[:, 1, :], in0=ctr[:, 1, :], scalar1=pwr[:, jj, j:j + 1], scalar2=None, op0=ALU.mult), [ctr], [ly])
                        V(lambda e: e.scalar_tensor_tensor(out=ly[:, 1, :], in0=ctr[:, 0, :], scalar=npwi[:, jj, j:j + 1], in1=ly[:, 1, :], op0=ALU.mult, op1=ALU.add), [ctr, ly], [ly])
                        A(lambda e: e.copy(out=LY[st][k][:, tau, :, :], in_=ly[:]), [ly], [LY[st][k]])
                        ps = psT.next()
                        p.ops("pe", [lambda e: e.matmul(ps[:, 0:64], lhsT=Bz[st][k][:, 0, :], rhs=ly[:, 0, :], start=True, stop=False),
                                     lambda e: e.matmul(ps[:, 0:64], lhsT=Bz[st][k][:, 1, :], rhs=ly[:, 1, :], start=False, stop=True)],
                              reads=[Bz[st][k], ly], writes=[ps])
                        A(lambda e: e.copy(out=KT[st][k][:, tau, :], in_=ps[:, 0:64]), [ps], [KT[st][k]])
                    for (tab, addc) in ((sinN[st][k], 0.0), (cosN[st][k], 0.25)):
                        V(lambda e: e.tensor_scalar(out=turN[:], in0=tI[:], scalar1=sm["th8"][:, j:j + 1], scalar2=addc, op0=ALU.mult, op1=ALU.add), [tI], [turN])
                        V(lambda e: e.tensor_copy(out=turNi[:], in_=turN[:]), [turN], [turNi])
                        V(lambda e: e.tensor_tensor(out=turN[:], in0=turN[:], in1=turNi[:], op=ALU.subtract), [turN, turNi], [turN])
                        A(lambda e: e.activation(out=tab[:], in_=turN[:], func=AF.Sin, scale=TWO_PI), [turN], [tab])
                    V(lambda e: e.tensor_scalar(out=rho8T[st][k][:], in0=tI[:], scalar1=0.0, scalar2=sm["rho8"][:, j:j + 1], op0=ALU.mult, op1=ALU.add), [tI], [rho8T[st][k]])

            def run(gp, st, sq):
                q = gp // 4
                qq = gp % 4
                if qq < 3:
                    rows = slice(32 * qq, 32 * qq + 32); lcs = slice(32, 64); dds = slice(32 * qq, 32 * qq + 32)
                else:
                    rows = slice(64, 128); lcs = slice(0, 64); dds = slice(64, 128)
                orow = slice(32 * qq, 32 * qq + 32)
                uT = uTr.next()
                p.dma(lambda e: e.dma_start(out=uT[:], in_=uT_s[sq, q * 128:(q + 1) * 128, :]), uT, True)

                def ucols(k, s_):
                    if k == 0:
                        return slice(s_, S, 8)
                    return slice(S - 1 - s_, None, -8)
                Z = []
                for k in range(2):
                    pr, pi = psSt[k]
                    p.ops("pe", [lambda e, s_=s_: e.matmul(pr[:], lhsT=XT[st][k][:, s_, 0, :], rhs=uT[:, ucols(k, s_)], start=(s_ == 0), stop=(s_ == 7)) for s_ in range(8)],
                          reads=[XT[st][k], uT], writes=[pr])
                    p.ops("pe", [lambda e, s_=s_: e.matmul(pi[:], lhsT=XT[st][k][:, s_, 1, :], rhs=uT[:, ucols(k, s_)], start=(s_ == 0), stop=(s_ == 7)) for s_ in range(8)],
                          reads=[XT[st][k], uT], writes=[pi])
                    cs_, sn_ = cosN[st][k], sinN[st][k]
                    t0_, t1_, t2_, t3_ = tmpr.next(), tmpr.next(), tmpr.next(), tmpr.next()
                    V(lambda e: e.tensor_tensor(out=t0_[:], in0=pr[:], in1=cs_[:], op=ALU.mult), [pr, cs_], [t0_])
                    V(lambda e: e.tensor_tensor(out=t1_[:], in0=pi[:], in1=sn_[:], op=ALU.mult), [pi, sn_], [t1_])
                    V(lambda e: e.tensor_tensor(out=t2_[:], in0=pi[:], in1=cs_[:], op=ALU.mult), [pi, cs_], [t2_])
                    V(lambda e: e.tensor_tensor(out=t3_[:], in0=pr[:], in1=sn_[:], op=ALU.mult), [pr, sn_], [t3_])
                    w_r, w_i = wrr.next(), wrr.next()
                    V(lambda e: e.tensor_tensor(out=w_r[:], in0=t0_[:], in1=t1_[:], op=ALU.add), [t0_, t1_], [w_r])
                    V(lambda e: e.tensor_tensor(out=w_i[:], in0=t2_[:], in1=t3_[:], op=ALU.subtract), [t2_, t3_], [w_i])
                    R_r, R_i = Rrr.next(), Rrr.next()
                    V(lambda e: e.tensor_tensor_scan(out=R_r[:], data0=rho8T[st][k][:], data1=w_r[:], initial=0.0, op0=ALU.mult, op1=ALU.add), [rho8T[st][k], w_r], [R_r])
                    V(lambda e: e.tensor_tensor_scan(out=R_i[:], data0=rho8T[st][k][:], data1=w_i[:], initial=0.0, op0=ALU.mult, op1=ALU.add), [rho8T[st][k], w_i], [R_i])
                    t0_, t1_, t2_, t3_ = tmpr.next(), tmpr.next(), tmpr.next(), tmpr.next()
                    V(lambda e: e.tensor_tensor(out=t0_[:], in0=R_r[:], in1=cs_[:], op=ALU.mult), [R_r, cs_], [t0_])
                    V(lambda e: e.tensor_tensor(out=t1_[:], in0=R_i[:], in1=sn_[:], op=ALU.mult), [R_i, sn_], [t1_])
                    V(lambda e: e.tensor_tensor(out=t2_[:], in0=R_i[:], in1=cs_[:], op=ALU.mult), [R_i, cs_], [t2_])
                    V(lambda e: e.tensor_tensor(out=t3_[:], in0=R_r[:], in1=sn_[:], op=ALU.mult), [R_r, sn_], [t3_])
                    v_r, v_i = Vrr.next(), Vrr.next()
                    V(lambda e: e.tensor_tensor(out=v_r[:], in0=t0_[:], in1=t1_[:], op=ALU.subtract), [t0_, t1_], [v_r])
                    V(lambda e: e.tensor_tensor(out=v_i[:], in0=t2_[:], in1=t3_[:], op=ALU.add), [t2_, t3_], [v_i])
                    z_ = Zr_[k].next()
                    V(lambda e: e.tensor_tensor(out=z_[:, 0, :], in0=v_r[:], in1=pr[:], op=ALU.subtract), [v_r, pr], [z_])
                    V(lambda e: e.tensor_tensor(out=z_[:, 1, :], in0=v_i[:], in1=pi[:], op=ALU.subtract), [v_i, pi], [z_])
                    Z.append(z_)
                zo = zor.next()
                for tau in range(8):
                    py = psT.next()
                    fns = [
                        lambda e: e.matmul(py[rows, :], lhsT=LY[st][0][:, tau, 0, lcs], rhs=Z[0][:, 0, :], start=True, stop=False),
                        lambda e: e.matmul(py[rows, :], lhsT=LY[st][0][:, tau, 1, lcs], rhs=Z[0][:, 1, :], start=False, stop=False),
                        lambda e: e.matmul(py[rows, :], lhsT=LY[st][1][:, 7 - tau, 0, lcs], rhs=Z[1][:, 0, ::-1], start=False, stop=False),
                        lambda e: e.matmul(py[rows, :], lhsT=LY[st][1][:, 7 - tau, 1, lcs], rhs=Z[1][:, 1, ::-1], start=False, stop=False),
                    ]
                    for s_ in range(0, tau + 1):
                        fns.append(lambda e, s_=s_: e.matmul(py[rows, :], lhsT=KT[st][0][:, tau - s_, lcs], rhs=uT[:, s_:S:8], start=False, stop=False))
                    for s_ in range(tau, 8):
                        fns.append(lambda e, s_=s_: e.matmul(py[rows, :], lhsT=KT[st][1][:, s_ - tau, lcs], rhs=uT[:, s_:S:8], start=False, stop=False))
                    fns.append(lambda e: e.matmul(py[rows, :], lhsT=Dd[:, q, dds], rhs=uT[:, tau:S:8], start=False, stop=True))
                    p.ops("pe", fns, reads=[LY[st][0], LY[st][1], KT[st][0], KT[st][1], Dd, Z[0], Z[1], uT], writes=[py])
                    A(lambda e: e.activation(out=zo[rows, tau:S:8], in_=py[rows, :], func=AF.Gelu), [py], [zo])
                p.dma(lambda e: e.dma_start(out=zT_s[sq, gp * 32:(gp + 1) * 32, :], in_=zo[orow, :]), zo, False)

            prep(0, 0)
            for gp in range(16):
                st = gp % NSET
                for sq in range(nseq):
                    run(gp, st, sq)
                    if sq == 0 and gp + 1 < 16:
                        prep(gp + 1, (gp + 1) % NSET)
                stop(f'S_gp{gp}')
        new_phase()
        stop('S')

        with ExitStack() as es:
            def sb(name, shape, dt, dma=False, const=False):
                return p.buf(es.enter_context(nc.sbuf_tensor(name, list(shape), dt)), dma=dma, const=const)

            mstage = sb("mstage", [128, 256], F32, dma=True)
            ostage = sb("ostage", [128, 3, 64], F32, dma=True)
            maskB = sb("maskB", [128, 256], BF16); ones3 = sb("ones3", [128, 3, 64], BF16); identb = sb("identb", [128, 128], BF16)
            p.dma(lambda e: e.dma_start(out=mstage[:], in_=md["maskb"]), mstage, True)
            p.dma(lambda e: e.dma_start(out=ostage[:], in_=md["ones3"]), ostage, True)
            p.op("dve", lambda e: e.tensor_copy(out=maskB[:], in_=mstage[:]), reads=[mstage], writes=[maskB])
            p.op("dve", lambda e: e.tensor_copy(out=ones3[:], in_=ostage[:]), reads=[ostage], writes=[ones3])
            p.op("dve", lambda e: e.tensor_copy(out=identb[:], in_=ident[:]), reads=[ident], writes=[identb])
            maskB.const = True; ones3.const = True; identb.const = True
            qTr = Ring([sb(f"qTa{i}", [128, S], BF16, dma=True) for i in range(2)])
            kTr = Ring([sb(f"kTa{i}", [128, S + 2 * KPAD], BF16, dma=True) for i in range(2)])
            for b in kTr.bufs:
                p.op("pool", lambda e, b=b: e.memset(b[:], 0.0), writes=[b])
            vTr = Ring([sb(f"vTa{i}", [128, 48, 128], BF16, dma=True) for i in range(2)])
            acc = sb("acc", [128, 2, S], F32)
            rden = sb("rden", [128, S], F32)
            aTo = sb("aTo", [128, S], BF16, dma=True)
            PTr = Ring([sb(f"PT{i}", [128, 256], BF16) for i in range(4)])
            psS = Ring(psum[0:4]); psO = Ring(psum[4:8])
            SCALE = 64.0 ** -0.5
            for sq in range(nseq):
                for c in range(2):
                    for g in range(3):
                        d = DIL[g]; L = S // d; nb = L // 128 + 1
                        qT = qTr.next(); kT = kTr.next(); vT = vTr.next()
                        ch = 2 * g + c
                        p.dma(lambda e: e.dma_start(out=qT[:], in_=qT_s[sq, ch * 128:(ch + 1) * 128, :]), qT, True)
                        p.dma(lambda e: e.dma_start(out=kT[:, KPAD:KPAD + S], in_=kT_s[sq, ch * 128:(ch + 1) * 128, :]), kT, True)
                        p.dma(lambda e: e.dma_start(out=vT[:, 0:NBLK[g], :], in_=v_s[g][sq, :, :, c * 128:(c + 1) * 128]), vT, True)
                        for r in range(d):
                            for a in range(L // 128):
                                qsl = slice(r + d * 128 * a, r + d * 128 * a + d * 127 + 1, d)
                                pO = psO.next()
                                fns = []
                                pts = []
                                for hp in range(2):
                                    pb = 64 * hp
                                    pS = psS.next()
                                    ks = []
                                    for m in (a, a + 1):
                                        st = KPAD + r + d * (128 * m - 64)
                                        ks.append(slice(st, st + d * 127 + 1, d))
                                    p.ops("pe", [
                                        lambda e, pS=pS, pb=pb, ks=ks: e.matmul(pS[:, 0:128], lhsT=kT[pb:pb + 64, ks[0]], rhs=qT[pb:pb + 64, qsl], start=True, stop=False),
                                        lambda e, pS=pS, pb=pb, ks=ks: e.matmul(pS[:, 128:256], lhsT=kT[pb:pb + 64, ks[1]], rhs=qT[pb:pb + 64, qsl], start=False, stop=False),
                                        lambda e, pS=pS: e.matmul(pS[:, 0:256], lhsT=identb[:], rhs=maskB[:], start=False, stop=True),
                                    ], reads=[kT, qT, identb, maskB], writes=[pS])
                                    PT = PTr.next()
                                    p.op("act", lambda e, pS=pS, PT=PT: e.activation(out=PT[:], in_=pS[:, 0:256], func=AF.Exp, scale=SCALE), reads=[pS], writes=[PT])
                                    pts.append(PT)
                                    o1 = 1 if a == 0 else 0
                                    o2 = 2 if a + 1 == nb - 1 else 0
                                    b1 = r * nb + a; b2 = r * nb + a + 1
                                    fns += [
                                        lambda e, PT=PT, pb=pb, b1=b1, hp=hp: e.matmul(pO[pb:pb + 64, 0:128], lhsT=vT[:, b1, hp * 64:(hp + 1) * 64], rhs=PT[:, 0:128], start=True, stop=False),
                                        lambda e, PT=PT, pb=pb, b2=b2, hp=hp: e.matmul(pO[pb:pb + 64, 0:128], lhsT=vT[:, b2, hp * 64:(hp + 1) * 64], rhs=PT[:, 128:256], start=False, stop=False),
                                        lambda e, PT=PT, pb=pb, o1=o1: e.matmul(pO[pb:pb + 64, 128:256], lhsT=ones3[:, o1, :], rhs=PT[:, 0:128], start=False, stop=False),
                                        lambda e, PT=PT, pb=pb, o2=o2: e.matmul(pO[pb:pb + 64, 128:256], lhsT=ones3[:, o2, :], rhs=PT[:, 128:256], start=False, stop=True),
                                    ]
                                p.ops("pe", fns, reads=pts + [vT, ones3], writes=[pO])
                                pov = pO[:, 0:256].rearrange("p (n i) -> p n i", n=2)
                                if g == 0:
                                    p.op("dve", lambda e, pov=pov: e.tensor_copy(out=acc[:, :, qsl], in_=pov), reads=[pO], writes=[acc])
                                else:
                                    p.op("dve", lambda e, pov=pov: e.tensor_tensor(out=acc[:, :, qsl], in0=pov, in1=acc[:, :, qsl], op=ALU.add), reads=[pO, acc], writes=[acc])
                    p.op("dve", lambda e: e.reciprocal(out=rden[:], in_=acc[:, 1, :]), reads=[acc], writes=[rden])
                    p.op("dve", lambda e: e.tensor_tensor(out=aTo[:], in0=acc[:, 0, :], in1=rden[:], op=ALU.mult), reads=[acc, rden], writes=[aTo])
                    p.dma(lambda e: e.dma_start(out=aT_s[sq, c * 128:(c + 1) * 128, :], in_=aTo[:]), aTo, False)
        new_phase()
        stop('T')

        def load_w_bf16(sbf, wdst, src, K, N, tag, stg=None):
            piece = 1024 if N >= 1024 else N
            if stg is None:
                stg = Ring([sbf(f"wl_{tag}{i}", [128, piece], F32, dma=True) for i in range(2)])
            engs = ("dve", "pool", "act")
            n = 0
            for k in range(K):
                for c0 in range(0, N, piece):
                    w = min(piece, N - c0)
                    st = stg.next()
                    p.dma(lambda e, st=st, k=k, c0=c0, w=w: e.dma_start(out=st[:, 0:w], in_=src[k * 128:(k + 1) * 128, c0:c0 + w]), st, True)
                    eng = engs[n % 3]; n += 1
                    if eng == "act":
                        p.op("act", lambda e, st=st, k=k, c0=c0, w=w: e.copy(out=wdst[:, k, c0:c0 + w], in_=st[:, 0:w]), reads=[st], writes=[wdst])
                    else:
                        p.op(eng, lambda e, st=st, k=k, c0=c0, w=w: e.tensor_copy(out=wdst[:, k, c0:c0 + w], in_=st[:, 0:w]), reads=[st], writes=[wdst])
            wdst.const = True

        def layer_norm(sbufs, hpre, gB, bB, outt):
            stats, mv, rstd, hn = sbufs
            for n in range(2):
                p.op("dve", lambda e, n=n: e.bn_stats(out=stats[:, n, :], in_=hpre[:, n * 512:(n + 1) * 512]), reads=[hpre], writes=[stats])
            p.op("dve", lambda e: e.bn_aggr(out=mv[:], in_=stats[:].rearrange("p n s -> p (n s)")), reads=[stats], writes=[mv])
            p.op("act", lambda e: e.activation(out=rstd[:], in_=mv[:, 1:2], func=AF.Sqrt, bias=epsb[:, 0:1]), reads=[mv, epsb], writes=[rstd])
            p.op("dve", lambda e: e.reciprocal(out=rstd[:], in_=rstd[:]), reads=[rstd], writes=[rstd])
            p.op("dve", lambda e: e.tensor_scalar(out=hn[:], in0=hpre[:], scalar1=mv[:, 0:1], scalar2=rstd[:, 0:1], op0=ALU.subtract, op1=ALU.mult),
                 reads=[hpre, mv, rstd], writes=[hn])
            p.op("dve", lambda e: e.tensor_tensor(out=hn[:], in0=hn[:], in1=gB[:], op=ALU.mult), reads=[hn, gB], writes=[hn])
            p.op("dve", lambda e: e.tensor_tensor(out=outt[:], in0=hn[:], in1=bB[:], op=ALU.add), reads=[hn, bB], writes=[outt])

        with ExitStack() as es:
            def sb(name, shape, dt, dma=False, const=False):
                return p.buf(es.enter_context(nc.sbuf_tensor(name, list(shape), dt)), dma=dma, const=const)
            wgv = sb("wgv", [128, 4, D], BF16); wgg = sb("wgg", [128, 4, D], BF16); wab = sb("wab", [128, 2, D], BF16); wo = sb("wo", [128, 8, D], BF16)
            load_w_bf16(sb, wgv, md["wgv"], 4, D, "a"); load_w_bf16(sb, wgg, md["wgg"], 4, D, "b")
            load_w_bf16(sb, wab, md["wab"], 2, D, "c"); load_w_bf16(sb, wo, md["wo"], 8, D, "d")
            gB = sb("ln1gB", [128, D], F32, dma=True, const=True); bB = sb("ln1bB", [128, D], F32, dma=True, const=True)
            p.dma(lambda e: e.dma_start(out=gB[:], in_=md["ln1g"].partition_broadcast(128)), gB, True)
            p.dma(lambda e: e.dma_start(out=bB[:], in_=md["ln1b"].partition_broadcast(128)), bB, True)
            epsb = sb("epsb", [128, 1], F32)
            p.op("pool", lambda e: e.memset(epsb[:], LN_EPS), writes=[epsb])
            zT = sb("zTm", [128, 4, 512], BF16, dma=True); aT = sb("aTm", [128, 2, 512], BF16, dma=True); gT = sb("gTm", [128, 16, 512], BF16, dma=True)
            xs = Ring([sb(f"xm{i}", [128, D], F32, dma=True) for i in range(2)])
            mixT = sb("mixT", [128, 8, 512], BF16)
            sg = Ring([sb(f"sg{i}", [128, 512], F32) for i in range(2)])
            t1r = Ring([sb(f"t1m{i}", [128, 512], F32) for i in range(2)])
            t2r = Ring([sb(f"t2m{i}", [128, 512], F32) for i in range(2)])
            hpre = Ring([sb(f"hpre{i}", [128, D], F32) for i in range(2)])
            hout = Ring([sb(f"hout{i}", [128, D], F32, dma=True) for i in range(2)])
            hTt = Ring([sb(f"hTt{i}", [128, 8, 128], BF16, dma=True) for i in range(2)])
            lnb = (sb("st1", [128, 2, 6], F32), sb("mv1", [128, 2], F32), sb("rstd1", [128, 1], F32), sb("hn1", [128, D], F32))
            psr = Ring(psum)
            for sq in range(nseq):
                for tb in range(S // 512):
                    ts = slice(tb * 512, (tb + 1) * 512)
                    p.dma(lambda e: e.dma_start(out=zT[:], in_=zT_s[sq].rearrange("(k q) t -> q k t", q=128)[:, :, ts]), zT, True)
                    p.dma(lambda e: e.dma_start(out=aT[:], in_=aT_s[sq].rearrange("(k q) t -> q k t", q=128)[:, :, ts]), aT, True)
                    p.dma(lambda e: e.dma_start(out=gT[:], in_=gT_s[sq].rearrange("(k q) t -> q k t", q=128)[:, :, ts]), gT, True)
                    for do in range(8):
                        ds_ = slice(do * 128, (do + 1) * 128)
                        pA = psr.next(); pG = psr.next(); pB = psr.next()
                        p.ops("pe", [lambda e, k=k: e.matmul(pA[:], lhsT=wgv[:, k, ds_], rhs=zT[:, k, :], start=(k == 0), stop=(k == 3)) for k in range(4)], reads=[wgv, zT], writes=[pA])
                        p.ops("pe", [lambda e, k=k: e.matmul(pG[:], lhsT=wgg[:, k, ds_], rhs=zT[:, k, :], start=(k == 0), stop=(k == 3)) for k in range(4)], reads=[wgg, zT], writes=[pG])
                        p.ops("pe", [lambda e, k=k: e.matmul(pB[:], lhsT=wab[:, k, ds_], rhs=aT[:, k, :], start=(k == 0), stop=(k == 1)) for k in range(2)], reads=[wab, aT], writes=[pB])
                        sgt = sg.next(); t1 = t1r.next(); t2 = t2r.next()
                        p.op("act", lambda e: e.activation(out=sgt[:], in_=pG[:], func=AF.Sigmoid), reads=[pG], writes=[sgt])
                        p.op("dve", lambda e: e.tensor_tensor(out=t1[:], in0=pA[:], in1=sgt[:], op=ALU.mult), reads=[pA, sgt], writes=[t1])
                        p.op("dve", lambda e: e.tensor_tensor(out=t2[:], in0=pB[:], in1=gT[:, 8 + do, :], op=ALU.mult), reads=[pB, gT], writes=[t2])
                        p.op("dve", lambda e: e.tensor_tensor(out=t1[:], in0=t1[:], in1=gT[:, do, :], op=ALU.mult), reads=[t1, gT], writes=[t1])
                        p.op("dve", lambda e: e.tensor_tensor(out=mixT[:, do, :], in0=t1[:], in1=t2[:], op=ALU.add), reads=[t1, t2], writes=[mixT])
                    for i in range(4):
                        tok = slice(tb * 512 + i * 128, tb * 512 + (i + 1) * 128)
                        xt = xs.next()
                        p.dma(lambda e: e.dma_start(out=xt[:], in_=x_d[sq, tok, :]), xt, True)
                        hp_ = hpre.next()
                        for n in range(2):
                            ns = slice(n * 512, (n + 1) * 512)
                            po = psr.next()
                            p.ops("pe", [lambda e, k=k: e.matmul(po[:], lhsT=mixT[:, k, i * 128:(i + 1) * 128], rhs=wo[:, k, ns], start=(k == 0), stop=(k == 7)) for k in range(8)],
                                  reads=[mixT, wo], writes=[po])
                            p.op("dve", lambda e: e.scalar_tensor_tensor(out=hp_[:, ns], in0=xt[:, ns], scalar=ALPHA, in1=po[:], op0=ALU.mult, op1=ALU.add),
                                 reads=[xt, po], writes=[hp_])
                        ho = hout.next()
                        layer_norm(lnb, hp_, gB, bB, ho)
                        p.dma(lambda e: e.dma_start(out=h_s[sq, tok, :], in_=ho[:]), ho, False)
                        hT = hTt.next()
                        for kk in range(2):
                            pt = psr.next()
                            p.ops("pe", [lambda e, k4=k4: e.transpose(out=pt[:, k4 * 128:(k4 + 1) * 128], in_=ho[:, (kk * 4 + k4) * 128:(kk * 4 + k4 + 1) * 128], identity=ident[:])
                                         for k4 in range(4)], reads=[ho, ident], writes=[pt])
                            p.op("act", lambda e: e.copy(out=hT[:, kk * 4:(kk + 1) * 4, :], in_=pt[:].rearrange("p (k t) -> p k t", k=4)), reads=[pt], writes=[hT])
                        p.dma(lambda e: e.dma_start(out=hT_s[sq].rearrange("(k q) t -> q k t", q=128)[:, :, tok], in_=hT[:]), hT, False)
        new_phase()
        stop('M1')

        with ExitStack() as es:
            def sb(name, shape, dt, dma=False, const=False):
                return p.buf(es.enter_context(nc.sbuf_tensor(name, list(shape), dt)), dma=dma, const=const)
            wup = sb("wup", [128, 8, 2 * DFF], BF16); wdn = sb("wdn", [128, 22, D], BF16)
            stg_ = Ring([sb(f"wl_u{i}", [128, 1024], F32, dma=True) for i in range(2)])
            load_w_bf16(sb, wup, md["wup"], 8, 2 * DFF, "u", stg_); load_w_bf16(sb, wdn, md["wdn"], 22, D, "v", stg_)
            gB = sb("ln2gB", [128, D], F32, dma=True, const=True); bB = sb("ln2bB", [128, D], F32, dma=True, const=True)
            p.dma(lambda e: e.dma_start(out=gB[:], in_=md["ln2g"].partition_broadcast(128)), gB, True)
            p.dma(lambda e: e.dma_start(out=bB[:], in_=md["ln2b"].partition_broadcast(128)), bB, True)
            cw = sb("cw", [128, 44, 3], F32, dma=True, const=True); cbias = sb("cbias", [128, 44], F32, dma=True, const=True)
            p.dma(lambda e: e.dma_start(out=cw[:], in_=md["cw"]), cw, True)
            p.dma(lambda e: e.dma_start(out=cbias[:], in_=md["cbias"]), cbias, True)
            epsb = sb("epsb2", [128, 1], F32)
            p.op("pool", lambda e: e.memset(epsb[:], LN_EPS), writes=[epsb])
            hT = sb("hTf", [128, 8, 514], BF16, dma=True)
            hres = Ring([sb(f"hres{i}", [128, D], F32, dma=True) for i in range(1)])
            cvr = Ring([sb(f"cv{i}", [128, 512], F32) for i in range(4)])
            actT = sb("actT", [128, 22, 512], BF16)
            opre = sb("opre", [128, D], F32)
            oout = Ring([sb(f"oout{i}", [128, D], F32, dma=True) for i in range(1)])
            lnb = (sb("st2", [128, 2, 6], F32), sb("mv2", [128, 2], F32), sb("rstd2", [128, 1], F32), sb("hn2", [128, D], F32))
            psm = Ring(psum[0:6]); psh = Ring(psum[6:8])
            for sq in range(nseq):
                for tb in range(S // 512):
                    t0 = tb * 512
                    lo = max(t0 - 1, 0); hi = min(t0 + 513, S)
                    if t0 == 0:
                        p.op("pool", lambda e: e.memset(hT[:, :, 0:1], 0.0), writes=[hT])
                    if t0 + 512 == S:
                        p.op("pool", lambda e: e.memset(hT[:, :, 513:514], 0.0), writes=[hT])
                    p.dma(lambda e: e.dma_start(out=hT[:, :, lo - (t0 - 1):hi - (t0 - 1)], in_=hT_s[sq].rearrange("(k q) t -> q k t", q=128)[:, :, lo:hi]), hT, True)
                    for c in range(22):
                        cvs = []
                        for ch in (c, 22 + c):
                            cs_ = slice(ch * 128, (ch + 1) * 128)
                            pm = psm.next(); ph = psh.next()
                            p.ops("pe", [lambda e, k=k: e.matmul(pm[:], lhsT=wup[:, k, cs_], rhs=hT[:, k, 1:513], start=(k == 0), stop=(k == 7)) for k in range(8)]
                                  + [lambda e, k=k: e.matmul(ph[:, 0:2], lhsT=wup[:, k, cs_], rhs=hT[:, k, 0:514:513], start=(k == 0), stop=(k == 7)) for k in range(8)],
                                  reads=[wup, hT], writes=[pm, ph])
                            cv = cvr.next()
                            p.op("act", lambda e: e.activation(out=cv[:], in_=pm[:], func=AF.Identity, scale=cw[:, ch, 1:2], bias=cbias[:, ch:ch + 1]),
                                 reads=[pm, cw, cbias], writes=[cv])
                            p.op("dve", lambda e: e.scalar_tensor_tensor(out=cv[:, 1:512], in0=pm[:, 0:511], scalar=cw[:, ch, 0:1], in1=cv[:, 1:512], op0=ALU.mult, op1=ALU.add), reads=[pm, cw, cv], writes=[cv])
                            p.op("dve", lambda e: e.scalar_tensor_tensor(out=cv[:, 0:511], in0=pm[:, 1:512], scalar=cw[:, ch, 2:3], in1=cv[:, 0:511], op0=ALU.mult, op1=ALU.add), reads=[pm, cw, cv], writes=[cv])
                            p.op("dve", lambda e: e.scalar_tensor_tensor(out=cv[:, 0:1], in0=ph[:, 0:1], scalar=cw[:, ch, 0:1], in1=cv[:, 0:1], op0=ALU.mult, op1=ALU.add), reads=[ph, cw, cv], writes=[cv])
                            p.op("dve", lambda e: e.scalar_tensor_tensor(out=cv[:, 511:512], in0=ph[:, 1:2], scalar=cw[:, ch, 2:3], in1=cv[:, 511:512], op0=ALU.mult, op1=ALU.add), reads=[ph, cw, cv], writes=[cv])
                            cvs.append(cv)
                        p.op("act", lambda e: e.activation(out=cvs[0][:], in_=cvs[0][:], func=AF.Gelu), reads=[cvs[0]], writes=[cvs[0]])
                        p.op("dve", lambda e: e.tensor_tensor(out=actT[:, c, :], in0=cvs[0][:], in1=cvs[1][:], op=ALU.mult), reads=cvs, writes=[actT])
                    for i in range(4):
                        tok = slice(t0 + i * 128, t0 + (i + 1) * 128)
                        hr = hres.next()
                        p.dma(lambda e: e.dma_start(out=hr[:], in_=h_s[sq, tok, :]), hr, True)
                        for n in range(2):
                            ns = slice(n * 512, (n + 1) * 512)
                            po = psm.next()
                            p.ops("pe", [lambda e, k=k: e.matmul(po[:], lhsT=actT[:, k, i * 128:(i + 1) * 128], rhs=wdn[:, k, ns], start=(k == 0), stop=(k == 21)) for k in range(22)],
                                  reads=[actT, wdn], writes=[po])
                            p.op("dve", lambda e: e.scalar_tensor_tensor(out=opre[:, ns], in0=hr[:, ns], scalar=ALPHA, in1=po[:], op0=ALU.mult, op1=ALU.add),
                                 reads=[hr, po], writes=[opre])
                        oo = oout.next()
                        layer_norm(lnb, opre, gB, bB, oo)
                        p.dma(lambda e: e.dma_start(out=out_d[sq, tok, :], in_=oo[:]), oo, False)
        new_phase()


def _host_inputs(inputs, core, nseq=NSEQ):
    f32 = np.float32
    x = np.ascontiguousarray(inputs["x"][core * nseq:(core + 1) * nseq]).astype(f32)
    pos = np.ascontiguousarray(inputs["positions"][core * nseq:(core + 1) * nseq]).astype(np.int32)
    w_in = np.asarray(inputs["w_in"][0], f32)
    b_in = np.asarray(inputs["b_in"][0], f32)
    sw = np.arange(AW).reshape(-1, 2, 32)[:, ::-1, :].reshape(-1)
    q0, k0, v0, g0 = SSMW, SSMW + AW, SSMW + 2 * AW, SSMW + 3 * AW
    cols = np.concatenate([np.arange(0, SSMW), np.arange(q0, q0 + AW), np.arange(k0, k0 + AW),
                           q0 + sw, k0 + sw, np.arange(g0, g0 + 2 * D)])
    w_fm = np.ascontiguousarray(w_in[:, cols])
    b_fm = np.ascontiguousarray(b_in[cols].reshape(NFM // 128, 128).T)
    w_v = np.ascontiguousarray(w_in[:, v0:v0 + AW])
    b_v = np.ascontiguousarray(b_in[v0:v0 + AW].reshape(1, AW))
    half = 32
    inv_freq = (10000.0 ** (-np.arange(half, dtype=np.float64) * 2.0 / 64)).astype(f32)
    invf = np.zeros((128, 2), f32)
    for pp in range(128):
        invf[pp, 0] = inv_freq[pp % 32] / TWO_PI
        invf[pp, 1] = -TWO_PI if (pp % 64) < 32 else TWO_PI
    def tile_layout(a):
        return np.ascontiguousarray(a.reshape(2, 16, 2, 64).transpose(2, 3, 0, 1).reshape(128, 32)).astype(f32)
    lre_h = tile_layout(np.asarray(inputs["ssm_lam_re"][0], f32))
    lim_h = tile_layout(np.asarray(inputs["ssm_lam_im"][0], f32))
    ldt_h = tile_layout(np.broadcast_to(np.asarray(inputs["ssm_log_dt"][0], f32)[:, :, None], (2, 32, 64)).copy())

    def bz(b):
        o = np.zeros((128, 32, 128), f32)
        b = np.asarray(b, f32)
        for dr in range(2):
            for gp in range(16):
                for gl in range(2):
                    c0 = (gp % 4) * 32 + gl * 16
                    o[gl * 64:(gl + 1) * 64, dr * 16 + gp, c0:c0 + 16] = b[dr, 2 * gp + gl]
        return o

    def cb(c):
        o = np.zeros((32, 32, 128), f32)
        c = np.asarray(c, f32)
        for dr in range(2):
            for gp in range(16):
                for gl in range(2):
                    o[gl * 16:(gl + 1) * 16, dr * 16 + gp, gl * 64:(gl + 1) * 64] = c[dr, 2 * gp + gl]
        return o
    ssm = {"lre_h": lre_h, "lim_h": lim_h, "ldt_h": ldt_h,
           "bzr_h": bz(inputs["ssm_b_re"][0]), "bzi_h": bz(inputs["ssm_b_im"][0]),
           "cbr_h": cb(inputs["ssm_c_re"][0]), "cbi_h": cb(inputs["ssm_c_im"][0]),
           "dsk_h": np.ascontiguousarray(np.asarray(inputs["ssm_d"][0], f32).reshape(4, 128).T),
           "iota_h": np.arange(S, dtype=f32).reshape(1, S)}
    d = {"x": x, "pos": pos, "ident": np.eye(128, dtype=f32), "invf": invf,
         "w_in_fm": w_fm, "b_fm": b_fm, "w_v": w_v, "b_v": b_v}
    d.update(ssm)
    ii = np.arange(128)[:, None]; jj = np.arange(128)[None, :]
    maskb = np.concatenate([np.where(ii >= jj, 0.0, -30000.0), np.where(ii <= jj, 0.0, -30000.0)], axis=1).astype(f32)
    ones3 = np.zeros((128, 3, 64), f32)
    ones3[:, 0, :] = 1.0; ones3[64:, 1, :] = 1.0; ones3[:64, 2, :] = 1.0
    g = lambda n: np.ascontiguousarray(np.asarray(inputs[n][0], f32))
    cwh = np.ascontiguousarray(g("conv_w").reshape(3, 44, 128).transpose(2, 1, 0))
    cbh = np.ascontiguousarray(g("conv_b").reshape(44, 128).T)
    d.update({"maskb_h": maskb, "ones3_h": ones3, "wgv_h": g("w_glu_v"), "wgg_h": g("w_glu_g"), "wab_h": g("w_attn_br"), "wo_h": g("w_out"),
              "ln1g_h": g("ln1_g").reshape(1, D), "ln1b_h": g("ln1_b").reshape(1, D), "ln2g_h": g("ln2_g").reshape(1, D), "ln2b_h": g("ln2_b").reshape(1, D),
              "wup_h": g("w_up"), "wdn_h": g("w_down"), "cw_h": cwh, "cb_h": cbh})
    return d


def kernel(**inputs):
    nc = build()
    in_maps = [_host_inputs(inputs, c) for c in range(NCORES)]
    res = run_bass_kernel_spmd(nc, in_maps, core_ids=list(range(NCORES)))
    out = np.concatenate([r["out"] for r in res.results], axis=0)
    return out.astype(np.float32)
```

```python
import math
from contextlib import ExitStack

import numpy as np
import concourse.bass as bass
import concourse.mybir as mybir
from concourse.bass_utils import run_bass_kernel_spmd

F32 = mybir.dt.float32
BF16 = mybir.dt.bfloat16
I32 = mybir.dt.int32
AF = mybir.ActivationFunctionType
ALU = mybir.AluOpType
AX = mybir.AxisListType

S = 4096
D = 1024
NCORES = 8
NSEQ = 2
SSMW = 512
AW = 768
DFF = 2816
NFM = 5632
ALPHA = 2.0 ** 0.25
LN_EPS = 1e-5
TWO_PI = 2.0 * math.pi
DIL = (1, 4, 16)
KPAD = 1024


class Buf:
    __slots__ = ("t", "w", "r", "dsem", "const")

    def __init__(self, t, dsem=None, const=False):
        self.t = t
        self.w = None
        self.r = {}
        self.dsem = dsem
        self.const = const

    def __getitem__(self, k):
        return self.t[k]


class Prog:
    ENG = ("pe", "act", "dve", "pool", "sp")

    def __init__(self, nc, es, n_dsem=72):
        self.nc = nc
        self.engobj = {'pe': nc.tensor, 'act': nc.scalar, 'dve': nc.vector, 'pool': nc.gpsimd, 'sp': nc.sync}
        self.ninst = 0
        self.stopped = False
        self.esem = {e: es.enter_context(nc.semaphore("es_" + e)) for e in ("pe", "act", "dve", "pool")}
        self.ecount = {e: 0 for e in self.esem}
        self.dsems = [es.enter_context(nc.semaphore(f"ds{i}")) for i in range(n_dsem)]
        self.dcount = {id(s): 0 for s in self.dsems}
        self.dnext = 0
        self.waited = {e: {} for e in self.ENG}
        self.semobj = {}
        for s in list(self.esem.values()) + self.dsems:
            self.semobj[id(s)] = s

    def buf(self, t, dma=False, const=False):
        ds = None
        if dma:
            assert self.dnext < len(self.dsems), "out of DMA semaphores in this phase"
            ds = self.dsems[self.dnext]
            self.dnext += 1
        return Buf(t, ds, const)

    def _deps(self, reads, writes):
        deps = {}

        def add(ev):
            if ev is None:
                return
            k, v = ev
            if deps.get(k, 0) < v:
                deps[k] = v
        for b in reads:
            add(b.w)
        for b in writes:
            add(b.w)
            for k, v in b.r.items():
                add((k, v))
        return deps

    def _record(self, ev, reads, writes):
        for b in writes:
            b.w = ev
            b.r = {}
        for b in reads:
            if b.const:
                continue
            if b.r.get(ev[0], 0) < ev[1]:
                b.r[ev[0]] = ev[1]

    def _emit(self, eng, deps, fn, inc):
        e = self.engobj[eng]
        wd = self.waited[eng]
        own = id(self.esem[eng]) if eng in self.esem else None
        for k, v in deps.items():
            if eng == "pe" and k == own:
                continue
            if wd.get(k, 0) >= v:
                continue
            wd[k] = v
            e.wait_ge(self.semobj[k], v)
        if fn is None:
            return
        ins = fn(e)
        if inc is not None:
            ins.then_inc(inc[0], inc[1])
        self.ninst += 1

    def op(self, eng, fn, reads=(), writes=()):
        if self.stopped:
            return None
        deps = self._deps(reads, writes)
        self.ecount[eng] += 1
        sem = self.esem[eng]
        ev = (id(sem), self.ecount[eng])
        self._emit(eng, deps, fn, (sem, 1))
        self._record(ev, reads, writes)
        return ev

    def ops(self, eng, fns, reads=(), writes=()):
        assert eng == "pe"
        if self.stopped:
            return None
        deps = self._deps(reads, writes)
        for fn in fns[:-1]:
            self._emit(eng, deps, fn, None)
            deps = {}
        self.ecount[eng] += 1
        sem = self.esem[eng]
        ev = (id(sem), self.ecount[eng])
        self._emit(eng, deps, fns[-1], (sem, 1))
        self._record(ev, reads, writes)
        return ev

    def dma(self, fn, sb, load, reads=(), writes=(), q="sp"):
        if self.stopped:
            return None
        reads = list(reads)
        writes = list(writes)
        if load:
            writes.append(sb)
        else:
            reads.append(sb)
        deps = self._deps(reads, writes)
        sem = sb.dsem
        assert sem is not None
        self.dcount[id(sem)] += 16
        ev = (id(sem), self.dcount[id(sem)])
        self._emit(q, deps, fn, (sem, 16))
        self._record(ev, reads, writes)
        return ev

    def barrier(self):
        allev = {}
        for e, s in self.esem.items():
            if self.ecount[e]:
                allev[id(s)] = self.ecount[e]
        for s in self.dsems:
            if self.dcount[id(s)]:
                allev[id(s)] = self.dcount[id(s)]
        for eng in self.ENG:
            self._emit(eng, allev, None, None)
        self.dnext = 0

    def emit(self):
        pass


class StopBuild(Exception):
    pass


class Ring:
    def __init__(self, bufs):
        self.bufs = bufs
        self.i = 0

    def next(self):
        b = self.bufs[self.i % len(self.bufs)]
        self.i += 1
        return b


def build(nseq=NSEQ, debug=False, stop_after=None):
    nc = bass.Bass("TRN2", target_bir_lowering=False)

    def din(name, shape, dt=F32):
        return nc.dram_tensor(name, list(shape), dt, kind="ExternalInput").ap()

    dbg_kind = "ExternalOutput" if debug else "Internal"

    def dscr(name, shape, dt):
        return nc.dram_tensor(name, list(shape), dt, kind=dbg_kind).ap()

    x_d = din("x", [nseq, S, D])
    pos_d = din("pos", [nseq, S], I32)
    ident_d = din("ident", [128, 128])
    invf_d = din("invf", [128, 2])
    w_in_d = din("w_in_fm", [D, NFM])
    b_fm_d = din("b_fm", [128, NFM // 128])
    w_v_d = din("w_v", [D, AW])
    b_v_d = din("b_v", [1, AW])
    out_d = nc.dram_tensor("out", [nseq, S, D], F32, kind="ExternalOutput").ap()
    ssm_d = dict(
        lre=din("lre_h", [128, 32]), lim=din("lim_h", [128, 32]), ldt=din("ldt_h", [128, 32]),
        bzr=din("bzr_h", [128, 32, 128]), bzi=din("bzi_h", [128, 32, 128]),
        cbr=din("cbr_h", [32, 32, 128]), cbi=din("cbi_h", [32, 32, 128]),
        dsk=din("dsk_h", [128, 4]), iota=din("iota_h", [1, S]))
    zT_s = dscr("zT_s", [nseq, SSMW, S], BF16)
    aT_s = dscr("aT_s", [nseq, 256, S], BF16)
    h_s = dscr("h_s", [nseq, S, D], F32)
    hT_s = dscr("hT_s", [nseq, D, S], BF16)
    md = dict(maskb=din("maskb_h", [128, 256]), ones3=din("ones3_h", [128, 3, 64]),
              wgv=din("wgv_h", [512, D]), wgg=din("wgg_h", [512, D]), wab=din("wab_h", [256, D]), wo=din("wo_h", [D, D]),
              ln1g=din("ln1g_h", [1, D]), ln1b=din("ln1b_h", [1, D]), ln2g=din("ln2g_h", [1, D]), ln2b=din("ln2b_h", [1, D]),
              wup=din("wup_h", [D, 2 * DFF]), wdn=din("wdn_h", [DFF, D]), cw=din("cw_h", [128, 44, 3]), cbias=din("cb_h", [128, 44]),
              aT_s=aT_s, h_s=h_s, hT_s=hT_s)

    xT_s = dscr("xT_s", [nseq, D, S], BF16)
    uT_s = dscr("uT_s", [nseq, SSMW, S], BF16)
    qT_s = dscr("qT_s", [nseq, AW, S], BF16)
    kT_s = dscr("kT_s", [nseq, AW, S], BF16)
    gT_s = dscr("gT_s", [nseq, 2 * D, S], BF16)
    NBLK = [d * (S // d // 128 + 1) for d in DIL]
    v_s = [dscr(f"v_s{g}", [nseq, 128, NBLK[g], 256], BF16) for g in range(3)]

    with ExitStack() as es0:
        p = Prog(nc, es0)
        psum = [p.buf(es0.enter_context(nc.psum_tensor(f"ps{i}", [128, 512], F32))) for i in range(8)]
        ident = p.buf(es0.enter_context(nc.sbuf_tensor("ident_sb", [128, 128], F32)), dma=True, const=True)
        p.dma(lambda e: e.dma_start(out=ident[:], in_=ident_d), ident, True)
        p.dnext = 1

        def new_phase():
            p.barrier()
            p.dnext = 1

        def stop(tag):
            if stop_after == tag:
                p.stopped = True

        try:
            _phases(nc, p, psum, ident, nseq, locals_d=dict(x_d=x_d, pos_d=pos_d, invf_d=invf_d, w_in_d=w_in_d, b_fm_d=b_fm_d, w_v_d=w_v_d, b_v_d=b_v_d, out_d=out_d, xT_s=xT_s, uT_s=uT_s, qT_s=qT_s, kT_s=kT_s, gT_s=gT_s, v_s=v_s, NBLK=NBLK, ssm_d=ssm_d, zT_s=zT_s, md=md), new_phase=new_phase, stop=stop)
        except StopBuild:
            pass
        p.stopped = False
        p.barrier()
    print('instructions', p.ninst)
    return nc


def _phases(nc, p, psum, ident, nseq, locals_d, new_phase, stop):
    globals_ = locals_d
    x_d = globals_['x_d']; pos_d = globals_['pos_d']; invf_d = globals_['invf_d']; w_in_d = globals_['w_in_d']; b_fm_d = globals_['b_fm_d']
    w_v_d = globals_['w_v_d']; b_v_d = globals_['b_v_d']; out_d = globals_['out_d']; xT_s = globals_['xT_s']; uT_s = globals_['uT_s']
    qT_s = globals_['qT_s']; kT_s = globals_['kT_s']; gT_s = globals_['gT_s']; v_s = globals_['v_s']; NBLK = globals_['NBLK']
    ssm_d = globals_['ssm_d']; zT_s = globals_['zT_s']; md = globals_['md']
    aT_s = md['aT_s']; h_s = md['h_s']; hT_s = md['hT_s']
    if True:

        with ExitStack() as es:
            def sb(name, shape, dt, dma=False, const=False):
                return p.buf(es.enter_context(nc.sbuf_tensor(name, list(shape), dt)), dma=dma, const=const)

            wA = sb("wA", [128, 8, NFM], BF16)
            bfm = sb("bfm", [128, NFM // 128], F32, dma=True)
            invf = sb("invf_sb", [128, 2], F32, dma=True)
            p.dma(lambda e: e.dma_start(out=bfm[:], in_=b_fm_d), bfm, True)
            p.dma(lambda e: e.dma_start(out=invf[:], in_=invf_d), invf, True)
            WP = 1408
            wst = Ring([sb(f"wst{i}", [128, WP], F32, dma=True) for i in range(3)])
            cast_engs = ("dve", "act", "pool")
            ci = 0
            for k in range(8):
                for c in range(NFM // WP):
                    st = wst.next()
                    p.dma(lambda e, st=st, k=k, c=c: e.dma_start(
                        out=st[:], in_=w_in_d[k * 128:(k + 1) * 128, c * WP:(c + 1) * WP]), st, True)
                    eng = cast_engs[ci % 3]
                    ci += 1
                    if eng == "act":
                        p.op("act", lambda e, st=st, k=k, c=c: e.copy(out=wA[:, k, c * WP:(c + 1) * WP], in_=st[:]),
                             reads=[st], writes=[wA])
                    else:
                        p.op(eng, lambda e, st=st, k=k, c=c: e.tensor_copy(out=wA[:, k, c * WP:(c + 1) * WP], in_=st[:]),
                             reads=[st], writes=[wA])
            wA.const = True
            stop('A0')

            cosT = sb("cosT", [128, S], F32)
            sinT = sb("sinT", [128, S], F32)
            posi = sb("posi", [128, 1024], I32, dma=True)
            tur = sb("tur", [128, 1024], F32)
            turi = sb("turi", [128, 1024], I32)
            xs = [sb(f"xs{i}", [128, D], F32, dma=True) for i in range(4)]
            xT = Ring([sb(f"xT{j}", [128, 8, 512], BF16, dma=True) for j in range(2)])
            ev_bf = Ring([sb(f"evbf{j}", [128, 512], BF16, dma=True) for j in range(8)])
            rt = Ring([sb(f"rt{j}", [128, 512], F32) for j in range(4)])
            psr = Ring(psum)

            for sq in range(nseq):
                for c in range(S // 1024):
                    cs = slice(c * 1024, (c + 1) * 1024)
                    p.dma(lambda e, cs=cs: e.dma_start(out=posi[:], in_=pos_d[sq:sq + 1, cs].partition_broadcast(128)), posi, True)
                    for (tab, addc, scol) in ((sinT, 0.0, 1), (cosT, 0.25, None)):
                        p.op("dve", lambda e: e.tensor_copy(out=tur[:], in_=posi[:]), reads=[posi], writes=[tur])
                        p.op("dve", lambda e, addc=addc: e.tensor_scalar(out=tur[:], in0=tur[:], scalar1=invf[:, 0:1], scalar2=addc,
                                                                          op0=ALU.mult, op1=ALU.add), reads=[tur, invf], writes=[tur])
                        p.op("dve", lambda e: e.tensor_copy(out=turi[:], in_=tur[:]), reads=[tur], writes=[turi])
                        p.op("dve", lambda e: e.tensor_tensor(out=tur[:], in0=tur[:], in1=turi[:], op=ALU.subtract),
                             reads=[tur, turi], writes=[tur])
                        if scol is not None:
                            p.op("act", lambda e, tab=tab, cs=cs: e.activation(out=tab[:, cs], in_=tur[:], func=AF.Sin, scale=invf[:, 1:2]),
                                 reads=[tur, invf], writes=[tab])
                        else:
                            p.op("act", lambda e, tab=tab, cs=cs: e.activation(out=tab[:, cs], in_=tur[:], func=AF.Sin, scale=TWO_PI),
                                 reads=[tur], writes=[tab])
                stop('A1')
                def load_x(tb_):
                    for i in range(4):
                        p.dma(lambda e, i=i: e.dma_start(out=xs[i][:], in_=x_d[sq, tb_ * 512 + i * 128:tb_ * 512 + (i + 1) * 128, :]), xs[i], True)
                load_x(0)
                for tb in range(S // 512):
                    t0 = tb * 512
                    ts = slice(t0, t0 + 512)
                    xtile = xs
                    xTb = xT.next()
                    for k in range(8):
                        ps = psr.next()
                        p.ops("pe", [lambda e, ps=ps, i=i, k=k: e.transpose(out=ps[:, i * 128:(i + 1) * 128],
                                                                           in_=xtile[i][:, k * 128:(k + 1) * 128], identity=ident[:])
                                     for i in range(4)], reads=xtile + [ident], writes=[ps])
                        if k % 2 == 0:
                            p.op("act", lambda e, ps=ps, k=k: e.copy(out=xTb[:, k, :], in_=ps[:]), reads=[ps], writes=[xTb])
                        else:
                            p.op("dve", lambda e, ps=ps, k=k: e.tensor_copy(out=xTb[:, k, :], in_=ps[:]), reads=[ps], writes=[xTb])
                    if tb + 1 < S // 512:
                        load_x(tb + 1)
                    p.dma(lambda e: e.dma_start(out=xT_s[sq].rearrange("(k q) t -> q k t", q=128)[:, :, ts], in_=xTb[:]), xTb, False)

                    def proj(fo):
                        ps = psr.next()
                        p.ops("pe", [lambda e, ps=ps, k=k: e.matmul(ps[:], lhsT=wA[:, k, fo * 128:(fo + 1) * 128], rhs=xTb[:, k, :],
                                                                      start=(k == 0), stop=(k == 7)) for k in range(8)],
                              reads=[wA, xTb], writes=[ps])
                        return ps

                    for fo in range(4):
                        ps = proj(fo)
                        o = ev_bf.next()
                        p.op("act", lambda e, ps=ps, o=o, fo=fo: e.activation(out=o[:], in_=ps[:], func=AF.Identity, bias=bfm[:, fo:fo + 1]),
                             reads=[ps, bfm], writes=[o])
                        p.dma(lambda e, o=o, fo=fo: e.dma_start(out=uT_s[sq, fo * 128:(fo + 1) * 128, ts], in_=o[:]), o, False)
                    for which, dst in ((0, qT_s), (1, kT_s)):
                        for c in range(6):
                            fo = 4 + which * 6 + c
                            psa = proj(fo)
                            psb = proj(fo + 12)
                            t1 = rt.next()
                            t2 = rt.next()
                            p.op("dve", lambda e, psa=psa, t1=t1, fo=fo: e.scalar_tensor_tensor(
                                out=t1[:], in0=psa[:], scalar=bfm[:, fo:fo + 1], in1=cosT[:, ts], op0=ALU.add, op1=ALU.mult),
                                reads=[psa, bfm, cosT], writes=[t1])
                            p.op("dve", lambda e, psb=psb, t2=t2, fo=fo: e.scalar_tensor_tensor(
                                out=t2[:], in0=psb[:], scalar=bfm[:, fo + 12:fo + 13], in1=sinT[:, ts], op0=ALU.add, op1=ALU.mult),
                                reads=[psb, bfm, sinT], writes=[t2])
                            o = ev_bf.next()
                            p.op("pool", lambda e, o=o, t1=t1, t2=t2: e.tensor_tensor(out=o[:], in0=t1[:], in1=t2[:], op=ALU.add),
                                 reads=[t1, t2], writes=[o])
                            p.dma(lambda e, o=o, c=c, dst=dst: e.dma_start(out=dst[sq, c * 128:(c + 1) * 128, ts], in_=o[:]), o, False)
                    for c in range(16):
                        fo = 28 + c
                        ps = proj(fo)
                        o = ev_bf.next()
                        p.op("act", lambda e, ps=ps, o=o, fo=fo: e.activation(out=o[:], in_=ps[:], func=AF.Sigmoid, bias=bfm[:, fo:fo + 1]),
                             reads=[ps, bfm], writes=[o])
                        p.dma(lambda e, o=o, c=c: e.dma_start(out=gT_s[sq, c * 128:(c + 1) * 128, ts], in_=o[:]), o, False)
                    stop(f'A2_{tb}')
        new_phase()
        stop('A')

        with ExitStack() as es:
            def sb(name, shape, dt, dma=False, const=False):
                return p.buf(es.enter_context(nc.sbuf_tensor(name, list(shape), dt)), dma=dma, const=const)

            wV = sb("wV", [128, 8, AW], BF16)
            wvst = Ring([sb(f"wvst{i}", [128, AW], F32, dma=True) for i in range(2)])
            for k in range(8):
                st = wvst.next()
                p.dma(lambda e, st=st, k=k: e.dma_start(out=st[:], in_=w_v_d[k * 128:(k + 1) * 128, :]), st, True)
                p.op("dve", lambda e, st=st, k=k: e.tensor_copy(out=wV[:, k, :], in_=st[:]), reads=[st], writes=[wV])
            wV.const = True
            bv = sb("bv", [128, AW], F32, dma=True, const=True)
            p.dma(lambda e: e.dma_start(out=bv[:], in_=b_v_d.partition_broadcast(128)), bv, True)
            xTf = sb("xTf", [128, 8, S], BF16, dma=True)
            VCH = 12
            vring = Ring([sb(f"vstg{j}", [128, VCH, 256], BF16, dma=True) for j in range(2)])
            psr = Ring(psum)
            for sq in range(nseq):
                p.dma(lambda e: e.dma_start(out=xTf[:], in_=xT_s[sq].rearrange("(k q) t -> q k t", q=128)), xTf, True)
                for g in range(3):
                    d = DIL[g]
                    L = S // d
                    nb = L // 128 + 1
                    blocks = [(r, m) for r in range(d) for m in range(nb)]
                    for c0 in range(0, len(blocks), VCH):
                        chunk = blocks[c0:c0 + VCH]
                        stg = vring.next()
                        p.op("pool", lambda e, stg=stg: e.memset(stg[:], 0.0), writes=[stg])
                        for j, (r, m) in enumerate(chunk):
                            lo = 64 + 128 * (m - 1)
                            i0 = max(0, -lo)
                            i1 = min(128, L - lo)
                            M = i1 - i0
                            tok0 = r + d * (lo + i0)
                            ps = psr.next()
                            p.ops("pe", [lambda e, ps=ps, k=k, tok0=tok0, M=M, i0=i0, d=d, g=g: e.matmul(
                                ps[i0:i0 + M, 0:256], lhsT=xTf[:, k, tok0:tok0 + d * (M - 1) + 1:d], rhs=wV[:, k, g * 256:(g + 1) * 256],
                                start=(k == 0), stop=(k == 7)) for k in range(8)], reads=[xTf, wV], writes=[ps])
                            p.op("dve", lambda e, ps=ps, stg=stg, j=j, i0=i0, M=M, g=g: e.tensor_tensor(
                                out=stg[i0:i0 + M, j, :], in0=ps[i0:i0 + M, 0:256], in1=bv[i0:i0 + M, g * 256:(g + 1) * 256], op=ALU.add),
                                reads=[ps, bv], writes=[stg])
                        p.dma(lambda e, stg=stg, c0=c0, n=len(chunk), g=g: e.dma_start(out=v_s[g][sq, :, c0:c0 + n, :], in_=stg[:, 0:n, :]), stg, False)
                        stop(f'V{g}_{c0}')
                    stop(f'V{g}')
        new_phase()

        with ExitStack() as es:
            def sb(name, shape, dt, dma=False, const=False):
                return p.buf(es.enter_context(nc.sbuf_tensor(name, list(shape), dt)), dma=dma, const=const)

            NT = 32
            NCH = S // 8
            lre = sb("lre", [128, NT], F32, dma=True); lim = sb("lim", [128, NT], F32, dma=True); ldt = sb("ldt", [128, NT], F32, dma=True)
            p.dma(lambda e: e.dma_start(out=lre[:], in_=ssm_d["lre"]), lre, True)
            p.dma(lambda e: e.dma_start(out=lim[:], in_=ssm_d["lim"]), lim, True)
            p.dma(lambda e: e.dma_start(out=ldt[:], in_=ssm_d["ldt"]), ldt, True)
            dsk = sb("dsk", [128, 4], F32, dma=True)
            p.dma(lambda e: e.dma_start(out=dsk[:], in_=ssm_d["dsk"]), dsk, True)
            tI = sb("tI", [128, NCH], F32, dma=True, const=True)
            p.dma(lambda e: e.dma_start(out=tI[:], in_=ssm_d["iota"][:, 0:NCH].partition_broadcast(128)), tI, True)
            sm = {n: sb("sm_" + n, [128, NT], F32) for n in
                  ("dt", "xr", "xi", "rho", "th", "t0", "t1", "f", "sinx", "cosx", "sinh", "em1", "am1", "abi", "den", "kr", "ki", "u0", "u1",
                   "rho8", "th8", "pm", "pc", "ps")}
            smi = sb("smi", [128, NT], I32)
            pwr = sb("pwr", [128, 16, NT], F32); pwi = sb("pwi", [128, 16, NT], F32); npwi = sb("npwi", [128, 16, NT], F32)

            def V(fn, reads, writes):
                return p.op("dve", fn, reads=reads, writes=writes)

            def A(fn, reads, writes):
                return p.op("act", fn, reads=reads, writes=writes)

            def tt(o, a, b, op):
                V(lambda e: e.tensor_tensor(out=o[:], in0=a[:], in1=b[:], op=op), [a, b], [o])

            def tsc(o, a, s1, op0, s2=None, op1=None):
                if op1 is None:
                    V(lambda e: e.tensor_scalar(out=o[:], in0=a[:], scalar1=s1, scalar2=None, op0=op0), [a], [o])
                else:
                    V(lambda e: e.tensor_scalar(out=o[:], in0=a[:], scalar1=s1, scalar2=s2, op0=op0, op1=op1), [a], [o])

            def frac_sin(o, turns_src, mul, add):
                tsc(sm["t0"], turns_src, mul, ALU.mult, add, ALU.add)
                V(lambda e: e.tensor_copy(out=smi[:], in_=sm["t0"][:]), [sm["t0"]], [smi])
                tt(sm["f"], sm["t0"], smi, ALU.subtract)
                A(lambda e: e.activation(out=o[:], in_=sm["f"][:], func=AF.Sin, scale=TWO_PI), [sm["f"]], [o])

            A(lambda e: e.activation(out=sm["dt"][:], in_=ldt[:], func=AF.Exp), [ldt], [sm["dt"]])
            tt(sm["xr"], lre, sm["dt"], ALU.mult)
            tt(sm["xi"], lim, sm["dt"], ALU.mult)
            A(lambda e: e.activation(out=sm["rho"][:], in_=sm["xr"][:], func=AF.Exp), [sm["xr"]], [sm["rho"]])
            A(lambda e: e.activation(out=sm["rho8"][:], in_=sm["xr"][:], func=AF.Exp, scale=8.0), [sm["xr"]], [sm["rho8"]])
            tsc(sm["th"], sm["xi"], 1.0 / TWO_PI, ALU.mult)
            tsc(sm["th8"], sm["th"], 8.0, ALU.mult)
            frac_sin(sm["sinx"], sm["th"], 1.0, 0.0)
            frac_sin(sm["cosx"], sm["th"], 1.0, 0.25)
            frac_sin(sm["sinh"], sm["th"], 0.5, 0.0)
            tsc(sm["em1"], sm["xr"], 0.2, ALU.mult, 1.0, ALU.add)
            for cdiv in (0.25, 1.0 / 3.0, 0.5):
                tt(sm["em1"], sm["em1"], sm["xr"], ALU.mult)
                tsc(sm["em1"], sm["em1"], cdiv, ALU.mult, 1.0, ALU.add)
            tt(sm["em1"], sm["em1"], sm["xr"], ALU.mult)
            tt(sm["am1"], sm["em1"], sm["cosx"], ALU.mult)
            tt(sm["u0"], sm["sinh"], sm["sinh"], ALU.mult)
            V(lambda e: e.scalar_tensor_tensor(out=sm["am1"][:], in0=sm["u0"][:], scalar=-2.0, in1=sm["am1"][:], op0=ALU.mult, op1=ALU.add),
              [sm["u0"], sm["am1"]], [sm["am1"]])
            tt(sm["abi"], sm["rho"], sm["sinx"], ALU.mult)
            tt(sm["den"], lre, lre, ALU.mult)
            tt(sm["u0"], lim, lim, ALU.mult)
            tt(sm["den"], sm["den"], sm["u0"], ALU.add)
            V(lambda e: e.reciprocal(out=sm["den"][:], in_=sm["den"][:]), [sm["den"]], [sm["den"]])
            tt(sm["u0"], sm["am1"], lre, ALU.mult)
            tt(sm["u1"], sm["abi"], lim, ALU.mult)
            tt(sm["u0"], sm["u0"], sm["u1"], ALU.add)
            tt(sm["kr"], sm["u0"], sm["den"], ALU.mult)
            tt(sm["u0"], sm["abi"], lre, ALU.mult)
            tt(sm["u1"], sm["am1"], lim, ALU.mult)
            tt(sm["u0"], sm["u0"], sm["u1"], ALU.subtract)
            tt(sm["ki"], sm["u0"], sm["den"], ALU.mult)
            tsc(sm["t1"], sm["ki"], -1.0, ALU.mult)
            nki = sb("nki", [128, NT], F32)
            V(lambda e: e.tensor_copy(out=nki[:], in_=sm["t1"][:]), [sm["t1"]], [nki])
            for jj in range(16):
                jv = float(jj - 7)
                A(lambda e, jv=jv: e.activation(out=sm["pm"][:], in_=sm["xr"][:], func=AF.Exp, scale=jv), [sm["xr"]], [sm["pm"]])
                frac_sin(sm["ps"], sm["th"], jv, 0.0)
                frac_sin(sm["pc"], sm["th"], jv, 0.25)
                V(lambda e, jj=jj: e.tensor_tensor(out=pwr[:, jj, :], in0=sm["pm"][:], in1=sm["pc"][:], op=ALU.mult), [sm["pm"], sm["pc"]], [pwr])
                V(lambda e, jj=jj: e.tensor_tensor(out=pwi[:, jj, :], in0=sm["pm"][:], in1=sm["ps"][:], op=ALU.mult), [sm["pm"], sm["ps"]], [pwi])
            V(lambda e: e.tensor_scalar(out=npwi[:], in0=pwi[:], scalar1=-1.0, scalar2=None, op0=ALU.mult), [pwi], [npwi])
            for b_ in (pwr, pwi, npwi, sm["kr"], sm["ki"], nki, sm["rho8"], sm["th8"]):
                b_.const = True

            Dd = sb("Dd", [128, 4, 128], BF16)
            for q in range(4):
                V(lambda e, q=q: e.tensor_scalar(out=Dd[:, q, :], in0=ident[:], scalar1=dsk[:, q:q + 1], scalar2=None, op0=ALU.mult),
                  [ident, dsk], [Dd])
            Dd.const = True

            NSET = 2
            bz = [[sb(f"bz{i}_{k}", [128, 2, 128], F32, dma=True) for k in range(2)] for i in range(NSET)]
            cbt = [[sb(f"cbt{i}_{k}", [32, 2, 128], F32, dma=True) for k in range(2)] for i in range(NSET)]
            Bz = [[sb(f"Bz{i}_{k}", [128, 2, 128], F32) for k in range(2)] for i in range(NSET)]
            CT = [[sb(f"CT{i}_{k}", [128, 2, 64], F32) for k in range(2)] for i in range(NSET)]
            XT = [[sb(f"XT{i}_{k}", [128, 8, 2, 128], BF16) for k in range(2)] for i in range(NSET)]
            KT = [[sb(f"KT{i}_{k}", [128, 8, 64], BF16) for k in range(2)] for i in range(NSET)]
            LY = [[sb(f"LY{i}_{k}", [128, 8, 2, 64], BF16) for k in range(2)] for i in range(NSET)]
            cosN = [[sb(f"cosN{i}_{k}", [128, NCH], F32) for k in range(2)] for i in range(NSET)]
            sinN = [[sb(f"sinN{i}_{k}", [128, NCH], F32) for k in range(2)] for i in range(NSET)]
            rho8T = [[sb(f"rho8T{i}_{k}", [128, NCH], F32) for k in range(2)] for i in range(NSET)]
            for i in range(NSET):
                for k in range(2):
                    p.op("pool", lambda e, i=i, k=k: e.memset(CT[i][k][:], 0.0), writes=[CT[i][k]])
            xtmp = Ring([sb(f"xtmp{i}", [128, 2, 128], F32) for i in range(3)])
            lyf = Ring([sb(f"lyf{i}", [128, 2, 64], F32) for i in range(3)])
            turN = sb("turN", [128, NCH], F32); turNi = sb("turNi", [128, NCH], I32)
            uTr = Ring([sb(f"uTc{i}", [128, S], BF16, dma=True) for i in range(2)])
            uDr = Ring([sb(f"uD{i}", [128, 8, NCH], BF16) for i in range(2)])
            tmpr = Ring([sb(f"tmpS{i}", [128, NCH], F32) for i in range(8)])
            wrr = Ring([sb(f"wS{i}", [128, NCH], F32) for i in range(4)])
            Rrr = Ring([sb(f"RS{i}", [128, NCH], F32) for i in range(4)])
            Vrr = Ring([sb(f"VS{i}", [128, NCH], F32) for i in range(4)])
            Zr_ = [Ring([sb(f"ZS{k}_{i}", [128, 2, NCH], BF16) for i in range(2)]) for k in range(2)]
            zor = Ring([sb(f"zo{i}", [128, S], BF16, dma=True) for i in range(2)])
            psT = Ring(psum[4:8])
            psSt = [psum[0:2], psum[2:4]]

            def cmul(o, orow, oi_row, src, sr, si, nsi):
                V(lambda e: e.tensor_scalar(out=o[:, 0, :], in0=src[:, 0, :], scalar1=sr, scalar2=None, op0=ALU.mult), [src], [o])
                V(lambda e: e.scalar_tensor_tensor(out=o[:, 0, :], in0=src[:, 1, :], scalar=nsi, in1=o[:, 0, :], op0=ALU.mult, op1=ALU.add), [src, o], [o])
                V(lambda e: e.tensor_scalar(out=o[:, 1, :], in0=src[:, 1, :], scalar1=sr, scalar2=None, op0=ALU.mult), [src], [o])
                V(lambda e: e.scalar_tensor_tensor(out=o[:, 1, :], in0=src[:, 0, :], scalar=si, in1=o[:, 1, :], op0=ALU.mult, op1=ALU.add), [src, o], [o])

            def prep(gp, st):
                for k in range(2):
                    j = k * 16 + gp
                    p.dma(lambda e: e.dma_start(out=bz[st][k][:, 0, :], in_=ssm_d["bzr"][:, j, :]), bz[st][k], True)
                    p.dma(lambda e: e.dma_start(out=bz[st][k][:, 1, :], in_=ssm_d["bzi"][:, j, :]), bz[st][k], True)
                    p.dma(lambda e: e.dma_start(out=cbt[st][k][:, 0, :], in_=ssm_d["cbr"][:, j, :]), cbt[st][k], True)
                    p.dma(lambda e: e.dma_start(out=cbt[st][k][:, 1, :], in_=ssm_d["cbi"][:, j, :]), cbt[st][k], True)
                    cmul(Bz[st][k], None, None, bz[st][k], sm["kr"][:, j:j + 1], sm["ki"][:, j:j + 1], nki[:, j:j + 1])
                    ps = psT.next()
                    p.ops("pe", [lambda e: e.transpose(out=ps[:, 0:32], in_=cbt[st][k][:, 0, :], identity=ident[0:32, 0:32]),
                                 lambda e: e.transpose(out=ps[:, 32:64], in_=cbt[st][k][:, 1, :], identity=ident[0:32, 0:32])],
                          reads=[cbt[st][k], ident], writes=[ps])
                    A(lambda e: e.copy(out=CT[st][k][:, 0, 32:64], in_=ps[:, 0:32]), [ps], [CT[st][k]])
                    A(lambda e: e.mul(out=CT[st][k][:, 1, 32:64], in_=ps[:, 32:64], mul=-1.0), [ps], [CT[st][k]])
                    for s_ in range(8):
                        if s_ == 0:
                            xs_ = Bz[st][k]
                        else:
                            xs_ = xtmp.next()
                            jj = 7 - s_
                            cmul(xs_, None, None, Bz[st][k], pwr[:, jj, j:j + 1], pwi[:, jj, j:j + 1], npwi[:, jj, j:j + 1])
                        ps = psT.next()
                        p.ops("pe", [lambda e: e.transpose(out=ps[:, 0:128], in_=xs_[:, 0, :], identity=ident[:]),
                                     lambda e: e.transpose(out=ps[:, 128:256], in_=xs_[:, 1, :], identity=ident[:])],
                              reads=[xs_, ident], writes=[ps])
                        A(lambda e: e.copy(out=XT[st][k][:, s_, :, :], in_=ps[:, 0:256].rearrange("p (r c) -> p r c", r=2)), [ps], [XT[st][k]])
                    for tau in range(8):
                        ly = lyf.next()
                        jj = 7 + tau
                        ctr = CT[st][k]
                        V(lambda e: e.tensor_scalar(out=ly[:, 0, :], in0=ctr[:, 0, :], scalar1=pwr[:, jj, j:j + 1], scalar2=None, op0=ALU.mult), [ctr], [ly])
                        V(lambda e: e.scalar_tensor_tensor(out=ly[:, 0, :], in0=ctr[:, 1, :], scalar=pwi[:, jj, j:j + 1], in1=ly[:, 0, :], op0=ALU.mult, op1=ALU.add), [ctr, ly], [ly])
                        V(lambda e: e.tensor_scalar(out=ly[:, 1, :], in0=ctr[:, 1, :], scalar1=pwr[:, jj, j:j + 1], scalar2=None, op0=ALU.mult), [ctr], [ly])
                        V(lambda e: e.scalar_tensor_tensor(out=ly[:, 1, :], in0=ctr[:, 0, :], scalar=npwi[:, jj, j:j + 1], in1=ly[:, 1, :], op0=ALU.mult, op1=ALU.add), [ctr, ly], [ly])
                        A(lambda e: e.copy(out=LY[st][k][:, tau, :, :], in_=ly[:]), [ly], [LY[st][k]])
                        ps = psT.next()
                        p.ops("pe", [lambda e: e.matmul(ps[:, 0:64], lhsT=Bz[st][k][:, 0, :], rhs=ly[:, 0, :], start=True, stop=False),
                                     lambda e: e.matmul(ps[:, 0:64], lhsT=Bz[st][k][:, 1, :], rhs=ly[:, 1, :], start=False, stop=True)],
                              reads=[Bz[st][k], ly], writes=[ps])
                        A(lambda e: e.copy(out=KT[st][k][:, tau, :], in_=ps[:, 0:64]), [ps], [KT[st][k]])
                    for (tab, addc) in ((sinN[st][k], 0.0), (cosN[st][k], 0.25)):
                        V(lambda e: e.tensor_scalar(out=turN[:], in0=tI[:], scalar1=sm["th8"][:, j:j + 1], scalar2=addc, op0=ALU.mult, op1=ALU.add), [tI], [turN])
                        V(lambda e: e.tensor_copy(out=turNi[:], in_=turN[:]), [turN], [turNi])
                        V(lambda e: e.tensor_tensor(out=turN[:], in0=turN[:], in1=turNi[:], op=ALU.subtract), [turN, turNi], [turN])
                        A(lambda e: e.activation(out=tab[:], in_=turN[:], func=AF.Sin, scale=TWO_PI), [turN], [tab])
                    V(lambda e: e.tensor_scalar(out=rho8T[st][k][:], in0=tI[:], scalar1=0.0, scalar2=sm["rho8"][:, j:j + 1], op0=ALU.mult, op1=ALU.add), [tI], [rho8T[st][k]])

            Ssb = Ring([sb(f"Ssb{i}", [128, 4, NCH], F32) for i in range(2)])

            def geom(gp):
                q = gp // 4
                qq = gp % 4
                if qq < 3:
                    return q, slice(32 * qq, 32 * qq + 32), slice(32, 64), slice(32 * qq, 32 * qq + 32), slice(32 * qq, 32 * qq + 32)
                return q, slice(64, 128), slice(0, 64), slice(64, 128), slice(32 * qq, 32 * qq + 32)

            def stageA(gp, st, sq):
                q = gp // 4
                uT = uTr.next()
                p.dma(lambda e: e.dma_start(out=uT[:], in_=uT_s[sq, q * 128:(q + 1) * 128, :]), uT, True)
                uD = uDr.next()
                A(lambda e: e.copy(out=uD[:], in_=uT[:].rearrange("p (n s) -> p s n", s=8)), [uT], [uD])
                ss = Ssb.next()
                for k in range(2):
                    for ri in range(2):
                        pb_ = psSt[k][ri]
                        if k == 0:
                            fns = [lambda e, s_=s_: e.matmul(pb_[:], lhsT=XT[st][k][:, s_, ri, :], rhs=uD[:, s_, :], start=(s_ == 0), stop=(s_ == 7)) for s_ in range(8)]
                        else:
                            fns = [lambda e, s_=s_: e.matmul(pb_[:], lhsT=XT[st][k][:, s_, ri, :], rhs=uD[:, 7 - s_, ::-1], start=(s_ == 0), stop=(s_ == 7)) for s_ in range(8)]
                        p.ops("pe", fns, reads=[XT[st][k], uD], writes=[pb_])
                        A(lambda e: e.copy(out=ss[:, 2 * k + ri, :], in_=pb_[:]), [pb_], [ss])
                return (gp, st, sq, uD, ss)

            def stageB(ctx):
                gp, st, sq, uD, ss = ctx
                T = [[tmpr.next() for _ in range(4)] for k in range(2)]
                for (ti, si, tab) in ((0, 0, cosN), (1, 1, sinN), (2, 1, cosN), (3, 0, sinN)):
                    for k in range(2):
                        V(lambda e, k=k: e.tensor_tensor(out=T[k][ti][:], in0=ss[:, 2 * k + si, :], in1=tab[st][k][:], op=ALU.mult), [ss, tab[st][k]], [T[k][ti]])
                W = [[wrr.next(), wrr.next()] for k in range(2)]
                for k in range(2):
                    V(lambda e, k=k: e.tensor_tensor(out=W[k][0][:], in0=T[k][0][:], in1=T[k][1][:], op=ALU.add), [T[k][0], T[k][1]], [W[k][0]])
                for k in range(2):
                    V(lambda e, k=k: e.tensor_tensor(out=W[k][1][:], in0=T[k][2][:], in1=T[k][3][:], op=ALU.subtract), [T[k][2], T[k][3]], [W[k][1]])
                R = [[Rrr.next(), Rrr.next()] for k in range(2)]
                for ri in range(2):
                    for k in range(2):
                        V(lambda e, k=k, ri=ri: e.tensor_tensor_scan(out=R[k][ri][:], data0=rho8T[st][k][:], data1=W[k][ri][:], initial=0.0, op0=ALU.mult, op1=ALU.add),
                          [rho8T[st][k], W[k][ri]], [R[k][ri]])
                T = [[tmpr.next() for _ in range(4)] for k in range(2)]
                for (ti, si, tab) in ((0, 0, cosN), (1, 1, sinN), (2, 1, cosN), (3, 0, sinN)):
                    for k in range(2):
                        V(lambda e, k=k: e.tensor_tensor(out=T[k][ti][:], in0=R[k][si][:], in1=tab[st][k][:], op=ALU.mult), [R[k][si], tab[st][k]], [T[k][ti]])
                Vv = [[Vrr.next(), Vrr.next()] for k in range(2)]
                for k in range(2):
                    V(lambda e, k=k: e.tensor_tensor(out=Vv[k][0][:], in0=T[k][0][:], in1=T[k][1][:], op=ALU.subtract), [T[k][0], T[k][1]], [Vv[k][0]])
                for k in range(2):
                    V(lambda e, k=k: e.tensor_tensor(out=Vv[k][1][:], in0=T[k][2][:], in1=T[k][3][:], op=ALU.add), [T[k][2], T[k][3]], [Vv[k][1]])
                Z = [Zr_[k].next() for k in range(2)]
                for ri in range(2):
                    for k in range(2):
                        V(lambda e, k=k, ri=ri: e.tensor_tensor(out=Z[k][:, ri, :], in0=Vv[k][ri][:], in1=ss[:, 2 * k + ri, :], op=ALU.subtract), [Vv[k][ri], ss], [Z[k]])
                return ctx + (Z,)

            def stageC(ctx):
                gp, st, sq, uD, ss, Z = ctx
                q, rows, lcs, dds, orow = geom(gp)
                zo = zor.next()
                for tau in range(8):
                    py = psT.next()
                    fns = [
                        lambda e: e.matmul(py[rows, :], lhsT=LY[st][0][:, tau, 0, lcs], rhs=Z[0][:, 0, :], start=True, stop=False),
                        lambda e: e.matmul(py[rows, :], lhsT=LY[st][0][:, tau, 1, lcs], rhs=Z[0][:, 1, :], start=False, stop=False),
                        lambda e: e.matmul(py[rows, :], lhsT=LY[st][1][:, 7 - tau, 0, lcs], rhs=Z[1][:, 0, ::-1], start=False, stop=False),
                        lambda e: e.matmul(py[rows, :], lhsT=LY[st][1][:, 7 - tau, 1, lcs], rhs=Z[1][:, 1, ::-1], start=False, stop=False),
                    ]
                    for s_ in range(0, tau + 1):
                        fns.append(lambda e, s_=s_: e.matmul(py[rows, :], lhsT=KT[st][0][:, tau - s_, lcs], rhs=uD[:, s_, :], start=False, stop=False))
                    for s_ in range(tau, 8):
                        fns.append(lambda e, s_=s_: e.matmul(py[rows, :], lhsT=KT[st][1][:, s_ - tau, lcs], rhs=uD[:, s_, :], start=False, stop=False))
                    fns.append(lambda e: e.matmul(py[rows, :], lhsT=Dd[:, q, dds], rhs=uD[:, tau, :], start=False, stop=True))
                    p.ops("pe", fns, reads=[LY[st][0], LY[st][1], KT[st][0], KT[st][1], Dd, Z[0], Z[1], uD], writes=[py])
                    A(lambda e: e.activation(out=zo[rows, tau:S:8], in_=py[rows, :], func=AF.Gelu), [py], [zo])
                p.dma(lambda e: e.dma_start(out=zT_s[sq, gp * 32:(gp + 1) * 32, :], in_=zo[orow, :]), zo, False)

            runs = [(gp, gp % NSET, sq) for gp in range(16) for sq in range(nseq)]
            prep(0, 0)
            ctxA = stageA(*runs[0])
            for i, (gp, st, sq) in enumerate(runs):
                if sq == 0 and gp + 1 < 16:
                    prep(gp + 1, (gp + 1) % NSET)
                nxt = stageA(*runs[i + 1]) if i + 1 < len(runs) else None
                ctxB = stageB(ctxA)
                stageC(ctxB)
                ctxA = nxt
                if sq == nseq - 1:
                    stop(f'S_gp{gp}')
        new_phase()
        stop('S')

        with ExitStack() as es:
            def sb(name, shape, dt, dma=False, const=False):
                return p.buf(es.enter_context(nc.sbuf_tensor(name, list(shape), dt)), dma=dma, const=const)

            mstage = sb("mstage", [128, 256], F32, dma=True)
            ostage = sb("ostage", [128, 3, 64], F32, dma=True)
            maskB = sb("maskB", [128, 256], BF16); ones3 = sb("ones3", [128, 3, 64], BF16); identb = sb("identb", [128, 128], BF16)
            p.dma(lambda e: e.dma_start(out=mstage[:], in_=md["maskb"]), mstage, True)
            p.dma(lambda e: e.dma_start(out=ostage[:], in_=md["ones3"]), ostage, True)
            p.op("dve", lambda e: e.tensor_copy(out=maskB[:], in_=mstage[:]), reads=[mstage], writes=[maskB])
            p.op("dve", lambda e: e.tensor_copy(out=ones3[:], in_=ostage[:]), reads=[ostage], writes=[ones3])
            p.op("dve", lambda e: e.tensor_copy(out=identb[:], in_=ident[:]), reads=[ident], writes=[identb])
            maskB.const = True; ones3.const = True; identb.const = True
            qTr = Ring([sb(f"qTa{i}", [128, S], BF16, dma=True) for i in range(2)])
            kTr = Ring([sb(f"kTa{i}", [128, S + 2 * KPAD], BF16, dma=True) for i in range(2)])
            for b in kTr.bufs:
                p.op("pool", lambda e, b=b: e.memset(b[:], 0.0), writes=[b])
            vTr = Ring([sb(f"vTa{i}", [128, 48, 128], BF16, dma=True) for i in range(2)])
            acc = sb("acc", [128, 2, S], F32)
            rden = sb("rden", [128, S], F32)
            aTo = sb("aTo", [128, S], BF16, dma=True)
            PTr = Ring([sb(f"PT{i}", [128, 256], BF16) for i in range(4)])
            psS = Ring(psum[0:4]); psO = Ring(psum[4:8])
            SCALE = 64.0 ** -0.5
            for sq in range(nseq):
                for c in range(2):
                    for g in range(3):
                        d = DIL[g]; L = S // d; nb = L // 128 + 1
                        qT = qTr.next(); kT = kTr.next(); vT = vTr.next()
                        ch = 2 * g + c
                        p.dma(lambda e: e.dma_start(out=qT[:], in_=qT_s[sq, ch * 128:(ch + 1) * 128, :]), qT, True)
                        p.dma(lambda e: e.dma_start(out=kT[:, KPAD:KPAD + S], in_=kT_s[sq, ch * 128:(ch + 1) * 128, :]), kT, True)
                        p.dma(lambda e: e.dma_start(out=vT[:, 0:NBLK[g], :], in_=v_s[g][sq, :, :, c * 128:(c + 1) * 128]), vT, True)
                        for r in range(d):
                            for a in range(L // 128):
                                qsl = slice(r + d * 128 * a, r + d * 128 * a + d * 127 + 1, d)
                                pO = psO.next()
                                fns = []
                                pts = []
                                for hp in range(2):
                                    pb = 64 * hp
                                    pS = psS.next()
                                    ks = []
                                    for m in (a, a + 1):
                                        st = KPAD + r + d * (128 * m - 64)
                                        ks.append(slice(st, st + d * 127 + 1, d))
                                    p.ops("pe", [
                                        lambda e, pS=pS, pb=pb, ks=ks: e.matmul(pS[:, 0:128], lhsT=kT[pb:pb + 64, ks[0]], rhs=qT[pb:pb + 64, qsl], start=True, stop=False),
                                        lambda e, pS=pS, pb=pb, ks=ks: e.matmul(pS[:, 128:256], lhsT=kT[pb:pb + 64, ks[1]], rhs=qT[pb:pb + 64, qsl], start=False, stop=False),
                                        lambda e, pS=pS: e.matmul(pS[:, 0:256], lhsT=identb[:], rhs=maskB[:], start=False, stop=True),
                                    ], reads=[kT, qT, identb, maskB], writes=[pS])
                                    PT = PTr.next()
                                    p.op("act", lambda e, pS=pS, PT=PT: e.activation(out=PT[:], in_=pS[:, 0:256], func=AF.Exp, scale=SCALE), reads=[pS], writes=[PT])
                                    pts.append(PT)
                                    o1 = 1 if a == 0 else 0
                                    o2 = 2 if a + 1 == nb - 1 else 0
                                    b1 = r * nb + a; b2 = r * nb + a + 1
                                    fns += [
                                        lambda e, PT=PT, pb=pb, b1=b1, hp=hp: e.matmul(pO[pb:pb + 64, 0:128], lhsT=vT[:, b1, hp * 64:(hp + 1) * 64], rhs=PT[:, 0:128], start=True, stop=False),
                                        lambda e, PT=PT, pb=pb, b2=b2, hp=hp: e.matmul(pO[pb:pb + 64, 0:128], lhsT=vT[:, b2, hp * 64:(hp + 1) * 64], rhs=PT[:, 128:256], start=False, stop=False),
                                        lambda e, PT=PT, pb=pb, o1=o1: e.matmul(pO[pb:pb + 64, 128:256], lhsT=ones3[:, o1, :], rhs=PT[:, 0:128], start=False, stop=False),
                                        lambda e, PT=PT, pb=pb, o2=o2: e.matmul(pO[pb:pb + 64, 128:256], lhsT=ones3[:, o2, :], rhs=PT[:, 128:256], start=False, stop=True),
                                    ]
                                p.ops("pe", fns, reads=pts + [vT, ones3], writes=[pO])
                                pov = pO[:, 0:256].rearrange("p (n i) -> p n i", n=2)
                                if g == 0:
                                    p.op("dve", lambda e, pov=pov: e.tensor_copy(out=acc[:, :, qsl], in_=pov), reads=[pO], writes=[acc])
                                else:
                                    p.op("dve", lambda e, pov=pov: e.tensor_tensor(out=acc[:, :, qsl], in0=pov, in1=acc[:, :, qsl], op=ALU.add), reads=[pO, acc], writes=[acc])
                    p.op("dve", lambda e: e.reciprocal(out=rden[:], in_=acc[:, 1, :]), reads=[acc], writes=[rden])
                    p.op("dve", lambda e: e.tensor_tensor(out=aTo[:], in0=acc[:, 0, :], in1=rden[:], op=ALU.mult), reads=[acc, rden], writes=[aTo])
                    p.dma(lambda e: e.dma_start(out=aT_s[sq, c * 128:(c + 1) * 128, :], in_=aTo[:]), aTo, False)
        new_phase()
        stop('T')

        def load_w_bf16(sbf, wdst, src, K, N, tag, stg=None):
            piece = 1024 if N >= 1024 else N
            if stg is None:
                stg = Ring([sbf(f"wl_{tag}{i}", [128, piece], F32, dma=True) for i in range(2)])
            engs = ("dve", "pool", "act")
            n = 0
            for k in range(K):
                for c0 in range(0, N, piece):
                    w = min(piece, N - c0)
                    st = stg.next()
                    p.dma(lambda e, st=st, k=k, c0=c0, w=w: e.dma_start(out=st[:, 0:w], in_=src[k * 128:(k + 1) * 128, c0:c0 + w]), st, True)
                    eng = engs[n % 3]; n += 1
                    if eng == "act":
                        p.op("act", lambda e, st=st, k=k, c0=c0, w=w: e.copy(out=wdst[:, k, c0:c0 + w], in_=st[:, 0:w]), reads=[st], writes=[wdst])
                    else:
                        p.op(eng, lambda e, st=st, k=k, c0=c0, w=w: e.tensor_copy(out=wdst[:, k, c0:c0 + w], in_=st[:, 0:w]), reads=[st], writes=[wdst])
            wdst.const = True

        def layer_norm(sbufs, hpre, gB, bB, outt):
            stats, mv, rstd, hn = sbufs
            for n in range(2):
                p.op("dve", lambda e, n=n: e.bn_stats(out=stats[:, n, :], in_=hpre[:, n * 512:(n + 1) * 512]), reads=[hpre], writes=[stats])
            p.op("dve", lambda e: e.bn_aggr(out=mv[:], in_=stats[:].rearrange("p n s -> p (n s)")), reads=[stats], writes=[mv])
            p.op("act", lambda e: e.activation(out=rstd[:], in_=mv[:, 1:2], func=AF.Sqrt, bias=epsb[:, 0:1]), reads=[mv, epsb], writes=[rstd])
            p.op("dve", lambda e: e.reciprocal(out=rstd[:], in_=rstd[:]), reads=[rstd], writes=[rstd])
            p.op("dve", lambda e: e.tensor_scalar(out=hn[:], in0=hpre[:], scalar1=mv[:, 0:1], scalar2=rstd[:, 0:1], op0=ALU.subtract, op1=ALU.mult),
                 reads=[hpre, mv, rstd], writes=[hn])
            p.op("dve", lambda e: e.tensor_tensor(out=hn[:], in0=hn[:], in1=gB[:], op=ALU.mult), reads=[hn, gB], writes=[hn])
            p.op("dve", lambda e: e.tensor_tensor(out=outt[:], in0=hn[:], in1=bB[:], op=ALU.add), reads=[hn, bB], writes=[outt])

        with ExitStack() as es:
            def sb(name, shape, dt, dma=False, const=False):
                return p.buf(es.enter_context(nc.sbuf_tensor(name, list(shape), dt)), dma=dma, const=const)
            wgv = sb("wgv", [128, 4, D], BF16); wgg = sb("wgg", [128, 4, D], BF16); wab = sb("wab", [128, 2, D], BF16); wo = sb("wo", [128, 8, D], BF16)
            load_w_bf16(sb, wgv, md["wgv"], 4, D, "a"); load_w_bf16(sb, wgg, md["wgg"], 4, D, "b")
            load_w_bf16(sb, wab, md["wab"], 2, D, "c"); load_w_bf16(sb, wo, md["wo"], 8, D, "d")
            gB = sb("ln1gB", [128, D], F32, dma=True, const=True); bB = sb("ln1bB", [128, D], F32, dma=True, const=True)
            p.dma(lambda e: e.dma_start(out=gB[:], in_=md["ln1g"].partition_broadcast(128)), gB, True)
            p.dma(lambda e: e.dma_start(out=bB[:], in_=md["ln1b"].partition_broadcast(128)), bB, True)
            epsb = sb("epsb", [128, 1], F32)
            p.op("pool", lambda e: e.memset(epsb[:], LN_EPS), writes=[epsb])
            zT = sb("zTm", [128, 4, 512], BF16, dma=True); aT = sb("aTm", [128, 2, 512], BF16, dma=True); gT = sb("gTm", [128, 16, 512], BF16, dma=True)
            xs = Ring([sb(f"xm{i}", [128, D], F32, dma=True) for i in range(2)])
            mixT = sb("mixT", [128, 8, 512], BF16)
            sg = Ring([sb(f"sg{i}", [128, 512], F32) for i in range(2)])
            t1r = Ring([sb(f"t1m{i}", [128, 512], F32) for i in range(2)])
            t2r = Ring([sb(f"t2m{i}", [128, 512], F32) for i in range(2)])
            hpre = Ring([sb(f"hpre{i}", [128, D], F32) for i in range(2)])
            hout = Ring([sb(f"hout{i}", [128, D], F32, dma=True) for i in range(2)])
            hTt = Ring([sb(f"hTt{i}", [128, 8, 128], BF16, dma=True) for i in range(2)])
            lnb = (sb("st1", [128, 2, 6], F32), sb("mv1", [128, 2], F32), sb("rstd1", [128, 1], F32), sb("hn1", [128, D], F32))
            psr = Ring(psum)
            for sq in range(nseq):
                for tb in range(S // 512):
                    ts = slice(tb * 512, (tb + 1) * 512)
                    p.dma(lambda e: e.dma_start(out=zT[:], in_=zT_s[sq].rearrange("(k q) t -> q k t", q=128)[:, :, ts]), zT, True)
                    p.dma(lambda e: e.dma_start(out=aT[:], in_=aT_s[sq].rearrange("(k q) t -> q k t", q=128)[:, :, ts]), aT, True)
                    p.dma(lambda e: e.dma_start(out=gT[:], in_=gT_s[sq].rearrange("(k q) t -> q k t", q=128)[:, :, ts]), gT, True)
                    for do in range(8):
                        ds_ = slice(do * 128, (do + 1) * 128)
                        pA = psr.next(); pG = psr.next(); pB = psr.next()
                        p.ops("pe", [lambda e, k=k: e.matmul(pA[:], lhsT=wgv[:, k, ds_], rhs=zT[:, k, :], start=(k == 0), stop=(k == 3)) for k in range(4)], reads=[wgv, zT], writes=[pA])
                        p.ops("pe", [lambda e, k=k: e.matmul(pG[:], lhsT=wgg[:, k, ds_], rhs=zT[:, k, :], start=(k == 0), stop=(k == 3)) for k in range(4)], reads=[wgg, zT], writes=[pG])
                        p.ops("pe", [lambda e, k=k: e.matmul(pB[:], lhsT=wab[:, k, ds_], rhs=aT[:, k, :], start=(k == 0), stop=(k == 1)) for k in range(2)], reads=[wab, aT], writes=[pB])
                        sgt = sg.next(); t1 = t1r.next(); t2 = t2r.next()
                        p.op("act", lambda e: e.activation(out=sgt[:], in_=pG[:], func=AF.Sigmoid), reads=[pG], writes=[sgt])
                        p.op("dve", lambda e: e.tensor_tensor(out=t1[:], in0=pA[:], in1=sgt[:], op=ALU.mult), reads=[pA, sgt], writes=[t1])
                        p.op("dve", lambda e: e.tensor_tensor(out=t2[:], in0=pB[:], in1=gT[:, 8 + do, :], op=ALU.mult), reads=[pB, gT], writes=[t2])
                        p.op("dve", lambda e: e.tensor_tensor(out=t1[:], in0=t1[:], in1=gT[:, do, :], op=ALU.mult), reads=[t1, gT], writes=[t1])
                        p.op("dve", lambda e: e.tensor_tensor(out=mixT[:, do, :], in0=t1[:], in1=t2[:], op=ALU.add), reads=[t1, t2], writes=[mixT])
                    for i in range(4):
                        tok = slice(tb * 512 + i * 128, tb * 512 + (i + 1) * 128)
                        xt = xs.next()
                        p.dma(lambda e: e.dma_start(out=xt[:], in_=x_d[sq, tok, :]), xt, True)
                        hp_ = hpre.next()
                        for n in range(2):
                            ns = slice(n * 512, (n + 1) * 512)
                            po = psr.next()
                            p.ops("pe", [lambda e, k=k: e.matmul(po[:], lhsT=mixT[:, k, i * 128:(i + 1) * 128], rhs=wo[:, k, ns], start=(k == 0), stop=(k == 7)) for k in range(8)],
                                  reads=[mixT, wo], writes=[po])
                            p.op("dve", lambda e: e.scalar_tensor_tensor(out=hp_[:, ns], in0=xt[:, ns], scalar=ALPHA, in1=po[:], op0=ALU.mult, op1=ALU.add),
                                 reads=[xt, po], writes=[hp_])
                        ho = hout.next()
                        layer_norm(lnb, hp_, gB, bB, ho)
                        p.dma(lambda e: e.dma_start(out=h_s[sq, tok, :], in_=ho[:]), ho, False)
                        hT = hTt.next()
                        for kk in range(2):
                            pt = psr.next()
                            p.ops("pe", [lambda e, k4=k4: e.transpose(out=pt[:, k4 * 128:(k4 + 1) * 128], in_=ho[:, (kk * 4 + k4) * 128:(kk * 4 + k4 + 1) * 128], identity=ident[:])
                                         for k4 in range(4)], reads=[ho, ident], writes=[pt])
                            p.op("act", lambda e: e.copy(out=hT[:, kk * 4:(kk + 1) * 4, :], in_=pt[:].rearrange("p (k t) -> p k t", k=4)), reads=[pt], writes=[hT])
                        p.dma(lambda e: e.dma_start(out=hT_s[sq].rearrange("(k q) t -> q k t", q=128)[:, :, tok], in_=hT[:]), hT, False)
        new_phase()
        stop('M1')

        with ExitStack() as es:
            def sb(name, shape, dt, dma=False, const=False):
                return p.buf(es.enter_context(nc.sbuf_tensor(name, list(shape), dt)), dma=dma, const=const)
            wup = sb("wup", [128, 8, 2 * DFF], BF16); wdn = sb("wdn", [128, 22, D], BF16)
            stg_ = Ring([sb(f"wl_u{i}", [128, 1024], F32, dma=True) for i in range(2)])
            load_w_bf16(sb, wup, md["wup"], 8, 2 * DFF, "u", stg_); load_w_bf16(sb, wdn, md["wdn"], 22, D, "v", stg_)
            gB = sb("ln2gB", [128, D], F32, dma=True, const=True); bB = sb("ln2bB", [128, D], F32, dma=True, const=True)
            p.dma(lambda e: e.dma_start(out=gB[:], in_=md["ln2g"].partition_broadcast(128)), gB, True)
            p.dma(lambda e: e.dma_start(out=bB[:], in_=md["ln2b"].partition_broadcast(128)), bB, True)
            cw = sb("cw", [128, 44, 3], F32, dma=True, const=True); cbias = sb("cbias", [128, 44], F32, dma=True, const=True)
            p.dma(lambda e: e.dma_start(out=cw[:], in_=md["cw"]), cw, True)
            p.dma(lambda e: e.dma_start(out=cbias[:], in_=md["cbias"]), cbias, True)
            epsb = sb("epsb2", [128, 1], F32)
            p.op("pool", lambda e: e.memset(epsb[:], LN_EPS), writes=[epsb])
            hT = sb("hTf", [128, 8, 514], BF16, dma=True)
            hres = Ring([sb(f"hres{i}", [128, D], F32, dma=True) for i in range(1)])
            cvr = Ring([sb(f"cv{i}", [128, 512], F32) for i in range(4)])
            actT = sb("actT", [128, 22, 512], BF16)
            opre = sb("opre", [128, D], F32)
            oout = Ring([sb(f"oout{i}", [128, D], F32, dma=True) for i in range(1)])
            lnb = (sb("st2", [128, 2, 6], F32), sb("mv2", [128, 2], F32), sb("rstd2", [128, 1], F32), sb("hn2", [128, D], F32))
            psm = Ring(psum[0:6]); psh = Ring(psum[6:8])
            for sq in range(nseq):
                for tb in range(S // 512):
                    t0 = tb * 512
                    lo = max(t0 - 1, 0); hi = min(t0 + 513, S)
                    if t0 == 0:
                        p.op("pool", lambda e: e.memset(hT[:, :, 0:1], 0.0), writes=[hT])
                    if t0 + 512 == S:
                        p.op("pool", lambda e: e.memset(hT[:, :, 513:514], 0.0), writes=[hT])
                    p.dma(lambda e: e.dma_start(out=hT[:, :, lo - (t0 - 1):hi - (t0 - 1)], in_=hT_s[sq].rearrange("(k q) t -> q k t", q=128)[:, :, lo:hi]), hT, True)
                    for c in range(22):
                        cvs = []
                        for ch in (c, 22 + c):
                            cs_ = slice(ch * 128, (ch + 1) * 128)
                            pm = psm.next(); ph = psh.next()
                            p.ops("pe", [lambda e, k=k: e.matmul(pm[:], lhsT=wup[:, k, cs_], rhs=hT[:, k, 1:513], start=(k == 0), stop=(k == 7)) for k in range(8)]
                                  + [lambda e, k=k: e.matmul(ph[:, 0:2], lhsT=wup[:, k, cs_], rhs=hT[:, k, 0:514:513], start=(k == 0), stop=(k == 7)) for k in range(8)],
                                  reads=[wup, hT], writes=[pm, ph])
                            cv = cvr.next()
                            p.op("act", lambda e: e.activation(out=cv[:], in_=pm[:], func=AF.Identity, scale=cw[:, ch, 1:2], bias=cbias[:, ch:ch + 1]),
                                 reads=[pm, cw, cbias], writes=[cv])
                            p.op("dve", lambda e: e.scalar_tensor_tensor(out=cv[:, 1:512], in0=pm[:, 0:511], scalar=cw[:, ch, 0:1], in1=cv[:, 1:512], op0=ALU.mult, op1=ALU.add), reads=[pm, cw, cv], writes=[cv])
                            p.op("dve", lambda e: e.scalar_tensor_tensor(out=cv[:, 0:511], in0=pm[:, 1:512], scalar=cw[:, ch, 2:3], in1=cv[:, 0:511], op0=ALU.mult, op1=ALU.add), reads=[pm, cw, cv], writes=[cv])
                            p.op("dve", lambda e: e.scalar_tensor_tensor(out=cv[:, 0:1], in0=ph[:, 0:1], scalar=cw[:, ch, 0:1], in1=cv[:, 0:1], op0=ALU.mult, op1=ALU.add), reads=[ph, cw, cv], writes=[cv])
                            p.op("dve", lambda e: e.scalar_tensor_tensor(out=cv[:, 511:512], in0=ph[:, 1:2], scalar=cw[:, ch, 2:3], in1=cv[:, 511:512], op0=ALU.mult, op1=ALU.add), reads=[ph, cw, cv], writes=[cv])
                            cvs.append(cv)
                        p.op("act", lambda e: e.activation(out=cvs[0][:], in_=cvs[0][:], func=AF.Gelu), reads=[cvs[0]], writes=[cvs[0]])
                        p.op("dve", lambda e: e.tensor_tensor(out=actT[:, c, :], in0=cvs[0][:], in1=cvs[1][:], op=ALU.mult), reads=cvs, writes=[actT])
                    for i in range(4):
                        tok = slice(t0 + i * 128, t0 + (i + 1) * 128)
                        hr = hres.next()
                        p.dma(lambda e: e.dma_start(out=hr[:], in_=h_s[sq, tok, :]), hr, True)
                        for n in range(2):
                            ns = slice(n * 512, (n + 1) * 512)
                            po = psm.next()
                            p.ops("pe", [lambda e, k=k: e.matmul(po[:], lhsT=actT[:, k, i * 128:(i + 1) * 128], rhs=wdn[:, k, ns], start=(k == 0), stop=(k == 21)) for k in range(22)],
                                  reads=[actT, wdn], writes=[po])
                            p.op("dve", lambda e: e.scalar_tensor_tensor(out=opre[:, ns], in0=hr[:, ns], scalar=ALPHA, in1=po[:], op0=ALU.mult, op1=ALU.add),
                                 reads=[hr, po], writes=[opre])
                        oo = oout.next()
                        layer_norm(lnb, opre, gB, bB, oo)
                        p.dma(lambda e: e.dma_start(out=out_d[sq, tok, :], in_=oo[:]), oo, False)
        new_phase()


def _host_inputs(inputs, core, nseq=NSEQ):
    f32 = np.float32
    x = np.ascontiguousarray(inputs["x"][core * nseq:(core + 1) * nseq]).astype(f32)
    pos = np.ascontiguousarray(inputs["positions"][core * nseq:(core + 1) * nseq]).astype(np.int32)
    w_in = np.asarray(inputs["w_in"][0], f32)
    b_in = np.asarray(inputs["b_in"][0], f32)
    sw = np.arange(AW).reshape(-1, 2, 32)[:, ::-1, :].reshape(-1)
    q0, k0, v0, g0 = SSMW, SSMW + AW, SSMW + 2 * AW, SSMW + 3 * AW
    cols = np.concatenate([np.arange(0, SSMW), np.arange(q0, q0 + AW), np.arange(k0, k0 + AW),
                           q0 + sw, k0 + sw, np.arange(g0, g0 + 2 * D)])
    w_fm = np.ascontiguousarray(w_in[:, cols])
    b_fm = np.ascontiguousarray(b_in[cols].reshape(NFM // 128, 128).T)
    w_v = np.ascontiguousarray(w_in[:, v0:v0 + AW])
    b_v = np.ascontiguousarray(b_in[v0:v0 + AW].reshape(1, AW))
    half = 32
    inv_freq = (10000.0 ** (-np.arange(half, dtype=np.float64) * 2.0 / 64)).astype(f32)
    invf = np.zeros((128, 2), f32)
    for pp in range(128):
        invf[pp, 0] = inv_freq[pp % 32] / TWO_PI
        invf[pp, 1] = -TWO_PI if (pp % 64) < 32 else TWO_PI
    def tile_layout(a):
        return np.ascontiguousarray(a.reshape(2, 16, 2, 64).transpose(2, 3, 0, 1).reshape(128, 32)).astype(f32)
    lre_h = tile_layout(np.asarray(inputs["ssm_lam_re"][0], f32))
    lim_h = tile_layout(np.asarray(inputs["ssm_lam_im"][0], f32))
    ldt_h = tile_layout(np.broadcast_to(np.asarray(inputs["ssm_log_dt"][0], f32)[:, :, None], (2, 32, 64)).copy())

    def bz(b):
        o = np.zeros((128, 32, 128), f32)
        b = np.asarray(b, f32)
        for dr in range(2):
            for gp in range(16):
                for gl in range(2):
                    c0 = (gp % 4) * 32 + gl * 16
                    o[gl * 64:(gl + 1) * 64, dr * 16 + gp, c0:c0 + 16] = b[dr, 2 * gp + gl]
        return o

    def cb(c):
        o = np.zeros((32, 32, 128), f32)
        c = np.asarray(c, f32)
        for dr in range(2):
            for gp in range(16):
                for gl in range(2):
                    o[gl * 16:(gl + 1) * 16, dr * 16 + gp, gl * 64:(gl + 1) * 64] = c[dr, 2 * gp + gl]
        return o
    ssm = {"lre_h": lre_h, "lim_h": lim_h, "ldt_h": ldt_h,
           "bzr_h": bz(inputs["ssm_b_re"][0]), "bzi_h": bz(inputs["ssm_b_im"][0]),
           "cbr_h": cb(inputs["ssm_c_re"][0]), "cbi_h": cb(inputs["ssm_c_im"][0]),
           "dsk_h": np.ascontiguousarray(np.asarray(inputs["ssm_d"][0], f32).reshape(4, 128).T),
           "iota_h": np.arange(S, dtype=f32).reshape(1, S)}
    d = {"x": x, "pos": pos, "ident": np.eye(128, dtype=f32), "invf": invf,
         "w_in_fm": w_fm, "b_fm": b_fm, "w_v": w_v, "b_v": b_v}
    d.update(ssm)
    ii = np.arange(128)[:, None]; jj = np.arange(128)[None, :]
    maskb = np.concatenate([np.where(ii >= jj, 0.0, -30000.0), np.where(ii <= jj, 0.0, -30000.0)], axis=1).astype(f32)
    ones3 = np.zeros((128, 3, 64), f32)
    ones3[:, 0, :] = 1.0; ones3[64:, 1, :] = 1.0; ones3[:64, 2, :] = 1.0
    g = lambda n: np.ascontiguousarray(np.asarray(inputs[n][0], f32))
    cwh = np.ascontiguousarray(g("conv_w").reshape(3, 44, 128).transpose(2, 1, 0))
    cbh = np.ascontiguousarray(g("conv_b").reshape(44, 128).T)
    d.update({"maskb_h": maskb, "ones3_h": ones3, "wgv_h": g("w_glu_v"), "wgg_h": g("w_glu_g"), "wab_h": g("w_attn_br"), "wo_h": g("w_out"),
              "ln1g_h": g("ln1_g").reshape(1, D), "ln1b_h": g("ln1_b").reshape(1, D), "ln2g_h": g("ln2_g").reshape(1, D), "ln2b_h": g("ln2_b").reshape(1, D),
              "wup_h": g("w_up"), "wdn_h": g("w_down"), "cw_h": cwh, "cb_h": cbh})
    return d


def kernel(**inputs):
    nc = build()
    in_maps = [_host_inputs(inputs, c) for c in range(NCORES)]
    res = run_bass_kernel_spmd(nc, in_maps, core_ids=list(range(NCORES)))
    out = np.concatenate([r["out"] for r in res.results], axis=0)
    return out.astype(np.float32)
```

```python
import math
from contextlib import ExitStack

import numpy as np
import concourse.bass as bass
import concourse.mybir as mybir
from concourse.bass_utils import run_bass_kernel_spmd

F32 = mybir.dt.float32
BF16 = mybir.dt.bfloat16
I32 = mybir.dt.int32
AF = mybir.ActivationFunctionType
ALU = mybir.AluOpType
AX = mybir.AxisListType

S = 4096
D = 1024
NCORES = 8
NSEQ = 2
SSMW = 512
AW = 768
DFF = 2816
NFM = 5632
ALPHA = 2.0 ** 0.25
LN_EPS = 1e-5
TWO_PI = 2.0 * math.pi
DIL = (1, 4, 16)
KPAD = 1024


class Buf:
    __slots__ = ("t", "w", "r", "dsem", "const")

    def __init__(self, t, dsem=None, const=False):
        self.t = t
        self.w = None
        self.r = {}
        self.dsem = dsem
        self.const = const

    def __getitem__(self, k):
        return self.t[k]


class Prog:
    ENG = ("pe", "act", "dve", "pool", "sp")

    def __init__(self, nc, es, n_dsem=72):
        self.nc = nc
        self.engobj = {'pe': nc.tensor, 'act': nc.scalar, 'dve': nc.vector, 'pool': nc.gpsimd, 'sp': nc.sync}
        self.ninst = 0
        self.stopped = False
        self.esem = {e: es.enter_context(nc.semaphore("es_" + e)) for e in ("pe", "act", "dve", "pool")}
        self.ecount = {e: 0 for e in self.esem}
        self.dsems = [es.enter_context(nc.semaphore(f"ds{i}")) for i in range(n_dsem)]
        self.dcount = {id(s): 0 for s in self.dsems}
        self.dnext = 0
        self.waited = {e: {} for e in self.ENG}
        self.semobj = {}
        for s in list(self.esem.values()) + self.dsems:
            self.semobj[id(s)] = s

    def buf(self, t, dma=False, const=False):
        ds = None
        if dma:
            assert self.dnext < len(self.dsems), "out of DMA semaphores in this phase"
            ds = self.dsems[self.dnext]
            self.dnext += 1
        return Buf(t, ds, const)

    def _deps(self, reads, writes):
        deps = {}

        def add(ev):
            if ev is None:
                return
            k, v = ev
            if deps.get(k, 0) < v:
                deps[k] = v
        for b in reads:
            add(b.w)
        for b in writes:
            add(b.w)
            for k, v in b.r.items():
                add((k, v))
        return deps

    def _record(self, ev, reads, writes):
        for b in writes:
            b.w = ev
            b.r = {}
        for b in reads:
            if b.const:
                continue
            if b.r.get(ev[0], 0) < ev[1]:
                b.r[ev[0]] = ev[1]

    def _emit(self, eng, deps, fn, inc):
        e = self.engobj[eng]
        wd = self.waited[eng]
        own = id(self.esem[eng]) if eng in self.esem else None
        for k, v in deps.items():
            if eng == "pe" and k == own:
                continue
            if wd.get(k, 0) >= v:
                continue
            wd[k] = v
            e.wait_ge(self.semobj[k], v)
        if fn is None:
            return
        ins = fn(e)
        if inc is not None:
            ins.then_inc(inc[0], inc[1])
        self.ninst += 1

    def op(self, eng, fn, reads=(), writes=()):
        if self.stopped:
            return None
        deps = self._deps(reads, writes)
        self.ecount[eng] += 1
        sem = self.esem[eng]
        ev = (id(sem), self.ecount[eng])
        self._emit(eng, deps, fn, (sem, 1))
        self._record(ev, reads, writes)
        return ev

    def ops(self, eng, fns, reads=(), writes=()):
        assert eng == "pe"
        if self.stopped:
            return None
        deps = self._deps(reads, writes)
        for fn in fns[:-1]:
            self._emit(eng, deps, fn, None)
            deps = {}
        self.ecount[eng] += 1
        sem = self.esem[eng]
        ev = (id(sem), self.ecount[eng])
        self._emit(eng, deps, fns[-1], (sem, 1))
        self._record(ev, reads, writes)
        return ev

    def dma(self, fn, sb, load, reads=(), writes=(), q="sp"):
        if self.stopped:
            return None
        reads = list(reads)
        writes = list(writes)
        if load:
            writes.append(sb)
        else:
            reads.append(sb)
        deps = self._deps(reads, writes)
        sem = sb.dsem
        assert sem is not None
        self.dcount[id(sem)] += 16
        ev = (id(sem), self.dcount[id(sem)])
        self._emit(q, deps, fn, (sem, 16))
        self._record(ev, reads, writes)
        return ev

    def dma_group(self, fns, sb, load, reads=(), writes=(), q="sp"):
        if self.stopped:
            return None
        reads = list(reads)
        writes = list(writes)
        if load:
            writes.append(sb)
        else:
            reads.append(sb)
        deps = self._deps(reads, writes)
        sem = sb.dsem
        ev = None
        for fn in fns:
            self.dcount[id(sem)] += 16
            ev = (id(sem), self.dcount[id(sem)])
            self._emit(q, deps, fn, (sem, 16))
            deps = {}
        self._record(ev, reads, writes)
        return ev

    def barrier(self):
        allev = {}
        for e, s in self.esem.items():
            if self.ecount[e]:
                allev[id(s)] = self.ecount[e]
        for s in self.dsems:
            if self.dcount[id(s)]:
                allev[id(s)] = self.dcount[id(s)]
        for eng in self.ENG:
            self._emit(eng, allev, None, None)
        self.dnext = 0

    def emit(self):
        pass


class StopBuild(Exception):
    pass


class Ring:
    def __init__(self, bufs):
        self.bufs = bufs
        self.i = 0

    def next(self):
        b = self.bufs[self.i % len(self.bufs)]
        self.i += 1
        return b


def build(nseq=NSEQ, debug=False, stop_after=None):
    nc = bass.Bass("TRN2", target_bir_lowering=False)

    def din(name, shape, dt=F32):
        return nc.dram_tensor(name, list(shape), dt, kind="ExternalInput").ap()

    dbg_kind = "ExternalOutput" if debug else "Internal"

    def dscr(name, shape, dt):
        return nc.dram_tensor(name, list(shape), dt, kind=dbg_kind).ap()

    x_d = din("x", [nseq, S, D])
    pos_d = din("pos", [nseq, S], I32)
    ident_d = din("ident", [128, 128])
    invf_d = din("invf", [128, 2])
    w_in_d = din("w_in_fm", [D, NFM])
    b_fm_d = din("b_fm", [128, NFM // 128])
    w_v_d = din("w_v", [D, AW])
    b_v_d = din("b_v", [1, AW])
    out_d = nc.dram_tensor("out", [nseq, S, D], F32, kind="ExternalOutput").ap()
    ssm_d = dict(
        lre=din("lre_h", [128, 32]), lim=din("lim_h", [128, 32]), ldt=din("ldt_h", [128, 32]),
        bzr=din("bzr_h", [128, 32, 128]), bzi=din("bzi_h", [128, 32, 128]),
        cbr=din("cbr_h", [32, 32, 128]), cbi=din("cbi_h", [32, 32, 128]),
        dsk=din("dsk_h", [128, 4]), iota=din("iota_h", [1, S]))
    zT_s = dscr("zT_s", [nseq, SSMW, S], BF16)
    aT_s = dscr("aT_s", [nseq, 256, S], BF16)
    h_s = dscr("h_s", [nseq, S, D], F32)
    hT_s = dscr("hT_s", [nseq, D, S], BF16)
    md = dict(maskb=din("maskb_h", [128, 256]), ones3=din("ones3_h", [128, 3, 64]),
              wgv=din("wgv_h", [512, D]), wgg=din("wgg_h", [512, D]), wab=din("wab_h", [256, D]), wo=din("wo_h", [D, D]),
              ln1g=din("ln1g_h", [1, D]), ln1b=din("ln1b_h", [1, D]), ln2g=din("ln2g_h", [1, D]), ln2b=din("ln2b_h", [1, D]),
              wup=din("wup_h", [D, 2 * DFF]), wdn=din("wdn_h", [DFF, D]), cw=din("cw_h", [128, 44, 3]), cbias=din("cb_h", [128, 44]),
              aT_s=aT_s, h_s=h_s, hT_s=hT_s)

    xT_s = dscr("xT_s", [nseq, D, S], BF16)
    uT_s = dscr("uT_s", [nseq, SSMW, S], BF16)
    qT_s = dscr("qT_s", [nseq, AW, S], BF16)
    kT_s = dscr("kT_s", [nseq, AW, S], BF16)
    gT_s = dscr("gT_s", [nseq, 2 * D, S], BF16)
    NBLK = [d * (S // d // 128 + 1) for d in DIL]
    v_s = [dscr(f"v_s{g}", [nseq, 128, NBLK[g], 256], BF16) for g in range(3)]

    with ExitStack() as es0:
        p = Prog(nc, es0)
        psum = [p.buf(es0.enter_context(nc.psum_tensor(f"ps{i}", [128, 512], F32))) for i in range(8)]
        ident = p.buf(es0.enter_context(nc.sbuf_tensor("ident_sb", [128, 128], F32)), dma=True, const=True)
        p.dma(lambda e: e.dma_start(out=ident[:], in_=ident_d), ident, True)
        p.dnext = 1
        wbf = {}

        def cast_w(key, src, R, C):
            dst = nc.dram_tensor(key + "_bf", [R, C], BF16, kind="Internal").ap()
            pb = p.buf(None, dma=True)
            p.dma_group([lambda e, r0=r0: e.dma_start(out=dst[r0:min(r0 + 128, R), :], in_=src[r0:min(r0 + 128, R), :], max_dma_last_dim=4096)
                         for r0 in range(0, R, 128)], pb, True, q="pool")
            wbf[key] = (dst, pb)
        cast_w("w_in", w_in_d, D, NFM)
        cast_w("w_v", w_v_d, D, AW)
        cast_w("wgv", md["wgv"], SSMW, D)
        cast_w("wgg", md["wgg"], SSMW, D)
        cast_w("wab", md["wab"], 256, D)
        cast_w("wo", md["wo"], D, D)
        cast_w("wup", md["wup"], D, 2 * DFF)
        cast_w("wdn", md["wdn"], DFF, D)
        NRES = p.dnext

        def new_phase():
            p.barrier()
            p.dnext = NRES

        def stop(tag):
            if stop_after == tag:
                p.stopped = True

        try:
            _phases(nc, p, psum, ident, nseq, locals_d=dict(x_d=x_d, pos_d=pos_d, invf_d=invf_d, w_in_d=w_in_d, b_fm_d=b_fm_d, w_v_d=w_v_d, b_v_d=b_v_d, out_d=out_d, xT_s=xT_s, uT_s=uT_s, qT_s=qT_s, kT_s=kT_s, gT_s=gT_s, v_s=v_s, NBLK=NBLK, ssm_d=ssm_d, zT_s=zT_s, md=md, wbf=wbf), new_phase=new_phase, stop=stop)
        except StopBuild:
            pass
        p.stopped = False
        p.barrier()
    print('instructions', p.ninst)
    return nc


def _phases(nc, p, psum, ident, nseq, locals_d, new_phase, stop):
    globals_ = locals_d
    x_d = globals_['x_d']; pos_d = globals_['pos_d']; invf_d = globals_['invf_d']; w_in_d = globals_['w_in_d']; b_fm_d = globals_['b_fm_d']
    w_v_d = globals_['w_v_d']; b_v_d = globals_['b_v_d']; out_d = globals_['out_d']; xT_s = globals_['xT_s']; uT_s = globals_['uT_s']
    qT_s = globals_['qT_s']; kT_s = globals_['kT_s']; gT_s = globals_['gT_s']; v_s = globals_['v_s']; NBLK = globals_['NBLK']
    ssm_d = globals_['ssm_d']; zT_s = globals_['zT_s']; md = globals_['md']
    aT_s = md['aT_s']; h_s = md['h_s']; hT_s = md['hT_s']; wbf = globals_['wbf']

    def load_wbf(wdst, key, K):
        src, pb = wbf[key]
        p.dma_group([lambda e, k=k: e.dma_start(out=wdst[:, k, :], in_=src[k * 128:(k + 1) * 128, :]) for k in range(K)], wdst, True, reads=[pb])
        wdst.const = True
    if True:

        with ExitStack() as es:
            def sb(name, shape, dt, dma=False, const=False):
                return p.buf(es.enter_context(nc.sbuf_tensor(name, list(shape), dt)), dma=dma, const=const)

            wA = sb("wA", [128, 8, NFM], BF16, dma=True)
            bfm = sb("bfm", [128, NFM // 128], F32, dma=True)
            invf = sb("invf_sb", [128, 2], F32, dma=True)
            p.dma(lambda e: e.dma_start(out=bfm[:], in_=b_fm_d), bfm, True)
            p.dma(lambda e: e.dma_start(out=invf[:], in_=invf_d), invf, True)
            load_wbf(wA, 'w_in', 8)
            stop('A0')

            cosT = sb("cosT", [128, S], F32)
            sinT = sb("sinT", [128, S], F32)
            posi = sb("posi", [128, 1024], I32, dma=True)
            tur = sb("tur", [128, 1024], F32)
            turi = sb("turi", [128, 1024], I32)
            xs = [sb(f"xs{i}", [128, D], F32, dma=True) for i in range(4)]
            xT = Ring([sb(f"xT{j}", [128, 8, 512], BF16, dma=True) for j in range(2)])
            ev_bf = Ring([sb(f"evbf{j}", [128, 512], BF16, dma=True) for j in range(8)])
            rt = Ring([sb(f"rt{j}", [128, 512], F32) for j in range(4)])
            psr = Ring(psum)

            for sq in range(nseq):
                for c in range(S // 1024):
                    cs = slice(c * 1024, (c + 1) * 1024)
                    p.dma(lambda e, cs=cs: e.dma_start(out=posi[:], in_=pos_d[sq:sq + 1, cs].partition_broadcast(128)), posi, True)
                    for (tab, addc, scol) in ((sinT, 0.0, 1), (cosT, 0.25, None)):
                        p.op("dve", lambda e: e.tensor_copy(out=tur[:], in_=posi[:]), reads=[posi], writes=[tur])
                        p.op("dve", lambda e, addc=addc: e.tensor_scalar(out=tur[:], in0=tur[:], scalar1=invf[:, 0:1], scalar2=addc,
                                                                          op0=ALU.mult, op1=ALU.add), reads=[tur, invf], writes=[tur])
                        p.op("dve", lambda e: e.tensor_copy(out=turi[:], in_=tur[:]), reads=[tur], writes=[turi])
                        p.op("dve", lambda e: e.tensor_tensor(out=tur[:], in0=tur[:], in1=turi[:], op=ALU.subtract),
                             reads=[tur, turi], writes=[tur])
                        if scol is not None:
                            p.op("act", lambda e, tab=tab, cs=cs: e.activation(out=tab[:, cs], in_=tur[:], func=AF.Sin, scale=invf[:, 1:2]),
                                 reads=[tur, invf], writes=[tab])
                        else:
                            p.op("act", lambda e, tab=tab, cs=cs: e.activation(out=tab[:, cs], in_=tur[:], func=AF.Sin, scale=TWO_PI),
                                 reads=[tur], writes=[tab])
                stop('A1')
                def load_x(tb_):
                    for i in range(4):
                        p.dma(lambda e, i=i: e.dma_start(out=xs[i][:], in_=x_d[sq, tb_ * 512 + i * 128:tb_ * 512 + (i + 1) * 128, :]), xs[i], True)
                load_x(0)
                for tb in range(S // 512):
                    t0 = tb * 512
                    ts = slice(t0, t0 + 512)
                    xtile = xs
                    xTb = xT.next()
                    for k in range(8):
                        ps = psr.next()
                        p.ops("pe", [lambda e, ps=ps, i=i, k=k: e.transpose(out=ps[:, i * 128:(i + 1) * 128],
                                                                           in_=xtile[i][:, k * 128:(k + 1) * 128], identity=ident[:])
                                     for i in range(4)], reads=xtile + [ident], writes=[ps])
                        if k % 2 == 0:
                            p.op("act", lambda e, ps=ps, k=k: e.copy(out=xTb[:, k, :], in_=ps[:]), reads=[ps], writes=[xTb])
                        else:
                            p.op("dve", lambda e, ps=ps, k=k: e.tensor_copy(out=xTb[:, k, :], in_=ps[:]), reads=[ps], writes=[xTb])
                    if tb + 1 < S // 512:
                        load_x(tb + 1)
                    p.dma(lambda e: e.dma_start(out=xT_s[sq].rearrange("(k q) t -> q k t", q=128)[:, :, ts], in_=xTb[:]), xTb, False)

                    def proj(fo):
                        ps = psr.next()
                        p.ops("pe", [lambda e, ps=ps, k=k: e.matmul(ps[:], lhsT=wA[:, k, fo * 128:(fo + 1) * 128], rhs=xTb[:, k, :],
                                                                      start=(k == 0), stop=(k == 7)) for k in range(8)],
                              reads=[wA, xTb], writes=[ps])
                        return ps

                    for fo in range(4):
                        ps = proj(fo)
                        o = ev_bf.next()
                        p.op("act", lambda e, ps=ps, o=o, fo=fo: e.activation(out=o[:], in_=ps[:], func=AF.Identity, bias=bfm[:, fo:fo + 1]),
                             reads=[ps, bfm], writes=[o])
                        p.dma(lambda e, o=o, fo=fo: e.dma_start(out=uT_s[sq, fo * 128:(fo + 1) * 128, ts], in_=o[:]), o, False)
                    for which, dst in ((0, qT_s), (1, kT_s)):
                        for c in range(6):
                            fo = 4 + which * 6 + c
                            psa = proj(fo)
                            psb = proj(fo + 12)
                            t1 = rt.next()
                            t2 = rt.next()
                            p.op("dve", lambda e, psa=psa, t1=t1, fo=fo: e.scalar_tensor_tensor(
                                out=t1[:], in0=psa[:], scalar=bfm[:, fo:fo + 1], in1=cosT[:, ts], op0=ALU.add, op1=ALU.mult),
                                reads=[psa, bfm, cosT], writes=[t1])
                            p.op("dve", lambda e, psb=psb, t2=t2, fo=fo: e.scalar_tensor_tensor(
                                out=t2[:], in0=psb[:], scalar=bfm[:, fo + 12:fo + 13], in1=sinT[:, ts], op0=ALU.add, op1=ALU.mult),
                                reads=[psb, bfm, sinT], writes=[t2])
                            o = ev_bf.next()
                            p.op("pool", lambda e, o=o, t1=t1, t2=t2: e.tensor_tensor(out=o[:], in0=t1[:], in1=t2[:], op=ALU.add),
                                 reads=[t1, t2], writes=[o])
                            p.dma(lambda e, o=o, c=c, dst=dst: e.dma_start(out=dst[sq, c * 128:(c + 1) * 128, ts], in_=o[:]), o, False)
                    for c in range(16):
                        fo = 28 + c
                        ps = proj(fo)
                        o = ev_bf.next()
                        p.op("act", lambda e, ps=ps, o=o, fo=fo: e.activation(out=o[:], in_=ps[:], func=AF.Sigmoid, bias=bfm[:, fo:fo + 1]),
                             reads=[ps, bfm], writes=[o])
                        p.dma(lambda e, o=o, c=c: e.dma_start(out=gT_s[sq, c * 128:(c + 1) * 128, ts], in_=o[:]), o, False)
                    stop(f'A2_{tb}')
        new_phase()
        stop('A')

        with ExitStack() as es:
            def sb(name, shape, dt, dma=False, const=False):
                return p.buf(es.enter_context(nc.sbuf_tensor(name, list(shape), dt)), dma=dma, const=const)

            wV = sb("wV", [128, 8, AW], BF16, dma=True)
            load_wbf(wV, 'w_v', 8)
            bv = sb("bv", [128, AW], F32, dma=True, const=True)
            p.dma(lambda e: e.dma_start(out=bv[:], in_=b_v_d.partition_broadcast(128)), bv, True)
            xTf = sb("xTf", [128, 8, S], BF16, dma=True)
            VCH = 12
            vring = Ring([sb(f"vstg{j}", [128, VCH, 256], BF16, dma=True) for j in range(2)])
            psr = Ring(psum)
            for sq in range(nseq):
                p.dma(lambda e: e.dma_start(out=xTf[:], in_=xT_s[sq].rearrange("(k q) t -> q k t", q=128)), xTf, True)
                for g in range(3):
                    d = DIL[g]
                    L = S // d
                    nb = L // 128 + 1
                    blocks = [(r, m) for r in range(d) for m in range(nb)]
                    for c0 in range(0, len(blocks), VCH):
                        chunk = blocks[c0:c0 + VCH]
                        stg = vring.next()
                        p.op("pool", lambda e, stg=stg: e.memset(stg[:], 0.0), writes=[stg])
                        for j, (r, m) in enumerate(chunk):
                            lo = 64 + 128 * (m - 1)
                            i0 = max(0, -lo)
                            i1 = min(128, L - lo)
                            M = i1 - i0
                            tok0 = r + d * (lo + i0)
                            ps = psr.next()
                            p.ops("pe", [lambda e, ps=ps, k=k, tok0=tok0, M=M, i0=i0, d=d, g=g: e.matmul(
                                ps[i0:i0 + M, 0:256], lhsT=xTf[:, k, tok0:tok0 + d * (M - 1) + 1:d], rhs=wV[:, k, g * 256:(g + 1) * 256],
                                start=(k == 0), stop=(k == 7)) for k in range(8)], reads=[xTf, wV], writes=[ps])
                            p.op("dve", lambda e, ps=ps, stg=stg, j=j, i0=i0, M=M, g=g: e.tensor_tensor(
                                out=stg[i0:i0 + M, j, :], in0=ps[i0:i0 + M, 0:256], in1=bv[i0:i0 + M, g * 256:(g + 1) * 256], op=ALU.add),
                                reads=[ps, bv], writes=[stg])
                        p.dma(lambda e, stg=stg, c0=c0, n=len(chunk), g=g: e.dma_start(out=v_s[g][sq, :, c0:c0 + n, :], in_=stg[:, 0:n, :]), stg, False)
                        stop(f'V{g}_{c0}')
                    stop(f'V{g}')
        new_phase()

        with ExitStack() as es:
            def sb(name, shape, dt, dma=False, const=False):
                return p.buf(es.enter_context(nc.sbuf_tensor(name, list(shape), dt)), dma=dma, const=const)

            NT = 32
            NCH = S // 8
            lre = sb("lre", [128, NT], F32, dma=True); lim = sb("lim", [128, NT], F32, dma=True); ldt = sb("ldt", [128, NT], F32, dma=True)
            p.dma(lambda e: e.dma_start(out=lre[:], in_=ssm_d["lre"]), lre, True)
            p.dma(lambda e: e.dma_start(out=lim[:], in_=ssm_d["lim"]), lim, True)
            p.dma(lambda e: e.dma_start(out=ldt[:], in_=ssm_d["ldt"]), ldt, True)
            dsk = sb("dsk", [128, 4], F32, dma=True)
            p.dma(lambda e: e.dma_start(out=dsk[:], in_=ssm_d["dsk"]), dsk, True)
            tI = sb("tI", [128, NCH], F32, dma=True, const=True)
            p.dma(lambda e: e.dma_start(out=tI[:], in_=ssm_d["iota"][:, 0:NCH].partition_broadcast(128)), tI, True)
            sm = {n: sb("sm_" + n, [128, NT], F32) for n in
                  ("dt", "xr", "xi", "rho", "th", "t0", "t1", "f", "sinx", "cosx", "sinh", "em1", "am1", "abi", "den", "kr", "ki", "u0", "u1",
                   "rho8", "th8", "pm", "pc", "ps")}
            smi = sb("smi", [128, NT], I32)
            pwr = sb("pwr", [128, 16, NT], F32); pwi = sb("pwi", [128, 16, NT], F32); npwi = sb("npwi", [128, 16, NT], F32)

            def V(fn, reads, writes):
                return p.op("dve", fn, reads=reads, writes=writes)

            def A(fn, reads, writes):
                return p.op("act", fn, reads=reads, writes=writes)

            def tt(o, a, b, op):
                V(lambda e: e.tensor_tensor(out=o[:], in0=a[:], in1=b[:], op=op), [a, b], [o])

            def tsc(o, a, s1, op0, s2=None, op1=None):
                if op1 is None:
                    V(lambda e: e.tensor_scalar(out=o[:], in0=a[:], scalar1=s1, scalar2=None, op0=op0), [a], [o])
                else:
                    V(lambda e: e.tensor_scalar(out=o[:], in0=a[:], scalar1=s1, scalar2=s2, op0=op0, op1=op1), [a], [o])

            def frac_sin(o, turns_src, mul, add):
                tsc(sm["t0"], turns_src, mul, ALU.mult, add, ALU.add)
                V(lambda e: e.tensor_copy(out=smi[:], in_=sm["t0"][:]), [sm["t0"]], [smi])
                tt(sm["f"], sm["t0"], smi, ALU.subtract)
                A(lambda e: e.activation(out=o[:], in_=sm["f"][:], func=AF.Sin, scale=TWO_PI), [sm["f"]], [o])

            A(lambda e: e.activation(out=sm["dt"][:], in_=ldt[:], func=AF.Exp), [ldt], [sm["dt"]])
            tt(sm["xr"], lre, sm["dt"], ALU.mult)
            tt(sm["xi"], lim, sm["dt"], ALU.mult)
            A(lambda e: e.activation(out=sm["rho"][:], in_=sm["xr"][:], func=AF.Exp), [sm["xr"]], [sm["rho"]])
            A(lambda e: e.activation(out=sm["rho8"][:], in_=sm["xr"][:], func=AF.Exp, scale=8.0), [sm["xr"]], [sm["rho8"]])
            tsc(sm["th"], sm["xi"], 1.0 / TWO_PI, ALU.mult)
            tsc(sm["th8"], sm["th"], 8.0, ALU.mult)
            frac_sin(sm["sinx"], sm["th"], 1.0, 0.0)
            frac_sin(sm["cosx"], sm["th"], 1.0, 0.25)
            frac_sin(sm["sinh"], sm["th"], 0.5, 0.0)
            tsc(sm["em1"], sm["xr"], 0.2, ALU.mult, 1.0, ALU.add)
            for cdiv in (0.25, 1.0 / 3.0, 0.5):
                tt(sm["em1"], sm["em1"], sm["xr"], ALU.mult)
                tsc(sm["em1"], sm["em1"], cdiv, ALU.mult, 1.0, ALU.add)
            tt(sm["em1"], sm["em1"], sm["xr"], ALU.mult)
            tt(sm["am1"], sm["em1"], sm["cosx"], ALU.mult)
            tt(sm["u0"], sm["sinh"], sm["sinh"], ALU.mult)
            V(lambda e: e.scalar_tensor_tensor(out=sm["am1"][:], in0=sm["u0"][:], scalar=-2.0, in1=sm["am1"][:], op0=ALU.mult, op1=ALU.add),
              [sm["u0"], sm["am1"]], [sm["am1"]])
            tt(sm["abi"], sm["rho"], sm["sinx"], ALU.mult)
            tt(sm["den"], lre, lre, ALU.mult)
            tt(sm["u0"], lim, lim, ALU.mult)
            tt(sm["den"], sm["den"], sm["u0"], ALU.add)
            V(lambda e: e.reciprocal(out=sm["den"][:], in_=sm["den"][:]), [sm["den"]], [sm["den"]])
            tt(sm["u0"], sm["am1"], lre, ALU.mult)
            tt(sm["u1"], sm["abi"], lim, ALU.mult)
            tt(sm["u0"], sm["u0"], sm["u1"], ALU.add)
            tt(sm["kr"], sm["u0"], sm["den"], ALU.mult)
            tt(sm["u0"], sm["abi"], lre, ALU.mult)
            tt(sm["u1"], sm["am1"], lim, ALU.mult)
            tt(sm["u0"], sm["u0"], sm["u1"], ALU.subtract)
            tt(sm["ki"], sm["u0"], sm["den"], ALU.mult)
            tsc(sm["t1"], sm["ki"], -1.0, ALU.mult)
            nki = sb("nki", [128, NT], F32)
            V(lambda e: e.tensor_copy(out=nki[:], in_=sm["t1"][:]), [sm["t1"]], [nki])
            for jj in range(16):
                jv = float(jj - 7)
                A(lambda e, jv=jv: e.activation(out=sm["pm"][:], in_=sm["xr"][:], func=AF.Exp, scale=jv), [sm["xr"]], [sm["pm"]])
                frac_sin(sm["ps"], sm["th"], jv, 0.0)
                frac_sin(sm["pc"], sm["th"], jv, 0.25)
                V(lambda e, jj=jj: e.tensor_tensor(out=pwr[:, jj, :], in0=sm["pm"][:], in1=sm["pc"][:], op=ALU.mult), [sm["pm"], sm["pc"]], [pwr])
                V(lambda e, jj=jj: e.tensor_tensor(out=pwi[:, jj, :], in0=sm["pm"][:], in1=sm["ps"][:], op=ALU.mult), [sm["pm"], sm["ps"]], [pwi])
            V(lambda e: e.tensor_scalar(out=npwi[:], in0=pwi[:], scalar1=-1.0, scalar2=None, op0=ALU.mult), [pwi], [npwi])
            for b_ in (pwr, pwi, npwi, sm["kr"], sm["ki"], nki, sm["rho8"], sm["th8"]):
                b_.const = True

            Dd = sb("Dd", [128, 4, 128], BF16)
            for q in range(4):
                V(lambda e, q=q: e.tensor_scalar(out=Dd[:, q, :], in0=ident[:], scalar1=dsk[:, q:q + 1], scalar2=None, op0=ALU.mult),
                  [ident, dsk], [Dd])
            Dd.const = True

            NSET = 2
            bz = [[sb(f"bz{i}_{k}", [128, 2, 128], F32, dma=True) for k in range(2)] for i in range(NSET)]
            cbt = [[sb(f"cbt{i}_{k}", [32, 2, 128], F32, dma=True) for k in range(2)] for i in range(NSET)]
            Bz = [[sb(f"Bz{i}_{k}", [128, 2, 128], F32) for k in range(2)] for i in range(NSET)]
            CT = [[sb(f"CT{i}_{k}", [128, 2, 64], F32) for k in range(2)] for i in range(NSET)]
            XT = [[sb(f"XT{i}_{k}", [128, 8, 2, 128], BF16) for k in range(2)] for i in range(NSET)]
            KT = [[sb(f"KT{i}_{k}", [128, 8, 64], BF16) for k in range(2)] for i in range(NSET)]
            LY = [[sb(f"LY{i}_{k}", [128, 8, 2, 64], BF16) for k in range(2)] for i in range(NSET)]
            cosN = [[sb(f"cosN{i}_{k}", [128, NCH], F32) for k in range(2)] for i in range(NSET)]
            sinN = [[sb(f"sinN{i}_{k}", [128, NCH], F32) for k in range(2)] for i in range(NSET)]
            rho8T = [[sb(f"rho8T{i}_{k}", [128, NCH], F32) for k in range(2)] for i in range(NSET)]
            for i in range(NSET):
                for k in range(2):
                    p.op("pool", lambda e, i=i, k=k: e.memset(CT[i][k][:], 0.0), writes=[CT[i][k]])
            xtmp = Ring([sb(f"xtmp{i}", [128, 2, 128], F32) for i in range(3)])
            lyf = Ring([sb(f"lyf{i}", [128, 2, 64], F32) for i in range(3)])
            turN = sb("turN", [128, NCH], F32); turNi = sb("turNi", [128, NCH], I32)
            uTr = Ring([sb(f"uTc{i}", [128, S], BF16, dma=True) for i in range(2)])
            uDr = Ring([sb(f"uD{i}", [128, 8, NCH], BF16) for i in range(2)])
            tmpr = Ring([sb(f"tmpS{i}", [128, NCH], F32) for i in range(8)])
            wrr = Ring([sb(f"wS{i}", [128, NCH], F32) for i in range(4)])
            Rrr = Ring([sb(f"RS{i}", [128, NCH], F32) for i in range(4)])
            Vrr = Ring([sb(f"VS{i}", [128, NCH], F32) for i in range(4)])
            Zr_ = [Ring([sb(f"ZS{k}_{i}", [128, 2, NCH], BF16) for i in range(2)]) for k in range(2)]
            zor = Ring([sb(f"zo{i}", [128, S], BF16, dma=True) for i in range(2)])
            psT = Ring(psum[4:8])
            psSt = [psum[0:2], psum[2:4]]

            def cmul(o, orow, oi_row, src, sr, si, nsi):
                V(lambda e: e.tensor_scalar(out=o[:, 0, :], in0=src[:, 0, :], scalar1=sr, scalar2=None, op0=ALU.mult), [src], [o])
                V(lambda e: e.scalar_tensor_tensor(out=o[:, 0, :], in0=src[:, 1, :], scalar=nsi, in1=o[:, 0, :], op0=ALU.mult, op1=ALU.add), [src, o], [o])
                V(lambda e: e.tensor_scalar(out=o[:, 1, :], in0=src[:, 1, :], scalar1=sr, scalar2=None, op0=ALU.mult), [src], [o])
                V(lambda e: e.scalar_tensor_tensor(out=o[:, 1, :], in0=src[:, 0, :], scalar=si, in1=o[:, 1, :], op0=ALU.mult, op1=ALU.add), [src, o], [o])

            def prep(gp, st):
                for k in range(2):
                    j = k * 16 + gp
                    p.dma(lambda e: e.dma_start(out=bz[st][k][:, 0, :], in_=ssm_d["bzr"][:, j, :]), bz[st][k], True)
                    p.dma(lambda e: e.dma_start(out=bz[st][k][:, 1, :], in_=ssm_d["bzi"][:, j, :]), bz[st][k], True)
                    p.dma(lambda e: e.dma_start(out=cbt[st][k][:, 0, :], in_=ssm_d["cbr"][:, j, :]), cbt[st][k], True)
                    p.dma(lambda e: e.dma_start(out=cbt[st][k][:, 1, :], in_=ssm_d["cbi"][:, j, :]), cbt[st][k], True)
                    cmul(Bz[st][k], None, None, bz[st][k], sm["kr"][:, j:j + 1], sm["ki"][:, j:j + 1], nki[:, j:j + 1])
                    ps = psT.next()
                    p.ops("pe", [lambda e: e.transpose(out=ps[:, 0:32], in_=cbt[st][k][:, 0, :], identity=ident[0:32, 0:32]),
                                 lambda e: e.transpose(out=ps[:, 32:64], in_=cbt[st][k][:, 1, :], identity=ident[0:32, 0:32])],
                          reads=[cbt[st][k], ident], writes=[ps])
                    A(lambda e: e.copy(out=CT[st][k][:, 0, 32:64], in_=ps[:, 0:32]), [ps], [CT[st][k]])
                    A(lambda e: e.mul(out=CT[st][k][:, 1, 32:64], in_=ps[:, 32:64], mul=-1.0), [ps], [CT[st][k]])
                    for s_ in range(8):
                        if s_ == 0:
                            xs_ = Bz[st][k]
                        else:
                            xs_ = xtmp.next()
                            jj = 7 - s_
                            cmul(xs_, None, None, Bz[st][k], pwr[:, jj, j:j + 1], pwi[:, jj, j:j + 1], npwi[:, jj, j:j + 1])
                        ps = psT.next()
                        p.ops("pe", [lambda e: e.transpose(out=ps[:, 0:128], in_=xs_[:, 0, :], identity=ident[:]),
                                     lambda e: e.transpose(out=ps[:, 128:256], in_=xs_[:, 1, :], identity=ident[:])],
                              reads=[xs_, ident], writes=[ps])
                        A(lambda e: e.copy(out=XT[st][k][:, s_, :, :], in_=ps[:, 0:256].rearrange("p (r c) -> p r c", r=2)), [ps], [XT[st][k]])
                    for tau in range(8):
                        ly = lyf.next()
                        jj = 7 + tau
                        ctr = CT[st][k]
                        V(lambda e: e.tensor_scalar(out=ly[:, 0, :], in0=ctr[:, 0, :], scalar1=pwr[:, jj, j:j + 1], scalar2=None, op0=ALU.mult), [ctr], [ly])
                        V(lambda e: e.scalar_tensor_tensor(out=ly[:, 0, :], in0=ctr[:, 1, :], scalar=pwi[:, jj, j:j + 1], in1=ly[:, 0, :], op0=ALU.mult, op1=ALU.add), [ctr, ly], [ly])
                        V(lambda e: e.tensor_scalar(out=ly[:, 1, :], in0=ctr[:, 1, :], scalar1=pwr[:, jj, j:j + 1], scalar2=None, op0=ALU.mult), [ctr], [ly])
                        V(lambda e: e.scalar_tensor_tensor(out=ly[:, 1, :], in0=ctr[:, 0, :], scalar=npwi[:, jj, j:j + 1], in1=ly[:, 1, :], op0=ALU.mult, op1=ALU.add), [ctr, ly], [ly])
                        A(lambda e: e.copy(out=LY[st][k][:, tau, :, :], in_=ly[:]), [ly], [LY[st][k]])
                        ps = psT.next()
                        p.ops("pe", [lambda e: e.matmul(ps[:, 0:64], lhsT=Bz[st][k][:, 0, :], rhs=ly[:, 0, :], start=True, stop=False),
                                     lambda e: e.matmul(ps[:, 0:64], lhsT=Bz[st][k][:, 1, :], rhs=ly[:, 1, :], start=False, stop=True)],
                              reads=[Bz[st][k], ly], writes=[ps])
                        A(lambda e: e.copy(out=KT[st][k][:, tau, :], in_=ps[:, 0:64]), [ps], [KT[st][k]])
                    for (tab, addc) in ((sinN[st][k], 0.0), (cosN[st][k], 0.25)):
                        V(lambda e: e.tensor_scalar(out=turN[:], in0=tI[:], scalar1=sm["th8"][:, j:j + 1], scalar2=addc, op0=ALU.mult, op1=ALU.add), [tI], [turN])
                        V(lambda e: e.tensor_copy(out=turNi[:], in_=turN[:]), [turN], [turNi])
                        V(lambda e: e.tensor_tensor(out=turN[:], in0=turN[:], in1=turNi[:], op=ALU.subtract), [turN, turNi], [turN])
                        A(lambda e: e.activation(out=tab[:], in_=turN[:], func=AF.Sin, scale=TWO_PI), [turN], [tab])
                    V(lambda e: e.tensor_scalar(out=rho8T[st][k][:], in0=tI[:], scalar1=0.0, scalar2=sm["rho8"][:, j:j + 1], op0=ALU.mult, op1=ALU.add), [tI], [rho8T[st][k]])

            Ssb = Ring([sb(f"Ssb{i}", [128, 4, NCH], F32) for i in range(2)])

            def geom(gp):
                q = gp // 4
                qq = gp % 4
                if qq < 3:
                    return q, slice(32 * qq, 32 * qq + 32), slice(32, 64), slice(32 * qq, 32 * qq + 32), slice(32 * qq, 32 * qq + 32)
                return q, slice(64, 128), slice(0, 64), slice(64, 128), slice(32 * qq, 32 * qq + 32)

            def stageA(gp, st, sq):
                q = gp // 4
                uT = uTr.next()
                p.dma(lambda e: e.dma_start(out=uT[:], in_=uT_s[sq, q * 128:(q + 1) * 128, :]), uT, True)
                uD = uDr.next()
                A(lambda e: e.copy(out=uD[:], in_=uT[:].rearrange("p (n s) -> p s n", s=8)), [uT], [uD])
                ss = Ssb.next()
                for k in range(2):
                    for ri in range(2):
                        pb_ = psSt[k][ri]
                        if k == 0:
                            fns = [lambda e, s_=s_: e.matmul(pb_[:], lhsT=XT[st][k][:, s_, ri, :], rhs=uD[:, s_, :], start=(s_ == 0), stop=(s_ == 7)) for s_ in range(8)]
                        else:
                            fns = [lambda e, s_=s_: e.matmul(pb_[:], lhsT=XT[st][k][:, s_, ri, :], rhs=uD[:, 7 - s_, ::-1], start=(s_ == 0), stop=(s_ == 7)) for s_ in range(8)]
                        p.ops("pe", fns, reads=[XT[st][k], uD], writes=[pb_])
                        A(lambda e: e.copy(out=ss[:, 2 * k + ri, :], in_=pb_[:]), [pb_], [ss])
                return (gp, st, sq, uD, ss)

            def stageB(ctx):
                gp, st, sq, uD, ss = ctx
                T = [[tmpr.next() for _ in range(4)] for k in range(2)]
                for (ti, si, tab) in ((0, 0, cosN), (1, 1, sinN), (2, 1, cosN), (3, 0, sinN)):
                    for k in range(2):
                        V(lambda e, k=k: e.tensor_tensor(out=T[k][ti][:], in0=ss[:, 2 * k + si, :], in1=tab[st][k][:], op=ALU.mult), [ss, tab[st][k]], [T[k][ti]])
                W = [[wrr.next(), wrr.next()] for k in range(2)]
                for k in range(2):
                    V(lambda e, k=k: e.tensor_tensor(out=W[k][0][:], in0=T[k][0][:], in1=T[k][1][:], op=ALU.add), [T[k][0], T[k][1]], [W[k][0]])
                for k in range(2):
                    V(lambda e, k=k: e.tensor_tensor(out=W[k][1][:], in0=T[k][2][:], in1=T[k][3][:], op=ALU.subtract), [T[k][2], T[k][3]], [W[k][1]])
                R = [[Rrr.next(), Rrr.next()] for k in range(2)]
                for ri in range(2):
                    for k in range(2):
                        V(lambda e, k=k, ri=ri: e.tensor_tensor_scan(out=R[k][ri][:], data0=rho8T[st][k][:], data1=W[k][ri][:], initial=0.0, op0=ALU.mult, op1=ALU.add),
                          [rho8T[st][k], W[k][ri]], [R[k][ri]])
                T = [[tmpr.next() for _ in range(4)] for k in range(2)]
                for (ti, si, tab) in ((0, 0, cosN), (1, 1, sinN), (2, 1, cosN), (3, 0, sinN)):
                    for k in range(2):
                        V(lambda e, k=k: e.tensor_tensor(out=T[k][ti][:], in0=R[k][si][:], in1=tab[st][k][:], op=ALU.mult), [R[k][si], tab[st][k]], [T[k][ti]])
                Vv = [[Vrr.next(), Vrr.next()] for k in range(2)]
                for k in range(2):
                    V(lambda e, k=k: e.tensor_tensor(out=Vv[k][0][:], in0=T[k][0][:], in1=T[k][1][:], op=ALU.subtract), [T[k][0], T[k][1]], [Vv[k][0]])
                for k in range(2):
                    V(lambda e, k=k: e.tensor_tensor(out=Vv[k][1][:], in0=T[k][2][:], in1=T[k][3][:], op=ALU.add), [T[k][2], T[k][3]], [Vv[k][1]])
                Z = [Zr_[k].next() for k in range(2)]
                for ri in range(2):
                    for k in range(2):
                        V(lambda e, k=k, ri=ri: e.tensor_tensor(out=Z[k][:, ri, :], in0=Vv[k][ri][:], in1=ss[:, 2 * k + ri, :], op=ALU.subtract), [Vv[k][ri], ss], [Z[k]])
                return ctx + (Z,)

            def stageC(ctx):
                gp, st, sq, uD, ss, Z = ctx
                q, rows, lcs, dds, orow = geom(gp)
                zo = zor.next()
                for tau in range(8):
                    py = psT.next()
                    fns = [
                        lambda e: e.matmul(py[rows, :], lhsT=LY[st][0][:, tau, 0, lcs], rhs=Z[0][:, 0, :], start=True, stop=False),
                        lambda e: e.matmul(py[rows, :], lhsT=LY[st][0][:, tau, 1, lcs], rhs=Z[0][:, 1, :], start=False, stop=False),
                        lambda e: e.matmul(py[rows, :], lhsT=LY[st][1][:, 7 - tau, 0, lcs], rhs=Z[1][:, 0, ::-1], start=False, stop=False),
                        lambda e: e.matmul(py[rows, :], lhsT=LY[st][1][:, 7 - tau, 1, lcs], rhs=Z[1][:, 1, ::-1], start=False, stop=False),
                    ]
                    for s_ in range(0, tau + 1):
                        fns.append(lambda e, s_=s_: e.matmul(py[rows, :], lhsT=KT[st][0][:, tau - s_, lcs], rhs=uD[:, s_, :], start=False, stop=False))
                    for s_ in range(tau, 8):
                        fns.append(lambda e, s_=s_: e.matmul(py[rows, :], lhsT=KT[st][1][:, s_ - tau, lcs], rhs=uD[:, s_, :], start=False, stop=False))
                    fns.append(lambda e: e.matmul(py[rows, :], lhsT=Dd[:, q, dds], rhs=uD[:, tau, :], start=False, stop=True))
                    p.ops("pe", fns, reads=[LY[st][0], LY[st][1], KT[st][0], KT[st][1], Dd, Z[0], Z[1], uD], writes=[py])
                    A(lambda e: e.activation(out=zo[rows, tau:S:8], in_=py[rows, :], func=AF.Gelu), [py], [zo])
                p.dma(lambda e: e.dma_start(out=zT_s[sq, gp * 32:(gp + 1) * 32, :], in_=zo[orow, :]), zo, False)

            runs = [(gp, gp % NSET, sq) for gp in range(16) for sq in range(nseq)]
            prep(0, 0)
            ctxA = stageA(*runs[0])
            for i, (gp, st, sq) in enumerate(runs):
                if sq == 0 and gp + 1 < 16:
                    prep(gp + 1, (gp + 1) % NSET)
                nxt = stageA(*runs[i + 1]) if i + 1 < len(runs) else None
                ctxB = stageB(ctxA)
                stageC(ctxB)
                ctxA = nxt
                if sq == nseq - 1:
                    stop(f'S_gp{gp}')
        new_phase()
        stop('S')

        with ExitStack() as es:
            def sb(name, shape, dt, dma=False, const=False):
                return p.buf(es.enter_context(nc.sbuf_tensor(name, list(shape), dt)), dma=dma, const=const)

            mstage = sb("mstage", [128, 256], F32, dma=True)
            ostage = sb("ostage", [128, 3, 64], F32, dma=True)
            maskB = sb("maskB", [128, 256], BF16); ones3 = sb("ones3", [128, 3, 64], BF16); identb = sb("identb", [128, 128], BF16)
            p.dma(lambda e: e.dma_start(out=mstage[:], in_=md["maskb"]), mstage, True)
            p.dma(lambda e: e.dma_start(out=ostage[:], in_=md["ones3"]), ostage, True)
            p.op("dve", lambda e: e.tensor_copy(out=maskB[:], in_=mstage[:]), reads=[mstage], writes=[maskB])
            p.op("dve", lambda e: e.tensor_copy(out=ones3[:], in_=ostage[:]), reads=[ostage], writes=[ones3])
            p.op("dve", lambda e: e.tensor_copy(out=identb[:], in_=ident[:]), reads=[ident], writes=[identb])
            maskB.const = True; ones3.const = True; identb.const = True
            qTr = Ring([sb(f"qTa{i}", [128, S], BF16, dma=True) for i in range(2)])
            kSr = Ring([sb(f"kSa{i}", [128, S], BF16, dma=True) for i in range(2)])
            qDr = Ring([sb(f"qDa{i}", [128, S], BF16) for i in range(2)])
            kTr = Ring([sb(f"kTa{i}", [128, S + 2 * KPAD], BF16) for i in range(2)])
            vTr = Ring([sb(f"vTa{i}", [128, 48, 128], BF16, dma=True) for i in range(2)])
            acc = sb("acc", [128, 2, S], F32)
            rden = sb("rden", [128, S], F32)
            aTo = sb("aTo", [128, S], BF16, dma=True)
            PTr = Ring([sb(f"PT{i}", [128, 256], BF16) for i in range(4)])
            psS = Ring(psum[0:4]); psO = Ring(psum[4:8])
            SCALE = 64.0 ** -0.5
            for sq in range(nseq):
                for c in range(2):
                    for g in range(3):
                        d = DIL[g]; L = S // d; nb = L // 128 + 1
                        qN = qTr.next(); kS = kSr.next(); qT = qDr.next(); kT = kTr.next(); vT = vTr.next()
                        ch = 2 * g + c
                        LP = L + 128
                        p.dma(lambda e: e.dma_start(out=qN[:], in_=qT_s[sq, ch * 128:(ch + 1) * 128, :]), qN, True)
                        p.dma(lambda e: e.dma_start(out=kS[:], in_=kT_s[sq, ch * 128:(ch + 1) * 128, :]), kS, True)
                        p.op("pool", lambda e: e.memset(kT[:, 0:d * LP], 0.0), writes=[kT])
                        p.op("act", lambda e: e.copy(out=qT[:].rearrange("p (r i) -> p r i", r=d), in_=qN[:].rearrange("p (i r) -> p r i", r=d)), reads=[qN], writes=[qT])
                        p.op("dve", lambda e: e.tensor_copy(out=kT[:, 0:d * LP].rearrange("p (r i) -> p r i", r=d)[:, :, 64:64 + L], in_=kS[:].rearrange("p (i r) -> p r i", r=d)),
                             reads=[kS], writes=[kT])
                        p.dma(lambda e: e.dma_start(out=vT[:, 0:NBLK[g], :], in_=v_s[g][sq, :, :, c * 128:(c + 1) * 128]), vT, True)
                        for r in range(d):
                            for a in range(L // 128):
                                qsl = slice(r + d * 128 * a, r + d * 128 * a + d * 127 + 1, d)
                                qcs = slice(r * L + 128 * a, r * L + 128 * a + 128)
                                pO = psO.next()
                                fns = []
                                pts = []
                                for hp in range(2):
                                    pb = 64 * hp
                                    pS = psS.next()
                                    ks = []
                                    for m in (a, a + 1):
                                        st = r * LP + 128 * m
                                        ks.append(slice(st, st + 128))
                                    p.ops("pe", [
                                        lambda e, pS=pS, pb=pb, ks=ks: e.matmul(pS[:, 0:128], lhsT=kT[pb:pb + 64, ks[0]], rhs=qT[pb:pb + 64, qcs], start=True, stop=False),
                                        lambda e, pS=pS, pb=pb, ks=ks: e.matmul(pS[:, 128:256], lhsT=kT[pb:pb + 64, ks[1]], rhs=qT[pb:pb + 64, qcs], start=False, stop=False),
                                        lambda e, pS=pS: e.matmul(pS[:, 0:256], lhsT=identb[:], rhs=maskB[:], start=False, stop=True),
                                    ], reads=[kT, qT, identb, maskB], writes=[pS])
                                    PT = PTr.next()
                                    p.op("act", lambda e, pS=pS, PT=PT: e.activation(out=PT[:], in_=pS[:, 0:256], func=AF.Exp, scale=SCALE), reads=[pS], writes=[PT])
                                    pts.append(PT)
                                    o1 = 1 if a == 0 else 0
                                    o2 = 2 if a + 1 == nb - 1 else 0
                                    b1 = r * nb + a; b2 = r * nb + a + 1
                                    fns += [
                                        lambda e, PT=PT, pb=pb, b1=b1, hp=hp: e.matmul(pO[pb:pb + 64, 0:128], lhsT=vT[:, b1, hp * 64:(hp + 1) * 64], rhs=PT[:, 0:128], start=True, stop=False),
                                        lambda e, PT=PT, pb=pb, b2=b2, hp=hp: e.matmul(pO[pb:pb + 64, 0:128], lhsT=vT[:, b2, hp * 64:(hp + 1) * 64], rhs=PT[:, 128:256], start=False, stop=False),
                                        lambda e, PT=PT, pb=pb, o1=o1: e.matmul(pO[pb:pb + 64, 128:256], lhsT=ones3[:, o1, :], rhs=PT[:, 0:128], start=False, stop=False),
                                        lambda e, PT=PT, pb=pb, o2=o2: e.matmul(pO[pb:pb + 64, 128:256], lhsT=ones3[:, o2, :], rhs=PT[:, 128:256], start=False, stop=True),
                                    ]
                                p.ops("pe", fns, reads=pts + [vT, ones3], writes=[pO])
                                pov = pO[:, 0:256].rearrange("p (n i) -> p n i", n=2)
                                if g == 0:
                                    p.op("dve", lambda e, pov=pov: e.tensor_copy(out=acc[:, :, qsl], in_=pov), reads=[pO], writes=[acc])
                                else:
                                    p.op("dve", lambda e, pov=pov: e.tensor_tensor(out=acc[:, :, qsl], in0=pov, in1=acc[:, :, qsl], op=ALU.add), reads=[pO, acc], writes=[acc])
                    p.op("dve", lambda e: e.reciprocal(out=rden[:], in_=acc[:, 1, :]), reads=[acc], writes=[rden])
                    p.op("dve", lambda e: e.tensor_tensor(out=aTo[:], in0=acc[:, 0, :], in1=rden[:], op=ALU.mult), reads=[acc, rden], writes=[aTo])
                    p.dma(lambda e: e.dma_start(out=aT_s[sq, c * 128:(c + 1) * 128, :], in_=aTo[:]), aTo, False)
        new_phase()
        stop('T')

        def load_w_bf16(sbf, wdst, src, K, N, tag, stg=None):
            piece = 1024 if N >= 1024 else N
            if stg is None:
                stg = Ring([sbf(f"wl_{tag}{i}", [128, piece], F32, dma=True) for i in range(2)])
            engs = ("dve", "pool", "act")
            n = 0
            for k in range(K):
                for c0 in range(0, N, piece):
                    w = min(piece, N - c0)
                    st = stg.next()
                    p.dma(lambda e, st=st, k=k, c0=c0, w=w: e.dma_start(out=st[:, 0:w], in_=src[k * 128:(k + 1) * 128, c0:c0 + w]), st, True)
                    eng = engs[n % 3]; n += 1
                    if eng == "act":
                        p.op("act", lambda e, st=st, k=k, c0=c0, w=w: e.copy(out=wdst[:, k, c0:c0 + w], in_=st[:, 0:w]), reads=[st], writes=[wdst])
                    else:
                        p.op(eng, lambda e, st=st, k=k, c0=c0, w=w: e.tensor_copy(out=wdst[:, k, c0:c0 + w], in_=st[:, 0:w]), reads=[st], writes=[wdst])
            wdst.const = True

        def layer_norm(sbufs, hpre, gB, bB, outt):
            stats, mv, rstd, hn = sbufs
            for n in range(2):
                p.op("dve", lambda e, n=n: e.bn_stats(out=stats[:, n, :], in_=hpre[:, n * 512:(n + 1) * 512]), reads=[hpre], writes=[stats])
            p.op("dve", lambda e: e.bn_aggr(out=mv[:], in_=stats[:].rearrange("p n s -> p (n s)")), reads=[stats], writes=[mv])
            p.op("act", lambda e: e.activation(out=rstd[:], in_=mv[:, 1:2], func=AF.Sqrt, bias=epsb[:, 0:1]), reads=[mv, epsb], writes=[rstd])
            p.op("dve", lambda e: e.reciprocal(out=rstd[:], in_=rstd[:]), reads=[rstd], writes=[rstd])
            p.op("dve", lambda e: e.tensor_scalar(out=hn[:], in0=hpre[:], scalar1=mv[:, 0:1], scalar2=rstd[:, 0:1], op0=ALU.subtract, op1=ALU.mult),
                 reads=[hpre, mv, rstd], writes=[hn])
            p.op("dve", lambda e: e.tensor_tensor(out=hn[:], in0=hn[:], in1=gB[:], op=ALU.mult), reads=[hn, gB], writes=[hn])
            p.op("dve", lambda e: e.tensor_tensor(out=outt[:], in0=hn[:], in1=bB[:], op=ALU.add), reads=[hn, bB], writes=[outt])

        with ExitStack() as es:
            def sb(name, shape, dt, dma=False, const=False):
                return p.buf(es.enter_context(nc.sbuf_tensor(name, list(shape), dt)), dma=dma, const=const)
            wgv = sb("wgv", [128, 4, D], BF16, dma=True); wgg = sb("wgg", [128, 4, D], BF16, dma=True); wab = sb("wab", [128, 2, D], BF16, dma=True); wo = sb("wo", [128, 8, D], BF16, dma=True)
            load_wbf(wgv, "wgv", 4); load_wbf(wgg, "wgg", 4); load_wbf(wab, "wab", 2); load_wbf(wo, "wo", 8)
            gB = sb("ln1gB", [128, D], F32, dma=True, const=True); bB = sb("ln1bB", [128, D], F32, dma=True, const=True)
            p.dma(lambda e: e.dma_start(out=gB[:], in_=md["ln1g"].partition_broadcast(128)), gB, True)
            p.dma(lambda e: e.dma_start(out=bB[:], in_=md["ln1b"].partition_broadcast(128)), bB, True)
            epsb = sb("epsb", [128, 1], F32)
            p.op("pool", lambda e: e.memset(epsb[:], LN_EPS), writes=[epsb])
            zTr_ = Ring([sb(f"zTm{i}", [128, 4, 512], BF16, dma=True) for i in range(2)]); aTr_ = Ring([sb(f"aTm{i}", [128, 2, 512], BF16, dma=True) for i in range(2)])
            gTr_ = Ring([sb(f"gTm{i}", [128, 16, 512], BF16, dma=True) for i in range(2)])
            xs = Ring([sb(f"xm{i}", [128, D], F32, dma=True) for i in range(2)])
            mixT = sb("mixT", [128, 8, 512], BF16)
            sg = Ring([sb(f"sg{i}", [128, 512], F32) for i in range(2)])
            t1r = Ring([sb(f"t1m{i}", [128, 512], F32) for i in range(2)])
            t2r = Ring([sb(f"t2m{i}", [128, 512], F32) for i in range(2)])
            hpre = Ring([sb(f"hpre{i}", [128, D], F32) for i in range(2)])
            hout = Ring([sb(f"hout{i}", [128, D], F32, dma=True) for i in range(2)])
            hTt = Ring([sb(f"hTt{i}", [128, 8, 128], BF16, dma=True) for i in range(2)])
            lnb = (sb("st1", [128, 2, 6], F32), sb("mv1", [128, 2], F32), sb("rstd1", [128, 1], F32), sb("hn1", [128, D], F32))
            psr = Ring(psum)
            def load_m1(sq_, tb_):
                ts_ = slice(tb_ * 512, (tb_ + 1) * 512)
                zT_ = zTr_.next(); aT_ = aTr_.next(); gT_ = gTr_.next()
                p.dma(lambda e: e.dma_start(out=zT_[:], in_=zT_s[sq_].rearrange("(k q) t -> q k t", q=128)[:, :, ts_]), zT_, True)
                p.dma(lambda e: e.dma_start(out=aT_[:], in_=aT_s[sq_].rearrange("(k q) t -> q k t", q=128)[:, :, ts_]), aT_, True)
                p.dma(lambda e: e.dma_start(out=gT_[:], in_=gT_s[sq_].rearrange("(k q) t -> q k t", q=128)[:, :, ts_]), gT_, True)
                return zT_, aT_, gT_
            blocks_m1 = [(sq_, tb_) for sq_ in range(nseq) for tb_ in range(S // 512)]
            nxt_in = load_m1(*blocks_m1[0])
            for bi, (sq, tb) in enumerate(blocks_m1):
                if True:
                    ts = slice(tb * 512, (tb + 1) * 512)
                    zT, aT, gT = nxt_in
                    if bi + 1 < len(blocks_m1):
                        nxt_in = load_m1(*blocks_m1[bi + 1])
                    for do in range(8):
                        ds_ = slice(do * 128, (do + 1) * 128)
                        pA = psr.next(); pG = psr.next(); pB = psr.next()
                        p.ops("pe", [lambda e, k=k: e.matmul(pA[:], lhsT=wgv[:, k, ds_], rhs=zT[:, k, :], start=(k == 0), stop=(k == 3)) for k in range(4)], reads=[wgv, zT], writes=[pA])
                        p.ops("pe", [lambda e, k=k: e.matmul(pG[:], lhsT=wgg[:, k, ds_], rhs=zT[:, k, :], start=(k == 0), stop=(k == 3)) for k in range(4)], reads=[wgg, zT], writes=[pG])
                        p.ops("pe", [lambda e, k=k: e.matmul(pB[:], lhsT=wab[:, k, ds_], rhs=aT[:, k, :], start=(k == 0), stop=(k == 1)) for k in range(2)], reads=[wab, aT], writes=[pB])
                        sgt = sg.next(); t1 = t1r.next(); t2 = t2r.next()
                        p.op("act", lambda e: e.activation(out=sgt[:], in_=pG[:], func=AF.Sigmoid), reads=[pG], writes=[sgt])
                        p.op("dve", lambda e: e.tensor_tensor(out=t1[:], in0=pA[:], in1=sgt[:], op=ALU.mult), reads=[pA, sgt], writes=[t1])
                        p.op("dve", lambda e: e.tensor_tensor(out=t2[:], in0=pB[:], in1=gT[:, 8 + do, :], op=ALU.mult), reads=[pB, gT], writes=[t2])
                        p.op("dve", lambda e: e.tensor_tensor(out=t1[:], in0=t1[:], in1=gT[:, do, :], op=ALU.mult), reads=[t1, gT], writes=[t1])
                        p.op("dve", lambda e: e.tensor_tensor(out=mixT[:, do, :], in0=t1[:], in1=t2[:], op=ALU.add), reads=[t1, t2], writes=[mixT])
                    for i in range(4):
                        tok = slice(tb * 512 + i * 128, tb * 512 + (i + 1) * 128)
                        xt = xs.next()
                        p.dma(lambda e: e.dma_start(out=xt[:], in_=x_d[sq, tok, :]), xt, True)
                        hp_ = hpre.next()
                        for n in range(2):
                            ns = slice(n * 512, (n + 1) * 512)
                            po = psr.next()
                            p.ops("pe", [lambda e, k=k: e.matmul(po[:], lhsT=mixT[:, k, i * 128:(i + 1) * 128], rhs=wo[:, k, ns], start=(k == 0), stop=(k == 7)) for k in range(8)],
                                  reads=[mixT, wo], writes=[po])
                            p.op("dve", lambda e: e.scalar_tensor_tensor(out=hp_[:, ns], in0=xt[:, ns], scalar=ALPHA, in1=po[:], op0=ALU.mult, op1=ALU.add),
                                 reads=[xt, po], writes=[hp_])
                        ho = hout.next()
                        layer_norm(lnb, hp_, gB, bB, ho)
                        p.dma(lambda e: e.dma_start(out=h_s[sq, tok, :], in_=ho[:]), ho, False)
                        hT = hTt.next()
                        for kk in range(2):
                            pt = psr.next()
                            p.ops("pe", [lambda e, k4=k4: e.transpose(out=pt[:, k4 * 128:(k4 + 1) * 128], in_=ho[:, (kk * 4 + k4) * 128:(kk * 4 + k4 + 1) * 128], identity=ident[:])
                                         for k4 in range(4)], reads=[ho, ident], writes=[pt])
                            p.op("act", lambda e: e.copy(out=hT[:, kk * 4:(kk + 1) * 4, :], in_=pt[:].rearrange("p (k t) -> p k t", k=4)), reads=[pt], writes=[hT])
                        p.dma(lambda e: e.dma_start(out=hT_s[sq].rearrange("(k q) t -> q k t", q=128)[:, :, tok], in_=hT[:]), hT, False)
        new_phase()
        stop('M1')

        with ExitStack() as es:
            def sb(name, shape, dt, dma=False, const=False):
                return p.buf(es.enter_context(nc.sbuf_tensor(name, list(shape), dt)), dma=dma, const=const)
            wup = sb("wup", [128, 8, 2 * DFF], BF16, dma=True); wdn = sb("wdn", [128, 22, D], BF16, dma=True)
            load_wbf(wup, "wup", 8); load_wbf(wdn, "wdn", 22)
            gB = sb("ln2gB", [128, D], F32, dma=True, const=True); bB = sb("ln2bB", [128, D], F32, dma=True, const=True)
            p.dma(lambda e: e.dma_start(out=gB[:], in_=md["ln2g"].partition_broadcast(128)), gB, True)
            p.dma(lambda e: e.dma_start(out=bB[:], in_=md["ln2b"].partition_broadcast(128)), bB, True)
            cw = sb("cw", [128, 44, 3], F32, dma=True, const=True); cbias = sb("cbias", [128, 44], F32, dma=True, const=True)
            p.dma(lambda e: e.dma_start(out=cw[:], in_=md["cw"]), cw, True)
            p.dma(lambda e: e.dma_start(out=cbias[:], in_=md["cbias"]), cbias, True)
            epsb = sb("epsb2", [128, 1], F32)
            p.op("pool", lambda e: e.memset(epsb[:], LN_EPS), writes=[epsb])
            hT = sb("hTf", [128, 8, 514], BF16, dma=True)
            hres = Ring([sb(f"hres{i}", [128, D], F32, dma=True) for i in range(1)])
            cvr = Ring([sb(f"cv{i}", [128, 512], F32) for i in range(8)])
            actT = sb("actT", [128, 22, 512], BF16)
            opre = sb("opre", [128, D], F32)
            oout = Ring([sb(f"oout{i}", [128, D], F32, dma=True) for i in range(1)])
            lnb = (sb("st2", [128, 2, 6], F32), sb("mv2", [128, 2], F32), sb("rstd2", [128, 1], F32), sb("hn2", [128, D], F32))
            psm = Ring(psum[0:6]); psh = Ring(psum[6:8])
            for sq in range(nseq):
                for tb in range(S // 512):
                    t0 = tb * 512
                    lo = max(t0 - 1, 0); hi = min(t0 + 513, S)
                    if t0 == 0:
                        p.op("pool", lambda e: e.memset(hT[:, :, 0:1], 0.0), writes=[hT])
                    if t0 + 512 == S:
                        p.op("pool", lambda e: e.memset(hT[:, :, 513:514], 0.0), writes=[hT])
                    p.dma(lambda e: e.dma_start(out=hT[:, :, lo - (t0 - 1):hi - (t0 - 1)], in_=hT_s[sq].rearrange("(k q) t -> q k t", q=128)[:, :, lo:hi]), hT, True)
                    for c in range(22):
                        cvs = []
                        for ch in (c, 22 + c):
                            cs_ = slice(ch * 128, (ch + 1) * 128)
                            pm = psm.next(); ph = psh.next()
                            p.ops("pe", [lambda e, k=k: e.matmul(pm[:], lhsT=wup[:, k, cs_], rhs=hT[:, k, 1:513], start=(k == 0), stop=(k == 7)) for k in range(8)]
                                  + [lambda e, k=k: e.matmul(ph[:, 0:2], lhsT=wup[:, k, cs_], rhs=hT[:, k, 0:514:513], start=(k == 0), stop=(k == 7)) for k in range(8)],
                                  reads=[wup, hT], writes=[pm, ph])
                            cv = cvr.next()
                            p.op("act", lambda e: e.activation(out=cv[:], in_=pm[:], func=AF.Identity, scale=cw[:, ch, 1:2], bias=cbias[:, ch:ch + 1]),
                                 reads=[pm, cw, cbias], writes=[cv])
                            p.op("dve", lambda e: e.scalar_tensor_tensor(out=cv[:, 1:512], in0=pm[:, 0:511], scalar=cw[:, ch, 0:1], in1=cv[:, 1:512], op0=ALU.mult, op1=ALU.add), reads=[pm, cw, cv], writes=[cv])
                            p.op("dve", lambda e: e.scalar_tensor_tensor(out=cv[:, 0:511], in0=pm[:, 1:512], scalar=cw[:, ch, 2:3], in1=cv[:, 0:511], op0=ALU.mult, op1=ALU.add), reads=[pm, cw, cv], writes=[cv])
                            p.op("dve", lambda e: e.scalar_tensor_tensor(out=cv[:, 0:1], in0=ph[:, 0:1], scalar=cw[:, ch, 0:1], in1=cv[:, 0:1], op0=ALU.mult, op1=ALU.add), reads=[ph, cw, cv], writes=[cv])
                            p.op("dve", lambda e: e.scalar_tensor_tensor(out=cv[:, 511:512], in0=ph[:, 1:2], scalar=cw[:, ch, 2:3], in1=cv[:, 511:512], op0=ALU.mult, op1=ALU.add), reads=[ph, cw, cv], writes=[cv])
                            cvs.append(cv)
                        p.op("act", lambda e: e.activation(out=cvs[0][:], in_=cvs[0][:], func=AF.Gelu), reads=[cvs[0]], writes=[cvs[0]])
                        p.op("pool", lambda e: e.tensor_tensor(out=actT[:, c, :], in0=cvs[0][:], in1=cvs[1][:], op=ALU.mult), reads=cvs, writes=[actT])
                    for i in range(4):
                        tok = slice(t0 + i * 128, t0 + (i + 1) * 128)
                        hr = hres.next()
                        p.dma(lambda e: e.dma_start(out=hr[:], in_=h_s[sq, tok, :]), hr, True)
                        for n in range(2):
                            ns = slice(n * 512, (n + 1) * 512)
                            po = psm.next()
                            p.ops("pe", [lambda e, k=k: e.matmul(po[:], lhsT=actT[:, k, i * 128:(i + 1) * 128], rhs=wdn[:, k, ns], start=(k == 0), stop=(k == 21)) for k in range(22)],
                                  reads=[actT, wdn], writes=[po])
                            p.op("dve", lambda e: e.scalar_tensor_tensor(out=opre[:, ns], in0=hr[:, ns], scalar=ALPHA, in1=po[:], op0=ALU.mult, op1=ALU.add),
                                 reads=[hr, po], writes=[opre])
                        oo = oout.next()
                        layer_norm(lnb, opre, gB, bB, oo)
                        p.dma(lambda e: e.dma_start(out=out_d[sq, tok, :], in_=oo[:]), oo, False)
        new_phase()


def _host_inputs(inputs, core, nseq=NSEQ):
    f32 = np.float32
    x = np.ascontiguousarray(inputs["x"][core * nseq:(core + 1) * nseq]).astype(f32)
    pos = np.ascontiguousarray(inputs["positions"][core * nseq:(core + 1) * nseq]).astype(np.int32)
    w_in = np.asarray(inputs["w_in"][0], f32)
    b_in = np.asarray(inputs["b_in"][0], f32)
    sw = np.arange(AW).reshape(-1, 2, 32)[:, ::-1, :].reshape(-1)
    q0, k0, v0, g0 = SSMW, SSMW + AW, SSMW + 2 * AW, SSMW + 3 * AW
    cols = np.concatenate([np.arange(0, SSMW), np.arange(q0, q0 + AW), np.arange(k0, k0 + AW),
                           q0 + sw, k0 + sw, np.arange(g0, g0 + 2 * D)])
    w_fm = np.ascontiguousarray(w_in[:, cols])
    b_fm = np.ascontiguousarray(b_in[cols].reshape(NFM // 128, 128).T)
    w_v = np.ascontiguousarray(w_in[:, v0:v0 + AW])
    b_v = np.ascontiguousarray(b_in[v0:v0 + AW].reshape(1, AW))
    half = 32
    inv_freq = (10000.0 ** (-np.arange(half, dtype=np.float64) * 2.0 / 64)).astype(f32)
    invf = np.zeros((128, 2), f32)
    for pp in range(128):
        invf[pp, 0] = inv_freq[pp % 32] / TWO_PI
        invf[pp, 1] = -TWO_PI if (pp % 64) < 32 else TWO_PI
    def tile_layout(a):
        return np.ascontiguousarray(a.reshape(2, 16, 2, 64).transpose(2, 3, 0, 1).reshape(128, 32)).astype(f32)
    lre_h = tile_layout(np.asarray(inputs["ssm_lam_re"][0], f32))
    lim_h = tile_layout(np.asarray(inputs["ssm_lam_im"][0], f32))
    ldt_h = tile_layout(np.broadcast_to(np.asarray(inputs["ssm_log_dt"][0], f32)[:, :, None], (2, 32, 64)).copy())

    def bz(b):
        o = np.zeros((128, 32, 128), f32)
        b = np.asarray(b, f32)
        for dr in range(2):
            for gp in range(16):
                for gl in range(2):
                    c0 = (gp % 4) * 32 + gl * 16
                    o[gl * 64:(gl + 1) * 64, dr * 16 + gp, c0:c0 + 16] = b[dr, 2 * gp + gl]
        return o

    def cb(c):
        o = np.zeros((32, 32, 128), f32)
        c = np.asarray(c, f32)
        for dr in range(2):
            for gp in range(16):
                for gl in range(2):
                    o[gl * 16:(gl + 1) * 16, dr * 16 + gp, gl * 64:(gl + 1) * 64] = c[dr, 2 * gp + gl]
        return o
    ssm = {"lre_h": lre_h, "lim_h": lim_h, "ldt_h": ldt_h,
           "bzr_h": bz(inputs["ssm_b_re"][0]), "bzi_h": bz(inputs["ssm_b_im"][0]),
           "cbr_h": cb(inputs["ssm_c_re"][0]), "cbi_h": cb(inputs["ssm_c_im"][0]),
           "dsk_h": np.ascontiguousarray(np.asarray(inputs["ssm_d"][0], f32).reshape(4, 128).T),
           "iota_h": np.arange(S, dtype=f32).reshape(1, S)}
    d = {"x": x, "pos": pos, "ident": np.eye(128, dtype=f32), "invf": invf,
         "w_in_fm": w_fm, "b_fm": b_fm, "w_v": w_v, "b_v": b_v}
    d.update(ssm)
    ii = np.arange(128)[:, None]; jj = np.arange(128)[None, :]
    maskb = np.concatenate([np.where(ii >= jj, 0.0, -30000.0), np.where(ii <= jj, 0.0, -30000.0)], axis=1).astype(f32)
    ones3 = np.zeros((128, 3, 64), f32)
    ones3[:, 0, :] = 1.0; ones3[64:, 1, :] = 1.0; ones3[:64, 2, :] = 1.0
    g = lambda n: np.ascontiguousarray(np.asarray(inputs[n][0], f32))
    cwh = np.ascontiguousarray(g("conv_w").reshape(3, 44, 128).transpose(2, 1, 0))
    cbh = np.ascontiguousarray(g("conv_b").reshape(44, 128).T)
    d.update({"maskb_h": maskb, "ones3_h": ones3, "wgv_h": g("w_glu_v"), "wgg_h": g("w_glu_g"), "wab_h": g("w_attn_br"), "wo_h": g("w_out"),
              "ln1g_h": g("ln1_g").reshape(1, D), "ln1b_h": g("ln1_b").reshape(1, D), "ln2g_h": g("ln2_g").reshape(1, D), "ln2b_h": g("ln2_b").reshape(1, D),
              "wup_h": g("w_up"), "wdn_h": g("w_down"), "cw_h": cwh, "cb_h": cbh})
    return d


def kernel(**inputs):
    nc = build()
    in_maps = [_host_inputs(inputs, c) for c in range(NCORES)]
    res = run_bass_kernel_spmd(nc, in_maps, core_ids=list(range(NCORES)))
    out = np.concatenate([r["out"] for r in res.results], axis=0)
    return out.astype(np.float32)
```

```python
import math
from contextlib import ExitStack

import numpy as np
import concourse.bass as bass
import concourse.mybir as mybir
from concourse.bass_utils import run_bass_kernel_spmd

F32 = mybir.dt.float32
BF16 = mybir.dt.bfloat16
I32 = mybir.dt.int32
AF = mybir.ActivationFunctionType
ALU = mybir.AluOpType
AX = mybir.AxisListType

S = 4096
D = 1024
NCORES = 8
NSEQ = 2
SSMW = 512
AW = 768
DFF = 2816
NFM = 5632
ALPHA = 2.0 ** 0.25
LN_EPS = 1e-5
TWO_PI = 2.0 * math.pi
DIL = (1, 4, 16)
KPAD = 1024


class Buf:
    __slots__ = ("t", "w", "r", "dsem", "const")

    def __init__(self, t, dsem=None, const=False):
        self.t = t
        self.w = None
        self.r = {}
        self.dsem = dsem
        self.const = const

    def __getitem__(self, k):
        return self.t[k]


class Prog:
    ENG = ("pe", "act", "dve", "pool", "sp")

    def __init__(self, nc, es, n_dsem=72):
        self.nc = nc
        self.engobj = {'pe': nc.tensor, 'act': nc.scalar, 'dve': nc.vector, 'pool': nc.gpsimd, 'sp': nc.sync}
        self.ninst = 0
        self.stopped = False
        self.esem = {e: es.enter_context(nc.semaphore("es_" + e)) for e in ("pe", "act", "dve", "pool")}
        self.ecount = {e: 0 for e in self.esem}
        self.dsems = [es.enter_context(nc.semaphore(f"ds{i}")) for i in range(n_dsem)]
        self.dcount = {id(s): 0 for s in self.dsems}
        self.dnext = 0
        self.waited = {e: {} for e in self.ENG}
        self.semobj = {}
        for s in list(self.esem.values()) + self.dsems:
            self.semobj[id(s)] = s

    def buf(self, t, dma=False, const=False):
        ds = None
        if dma:
            assert self.dnext < len(self.dsems), "out of DMA semaphores in this phase"
            ds = self.dsems[self.dnext]
            self.dnext += 1
        return Buf(t, ds, const)

    def _deps(self, reads, writes):
        deps = {}

        def add(ev):
            if ev is None:
                return
            k, v = ev
            if deps.get(k, 0) < v:
                deps[k] = v
        for b in reads:
            add(b.w)
        for b in writes:
            add(b.w)
            for k, v in b.r.items():
                add((k, v))
        return deps

    def _record(self, ev, reads, writes):
        for b in writes:
            b.w = ev
            b.r = {}
        for b in reads:
            if b.const:
                continue
            if b.r.get(ev[0], 0) < ev[1]:
                b.r[ev[0]] = ev[1]

    def _emit(self, eng, deps, fn, inc):
        e = self.engobj[eng]
        wd = self.waited[eng]
        own = id(self.esem[eng]) if eng in self.esem else None
        for k, v in deps.items():
            if eng == "pe" and k == own:
                continue
            if wd.get(k, 0) >= v:
                continue
            wd[k] = v
            e.wait_ge(self.semobj[k], v)
        if fn is None:
            return
        ins = fn(e)
        if inc is not None:
            ins.then_inc(inc[0], inc[1])
        self.ninst += 1

    def op(self, eng, fn, reads=(), writes=()):
        if self.stopped:
            return None
        deps = self._deps(reads, writes)
        self.ecount[eng] += 1
        sem = self.esem[eng]
        ev = (id(sem), self.ecount[eng])
        self._emit(eng, deps, fn, (sem, 1))
        self._record(ev, reads, writes)
        return ev

    def ops(self, eng, fns, reads=(), writes=()):
        assert eng == "pe"
        if self.stopped:
            return None
        deps = self._deps(reads, writes)
        for fn in fns[:-1]:
            self._emit(eng, deps, fn, None)
            deps = {}
        self.ecount[eng] += 1
        sem = self.esem[eng]
        ev = (id(sem), self.ecount[eng])
        self._emit(eng, deps, fns[-1], (sem, 1))
        self._record(ev, reads, writes)
        return ev

    def dma(self, fn, sb, load, reads=(), writes=(), q="sp"):
        if self.stopped:
            return None
        reads = list(reads)
        writes = list(writes)
        if load:
            writes.append(sb)
        else:
            reads.append(sb)
        deps = self._deps(reads, writes)
        sem = sb.dsem
        assert sem is not None
        self.dcount[id(sem)] += 16
        ev = (id(sem), self.dcount[id(sem)])
        self._emit(q, deps, fn, (sem, 16))
        self._record(ev, reads, writes)
        return ev

    def dma_group(self, fns, sb, load, reads=(), writes=(), q="sp"):
        if self.stopped:
            return None
        reads = list(reads)
        writes = list(writes)
        if load:
            writes.append(sb)
        else:
            reads.append(sb)
        deps = self._deps(reads, writes)
        sem = sb.dsem
        ev = None
        for fn in fns:
            self.dcount[id(sem)] += 16
            ev = (id(sem), self.dcount[id(sem)])
            self._emit(q, deps, fn, (sem, 16))
            deps = {}
        self._record(ev, reads, writes)
        return ev

    def barrier(self):
        allev = {}
        for e, s in self.esem.items():
            if self.ecount[e]:
                allev[id(s)] = self.ecount[e]
        for s in self.dsems:
            if self.dcount[id(s)]:
                allev[id(s)] = self.dcount[id(s)]
        for eng in self.ENG:
            self._emit(eng, allev, None, None)
        self.dnext = 0

    def emit(self):
        pass


class StopBuild(Exception):
    pass


class Ring:
    def __init__(self, bufs):
        self.bufs = bufs
        self.i = 0

    def next(self):
        b = self.bufs[self.i % len(self.bufs)]
        self.i += 1
        return b


def build(nseq=NSEQ, debug=False, stop_after=None):
    nc = bass.Bass("TRN2", target_bir_lowering=False)

    def din(name, shape, dt=F32):
        return nc.dram_tensor(name, list(shape), dt, kind="ExternalInput").ap()

    dbg_kind = "ExternalOutput" if debug else "Internal"

    def dscr(name, shape, dt):
        return nc.dram_tensor(name, list(shape), dt, kind=dbg_kind).ap()

    x_d = din("x", [nseq, S, D])
    pos_d = din("pos", [nseq, S], I32)
    ident_d = din("ident", [128, 128])
    invf_d = din("invf", [128, 2])
    w_in_d = din("w_in_fm", [D, NFM])
    b_fm_d = din("b_fm", [128, NFM // 128])
    w_v_d = din("w_v", [D, AW])
    b_v_d = din("b_v", [1, AW])
    out_d = nc.dram_tensor("out", [nseq, S, D], F32, kind="ExternalOutput").ap()
    ssm_d = dict(
        lre=din("lre_h", [128, 32]), lim=din("lim_h", [128, 32]), ldt=din("ldt_h", [128, 32]),
        bzr=din("bzr_h", [128, 32, 128]), bzi=din("bzi_h", [128, 32, 128]),
        cbr=din("cbr_h", [32, 32, 128]), cbi=din("cbi_h", [32, 32, 128]),
        dsk=din("dsk_h", [128, 4]), iota=din("iota_h", [1, S]))
    zT_s = dscr("zT_s", [nseq, SSMW, S], BF16)
    aT_s = dscr("aT_s", [nseq, 256, S], BF16)
    h_s = dscr("h_s", [nseq, S, D], F32)
    hT_s = dscr("hT_s", [nseq, D, S], BF16)
    md = dict(maskb=din("maskb_h", [128, 256]), ones3=din("ones3_h", [128, 3, 64]),
              wgv=din("wgv_h", [512, D]), wgg=din("wgg_h", [512, D]), wab=din("wab_h", [256, D]), wo=din("wo_h", [D, D]),
              ln1g=din("ln1g_h", [1, D]), ln1b=din("ln1b_h", [1, D]), ln2g=din("ln2g_h", [1, D]), ln2b=din("ln2b_h", [1, D]),
              wup=din("wup_h", [D, 2 * DFF]), wdn=din("wdn_h", [DFF, D]), cw=din("cw_h", [128, 44, 3]), cbias=din("cb_h", [128, 44]),
              aT_s=aT_s, h_s=h_s, hT_s=hT_s)

    xT_s = dscr("xT_s", [nseq, D, S], BF16)
    uT_s = dscr("uT_s", [nseq, SSMW, S], BF16)
    qT_s = dscr("qT_s", [nseq, AW, S], BF16)
    kT_s = dscr("kT_s", [nseq, AW, S], BF16)
    gT_s = dscr("gT_s", [nseq, 2 * D, S], BF16)
    NBLK = [d * (S // d // 128 + 1) for d in DIL]
    v_s = [dscr(f"v_s{g}", [nseq, 128, NBLK[g], 256], BF16) for g in range(3)]

    with ExitStack() as es0:
        p = Prog(nc, es0)
        psum = [p.buf(es0.enter_context(nc.psum_tensor(f"ps{i}", [128, 512], F32))) for i in range(8)]
        ident = p.buf(es0.enter_context(nc.sbuf_tensor("ident_sb", [128, 128], F32)), dma=True, const=True)
        p.dma(lambda e: e.dma_start(out=ident[:], in_=ident_d), ident, True)
        p.dnext = 1
        wbf = {}

        def cast_w(key, src, R, C):
            dst = nc.dram_tensor(key + "_bf", [R, C], BF16, kind="Internal").ap()
            pb = p.buf(None, dma=True)
            p.dma_group([lambda e, r0=r0: e.dma_start(out=dst[r0:min(r0 + 128, R), :], in_=src[r0:min(r0 + 128, R), :], max_dma_last_dim=4096)
                         for r0 in range(0, R, 128)], pb, True, q="pool")
            wbf[key] = (dst, pb)
        cast_w("w_in", w_in_d, D, NFM)
        cast_w("w_v", w_v_d, D, AW)
        cast_w("wgv", md["wgv"], SSMW, D)
        cast_w("wgg", md["wgg"], SSMW, D)
        cast_w("wab", md["wab"], 256, D)
        cast_w("wo", md["wo"], D, D)
        cast_w("wup", md["wup"], D, 2 * DFF)
        cast_w("wdn", md["wdn"], DFF, D)
        NRES = p.dnext

        def new_phase():
            p.barrier()
            p.dnext = NRES

        def stop(tag):
            if stop_after == tag:
                p.stopped = True

        try:
            _phases(nc, p, psum, ident, nseq, locals_d=dict(x_d=x_d, pos_d=pos_d, invf_d=invf_d, w_in_d=w_in_d, b_fm_d=b_fm_d, w_v_d=w_v_d, b_v_d=b_v_d, out_d=out_d, xT_s=xT_s, uT_s=uT_s, qT_s=qT_s, kT_s=kT_s, gT_s=gT_s, v_s=v_s, NBLK=NBLK, ssm_d=ssm_d, zT_s=zT_s, md=md, wbf=wbf), new_phase=new_phase, stop=stop)
        except StopBuild:
            pass
        p.stopped = False
        p.barrier()
    print('instructions', p.ninst)
    return nc


def _phases(nc, p, psum, ident, nseq, locals_d, new_phase, stop):
    globals_ = locals_d
    x_d = globals_['x_d']; pos_d = globals_['pos_d']; invf_d = globals_['invf_d']; w_in_d = globals_['w_in_d']; b_fm_d = globals_['b_fm_d']
    w_v_d = globals_['w_v_d']; b_v_d = globals_['b_v_d']; out_d = globals_['out_d']; xT_s = globals_['xT_s']; uT_s = globals_['uT_s']
    qT_s = globals_['qT_s']; kT_s = globals_['kT_s']; gT_s = globals_['gT_s']; v_s = globals_['v_s']; NBLK = globals_['NBLK']
    ssm_d = globals_['ssm_d']; zT_s = globals_['zT_s']; md = globals_['md']
    aT_s = md['aT_s']; h_s = md['h_s']; hT_s = md['hT_s']; wbf = globals_['wbf']

    def load_wbf(wdst, key, K):
        src, pb = wbf[key]
        p.dma_group([lambda e, k=k: e.dma_start(out=wdst[:, k, :], in_=src[k * 128:(k + 1) * 128, :]) for k in range(K)], wdst, True, reads=[pb])
        wdst.const = True
    if True:

        with ExitStack() as es:
            def sb(name, shape, dt, dma=False, const=False):
                return p.buf(es.enter_context(nc.sbuf_tensor(name, list(shape), dt)), dma=dma, const=const)

            wA = sb("wA", [128, 8, NFM], BF16, dma=True)
            bfm = sb("bfm", [128, NFM // 128], F32, dma=True)
            invf = sb("invf_sb", [128, 2], F32, dma=True)
            p.dma(lambda e: e.dma_start(out=bfm[:], in_=b_fm_d), bfm, True)
            p.dma(lambda e: e.dma_start(out=invf[:], in_=invf_d), invf, True)
            load_wbf(wA, 'w_in', 8)
            stop('A0')

            cosT = sb("cosT", [128, S], F32)
            sinT = sb("sinT", [128, S], F32)
            posi = sb("posi", [128, 1024], I32, dma=True)
            tur = sb("tur", [128, 1024], F32)
            turi = sb("turi", [128, 1024], I32)
            xs = [sb(f"xs{i}", [128, D], F32, dma=True) for i in range(4)]
            xT = Ring([sb(f"xT{j}", [128, 8, 512], BF16, dma=True) for j in range(2)])
            ev_bf = Ring([sb(f"evbf{j}", [128, 512], BF16, dma=True) for j in range(8)])
            rt = Ring([sb(f"rt{j}", [128, 512], F32) for j in range(4)])
            psr = Ring(psum)

            for sq in range(nseq):
                for c in range(S // 1024):
                    cs = slice(c * 1024, (c + 1) * 1024)
                    p.dma(lambda e, cs=cs: e.dma_start(out=posi[:], in_=pos_d[sq:sq + 1, cs].partition_broadcast(128)), posi, True)
                    for (tab, addc, scol) in ((sinT, 0.0, 1), (cosT, 0.25, None)):
                        p.op("dve", lambda e: e.tensor_copy(out=tur[:], in_=posi[:]), reads=[posi], writes=[tur])
                        p.op("dve", lambda e, addc=addc: e.tensor_scalar(out=tur[:], in0=tur[:], scalar1=invf[:, 0:1], scalar2=addc,
                                                                          op0=ALU.mult, op1=ALU.add), reads=[tur, invf], writes=[tur])
                        p.op("dve", lambda e: e.tensor_copy(out=turi[:], in_=tur[:]), reads=[tur], writes=[turi])
                        p.op("dve", lambda e: e.tensor_tensor(out=tur[:], in0=tur[:], in1=turi[:], op=ALU.subtract),
                             reads=[tur, turi], writes=[tur])
                        if scol is not None:
                            p.op("act", lambda e, tab=tab, cs=cs: e.activation(out=tab[:, cs], in_=tur[:], func=AF.Sin, scale=invf[:, 1:2]),
                                 reads=[tur, invf], writes=[tab])
                        else:
                            p.op("act", lambda e, tab=tab, cs=cs: e.activation(out=tab[:, cs], in_=tur[:], func=AF.Sin, scale=TWO_PI),
                                 reads=[tur], writes=[tab])
                stop('A1')
                def load_x(tb_):
                    for i in range(4):
                        p.dma(lambda e, i=i: e.dma_start(out=xs[i][:], in_=x_d[sq, tb_ * 512 + i * 128:tb_ * 512 + (i + 1) * 128, :]), xs[i], True)
                load_x(0)
                for tb in range(S // 512):
                    t0 = tb * 512
                    ts = slice(t0, t0 + 512)
                    xtile = xs
                    xTb = xT.next()
                    for k in range(8):
                        ps = psr.next()
                        p.ops("pe", [lambda e, ps=ps, i=i, k=k: e.transpose(out=ps[:, i * 128:(i + 1) * 128],
                                                                           in_=xtile[i][:, k * 128:(k + 1) * 128], identity=ident[:])
                                     for i in range(4)], reads=xtile + [ident], writes=[ps])
                        if k % 2 == 0:
                            p.op("act", lambda e, ps=ps, k=k: e.copy(out=xTb[:, k, :], in_=ps[:]), reads=[ps], writes=[xTb])
                        else:
                            p.op("dve", lambda e, ps=ps, k=k: e.tensor_copy(out=xTb[:, k, :], in_=ps[:]), reads=[ps], writes=[xTb])
                    if tb + 1 < S // 512:
                        load_x(tb + 1)
                    p.dma(lambda e: e.dma_start(out=xT_s[sq].rearrange("(k q) t -> q k t", q=128)[:, :, ts], in_=xTb[:]), xTb, False)

                    def proj(fo):
                        ps = psr.next()
                        p.ops("pe", [lambda e, ps=ps, k=k: e.matmul(ps[:], lhsT=wA[:, k, fo * 128:(fo + 1) * 128], rhs=xTb[:, k, :],
                                                                      start=(k == 0), stop=(k == 7)) for k in range(8)],
                              reads=[wA, xTb], writes=[ps])
                        return ps

                    for fo in range(4):
                        ps = proj(fo)
                        o = ev_bf.next()
                        p.op("act", lambda e, ps=ps, o=o, fo=fo: e.activation(out=o[:], in_=ps[:], func=AF.Identity, bias=bfm[:, fo:fo + 1]),
                             reads=[ps, bfm], writes=[o])
                        p.dma(lambda e, o=o, fo=fo: e.dma_start(out=uT_s[sq, fo * 128:(fo + 1) * 128, ts], in_=o[:]), o, False)
                    for which, dst in ((0, qT_s), (1, kT_s)):
                        for c in range(6):
                            fo = 4 + which * 6 + c
                            psa = proj(fo)
                            psb = proj(fo + 12)
                            t1 = rt.next()
                            t2 = rt.next()
                            p.op("dve", lambda e, psa=psa, t1=t1, fo=fo: e.scalar_tensor_tensor(
                                out=t1[:], in0=psa[:], scalar=bfm[:, fo:fo + 1], in1=cosT[:, ts], op0=ALU.add, op1=ALU.mult),
                                reads=[psa, bfm, cosT], writes=[t1])
                            p.op("dve", lambda e, psb=psb, t2=t2, fo=fo: e.scalar_tensor_tensor(
                                out=t2[:], in0=psb[:], scalar=bfm[:, fo + 12:fo + 13], in1=sinT[:, ts], op0=ALU.add, op1=ALU.mult),
                                reads=[psb, bfm, sinT], writes=[t2])
                            o = ev_bf.next()
                            p.op("pool", lambda e, o=o, t1=t1, t2=t2: e.tensor_tensor(out=o[:], in0=t1[:], in1=t2[:], op=ALU.add),
                                 reads=[t1, t2], writes=[o])
                            p.dma(lambda e, o=o, c=c, dst=dst: e.dma_start(out=dst[sq, c * 128:(c + 1) * 128, ts], in_=o[:]), o, False)
                    for c in range(16):
                        fo = 28 + c
                        ps = proj(fo)
                        o = ev_bf.next()
                        p.op("act", lambda e, ps=ps, o=o, fo=fo: e.activation(out=o[:], in_=ps[:], func=AF.Sigmoid, bias=bfm[:, fo:fo + 1]),
                             reads=[ps, bfm], writes=[o])
                        p.dma(lambda e, o=o, c=c: e.dma_start(out=gT_s[sq, c * 128:(c + 1) * 128, ts], in_=o[:]), o, False)
                    stop(f'A2_{tb}')
        new_phase()
        stop('A')

        with ExitStack() as es:
            def sb(name, shape, dt, dma=False, const=False):
                return p.buf(es.enter_context(nc.sbuf_tensor(name, list(shape), dt)), dma=dma, const=const)

            wV = sb("wV", [128, 8, AW], BF16, dma=True)
            load_wbf(wV, 'w_v', 8)
            bv = sb("bv", [128, AW], F32, dma=True, const=True)
            p.dma(lambda e: e.dma_start(out=bv[:], in_=b_v_d.partition_broadcast(128)), bv, True)
            xTf = sb("xTf", [128, 8, S], BF16, dma=True)
            VCH = 12
            vring = Ring([sb(f"vstg{j}", [128, VCH, 256], BF16, dma=True) for j in range(2)])
            psr = Ring(psum)
            for sq in range(nseq):
                p.dma(lambda e: e.dma_start(out=xTf[:], in_=xT_s[sq].rearrange("(k q) t -> q k t", q=128)), xTf, True)
                for g in range(3):
                    d = DIL[g]
                    L = S // d
                    nb = L // 128 + 1
                    blocks = [(r, m) for r in range(d) for m in range(nb)]
                    for c0 in range(0, len(blocks), VCH):
                        chunk = blocks[c0:c0 + VCH]
                        stg = vring.next()
                        p.op("pool", lambda e, stg=stg: e.memset(stg[:], 0.0), writes=[stg])
                        for j, (r, m) in enumerate(chunk):
                            lo = 64 + 128 * (m - 1)
                            i0 = max(0, -lo)
                            i1 = min(128, L - lo)
                            M = i1 - i0
                            tok0 = r + d * (lo + i0)
                            ps = psr.next()
                            p.ops("pe", [lambda e, ps=ps, k=k, tok0=tok0, M=M, i0=i0, d=d, g=g: e.matmul(
                                ps[i0:i0 + M, 0:256], lhsT=xTf[:, k, tok0:tok0 + d * (M - 1) + 1:d], rhs=wV[:, k, g * 256:(g + 1) * 256],
                                start=(k == 0), stop=(k == 7)) for k in range(8)], reads=[xTf, wV], writes=[ps])
                            p.op("dve", lambda e, ps=ps, stg=stg, j=j, i0=i0, M=M, g=g: e.tensor_tensor(
                                out=stg[i0:i0 + M, j, :], in0=ps[i0:i0 + M, 0:256], in1=bv[i0:i0 + M, g * 256:(g + 1) * 256], op=ALU.add),
                                reads=[ps, bv], writes=[stg])
                        p.dma(lambda e, stg=stg, c0=c0, n=len(chunk), g=g: e.dma_start(out=v_s[g][sq, :, c0:c0 + n, :], in_=stg[:, 0:n, :]), stg, False)
                        stop(f'V{g}_{c0}')
                    stop(f'V{g}')
        new_phase()

        with ExitStack() as es:
            def sb(name, shape, dt, dma=False, const=False):
                return p.buf(es.enter_context(nc.sbuf_tensor(name, list(shape), dt)), dma=dma, const=const)

            NT = 32
            NCH = S // 8
            lre = sb("lre", [128, NT], F32, dma=True); lim = sb("lim", [128, NT], F32, dma=True); ldt = sb("ldt", [128, NT], F32, dma=True)
            p.dma(lambda e: e.dma_start(out=lre[:], in_=ssm_d["lre"]), lre, True)
            p.dma(lambda e: e.dma_start(out=lim[:], in_=ssm_d["lim"]), lim, True)
            p.dma(lambda e: e.dma_start(out=ldt[:], in_=ssm_d["ldt"]), ldt, True)
            dsk = sb("dsk", [128, 4], F32, dma=True)
            p.dma(lambda e: e.dma_start(out=dsk[:], in_=ssm_d["dsk"]), dsk, True)
            tI = sb("tI", [128, NCH], F32, dma=True, const=True)
            p.dma(lambda e: e.dma_start(out=tI[:], in_=ssm_d["iota"][:, 0:NCH].partition_broadcast(128)), tI, True)
            sm = {n: sb("sm_" + n, [128, NT], F32) for n in
                  ("dt", "xr", "xi", "rho", "th", "t0", "t1", "f", "sinx", "cosx", "sinh", "em1", "am1", "abi", "den", "kr", "ki", "u0", "u1",
                   "rho8", "th8", "pm", "pc", "ps")}
            smi = sb("smi", [128, NT], I32)
            pwr = sb("pwr", [128, 16, NT], F32); pwi = sb("pwi", [128, 16, NT], F32); npwi = sb("npwi", [128, 16, NT], F32)

            def V(fn, reads, writes):
                return p.op("dve", fn, reads=reads, writes=writes)

            def A(fn, reads, writes):
                return p.op("act", fn, reads=reads, writes=writes)

            def tt(o, a, b, op):
                V(lambda e: e.tensor_tensor(out=o[:], in0=a[:], in1=b[:], op=op), [a, b], [o])

            def tsc(o, a, s1, op0, s2=None, op1=None):
                if op1 is None:
                    V(lambda e: e.tensor_scalar(out=o[:], in0=a[:], scalar1=s1, scalar2=None, op0=op0), [a], [o])
                else:
                    V(lambda e: e.tensor_scalar(out=o[:], in0=a[:], scalar1=s1, scalar2=s2, op0=op0, op1=op1), [a], [o])

            def frac_sin(o, turns_src, mul, add):
                tsc(sm["t0"], turns_src, mul, ALU.mult, add, ALU.add)
                V(lambda e: e.tensor_copy(out=smi[:], in_=sm["t0"][:]), [sm["t0"]], [smi])
                tt(sm["f"], sm["t0"], smi, ALU.subtract)
                A(lambda e: e.activation(out=o[:], in_=sm["f"][:], func=AF.Sin, scale=TWO_PI), [sm["f"]], [o])

            A(lambda e: e.activation(out=sm["dt"][:], in_=ldt[:], func=AF.Exp), [ldt], [sm["dt"]])
            tt(sm["xr"], lre, sm["dt"], ALU.mult)
            tt(sm["xi"], lim, sm["dt"], ALU.mult)
            A(lambda e: e.activation(out=sm["rho"][:], in_=sm["xr"][:], func=AF.Exp), [sm["xr"]], [sm["rho"]])
            A(lambda e: e.activation(out=sm["rho8"][:], in_=sm["xr"][:], func=AF.Exp, scale=8.0), [sm["xr"]], [sm["rho8"]])
            tsc(sm["th"], sm["xi"], 1.0 / TWO_PI, ALU.mult)
            tsc(sm["th8"], sm["th"], 8.0, ALU.mult)
            frac_sin(sm["sinx"], sm["th"], 1.0, 0.0)
            frac_sin(sm["cosx"], sm["th"], 1.0, 0.25)
            frac_sin(sm["sinh"], sm["th"], 0.5, 0.0)
            tsc(sm["em1"], sm["xr"], 0.2, ALU.mult, 1.0, ALU.add)
            for cdiv in (0.25, 1.0 / 3.0, 0.5):
                tt(sm["em1"], sm["em1"], sm["xr"], ALU.mult)
                tsc(sm["em1"], sm["em1"], cdiv, ALU.mult, 1.0, ALU.add)
            tt(sm["em1"], sm["em1"], sm["xr"], ALU.mult)
            tt(sm["am1"], sm["em1"], sm["cosx"], ALU.mult)
            tt(sm["u0"], sm["sinh"], sm["sinh"], ALU.mult)
            V(lambda e: e.scalar_tensor_tensor(out=sm["am1"][:], in0=sm["u0"][:], scalar=-2.0, in1=sm["am1"][:], op0=ALU.mult, op1=ALU.add),
              [sm["u0"], sm["am1"]], [sm["am1"]])
            tt(sm["abi"], sm["rho"], sm["sinx"], ALU.mult)
            tt(sm["den"], lre, lre, ALU.mult)
            tt(sm["u0"], lim, lim, ALU.mult)
            tt(sm["den"], sm["den"], sm["u0"], ALU.add)
            V(lambda e: e.reciprocal(out=sm["den"][:], in_=sm["den"][:]), [sm["den"]], [sm["den"]])
            tt(sm["u0"], sm["am1"], lre, ALU.mult)
            tt(sm["u1"], sm["abi"], lim, ALU.mult)
            tt(sm["u0"], sm["u0"], sm["u1"], ALU.add)
            tt(sm["kr"], sm["u0"], sm["den"], ALU.mult)
            tt(sm["u0"], sm["abi"], lre, ALU.mult)
            tt(sm["u1"], sm["am1"], lim, ALU.mult)
            tt(sm["u0"], sm["u0"], sm["u1"], ALU.subtract)
            tt(sm["ki"], sm["u0"], sm["den"], ALU.mult)
            tsc(sm["t1"], sm["ki"], -1.0, ALU.mult)
            nki = sb("nki", [128, NT], F32)
            V(lambda e: e.tensor_copy(out=nki[:], in_=sm["t1"][:]), [sm["t1"]], [nki])
            for jj in range(16):
                jv = float(jj - 7)
                A(lambda e, jv=jv: e.activation(out=sm["pm"][:], in_=sm["xr"][:], func=AF.Exp, scale=jv), [sm["xr"]], [sm["pm"]])
                frac_sin(sm["ps"], sm["th"], jv, 0.0)
                frac_sin(sm["pc"], sm["th"], jv, 0.25)
                V(lambda e, jj=jj: e.tensor_tensor(out=pwr[:, jj, :], in0=sm["pm"][:], in1=sm["pc"][:], op=ALU.mult), [sm["pm"], sm["pc"]], [pwr])
                V(lambda e, jj=jj: e.tensor_tensor(out=pwi[:, jj, :], in0=sm["pm"][:], in1=sm["ps"][:], op=ALU.mult), [sm["pm"], sm["ps"]], [pwi])
            V(lambda e: e.tensor_scalar(out=npwi[:], in0=pwi[:], scalar1=-1.0, scalar2=None, op0=ALU.mult), [pwi], [npwi])
            for b_ in (pwr, pwi, npwi, sm["kr"], sm["ki"], nki, sm["rho8"], sm["th8"]):
                b_.const = True

            Dd = sb("Dd", [128, 4, 128], BF16)
            for q in range(4):
                V(lambda e, q=q: e.tensor_scalar(out=Dd[:, q, :], in0=ident[:], scalar1=dsk[:, q:q + 1], scalar2=None, op0=ALU.mult),
                  [ident, dsk], [Dd])
            Dd.const = True

            NSET = 2
            bz = [[sb(f"bz{i}_{k}", [128, 2, 128], F32, dma=True) for k in range(2)] for i in range(NSET)]
            cbt = [[sb(f"cbt{i}_{k}", [32, 2, 128], F32, dma=True) for k in range(2)] for i in range(NSET)]
            Bz = [[sb(f"Bz{i}_{k}", [128, 2, 128], F32) for k in range(2)] for i in range(NSET)]
            CT = [[sb(f"CT{i}_{k}", [128, 2, 64], F32) for k in range(2)] for i in range(NSET)]
            XT = [[sb(f"XT{i}_{k}", [128, 8, 2, 128], BF16) for k in range(2)] for i in range(NSET)]
            KT = [[sb(f"KT{i}_{k}", [128, 8, 64], BF16) for k in range(2)] for i in range(NSET)]
            LY = [[sb(f"LY{i}_{k}", [128, 8, 2, 64], BF16) for k in range(2)] for i in range(NSET)]
            cosN = [[sb(f"cosN{i}_{k}", [128, NCH], F32) for k in range(2)] for i in range(NSET)]
            sinN = [[sb(f"sinN{i}_{k}", [128, NCH], F32) for k in range(2)] for i in range(NSET)]
            rho8T = [[sb(f"rho8T{i}_{k}", [128, NCH], F32) for k in range(2)] for i in range(NSET)]
            for i in range(NSET):
                for k in range(2):
                    p.op("pool", lambda e, i=i, k=k: e.memset(CT[i][k][:], 0.0), writes=[CT[i][k]])
            xtmp = Ring([sb(f"xtmp{i}", [128, 2, 128], F32) for i in range(3)])
            lyf = Ring([sb(f"lyf{i}", [128, 2, 64], F32) for i in range(3)])
            turN = sb("turN", [128, NCH], F32); turNi = sb("turNi", [128, NCH], I32)
            uTr = Ring([sb(f"uTc{i}", [128, S], BF16, dma=True) for i in range(2)])
            uDr = Ring([sb(f"uD{i}", [128, 8, NCH], BF16) for i in range(2)])
            tmpr = Ring([sb(f"tmpS{i}", [128, NCH], F32) for i in range(8)])
            wrr = Ring([sb(f"wS{i}", [128, NCH], F32) for i in range(4)])
            Rrr = Ring([sb(f"RS{i}", [128, NCH], F32) for i in range(4)])
            Vrr = Ring([sb(f"VS{i}", [128, NCH], F32) for i in range(4)])
            Zr_ = [Ring([sb(f"ZS{k}_{i}", [128, 2, NCH], BF16) for i in range(2)]) for k in range(2)]
            zor = Ring([sb(f"zo{i}", [128, S], BF16, dma=True) for i in range(2)])
            psT = Ring(psum[4:8])
            psSt = [psum[0:2], psum[2:4]]

            def cmul(o, orow, oi_row, src, sr, si, nsi):
                V(lambda e: e.tensor_scalar(out=o[:, 0, :], in0=src[:, 0, :], scalar1=sr, scalar2=None, op0=ALU.mult), [src], [o])
                V(lambda e: e.scalar_tensor_tensor(out=o[:, 0, :], in0=src[:, 1, :], scalar=nsi, in1=o[:, 0, :], op0=ALU.mult, op1=ALU.add), [src, o], [o])
                V(lambda e: e.tensor_scalar(out=o[:, 1, :], in0=src[:, 1, :], scalar1=sr, scalar2=None, op0=ALU.mult), [src], [o])
                V(lambda e: e.scalar_tensor_tensor(out=o[:, 1, :], in0=src[:, 0, :], scalar=si, in1=o[:, 1, :], op0=ALU.mult, op1=ALU.add), [src, o], [o])

            def prep(gp, st):
                for k in range(2):
                    j = k * 16 + gp
                    p.dma(lambda e: e.dma_start(out=bz[st][k][:, 0, :], in_=ssm_d["bzr"][:, j, :]), bz[st][k], True)
                    p.dma(lambda e: e.dma_start(out=bz[st][k][:, 1, :], in_=ssm_d["bzi"][:, j, :]), bz[st][k], True)
                    p.dma(lambda e: e.dma_start(out=cbt[st][k][:, 0, :], in_=ssm_d["cbr"][:, j, :]), cbt[st][k], True)
                    p.dma(lambda e: e.dma_start(out=cbt[st][k][:, 1, :], in_=ssm_d["cbi"][:, j, :]), cbt[st][k], True)
                    cmul(Bz[st][k], None, None, bz[st][k], sm["kr"][:, j:j + 1], sm["ki"][:, j:j + 1], nki[:, j:j + 1])
                    ps = psT.next()
                    p.ops("pe", [lambda e: e.transpose(out=ps[:, 0:32], in_=cbt[st][k][:, 0, :], identity=ident[0:32, 0:32]),
                                 lambda e: e.transpose(out=ps[:, 32:64], in_=cbt[st][k][:, 1, :], identity=ident[0:32, 0:32])],
                          reads=[cbt[st][k], ident], writes=[ps])
                    A(lambda e: e.copy(out=CT[st][k][:, 0, 32:64], in_=ps[:, 0:32]), [ps], [CT[st][k]])
                    A(lambda e: e.mul(out=CT[st][k][:, 1, 32:64], in_=ps[:, 32:64], mul=-1.0), [ps], [CT[st][k]])
                    for s_ in range(8):
                        if s_ == 0:
                            xs_ = Bz[st][k]
                        else:
                            xs_ = xtmp.next()
                            jj = 7 - s_
                            cmul(xs_, None, None, Bz[st][k], pwr[:, jj, j:j + 1], pwi[:, jj, j:j + 1], npwi[:, jj, j:j + 1])
                        ps = psT.next()
                        p.ops("pe", [lambda e: e.transpose(out=ps[:, 0:128], in_=xs_[:, 0, :], identity=ident[:]),
                                     lambda e: e.transpose(out=ps[:, 128:256], in_=xs_[:, 1, :], identity=ident[:])],
                              reads=[xs_, ident], writes=[ps])
                        A(lambda e: e.copy(out=XT[st][k][:, s_, :, :], in_=ps[:, 0:256].rearrange("p (r c) -> p r c", r=2)), [ps], [XT[st][k]])
                    for tau in range(8):
                        ly = lyf.next()
                        jj = 7 + tau
                        ctr = CT[st][k]
                        V(lambda e: e.tensor_scalar(out=ly[:, 0, :], in0=ctr[:, 0, :], scalar1=pwr[:, jj, j:j + 1], scalar2=None, op0=ALU.mult), [ctr], [ly])
                        V(lambda e: e.scalar_tensor_tensor(out=ly[:, 0, :], in0=ctr[:, 1, :], scalar=pwi[:, jj, j:j + 1], in1=ly[:, 0, :], op0=ALU.mult, op1=ALU.add), [ctr, ly], [ly])
                        V(lambda e: e.tensor_scalar(out=ly[:, 1, :], in0=ctr[:, 1, :], scalar1=pwr[:, jj, j:j + 1], scalar2=None, op0=ALU.mult), [ctr], [ly])
                        V(lambda e: e.scalar_tensor_tensor(out=ly[:, 1, :], in0=ctr[:, 0, :], scalar=npwi[:, jj, j:j + 1], in1=ly[:, 1, :], op0=ALU.mult, op1=ALU.add), [ctr, ly], [ly])
                        A(lambda e: e.copy(out=LY[st][k][:, tau, :, :], in_=ly[:]), [ly], [LY[st][k]])
                        ps = psT.next()
                        p.ops("pe", [lambda e: e.matmul(ps[:, 0:64], lhsT=Bz[st][k][:, 0, :], rhs=ly[:, 0, :], start=True, stop=False),
                                     lambda e: e.matmul(ps[:, 0:64], lhsT=Bz[st][k][:, 1, :], rhs=ly[:, 1, :], start=False, stop=True)],
                              reads=[Bz[st][k], ly], writes=[ps])
                        A(lambda e: e.copy(out=KT[st][k][:, tau, :], in_=ps[:, 0:64]), [ps], [KT[st][k]])
                    for (tab, addc) in ((sinN[st][k], 0.0), (cosN[st][k], 0.25)):
                        V(lambda e: e.tensor_scalar(out=turN[:], in0=tI[:], scalar1=sm["th8"][:, j:j + 1], scalar2=addc, op0=ALU.mult, op1=ALU.add), [tI], [turN])
                        V(lambda e: e.tensor_copy(out=turNi[:], in_=turN[:]), [turN], [turNi])
                        V(lambda e: e.tensor_tensor(out=turN[:], in0=turN[:], in1=turNi[:], op=ALU.subtract), [turN, turNi], [turN])
                        A(lambda e: e.activation(out=tab[:], in_=turN[:], func=AF.Sin, scale=TWO_PI), [turN], [tab])
                    V(lambda e: e.tensor_scalar(out=rho8T[st][k][:], in0=tI[:], scalar1=0.0, scalar2=sm["rho8"][:, j:j + 1], op0=ALU.mult, op1=ALU.add), [tI], [rho8T[st][k]])

            Ssb = Ring([sb(f"Ssb{i}", [128, 4, NCH], F32) for i in range(2)])

            def geom(gp):
                q = gp // 4
                qq = gp % 4
                if qq < 3:
                    return q, slice(32 * qq, 32 * qq + 32), slice(32, 64), slice(32 * qq, 32 * qq + 32), slice(32 * qq, 32 * qq + 32)
                return q, slice(64, 128), slice(0, 64), slice(64, 128), slice(32 * qq, 32 * qq + 32)

            def stageA(gp, st, sq):
                q = gp // 4
                uT = uTr.next()
                p.dma(lambda e: e.dma_start(out=uT[:], in_=uT_s[sq, q * 128:(q + 1) * 128, :]), uT, True)
                uD = uDr.next()
                A(lambda e: e.copy(out=uD[:], in_=uT[:].rearrange("p (n s) -> p s n", s=8)), [uT], [uD])
                ss = Ssb.next()
                for k in range(2):
                    for ri in range(2):
                        pb_ = psSt[k][ri]
                        if k == 0:
                            fns = [lambda e, s_=s_: e.matmul(pb_[:], lhsT=XT[st][k][:, s_, ri, :], rhs=uD[:, s_, :], start=(s_ == 0), stop=(s_ == 7)) for s_ in range(8)]
                        else:
                            fns = [lambda e, s_=s_: e.matmul(pb_[:], lhsT=XT[st][k][:, s_, ri, :], rhs=uD[:, 7 - s_, ::-1], start=(s_ == 0), stop=(s_ == 7)) for s_ in range(8)]
                        p.ops("pe", fns, reads=[XT[st][k], uD], writes=[pb_])
                        A(lambda e: e.copy(out=ss[:, 2 * k + ri, :], in_=pb_[:]), [pb_], [ss])
                return (gp, st, sq, uD, ss)

            def stageB(ctx):
                gp, st, sq, uD, ss = ctx
                T = [[tmpr.next() for _ in range(4)] for k in range(2)]
                for (ti, si, tab) in ((0, 0, cosN), (1, 1, sinN), (2, 1, cosN), (3, 0, sinN)):
                    for k in range(2):
                        V(lambda e, k=k: e.tensor_tensor(out=T[k][ti][:], in0=ss[:, 2 * k + si, :], in1=tab[st][k][:], op=ALU.mult), [ss, tab[st][k]], [T[k][ti]])
                W = [[wrr.next(), wrr.next()] for k in range(2)]
                for k in range(2):
                    V(lambda e, k=k: e.tensor_tensor(out=W[k][0][:], in0=T[k][0][:], in1=T[k][1][:], op=ALU.add), [T[k][0], T[k][1]], [W[k][0]])
                for k in range(2):
                    V(lambda e, k=k: e.tensor_tensor(out=W[k][1][:], in0=T[k][2][:], in1=T[k][3][:], op=ALU.subtract), [T[k][2], T[k][3]], [W[k][1]])
                R = [[Rrr.next(), Rrr.next()] for k in range(2)]
                for ri in range(2):
                    for k in range(2):
                        V(lambda e, k=k, ri=ri: e.tensor_tensor_scan(out=R[k][ri][:], data0=rho8T[st][k][:], data1=W[k][ri][:], initial=0.0, op0=ALU.mult, op1=ALU.add),
                          [rho8T[st][k], W[k][ri]], [R[k][ri]])
                T = [[tmpr.next() for _ in range(4)] for k in range(2)]
                for (ti, si, tab) in ((0, 0, cosN), (1, 1, sinN), (2, 1, cosN), (3, 0, sinN)):
                    for k in range(2):
                        V(lambda e, k=k: e.tensor_tensor(out=T[k][ti][:], in0=R[k][si][:], in1=tab[st][k][:], op=ALU.mult), [R[k][si], tab[st][k]], [T[k][ti]])
                Vv = [[Vrr.next(), Vrr.next()] for k in range(2)]
                for k in range(2):
                    V(lambda e, k=k: e.tensor_tensor(out=Vv[k][0][:], in0=T[k][0][:], in1=T[k][1][:], op=ALU.subtract), [T[k][0], T[k][1]], [Vv[k][0]])
                for k in range(2):
                    V(lambda e, k=k: e.tensor_tensor(out=Vv[k][1][:], in0=T[k][2][:], in1=T[k][3][:], op=ALU.add), [T[k][2], T[k][3]], [Vv[k][1]])
                Z = [Zr_[k].next() for k in range(2)]
                for ri in range(2):
                    for k in range(2):
                        V(lambda e, k=k, ri=ri: e.tensor_tensor(out=Z[k][:, ri, :], in0=Vv[k][ri][:], in1=ss[:, 2 * k + ri, :], op=ALU.subtract), [Vv[k][ri], ss], [Z[k]])
                return ctx + (Z,)

            def stageC(ctx):
                gp, st, sq, uD, ss, Z = ctx
                q, rows, lcs, dds, orow = geom(gp)
                zo = zor.next()
                for tau in range(8):
                    py = psT.next()
                    fns = [
                        lambda e: e.matmul(py[rows, :], lhsT=LY[st][0][:, tau, 0, lcs], rhs=Z[0][:, 0, :], start=True, stop=False),
                        lambda e: e.matmul(py[rows, :], lhsT=LY[st][0][:, tau, 1, lcs], rhs=Z[0][:, 1, :], start=False, stop=False),
                        lambda e: e.matmul(py[rows, :], lhsT=LY[st][1][:, 7 - tau, 0, lcs], rhs=Z[1][:, 0, ::-1], start=False, stop=False),
                        lambda e: e.matmul(py[rows, :], lhsT=LY[st][1][:, 7 - tau, 1, lcs], rhs=Z[1][:, 1, ::-1], start=False, stop=False),
                    ]
                    for s_ in range(0, tau + 1):
                        fns.append(lambda e, s_=s_: e.matmul(py[rows, :], lhsT=KT[st][0][:, tau - s_, lcs], rhs=uD[:, s_, :], start=False, stop=False))
                    for s_ in range(tau, 8):
                        fns.append(lambda e, s_=s_: e.matmul(py[rows, :], lhsT=KT[st][1][:, s_ - tau, lcs], rhs=uD[:, s_, :], start=False, stop=False))
                    fns.append(lambda e: e.matmul(py[rows, :], lhsT=Dd[:, q, dds], rhs=uD[:, tau, :], start=False, stop=True))
                    p.ops("pe", fns, reads=[LY[st][0], LY[st][1], KT[st][0], KT[st][1], Dd, Z[0], Z[1], uD], writes=[py])
                    A(lambda e: e.activation(out=zo[rows, tau:S:8], in_=py[rows, :], func=AF.Gelu), [py], [zo])
                p.dma(lambda e: e.dma_start(out=zT_s[sq, gp * 32:(gp + 1) * 32, :], in_=zo[orow, :]), zo, False)

            runs = [(gp, gp % NSET, sq) for gp in range(16) for sq in range(nseq)]
            prep(0, 0)
            ctxA = stageA(*runs[0])
            for i, (gp, st, sq) in enumerate(runs):
                if sq == 0 and gp + 1 < 16:
                    prep(gp + 1, (gp + 1) % NSET)
                nxt = stageA(*runs[i + 1]) if i + 1 < len(runs) else None
                ctxB = stageB(ctxA)
                stageC(ctxB)
                ctxA = nxt
                if sq == nseq - 1:
                    stop(f'S_gp{gp}')
        new_phase()
        stop('S')

        with ExitStack() as es:
            def sb(name, shape, dt, dma=False, const=False):
                return p.buf(es.enter_context(nc.sbuf_tensor(name, list(shape), dt)), dma=dma, const=const)

            mstage = sb("mstage", [128, 256], F32, dma=True)
            ostage = sb("ostage", [128, 3, 64], F32, dma=True)
            maskB = sb("maskB", [128, 256], BF16); ones3 = sb("ones3", [128, 3, 64], BF16); identb = sb("identb", [128, 128], BF16)
            p.dma(lambda e: e.dma_start(out=mstage[:], in_=md["maskb"]), mstage, True)
            p.dma(lambda e: e.dma_start(out=ostage[:], in_=md["ones3"]), ostage, True)
            p.op("dve", lambda e: e.tensor_copy(out=maskB[:], in_=mstage[:]), reads=[mstage], writes=[maskB])
            p.op("dve", lambda e: e.tensor_copy(out=ones3[:], in_=ostage[:]), reads=[ostage], writes=[ones3])
            p.op("dve", lambda e: e.tensor_copy(out=identb[:], in_=ident[:]), reads=[ident], writes=[identb])
            maskB.const = True; ones3.const = True; identb.const = True
            qTr = Ring([sb(f"qTa{i}", [128, S], BF16, dma=True) for i in range(2)])
            kSr = Ring([sb(f"kSa{i}", [128, S], BF16, dma=True) for i in range(2)])
            qDr = Ring([sb(f"qDa{i}", [128, S], BF16) for i in range(2)])
            kTr = Ring([sb(f"kTa{i}", [128, S + 2 * KPAD], BF16) for i in range(2)])
            vTr = Ring([sb(f"vTa{i}", [128, 48, 128], BF16, dma=True) for i in range(2)])
            acc = sb("acc", [128, 2, S], F32)
            rden = sb("rden", [128, S], F32)
            aTo = sb("aTo", [128, S], BF16, dma=True)
            PTr = Ring([sb(f"PT{i}", [128, 256], BF16) for i in range(6)])
            psS = Ring(psum[0:4]); psO = Ring(psum[4:8])
            SCALE = 64.0 ** -0.5
            for sq in range(nseq):
                for c in range(2):
                    for g in range(3):
                        d = DIL[g]; L = S // d; nb = L // 128 + 1
                        qN = qTr.next(); kS = kSr.next(); qT = qDr.next(); kT = kTr.next(); vT = vTr.next()
                        ch = 2 * g + c
                        LP = L + 128
                        p.dma(lambda e: e.dma_start(out=qN[:], in_=qT_s[sq, ch * 128:(ch + 1) * 128, :]), qN, True)
                        p.dma(lambda e: e.dma_start(out=kS[:], in_=kT_s[sq, ch * 128:(ch + 1) * 128, :]), kS, True)
                        p.op("pool", lambda e: e.memset(kT[:, 0:d * LP], 0.0), writes=[kT])
                        p.op("act", lambda e: e.copy(out=qT[:].rearrange("p (r i) -> p r i", r=d), in_=qN[:].rearrange("p (i r) -> p r i", r=d)), reads=[qN], writes=[qT])
                        p.op("dve", lambda e: e.tensor_copy(out=kT[:, 0:d * LP].rearrange("p (r i) -> p r i", r=d)[:, :, 64:64 + L], in_=kS[:].rearrange("p (i r) -> p r i", r=d)),
                             reads=[kS], writes=[kT])
                        p.dma(lambda e: e.dma_start(out=vT[:, 0:NBLK[g], :], in_=v_s[g][sq, :, :, c * 128:(c + 1) * 128]), vT, True)
                        def stS(r, a, qT=qT, kT=kT, L=L, LP=LP):
                            qcs = slice(r * L + 128 * a, r * L + 128 * a + 128)
                            pts = []
                            for hp in range(2):
                                pb = 64 * hp
                                pS = psS.next()
                                ks = [slice(r * LP + 128 * m, r * LP + 128 * m + 128) for m in (a, a + 1)]
                                p.ops("pe", [
                                    lambda e: e.matmul(pS[:, 0:128], lhsT=kT[pb:pb + 64, ks[0]], rhs=qT[pb:pb + 64, qcs], start=True, stop=False),
                                    lambda e: e.matmul(pS[:, 128:256], lhsT=kT[pb:pb + 64, ks[1]], rhs=qT[pb:pb + 64, qcs], start=False, stop=False),
                                    lambda e: e.matmul(pS[:, 0:256], lhsT=identb[:], rhs=maskB[:], start=False, stop=True),
                                ], reads=[kT, qT, identb, maskB], writes=[pS])
                                PT = PTr.next()
                                p.op("act", lambda e: e.activation(out=PT[:], in_=pS[:, 0:256], func=AF.Exp, scale=SCALE), reads=[pS], writes=[PT])
                                pts.append(PT)
                            return pts

                        def stPV(r, a, pts, vT=vT, d=d, nb=nb, g=g):
                            qsl = slice(r + d * 128 * a, r + d * 128 * a + d * 127 + 1, d)
                            pO = psO.next()
                            fns = []
                            for hp in range(2):
                                pb = 64 * hp
                                PT = pts[hp]
                                o1 = 1 if a == 0 else 0
                                o2 = 2 if a + 1 == nb - 1 else 0
                                b1 = r * nb + a; b2 = r * nb + a + 1
                                fns += [
                                    lambda e, PT=PT, pb=pb, b1=b1, hp=hp: e.matmul(pO[pb:pb + 64, 0:128], lhsT=vT[:, b1, hp * 64:(hp + 1) * 64], rhs=PT[:, 0:128], start=True, stop=False),
                                    lambda e, PT=PT, pb=pb, b2=b2, hp=hp: e.matmul(pO[pb:pb + 64, 0:128], lhsT=vT[:, b2, hp * 64:(hp + 1) * 64], rhs=PT[:, 128:256], start=False, stop=False),
                                    lambda e, PT=PT, pb=pb, o1=o1: e.matmul(pO[pb:pb + 64, 128:256], lhsT=ones3[:, o1, :], rhs=PT[:, 0:128], start=False, stop=False),
                                    lambda e, PT=PT, pb=pb, o2=o2: e.matmul(pO[pb:pb + 64, 128:256], lhsT=ones3[:, o2, :], rhs=PT[:, 128:256], start=False, stop=True),
                                ]
                            p.ops("pe", fns, reads=pts + [vT, ones3], writes=[pO])
                            pov = pO[:, 0:256].rearrange("p (n i) -> p n i", n=2)
                            if g == 0:
                                p.op("dve", lambda e: e.tensor_copy(out=acc[:, :, qsl], in_=pov), reads=[pO], writes=[acc])
                            else:
                                p.op("dve", lambda e: e.tensor_tensor(out=acc[:, :, qsl], in0=pov, in1=acc[:, :, qsl], op=ALU.add), reads=[pO, acc], writes=[acc])

                        units = [(r, a) for r in range(d) for a in range(L // 128)]
                        cur = stS(*units[0])
                        for ui, (r, a) in enumerate(units):
                            nxt = stS(*units[ui + 1]) if ui + 1 < len(units) else None
                            stPV(r, a, cur)
                            cur = nxt
                    p.op("dve", lambda e: e.reciprocal(out=rden[:], in_=acc[:, 1, :]), reads=[acc], writes=[rden])
                    p.op("dve", lambda e: e.tensor_tensor(out=aTo[:], in0=acc[:, 0, :], in1=rden[:], op=ALU.mult), reads=[acc, rden], writes=[aTo])
                    p.dma(lambda e: e.dma_start(out=aT_s[sq, c * 128:(c + 1) * 128, :], in_=aTo[:]), aTo, False)
        new_phase()
        stop('T')

        def load_w_bf16(sbf, wdst, src, K, N, tag, stg=None):
            piece = 1024 if N >= 1024 else N
            if stg is None:
                stg = Ring([sbf(f"wl_{tag}{i}", [128, piece], F32, dma=True) for i in range(2)])
            engs = ("dve", "pool", "act")
            n = 0
            for k in range(K):
                for c0 in range(0, N, piece):
                    w = min(piece, N - c0)
                    st = stg.next()
                    p.dma(lambda e, st=st, k=k, c0=c0, w=w: e.dma_start(out=st[:, 0:w], in_=src[k * 128:(k + 1) * 128, c0:c0 + w]), st, True)
                    eng = engs[n % 3]; n += 1
                    if eng == "act":
                        p.op("act", lambda e, st=st, k=k, c0=c0, w=w: e.copy(out=wdst[:, k, c0:c0 + w], in_=st[:, 0:w]), reads=[st], writes=[wdst])
                    else:
                        p.op(eng, lambda e, st=st, k=k, c0=c0, w=w: e.tensor_copy(out=wdst[:, k, c0:c0 + w], in_=st[:, 0:w]), reads=[st], writes=[wdst])
            wdst.const = True

        def layer_norm(sbufs, hpre, gB, bB, outt):
            stats, mv, rstd, hn = sbufs
            for n in range(2):
                p.op("dve", lambda e, n=n: e.bn_stats(out=stats[:, n, :], in_=hpre[:, n * 512:(n + 1) * 512]), reads=[hpre], writes=[stats])
            p.op("dve", lambda e: e.bn_aggr(out=mv[:], in_=stats[:].rearrange("p n s -> p (n s)")), reads=[stats], writes=[mv])
            p.op("act", lambda e: e.activation(out=rstd[:], in_=mv[:, 1:2], func=AF.Sqrt, bias=epsb[:, 0:1]), reads=[mv, epsb], writes=[rstd])
            p.op("dve", lambda e: e.reciprocal(out=rstd[:], in_=rstd[:]), reads=[rstd], writes=[rstd])
            p.op("dve", lambda e: e.tensor_scalar(out=hn[:], in0=hpre[:], scalar1=mv[:, 0:1], scalar2=rstd[:, 0:1], op0=ALU.subtract, op1=ALU.mult),
                 reads=[hpre, mv, rstd], writes=[hn])
            p.op("dve", lambda e: e.tensor_tensor(out=hn[:], in0=hn[:], in1=gB[:], op=ALU.mult), reads=[hn, gB], writes=[hn])
            p.op("dve", lambda e: e.tensor_tensor(out=outt[:], in0=hn[:], in1=bB[:], op=ALU.add), reads=[hn, bB], writes=[outt])

        with ExitStack() as es:
            def sb(name, shape, dt, dma=False, const=False):
                return p.buf(es.enter_context(nc.sbuf_tensor(name, list(shape), dt)), dma=dma, const=const)
            wgv = sb("wgv", [128, 4, D], BF16, dma=True); wgg = sb("wgg", [128, 4, D], BF16, dma=True); wab = sb("wab", [128, 2, D], BF16, dma=True); wo = sb("wo", [128, 8, D], BF16, dma=True)
            load_wbf(wgv, "wgv", 4); load_wbf(wgg, "wgg", 4); load_wbf(wab, "wab", 2); load_wbf(wo, "wo", 8)
            gB = sb("ln1gB", [128, D], F32, dma=True, const=True); bB = sb("ln1bB", [128, D], F32, dma=True, const=True)
            p.dma(lambda e: e.dma_start(out=gB[:], in_=md["ln1g"].partition_broadcast(128)), gB, True)
            p.dma(lambda e: e.dma_start(out=bB[:], in_=md["ln1b"].partition_broadcast(128)), bB, True)
            epsb = sb("epsb", [128, 1], F32)
            p.op("pool", lambda e: e.memset(epsb[:], LN_EPS), writes=[epsb])
            zTr_ = Ring([sb(f"zTm{i}", [128, 4, 512], BF16, dma=True) for i in range(2)]); aTr_ = Ring([sb(f"aTm{i}", [128, 2, 512], BF16, dma=True) for i in range(2)])
            gTr_ = Ring([sb(f"gTm{i}", [128, 16, 512], BF16, dma=True) for i in range(2)])
            xs = Ring([sb(f"xm{i}", [128, D], F32, dma=True) for i in range(2)])
            mixT = sb("mixT", [128, 8, 512], BF16)
            sg = Ring([sb(f"sg{i}", [128, 512], F32) for i in range(2)])
            t1r = Ring([sb(f"t1m{i}", [128, 512], F32) for i in range(2)])
            t2r = Ring([sb(f"t2m{i}", [128, 512], F32) for i in range(2)])
            hpre = Ring([sb(f"hpre{i}", [128, D], F32) for i in range(2)])
            hout = Ring([sb(f"hout{i}", [128, D], F32, dma=True) for i in range(2)])
            hTt = Ring([sb(f"hTt{i}", [128, 8, 128], BF16, dma=True) for i in range(2)])
            lnb = (sb("st1", [128, 2, 6], F32), sb("mv1", [128, 2], F32), sb("rstd1", [128, 1], F32), sb("hn1", [128, D], F32))
            psr = Ring(psum)
            def load_m1(sq_, tb_):
                ts_ = slice(tb_ * 512, (tb_ + 1) * 512)
                zT_ = zTr_.next(); aT_ = aTr_.next(); gT_ = gTr_.next()
                p.dma(lambda e: e.dma_start(out=zT_[:], in_=zT_s[sq_].rearrange("(k q) t -> q k t", q=128)[:, :, ts_]), zT_, True)
                p.dma(lambda e: e.dma_start(out=aT_[:], in_=aT_s[sq_].rearrange("(k q) t -> q k t", q=128)[:, :, ts_]), aT_, True)
                p.dma(lambda e: e.dma_start(out=gT_[:], in_=gT_s[sq_].rearrange("(k q) t -> q k t", q=128)[:, :, ts_]), gT_, True)
                return zT_, aT_, gT_
            blocks_m1 = [(sq_, tb_) for sq_ in range(nseq) for tb_ in range(S // 512)]
            nxt_in = load_m1(*blocks_m1[0])
            for bi, (sq, tb) in enumerate(blocks_m1):
                if True:
                    ts = slice(tb * 512, (tb + 1) * 512)
                    zT, aT, gT = nxt_in
                    if bi + 1 < len(blocks_m1):
                        nxt_in = load_m1(*blocks_m1[bi + 1])
                    for do in range(8):
                        ds_ = slice(do * 128, (do + 1) * 128)
                        pA = psr.next(); pG = psr.next(); pB = psr.next()
                        p.ops("pe", [lambda e, k=k: e.matmul(pA[:], lhsT=wgv[:, k, ds_], rhs=zT[:, k, :], start=(k == 0), stop=(k == 3)) for k in range(4)], reads=[wgv, zT], writes=[pA])
                        p.ops("pe", [lambda e, k=k: e.matmul(pG[:], lhsT=wgg[:, k, ds_], rhs=zT[:, k, :], start=(k == 0), stop=(k == 3)) for k in range(4)], reads=[wgg, zT], writes=[pG])
                        p.ops("pe", [lambda e, k=k: e.matmul(pB[:], lhsT=wab[:, k, ds_], rhs=aT[:, k, :], start=(k == 0), stop=(k == 1)) for k in range(2)], reads=[wab, aT], writes=[pB])
                        sgt = sg.next(); t1 = t1r.next(); t2 = t2r.next()
                        p.op("act", lambda e: e.activation(out=sgt[:], in_=pG[:], func=AF.Sigmoid), reads=[pG], writes=[sgt])
                        p.op("dve", lambda e: e.tensor_tensor(out=t1[:], in0=pA[:], in1=sgt[:], op=ALU.mult), reads=[pA, sgt], writes=[t1])
                        p.op("dve", lambda e: e.tensor_tensor(out=t2[:], in0=pB[:], in1=gT[:, 8 + do, :], op=ALU.mult), reads=[pB, gT], writes=[t2])
                        p.op("dve", lambda e: e.tensor_tensor(out=t1[:], in0=t1[:], in1=gT[:, do, :], op=ALU.mult), reads=[t1, gT], writes=[t1])
                        p.op("dve", lambda e: e.tensor_tensor(out=mixT[:, do, :], in0=t1[:], in1=t2[:], op=ALU.add), reads=[t1, t2], writes=[mixT])
                    for i in range(4):
                        tok = slice(tb * 512 + i * 128, tb * 512 + (i + 1) * 128)
                        xt = xs.next()
                        p.dma(lambda e: e.dma_start(out=xt[:], in_=x_d[sq, tok, :]), xt, True)
                        hp_ = hpre.next()
                        for n in range(2):
                            ns = slice(n * 512, (n + 1) * 512)
                            po = psr.next()
                            p.ops("pe", [lambda e, k=k: e.matmul(po[:], lhsT=mixT[:, k, i * 128:(i + 1) * 128], rhs=wo[:, k, ns], start=(k == 0), stop=(k == 7)) for k in range(8)],
                                  reads=[mixT, wo], writes=[po])
                            p.op("dve", lambda e: e.scalar_tensor_tensor(out=hp_[:, ns], in0=xt[:, ns], scalar=ALPHA, in1=po[:], op0=ALU.mult, op1=ALU.add),
                                 reads=[xt, po], writes=[hp_])
                        ho = hout.next()
                        layer_norm(lnb, hp_, gB, bB, ho)
                        p.dma(lambda e: e.dma_start(out=h_s[sq, tok, :], in_=ho[:]), ho, False)
                        hT = hTt.next()
                        for kk in range(2):
                            pt = psr.next()
                            p.ops("pe", [lambda e, k4=k4: e.transpose(out=pt[:, k4 * 128:(k4 + 1) * 128], in_=ho[:, (kk * 4 + k4) * 128:(kk * 4 + k4 + 1) * 128], identity=ident[:])
                                         for k4 in range(4)], reads=[ho, ident], writes=[pt])
                            p.op("act", lambda e: e.copy(out=hT[:, kk * 4:(kk + 1) * 4, :], in_=pt[:].rearrange("p (k t) -> p k t", k=4)), reads=[pt], writes=[hT])
                        p.dma(lambda e: e.dma_start(out=hT_s[sq].rearrange("(k q) t -> q k t", q=128)[:, :, tok], in_=hT[:]), hT, False)
        new_phase()
        stop('M1')

        with ExitStack() as es:
            def sb(name, shape, dt, dma=False, const=False):
                return p.buf(es.enter_context(nc.sbuf_tensor(name, list(shape), dt)), dma=dma, const=const)
            wup = sb("wup", [128, 8, 2 * DFF], BF16, dma=True); wdn = sb("wdn", [128, 22, D], BF16, dma=True)
            load_wbf(wup, "wup", 8); load_wbf(wdn, "wdn", 22)
            gB = sb("ln2gB", [128, D], F32, dma=True, const=True); bB = sb("ln2bB", [128, D], F32, dma=True, const=True)
            p.dma(lambda e: e.dma_start(out=gB[:], in_=md["ln2g"].partition_broadcast(128)), gB, True)
            p.dma(lambda e: e.dma_start(out=bB[:], in_=md["ln2b"].partition_broadcast(128)), bB, True)
            cw = sb("cw", [128, 44, 3], F32, dma=True, const=True); cbias = sb("cbias", [128, 44], F32, dma=True, const=True)
            p.dma(lambda e: e.dma_start(out=cw[:], in_=md["cw"]), cw, True)
            p.dma(lambda e: e.dma_start(out=cbias[:], in_=md["cbias"]), cbias, True)
            epsb = sb("epsb2", [128, 1], F32)
            p.op("pool", lambda e: e.memset(epsb[:], LN_EPS), writes=[epsb])
            hT = sb("hTf", [128, 8, 514], BF16, dma=True)
            hres = Ring([sb(f"hres{i}", [128, D], F32, dma=True) for i in range(1)])
            cvr = Ring([sb(f"cv{i}", [128, 512], F32) for i in range(8)])
            actT = sb("actT", [128, 22, 512], BF16)
            opre = sb("opre", [128, D], F32)
            oout = Ring([sb(f"oout{i}", [128, D], F32, dma=True) for i in range(1)])
            lnb = (sb("st2", [128, 2, 6], F32), sb("mv2", [128, 2], F32), sb("rstd2", [128, 1], F32), sb("hn2", [128, D], F32))
            psm = Ring(psum[0:5])
            psh = Ring([(psum[5], 0), (psum[6], 0), (psum[7], 0)])
            for sq in range(nseq):
                for tb in range(S // 512):
                    t0 = tb * 512
                    lo = max(t0 - 1, 0); hi = min(t0 + 513, S)
                    if t0 == 0:
                        p.op("pool", lambda e: e.memset(hT[:, :, 0:1], 0.0), writes=[hT])
                    if t0 + 512 == S:
                        p.op("pool", lambda e: e.memset(hT[:, :, 513:514], 0.0), writes=[hT])
                    p.dma(lambda e: e.dma_start(out=hT[:, :, lo - (t0 - 1):hi - (t0 - 1)], in_=hT_s[sq].rearrange("(k q) t -> q k t", q=128)[:, :, lo:hi]), hT, True)
                    for c in range(22):
                        cvs = []
                        for ch in (c, 22 + c):
                            cs_ = slice(ch * 128, (ch + 1) * 128)
                            pm = psm.next(); ph, hc = psh.next()
                            p.ops("pe", [lambda e, k=k: e.matmul(pm[:], lhsT=wup[:, k, cs_], rhs=hT[:, k, 1:513], start=(k == 0), stop=(k == 7)) for k in range(8)]
                                  + [lambda e, k=k: e.matmul(ph[:, hc:hc + 2], lhsT=wup[:, k, cs_], rhs=hT[:, k, 0:514:513], start=(k == 0), stop=(k == 7)) for k in range(8)],
                                  reads=[wup, hT], writes=[pm, ph])
                            cv = cvr.next()
                            p.op("act", lambda e: e.activation(out=cv[:], in_=pm[:], func=AF.Identity, scale=cw[:, ch, 1:2], bias=cbias[:, ch:ch + 1]),
                                 reads=[pm, cw, cbias], writes=[cv])
                            p.op("dve", lambda e: e.scalar_tensor_tensor(out=cv[:, 1:512], in0=pm[:, 0:511], scalar=cw[:, ch, 0:1], in1=cv[:, 1:512], op0=ALU.mult, op1=ALU.add), reads=[pm, cw, cv], writes=[cv])
                            p.op("dve", lambda e: e.scalar_tensor_tensor(out=cv[:, 0:511], in0=pm[:, 1:512], scalar=cw[:, ch, 2:3], in1=cv[:, 0:511], op0=ALU.mult, op1=ALU.add), reads=[pm, cw, cv], writes=[cv])
                            p.op("dve", lambda e: e.scalar_tensor_tensor(out=cv[:, 0:1], in0=ph[:, hc:hc + 1], scalar=cw[:, ch, 0:1], in1=cv[:, 0:1], op0=ALU.mult, op1=ALU.add), reads=[ph, cw, cv], writes=[cv])
                            p.op("dve", lambda e: e.scalar_tensor_tensor(out=cv[:, 511:512], in0=ph[:, hc + 1:hc + 2], scalar=cw[:, ch, 2:3], in1=cv[:, 511:512], op0=ALU.mult, op1=ALU.add), reads=[ph, cw, cv], writes=[cv])
                            cvs.append(cv)
                        p.op("act", lambda e: e.activation(out=cvs[0][:], in_=cvs[0][:], func=AF.Gelu), reads=[cvs[0]], writes=[cvs[0]])
                        p.op("pool", lambda e: e.tensor_tensor(out=actT[:, c, :], in0=cvs[0][:], in1=cvs[1][:], op=ALU.mult), reads=cvs, writes=[actT])
                    for i in range(4):
                        tok = slice(t0 + i * 128, t0 + (i + 1) * 128)
                        hr = hres.next()
                        p.dma(lambda e: e.dma_start(out=hr[:], in_=h_s[sq, tok, :]), hr, True)
                        for n in range(2):
                            ns = slice(n * 512, (n + 1) * 512)
                            po = psm.next()
                            p.ops("pe", [lambda e, k=k: e.matmul(po[:], lhsT=actT[:, k, i * 128:(i + 1) * 128], rhs=wdn[:, k, ns], start=(k == 0), stop=(k == 21)) for k in range(22)],
                                  reads=[actT, wdn], writes=[po])
                            p.op("dve", lambda e: e.scalar_tensor_tensor(out=opre[:, ns], in0=hr[:, ns], scalar=ALPHA, in1=po[:], op0=ALU.mult, op1=ALU.add),
                                 reads=[hr, po], writes=[opre])
                        oo = oout.next()
                        layer_norm(lnb, opre, gB, bB, oo)
                        p.dma(lambda e: e.dma_start(out=out_d[sq, tok, :], in_=oo[:]), oo, False)
        new_phase()


def _host_inputs(inputs, core, nseq=NSEQ):
    f32 = np.float32
    x = np.ascontiguousarray(inputs["x"][core * nseq:(core + 1) * nseq]).astype(f32)
    pos = np.ascontiguousarray(inputs["positions"][core * nseq:(core + 1) * nseq]).astype(np.int32)
    w_in = np.asarray(inputs["w_in"][0], f32)
    b_in = np.asarray(inputs["b_in"][0], f32)
    sw = np.arange(AW).reshape(-1, 2, 32)[:, ::-1, :].reshape(-1)
    q0, k0, v0, g0 = SSMW, SSMW + AW, SSMW + 2 * AW, SSMW + 3 * AW
    cols = np.concatenate([np.arange(0, SSMW), np.arange(q0, q0 + AW), np.arange(k0, k0 + AW),
                           q0 + sw, k0 + sw, np.arange(g0, g0 + 2 * D)])
    w_fm = np.ascontiguousarray(w_in[:, cols])
    b_fm = np.ascontiguousarray(b_in[cols].reshape(NFM // 128, 128).T)
    w_v = np.ascontiguousarray(w_in[:, v0:v0 + AW])
    b_v = np.ascontiguousarray(b_in[v0:v0 + AW].reshape(1, AW))
    half = 32
    inv_freq = (10000.0 ** (-np.arange(half, dtype=np.float64) * 2.0 / 64)).astype(f32)
    invf = np.zeros((128, 2), f32)
    for pp in range(128):
        invf[pp, 0] = inv_freq[pp % 32] / TWO_PI
        invf[pp, 1] = -TWO_PI if (pp % 64) < 32 else TWO_PI
    def tile_layout(a):
        return np.ascontiguousarray(a.reshape(2, 16, 2, 64).transpose(2, 3, 0, 1).reshape(128, 32)).astype(f32)
    lre_h = tile_layout(np.asarray(inputs["ssm_lam_re"][0], f32))
    lim_h = tile_layout(np.asarray(inputs["ssm_lam_im"][0], f32))
    ldt_h = tile_layout(np.broadcast_to(np.asarray(inputs["ssm_log_dt"][0], f32)[:, :, None], (2, 32, 64)).copy())

    def bz(b):
        o = np.zeros((128, 32, 128), f32)
        b = np.asarray(b, f32)
        for dr in range(2):
            for gp in range(16):
                for gl in range(2):
                    c0 = (gp % 4) * 32 + gl * 16
                    o[gl * 64:(gl + 1) * 64, dr * 16 + gp, c0:c0 + 16] = b[dr, 2 * gp + gl]
        return o

    def cb(c):
        o = np.zeros((32, 32, 128), f32)
        c = np.asarray(c, f32)
        for dr in range(2):
            for gp in range(16):
                for gl in range(2):
                    o[gl * 16:(gl + 1) * 16, dr * 16 + gp, gl * 64:(gl + 1) * 64] = c[dr, 2 * gp + gl]
        return o
    ssm = {"lre_h": lre_h, "lim_h": lim_h, "ldt_h": ldt_h,
           "bzr_h": bz(inputs["ssm_b_re"][0]), "bzi_h": bz(inputs["ssm_b_im"][0]),
           "cbr_h": cb(inputs["ssm_c_re"][0]), "cbi_h": cb(inputs["ssm_c_im"][0]),
           "dsk_h": np.ascontiguousarray(np.asarray(inputs["ssm_d"][0], f32).reshape(4, 128).T),
           "iota_h": np.arange(S, dtype=f32).reshape(1, S)}
    d = {"x": x, "pos": pos, "ident": np.eye(128, dtype=f32), "invf": invf,
         "w_in_fm": w_fm, "b_fm": b_fm, "w_v": w_v, "b_v": b_v}
    d.update(ssm)
    ii = np.arange(128)[:, None]; jj = np.arange(128)[None, :]
    maskb = np.concatenate([np.where(ii >= jj, 0.0, -30000.0), np.where(ii <= jj, 0.0, -30000.0)], axis=1).astype(f32)
    ones3 = np.zeros((128, 3, 64), f32)
    ones3[:, 0, :] = 1.0; ones3[64:, 1, :] = 1.0; ones3[:64, 2, :] = 1.0
    g = lambda n: np.ascontiguousarray(np.asarray(inputs[n][0], f32))
    cwh = np.ascontiguousarray(g("conv_w").reshape(3, 44, 128).transpose(2, 1, 0))
    cbh = np.ascontiguousarray(g("conv_b").reshape(44, 128).T)
    d.update({"maskb_h": maskb, "ones3_h": ones3, "wgv_h": g("w_glu_v"), "wgg_h": g("w_glu_g"), "wab_h": g("w_attn_br"), "wo_h": g("w_out"),
              "ln1g_h": g("ln1_g").reshape(1, D), "ln1b_h": g("ln1_b").reshape(1, D), "ln2g_h": g("ln2_g").reshape(1, D), "ln2b_h": g("ln2_b").reshape(1, D),
              "wup_h": g("w_up"), "wdn_h": g("w_down"), "cw_h": cwh, "cb_h": cbh})
    return d


def kernel(**inputs):
    nc = build()
    in_maps = [_host_inputs(inputs, c) for c in range(NCORES)]
    res = run_bass_kernel_spmd(nc, in_maps, core_ids=list(range(NCORES)))
    out = np.concatenate([r["out"] for r in res.results], axis=0)
    return out.astype(np.float32)
```

```python
import math
from contextlib import ExitStack

import numpy as np
import concourse.bass as bass
import concourse.mybir as mybir
from concourse.bass_utils import run_bass_kernel_spmd

F32 = mybir.dt.float32
BF16 = mybir.dt.bfloat16
I32 = mybir.dt.int32
AF = mybir.ActivationFunctionType
ALU = mybir.AluOpType
AX = mybir.AxisListType

S = 4096
D = 1024
NCORES = 8
NSEQ = 2
SSMW = 512
AW = 768
DFF = 2816
NFM = 5632
ALPHA = 2.0 ** 0.25
LN_EPS = 1e-5
TWO_PI = 2.0 * math.pi
DIL = (1, 4, 16)
KPAD = 1024


class Buf:
    __slots__ = ("t", "w", "r", "dsem", "const")

    def __init__(self, t, dsem=None, const=False):
        self.t = t
        self.w = None
        self.r = {}
        self.dsem = dsem
        self.const = const

    def __getitem__(self, k):
        return self.t[k]


class Prog:
    ENG = ("pe", "act", "dve", "pool", "sp")

    def __init__(self, nc, es, n_dsem=72):
        self.nc = nc
        self.engobj = {'pe': nc.tensor, 'act': nc.scalar, 'dve': nc.vector, 'pool': nc.gpsimd, 'sp': nc.sync}
        self.ninst = 0
        self.stopped = False
        self.esem = {e: es.enter_context(nc.semaphore("es_" + e)) for e in ("pe", "act", "dve", "pool")}
        self.ecount = {e: 0 for e in self.esem}
        self.dsems = [es.enter_context(nc.semaphore(f"ds{i}")) for i in range(n_dsem)]
        self.dcount = {id(s): 0 for s in self.dsems}
        self.dnext = 0
        self.waited = {e: {} for e in self.ENG}
        self.semobj = {}
        for s in list(self.esem.values()) + self.dsems:
            self.semobj[id(s)] = s

    def buf(self, t, dma=False, const=False):
        ds = None
        if dma:
            assert self.dnext < len(self.dsems), "out of DMA semaphores in this phase"
            ds = self.dsems[self.dnext]
            self.dnext += 1
        return Buf(t, ds, const)

    def _deps(self, reads, writes):
        deps = {}

        def add(ev):
            if ev is None:
                return
            k, v = ev
            if deps.get(k, 0) < v:
                deps[k] = v
        for b in reads:
            add(b.w)
        for b in writes:
            add(b.w)
            for k, v in b.r.items():
                add((k, v))
        return deps

    def _record(self, ev, reads, writes):
        for b in writes:
            b.w = ev
            b.r = {}
        for b in reads:
            if b.const:
                continue
            if b.r.get(ev[0], 0) < ev[1]:
                b.r[ev[0]] = ev[1]

    def _emit(self, eng, deps, fn, inc):
        e = self.engobj[eng]
        wd = self.waited[eng]
        own = id(self.esem[eng]) if eng in self.esem else None
        for k, v in deps.items():
            if eng == "pe" and k == own:
                continue
            if wd.get(k, 0) >= v:
                continue
            wd[k] = v
            e.wait_ge(self.semobj[k], v)
        if fn is None:
            return
        ins = fn(e)
        if inc is not None:
            ins.then_inc(inc[0], inc[1])
        self.ninst += 1

    def op(self, eng, fn, reads=(), writes=()):
        if self.stopped:
            return None
        deps = self._deps(reads, writes)
        self.ecount[eng] += 1
        sem = self.esem[eng]
        ev = (id(sem), self.ecount[eng])
        self._emit(eng, deps, fn, (sem, 1))
        self._record(ev, reads, writes)
        return ev

    def ops(self, eng, fns, reads=(), writes=()):
        assert eng == "pe"
        if self.stopped:
            return None
        deps = self._deps(reads, writes)
        for fn in fns[:-1]:
            self._emit(eng, deps, fn, None)
            deps = {}
        self.ecount[eng] += 1
        sem = self.esem[eng]
        ev = (id(sem), self.ecount[eng])
        self._emit(eng, deps, fns[-1], (sem, 1))
        self._record(ev, reads, writes)
        return ev

    def dma(self, fn, sb, load, reads=(), writes=(), q="sp"):
        if self.stopped:
            return None
        reads = list(reads)
        writes = list(writes)
        if load:
            writes.append(sb)
        else:
            reads.append(sb)
        deps = self._deps(reads, writes)
        sem = sb.dsem
        assert sem is not None
        self.dcount[id(sem)] += 16
        ev = (id(sem), self.dcount[id(sem)])
        self._emit(q, deps, fn, (sem, 16))
        self._record(ev, reads, writes)
        return ev

    def dma_group(self, fns, sb, load, reads=(), writes=(), q="sp"):
        if self.stopped:
            return None
        reads = list(reads)
        writes = list(writes)
        if load:
            writes.append(sb)
        else:
            reads.append(sb)
        deps = self._deps(reads, writes)
        sem = sb.dsem
        ev = None
        for fn in fns:
            self.dcount[id(sem)] += 16
            ev = (id(sem), self.dcount[id(sem)])
            self._emit(q, deps, fn, (sem, 16))
            deps = {}
        self._record(ev, reads, writes)
        return ev

    def barrier(self):
        allev = {}
        for e, s in self.esem.items():
            if self.ecount[e]:
                allev[id(s)] = self.ecount[e]
        for s in self.dsems:
            if self.dcount[id(s)]:
                allev[id(s)] = self.dcount[id(s)]
        for eng in self.ENG:
            self._emit(eng, allev, None, None)
        self.dnext = 0

    def emit(self):
        pass


class StopBuild(Exception):
    pass


class Ring:
    def __init__(self, bufs):
        self.bufs = bufs
        self.i = 0

    def next(self):
        b = self.bufs[self.i % len(self.bufs)]
        self.i += 1
        return b


def build(nseq=NSEQ, debug=False, stop_after=None):
    nc = bass.Bass("TRN2", target_bir_lowering=False)

    def din(name, shape, dt=F32):
        return nc.dram_tensor(name, list(shape), dt, kind="ExternalInput").ap()

    dbg_kind = "ExternalOutput" if debug else "Internal"

    def dscr(name, shape, dt):
        return nc.dram_tensor(name, list(shape), dt, kind=dbg_kind).ap()

    x_d = din("x", [nseq, S, D])
    pos_d = din("pos", [nseq, S], I32)
    ident_d = din("ident", [128, 128])
    invf_d = din("invf", [128, 2])
    w_in_d = din("w_in_fm", [D, NFM])
    b_fm_d = din("b_fm", [128, NFM // 128])
    w_v_d = din("w_v", [D, AW])
    b_v_d = din("b_v", [1, AW])
    out_d = nc.dram_tensor("out", [nseq, S, D], F32, kind="ExternalOutput").ap()
    ssm_d = dict(
        lre=din("lre_h", [128, 32]), lim=din("lim_h", [128, 32]), ldt=din("ldt_h", [128, 32]),
        bzr=din("bzr_h", [128, 32, 128]), bzi=din("bzi_h", [128, 32, 128]),
        cbr=din("cbr_h", [32, 32, 128]), cbi=din("cbi_h", [32, 32, 128]),
        dsk=din("dsk_h", [128, 4]), iota=din("iota_h", [1, S]))
    zT_s = dscr("zT_s", [nseq, SSMW, S], BF16)
    aT_s = dscr("aT_s", [nseq, 256, S], BF16)
    h_s = dscr("h_s", [nseq, S, D], F32)
    hT_s = dscr("hT_s", [nseq, D, S], BF16)
    md = dict(maskb=din("maskb_h", [128, 256]), ones3=din("ones3_h", [128, 3, 64]),
              wgv=din("wgv_h", [512, D]), wgg=din("wgg_h", [512, D]), wab=din("wab_h", [256, D]), wo=din("wo_h", [D, D]),
              ln1g=din("ln1g_h", [1, D]), ln1b=din("ln1b_h", [1, D]), ln2g=din("ln2g_h", [1, D]), ln2b=din("ln2b_h", [1, D]),
              wup=din("wup_h", [D, 2 * DFF]), wdn=din("wdn_h", [DFF, D]), cw=din("cw_h", [128, 44, 3]), cbias=din("cb_h", [128, 44]),
              aT_s=aT_s, h_s=h_s, hT_s=hT_s)

    xT_s = dscr("xT_s", [nseq, D, S], BF16)
    uT_s = dscr("uT_s", [nseq, SSMW, S], BF16)
    qT_s = dscr("qT_s", [nseq, AW, S], BF16)
    kT_s = dscr("kT_s", [nseq, AW, S], BF16)
    gT_s = dscr("gT_s", [nseq, 2 * D, S], BF16)
    NBLK = [d * (S // d // 128 + 1) for d in DIL]
    v_s = [dscr(f"v_s{g}", [nseq, 128, NBLK[g], 256], BF16) for g in range(3)]

    with ExitStack() as es0:
        p = Prog(nc, es0)
        psum = [p.buf(es0.enter_context(nc.psum_tensor(f"ps{i}", [128, 512], F32))) for i in range(8)]
        ident = p.buf(es0.enter_context(nc.sbuf_tensor("ident_sb", [128, 128], F32)), dma=True, const=True)
        p.dma(lambda e: e.dma_start(out=ident[:], in_=ident_d), ident, True)
        p.dnext = 1
        wbf = {}

        def cast_w(key, src, R, C):
            dst = nc.dram_tensor(key + "_bf", [R, C], BF16, kind="Internal").ap()
            pb = p.buf(None, dma=True)
            p.dma_group([lambda e, r0=r0: e.dma_start(out=dst[r0:min(r0 + 128, R), :], in_=src[r0:min(r0 + 128, R), :], max_dma_last_dim=4096)
                         for r0 in range(0, R, 128)], pb, True, q="pool")
            wbf[key] = (dst, pb)
        cast_w("w_in", w_in_d, D, NFM)
        cast_w("w_v", w_v_d, D, AW)
        cast_w("wgv", md["wgv"], SSMW, D)
        cast_w("wgg", md["wgg"], SSMW, D)
        cast_w("wab", md["wab"], 256, D)
        cast_w("wo", md["wo"], D, D)
        cast_w("wup", md["wup"], D, 2 * DFF)
        cast_w("wdn", md["wdn"], DFF, D)
        NRES = p.dnext

        def new_phase():
            p.barrier()
            p.dnext = NRES

        def stop(tag):
            if stop_after == tag:
                p.stopped = True

        try:
            _phases(nc, p, psum, ident, nseq, locals_d=dict(x_d=x_d, pos_d=pos_d, invf_d=invf_d, w_in_d=w_in_d, b_fm_d=b_fm_d, w_v_d=w_v_d, b_v_d=b_v_d, out_d=out_d, xT_s=xT_s, uT_s=uT_s, qT_s=qT_s, kT_s=kT_s, gT_s=gT_s, v_s=v_s, NBLK=NBLK, ssm_d=ssm_d, zT_s=zT_s, md=md, wbf=wbf), new_phase=new_phase, stop=stop)
        except StopBuild:
            pass
        p.stopped = False
        p.barrier()
    print('instructions', p.ninst)
    return nc


def _phases(nc, p, psum, ident, nseq, locals_d, new_phase, stop):
    globals_ = locals_d
    x_d = globals_['x_d']; pos_d = globals_['pos_d']; invf_d = globals_['invf_d']; w_in_d = globals_['w_in_d']; b_fm_d = globals_['b_fm_d']
    w_v_d = globals_['w_v_d']; b_v_d = globals_['b_v_d']; out_d = globals_['out_d']; xT_s = globals_['xT_s']; uT_s = globals_['uT_s']
    qT_s = globals_['qT_s']; kT_s = globals_['kT_s']; gT_s = globals_['gT_s']; v_s = globals_['v_s']; NBLK = globals_['NBLK']
    ssm_d = globals_['ssm_d']; zT_s = globals_['zT_s']; md = globals_['md']
    aT_s = md['aT_s']; h_s = md['h_s']; hT_s = md['hT_s']; wbf = globals_['wbf']

    def load_wbf(wdst, key, K):
        src, pb = wbf[key]
        p.dma_group([lambda e, k=k: e.dma_start(out=wdst[:, k, :], in_=src[k * 128:(k + 1) * 128, :]) for k in range(K)], wdst, True, reads=[pb])
        wdst.const = True
    if True:

        with ExitStack() as es:
            def sb(name, shape, dt, dma=False, const=False):
                return p.buf(es.enter_context(nc.sbuf_tensor(name, list(shape), dt)), dma=dma, const=const)

            wA = sb("wA", [128, 8, NFM], BF16, dma=True)
            bfm = sb("bfm", [128, NFM // 128], F32, dma=True)
            invf = sb("invf_sb", [128, 2], F32, dma=True)
            p.dma(lambda e: e.dma_start(out=bfm[:], in_=b_fm_d), bfm, True)
            p.dma(lambda e: e.dma_start(out=invf[:], in_=invf_d), invf, True)
            load_wbf(wA, 'w_in', 8)
            stop('A0')

            cosT = sb("cosT", [128, S], F32)
            sinT = sb("sinT", [128, S], F32)
            posi = sb("posi", [128, 1024], I32, dma=True)
            tur = sb("tur", [128, 1024], F32)
            turi = sb("turi", [128, 1024], I32)
            xs = [sb(f"xs{i}", [128, D], F32, dma=True) for i in range(4)]
            xT = Ring([sb(f"xT{j}", [128, 8, 512], BF16, dma=True) for j in range(2)])
            ev_bf = Ring([sb(f"evbf{j}", [128, 512], BF16, dma=True) for j in range(8)])
            rt = Ring([sb(f"rt{j}", [128, 512], F32) for j in range(4)])
            psr = Ring(psum)

            for sq in range(nseq):
                for c in range(S // 1024):
                    cs = slice(c * 1024, (c + 1) * 1024)
                    p.dma(lambda e, cs=cs: e.dma_start(out=posi[:], in_=pos_d[sq:sq + 1, cs].partition_broadcast(128)), posi, True)
                    for (tab, addc, scol) in ((sinT, 0.0, 1), (cosT, 0.25, None)):
                        p.op("dve", lambda e: e.tensor_copy(out=tur[:], in_=posi[:]), reads=[posi], writes=[tur])
                        p.op("dve", lambda e, addc=addc: e.tensor_scalar(out=tur[:], in0=tur[:], scalar1=invf[:, 0:1], scalar2=addc,
                                                                          op0=ALU.mult, op1=ALU.add), reads=[tur, invf], writes=[tur])
                        p.op("dve", lambda e: e.tensor_copy(out=turi[:], in_=tur[:]), reads=[tur], writes=[turi])
                        p.op("dve", lambda e: e.tensor_tensor(out=tur[:], in0=tur[:], in1=turi[:], op=ALU.subtract),
                             reads=[tur, turi], writes=[tur])
                        if scol is not None:
                            p.op("act", lambda e, tab=tab, cs=cs: e.activation(out=tab[:, cs], in_=tur[:], func=AF.Sin, scale=invf[:, 1:2]),
                                 reads=[tur, invf], writes=[tab])
                        else:
                            p.op("act", lambda e, tab=tab, cs=cs: e.activation(out=tab[:, cs], in_=tur[:], func=AF.Sin, scale=TWO_PI),
                                 reads=[tur], writes=[tab])
                stop('A1')
                def load_x(tb_):
                    for i in range(4):
                        p.dma(lambda e, i=i: e.dma_start(out=xs[i][:], in_=x_d[sq, tb_ * 512 + i * 128:tb_ * 512 + (i + 1) * 128, :]), xs[i], True)
                load_x(0)
                for tb in range(S // 512):
                    t0 = tb * 512
                    ts = slice(t0, t0 + 512)
                    xtile = xs
                    xTb = xT.next()
                    for k in range(8):
                        ps = psr.next()
                        p.ops("pe", [lambda e, ps=ps, i=i, k=k: e.transpose(out=ps[:, i * 128:(i + 1) * 128],
                                                                           in_=xtile[i][:, k * 128:(k + 1) * 128], identity=ident[:])
                                     for i in range(4)], reads=xtile + [ident], writes=[ps])
                        if k % 2 == 0:
                            p.op("act", lambda e, ps=ps, k=k: e.copy(out=xTb[:, k, :], in_=ps[:]), reads=[ps], writes=[xTb])
                        else:
                            p.op("dve", lambda e, ps=ps, k=k: e.tensor_copy(out=xTb[:, k, :], in_=ps[:]), reads=[ps], writes=[xTb])
                    if tb + 1 < S // 512:
                        load_x(tb + 1)
                    p.dma(lambda e: e.dma_start(out=xT_s[sq].rearrange("(k q) t -> q k t", q=128)[:, :, ts], in_=xTb[:]), xTb, False)

                    def proj(fo):
                        ps = psr.next()
                        p.ops("pe", [lambda e, ps=ps, k=k: e.matmul(ps[:], lhsT=wA[:, k, fo * 128:(fo + 1) * 128], rhs=xTb[:, k, :],
                                                                      start=(k == 0), stop=(k == 7)) for k in range(8)],
                              reads=[wA, xTb], writes=[ps])
                        return ps

                    for fo in range(4):
                        ps = proj(fo)
                        o = ev_bf.next()
                        p.op("act", lambda e, ps=ps, o=o, fo=fo: e.activation(out=o[:], in_=ps[:], func=AF.Identity, bias=bfm[:, fo:fo + 1]),
                             reads=[ps, bfm], writes=[o])
                        p.dma(lambda e, o=o, fo=fo: e.dma_start(out=uT_s[sq, fo * 128:(fo + 1) * 128, ts], in_=o[:]), o, False)
                    for which, dst in ((0, qT_s), (1, kT_s)):
                        for c in range(6):
                            fo = 4 + which * 6 + c
                            psa = proj(fo)
                            psb = proj(fo + 12)
                            t1 = rt.next()
                            t2 = rt.next()
                            p.op("dve", lambda e, psa=psa, t1=t1, fo=fo: e.scalar_tensor_tensor(
                                out=t1[:], in0=psa[:], scalar=bfm[:, fo:fo + 1], in1=cosT[:, ts], op0=ALU.add, op1=ALU.mult),
                                reads=[psa, bfm, cosT], writes=[t1])
                            p.op("dve", lambda e, psb=psb, t2=t2, fo=fo: e.scalar_tensor_tensor(
                                out=t2[:], in0=psb[:], scalar=bfm[:, fo + 12:fo + 13], in1=sinT[:, ts], op0=ALU.add, op1=ALU.mult),
                                reads=[psb, bfm, sinT], writes=[t2])
                            o = ev_bf.next()
                            p.op("pool", lambda e, o=o, t1=t1, t2=t2: e.tensor_tensor(out=o[:], in0=t1[:], in1=t2[:], op=ALU.add),
                                 reads=[t1, t2], writes=[o])
                            p.dma(lambda e, o=o, c=c, dst=dst: e.dma_start(out=dst[sq, c * 128:(c + 1) * 128, ts], in_=o[:]), o, False)
                    for c in range(16):
                        fo = 28 + c
                        ps = proj(fo)
                        o = ev_bf.next()
                        p.op("act", lambda e, ps=ps, o=o, fo=fo: e.activation(out=o[:], in_=ps[:], func=AF.Sigmoid, bias=bfm[:, fo:fo + 1]),
                             reads=[ps, bfm], writes=[o])
                        p.dma(lambda e, o=o, c=c: e.dma_start(out=gT_s[sq, c * 128:(c + 1) * 128, ts], in_=o[:]), o, False)
                    stop(f'A2_{tb}')
        new_phase()
        stop('A')

        with ExitStack() as es:
            def sb(name, shape, dt, dma=False, const=False):
                return p.buf(es.enter_context(nc.sbuf_tensor(name, list(shape), dt)), dma=dma, const=const)

            wV = sb("wV", [128, 8, AW], BF16, dma=True)
            load_wbf(wV, 'w_v', 8)
            bv = sb("bv", [128, AW], F32, dma=True, const=True)
            p.dma(lambda e: e.dma_start(out=bv[:], in_=b_v_d.partition_broadcast(128)), bv, True)
            xTf = sb("xTf", [128, 8, S], BF16, dma=True)
            VCH = 12
            vring = Ring([sb(f"vstg{j}", [128, VCH, 256], BF16, dma=True) for j in range(2)])
            psr = Ring(psum)
            for sq in range(nseq):
                p.dma(lambda e: e.dma_start(out=xTf[:], in_=xT_s[sq].rearrange("(k q) t -> q k t", q=128)), xTf, True)
                for g in range(3):
                    d = DIL[g]
                    L = S // d
                    nb = L // 128 + 1
                    blocks = [(r, m) for r in range(d) for m in range(nb)]
                    for c0 in range(0, len(blocks), VCH):
                        chunk = blocks[c0:c0 + VCH]
                        stg = vring.next()
                        p.op("pool", lambda e, stg=stg: e.memset(stg[:], 0.0), writes=[stg])
                        for j, (r, m) in enumerate(chunk):
                            lo = 64 + 128 * (m - 1)
                            i0 = max(0, -lo)
                            i1 = min(128, L - lo)
                            M = i1 - i0
                            tok0 = r + d * (lo + i0)
                            ps = psr.next()
                            p.ops("pe", [lambda e, ps=ps, k=k, tok0=tok0, M=M, i0=i0, d=d, g=g: e.matmul(
                                ps[i0:i0 + M, 0:256], lhsT=xTf[:, k, tok0:tok0 + d * (M - 1) + 1:d], rhs=wV[:, k, g * 256:(g + 1) * 256],
                                start=(k == 0), stop=(k == 7)) for k in range(8)], reads=[xTf, wV], writes=[ps])
                            p.op("dve", lambda e, ps=ps, stg=stg, j=j, i0=i0, M=M, g=g: e.tensor_tensor(
                                out=stg[i0:i0 + M, j, :], in0=ps[i0:i0 + M, 0:256], in1=bv[i0:i0 + M, g * 256:(g + 1) * 256], op=ALU.add),
                                reads=[ps, bv], writes=[stg])
                        p.dma(lambda e, stg=stg, c0=c0, n=len(chunk), g=g: e.dma_start(out=v_s[g][sq, :, c0:c0 + n, :], in_=stg[:, 0:n, :]), stg, False)
                        stop(f'V{g}_{c0}')
                    stop(f'V{g}')
        new_phase()

        with ExitStack() as es:
            def sb(name, shape, dt, dma=False, const=False):
                return p.buf(es.enter_context(nc.sbuf_tensor(name, list(shape), dt)), dma=dma, const=const)

            NT = 32
            NCH = S // 8
            lre = sb("lre", [128, NT], F32, dma=True); lim = sb("lim", [128, NT], F32, dma=True); ldt = sb("ldt", [128, NT], F32, dma=True)
            p.dma(lambda e: e.dma_start(out=lre[:], in_=ssm_d["lre"]), lre, True)
            p.dma(lambda e: e.dma_start(out=lim[:], in_=ssm_d["lim"]), lim, True)
            p.dma(lambda e: e.dma_start(out=ldt[:], in_=ssm_d["ldt"]), ldt, True)
            dsk = sb("dsk", [128, 4], F32, dma=True)
            p.dma(lambda e: e.dma_start(out=dsk[:], in_=ssm_d["dsk"]), dsk, True)
            tI = sb("tI", [128, NCH], F32, dma=True, const=True)
            p.dma(lambda e: e.dma_start(out=tI[:], in_=ssm_d["iota"][:, 0:NCH].partition_broadcast(128)), tI, True)
            sm = {n: sb("sm_" + n, [128, NT], F32) for n in
                  ("dt", "xr", "xi", "rho", "th", "t0", "t1", "f", "sinx", "cosx", "sinh", "em1", "am1", "abi", "den", "kr", "ki", "u0", "u1",
                   "rho8", "th8", "pm", "pc", "ps")}
            smi = sb("smi", [128, NT], I32)
            pwr = sb("pwr", [128, 16, NT], F32); pwi = sb("pwi", [128, 16, NT], F32); npwi = sb("npwi", [128, 16, NT], F32)

            def V(fn, reads, writes):
                return p.op("dve", fn, reads=reads, writes=writes)

            def A(fn, reads, writes):
                return p.op("act", fn, reads=reads, writes=writes)

            def tt(o, a, b, op):
                V(lambda e: e.tensor_tensor(out=o[:], in0=a[:], in1=b[:], op=op), [a, b], [o])

            def tsc(o, a, s1, op0, s2=None, op1=None):
                if op1 is None:
                    V(lambda e: e.tensor_scalar(out=o[:], in0=a[:], scalar1=s1, scalar2=None, op0=op0), [a], [o])
                else:
                    V(lambda e: e.tensor_scalar(out=o[:], in0=a[:], scalar1=s1, scalar2=s2, op0=op0, op1=op1), [a], [o])

            def frac_sin(o, turns_src, mul, add):
                tsc(sm["t0"], turns_src, mul, ALU.mult, add, ALU.add)
                V(lambda e: e.tensor_copy(out=smi[:], in_=sm["t0"][:]), [sm["t0"]], [smi])
                tt(sm["f"], sm["t0"], smi, ALU.subtract)
                A(lambda e: e.activation(out=o[:], in_=sm["f"][:], func=AF.Sin, scale=TWO_PI), [sm["f"]], [o])

            A(lambda e: e.activation(out=sm["dt"][:], in_=ldt[:], func=AF.Exp), [ldt], [sm["dt"]])
            tt(sm["xr"], lre, sm["dt"], ALU.mult)
            tt(sm["xi"], lim, sm["dt"], ALU.mult)
            A(lambda e: e.activation(out=sm["rho"][:], in_=sm["xr"][:], func=AF.Exp), [sm["xr"]], [sm["rho"]])
            A(lambda e: e.activation(out=sm["rho8"][:], in_=sm["xr"][:], func=AF.Exp, scale=8.0), [sm["xr"]], [sm["rho8"]])
            tsc(sm["th"], sm["xi"], 1.0 / TWO_PI, ALU.mult)
            tsc(sm["th8"], sm["th"], 8.0, ALU.mult)
            frac_sin(sm["sinx"], sm["th"], 1.0, 0.0)
            frac_sin(sm["cosx"], sm["th"], 1.0, 0.25)
            frac_sin(sm["sinh"], sm["th"], 0.5, 0.0)
            tsc(sm["em1"], sm["xr"], 0.2, ALU.mult, 1.0, ALU.add)
            for cdiv in (0.25, 1.0 / 3.0, 0.5):
                tt(sm["em1"], sm["em1"], sm["xr"], ALU.mult)
                tsc(sm["em1"], sm["em1"], cdiv, ALU.mult, 1.0, ALU.add)
            tt(sm["em1"], sm["em1"], sm["xr"], ALU.mult)
            tt(sm["am1"], sm["em1"], sm["cosx"], ALU.mult)
            tt(sm["u0"], sm["sinh"], sm["sinh"], ALU.mult)
            V(lambda e: e.scalar_tensor_tensor(out=sm["am1"][:], in0=sm["u0"][:], scalar=-2.0, in1=sm["am1"][:], op0=ALU.mult, op1=ALU.add),
              [sm["u0"], sm["am1"]], [sm["am1"]])
            tt(sm["abi"], sm["rho"], sm["sinx"], ALU.mult)
            tt(sm["den"], lre, lre, ALU.mult)
            tt(sm["u0"], lim, lim, ALU.mult)
            tt(sm["den"], sm["den"], sm["u0"], ALU.add)
            V(lambda e: e.reciprocal(out=sm["den"][:], in_=sm["den"][:]), [sm["den"]], [sm["den"]])
            tt(sm["u0"], sm["am1"], lre, ALU.mult)
            tt(sm["u1"], sm["abi"], lim, ALU.mult)
            tt(sm["u0"], sm["u0"], sm["u1"], ALU.add)
            tt(sm["kr"], sm["u0"], sm["den"], ALU.mult)
            tt(sm["u0"], sm["abi"], lre, ALU.mult)
            tt(sm["u1"], sm["am1"], lim, ALU.mult)
            tt(sm["u0"], sm["u0"], sm["u1"], ALU.subtract)
            tt(sm["ki"], sm["u0"], sm["den"], ALU.mult)
            tsc(sm["t1"], sm["ki"], -1.0, ALU.mult)
            nki = sb("nki", [128, NT], F32)
            V(lambda e: e.tensor_copy(out=nki[:], in_=sm["t1"][:]), [sm["t1"]], [nki])
            for jj in range(16):
                jv = float(jj - 7)
                A(lambda e, jv=jv: e.activation(out=sm["pm"][:], in_=sm["xr"][:], func=AF.Exp, scale=jv), [sm["xr"]], [sm["pm"]])
                frac_sin(sm["ps"], sm["th"], jv, 0.0)
                frac_sin(sm["pc"], sm["th"], jv, 0.25)
                V(lambda e, jj=jj: e.tensor_tensor(out=pwr[:, jj, :], in0=sm["pm"][:], in1=sm["pc"][:], op=ALU.mult), [sm["pm"], sm["pc"]], [pwr])
                V(lambda e, jj=jj: e.tensor_tensor(out=pwi[:, jj, :], in0=sm["pm"][:], in1=sm["ps"][:], op=ALU.mult), [sm["pm"], sm["ps"]], [pwi])
            V(lambda e: e.tensor_scalar(out=npwi[:], in0=pwi[:], scalar1=-1.0, scalar2=None, op0=ALU.mult), [pwi], [npwi])
            for b_ in (pwr, pwi, npwi, sm["kr"], sm["ki"], nki, sm["rho8"], sm["th8"]):
                b_.const = True

            Dd = sb("Dd", [128, 4, 128], BF16)
            for q in range(4):
                V(lambda e, q=q: e.tensor_scalar(out=Dd[:, q, :], in0=ident[:], scalar1=dsk[:, q:q + 1], scalar2=None, op0=ALU.mult),
                  [ident, dsk], [Dd])
            Dd.const = True

            NSET = 2
            bz = [[sb(f"bz{i}_{k}", [128, 2, 128], F32, dma=True) for k in range(2)] for i in range(NSET)]
            cbt = [[sb(f"cbt{i}_{k}", [32, 2, 128], F32, dma=True) for k in range(2)] for i in range(NSET)]
            Bz = [[sb(f"Bz{i}_{k}", [128, 2, 128], F32) for k in range(2)] for i in range(NSET)]
            CT = [[sb(f"CT{i}_{k}", [128, 2, 64], F32) for k in range(2)] for i in range(NSET)]
            XT = [[sb(f"XT{i}_{k}", [128, 8, 2, 128], BF16) for k in range(2)] for i in range(NSET)]
            KT = [[sb(f"KT{i}_{k}", [128, 8, 64], BF16) for k in range(2)] for i in range(NSET)]
            LY = [[sb(f"LY{i}_{k}", [128, 8, 2, 64], BF16) for k in range(2)] for i in range(NSET)]
            cosN = [[sb(f"cosN{i}_{k}", [128, NCH], F32) for k in range(2)] for i in range(NSET)]
            sinN = [[sb(f"sinN{i}_{k}", [128, NCH], F32) for k in range(2)] for i in range(NSET)]
            rho8T = [[sb(f"rho8T{i}_{k}", [128, NCH], F32) for k in range(2)] for i in range(NSET)]
            for i in range(NSET):
                for k in range(2):
                    p.op("pool", lambda e, i=i, k=k: e.memset(CT[i][k][:], 0.0), writes=[CT[i][k]])
            xtmp = Ring([sb(f"xtmp{i}", [128, 2, 128], F32) for i in range(3)])
            lyf = Ring([sb(f"lyf{i}", [128, 2, 64], F32) for i in range(3)])
            turN = sb("turN", [128, NCH], F32); turNi = sb("turNi", [128, NCH], I32)
            uTr = Ring([sb(f"uTc{i}", [128, S], BF16, dma=True) for i in range(2)])
            uDr = Ring([sb(f"uD{i}", [128, 8, NCH], BF16) for i in range(2)])
            tmpr = Ring([sb(f"tmpS{i}", [128, NCH], F32) for i in range(8)])
            wrr = Ring([sb(f"wS{i}", [128, NCH], F32) for i in range(4)])
            Rrr = Ring([sb(f"RS{i}", [128, NCH], F32) for i in range(4)])
            Vrr = Ring([sb(f"VS{i}", [128, NCH], F32) for i in range(4)])
            Zr_ = [Ring([sb(f"ZS{k}_{i}", [128, 2, NCH], BF16) for i in range(2)]) for k in range(2)]
            zor = Ring([sb(f"zo{i}", [128, S], BF16, dma=True) for i in range(2)])
            psT = Ring(psum[4:8])
            psSt = [psum[0:2], psum[2:4]]

            def cmul(o, orow, oi_row, src, sr, si, nsi):
                V(lambda e: e.tensor_scalar(out=o[:, 0, :], in0=src[:, 0, :], scalar1=sr, scalar2=None, op0=ALU.mult), [src], [o])
                V(lambda e: e.scalar_tensor_tensor(out=o[:, 0, :], in0=src[:, 1, :], scalar=nsi, in1=o[:, 0, :], op0=ALU.mult, op1=ALU.add), [src, o], [o])
                V(lambda e: e.tensor_scalar(out=o[:, 1, :], in0=src[:, 1, :], scalar1=sr, scalar2=None, op0=ALU.mult), [src], [o])
                V(lambda e: e.scalar_tensor_tensor(out=o[:, 1, :], in0=src[:, 0, :], scalar=si, in1=o[:, 1, :], op0=ALU.mult, op1=ALU.add), [src, o], [o])

            def prep(gp, st):
                for k in range(2):
                    j = k * 16 + gp
                    p.dma(lambda e: e.dma_start(out=bz[st][k][:, 0, :], in_=ssm_d["bzr"][:, j, :]), bz[st][k], True)
                    p.dma(lambda e: e.dma_start(out=bz[st][k][:, 1, :], in_=ssm_d["bzi"][:, j, :]), bz[st][k], True)
                    p.dma(lambda e: e.dma_start(out=cbt[st][k][:, 0, :], in_=ssm_d["cbr"][:, j, :]), cbt[st][k], True)
                    p.dma(lambda e: e.dma_start(out=cbt[st][k][:, 1, :], in_=ssm_d["cbi"][:, j, :]), cbt[st][k], True)
                    cmul(Bz[st][k], None, None, bz[st][k], sm["kr"][:, j:j + 1], sm["ki"][:, j:j + 1], nki[:, j:j + 1])
                    ps = psT.next()
                    p.ops("pe", [lambda e: e.transpose(out=ps[:, 0:32], in_=cbt[st][k][:, 0, :], identity=ident[0:32, 0:32]),
                                 lambda e: e.transpose(out=ps[:, 32:64], in_=cbt[st][k][:, 1, :], identity=ident[0:32, 0:32])],
                          reads=[cbt[st][k], ident], writes=[ps])
                    A(lambda e: e.copy(out=CT[st][k][:, 0, 32:64], in_=ps[:, 0:32]), [ps], [CT[st][k]])
                    A(lambda e: e.mul(out=CT[st][k][:, 1, 32:64], in_=ps[:, 32:64], mul=-1.0), [ps], [CT[st][k]])
                    for s_ in range(8):
                        if s_ == 0:
                            xs_ = Bz[st][k]
                        else:
                            xs_ = xtmp.next()
                            jj = 7 - s_
                            cmul(xs_, None, None, Bz[st][k], pwr[:, jj, j:j + 1], pwi[:, jj, j:j + 1], npwi[:, jj, j:j + 1])
                        ps = psT.next()
                        p.ops("pe", [lambda e: e.transpose(out=ps[:, 0:128], in_=xs_[:, 0, :], identity=ident[:]),
                                     lambda e: e.transpose(out=ps[:, 128:256], in_=xs_[:, 1, :], identity=ident[:])],
                              reads=[xs_, ident], writes=[ps])
                        A(lambda e: e.copy(out=XT[st][k][:, s_, :, :], in_=ps[:, 0:256].rearrange("p (r c) -> p r c", r=2)), [ps], [XT[st][k]])
                    for tau in range(8):
                        ly = lyf.next()
                        jj = 7 + tau
                        ctr = CT[st][k]
                        V(lambda e: e.tensor_scalar(out=ly[:, 0, :], in0=ctr[:, 0, :], scalar1=pwr[:, jj, j:j + 1], scalar2=None, op0=ALU.mult), [ctr], [ly])
                        V(lambda e: e.scalar_tensor_tensor(out=ly[:, 0, :], in0=ctr[:, 1, :], scalar=pwi[:, jj, j:j + 1], in1=ly[:, 0, :], op0=ALU.mult, op1=ALU.add), [ctr, ly], [ly])
                        V(lambda e: e.tensor_scalar(out=ly[:, 1, :], in0=ctr[:, 1, :], scalar1=pwr[:, jj, j:j + 1], scalar2=None, op0=ALU.mult), [ctr], [ly])
                        V(lambda e: e.scalar_tensor_tensor(out=ly[:, 1, :], in0=ctr[:, 0, :], scalar=npwi[:, jj, j:j + 1], in1=ly[:, 1, :], op0=ALU.mult, op1=ALU.add), [ctr, ly], [ly])
                        A(lambda e: e.copy(out=LY[st][k][:, tau, :, :], in_=ly[:]), [ly], [LY[st][k]])
                        ps = psT.next()
                        p.ops("pe", [lambda e: e.matmul(ps[:, 0:64], lhsT=Bz[st][k][:, 0, :], rhs=ly[:, 0, :], start=True, stop=False),
                                     lambda e: e.matmul(ps[:, 0:64], lhsT=Bz[st][k][:, 1, :], rhs=ly[:, 1, :], start=False, stop=True)],
                              reads=[Bz[st][k], ly], writes=[ps])
                        A(lambda e: e.copy(out=KT[st][k][:, tau, :], in_=ps[:, 0:64]), [ps], [KT[st][k]])
                    for (tab, addc) in ((sinN[st][k], 0.0), (cosN[st][k], 0.25)):
                        V(lambda e: e.tensor_scalar(out=turN[:], in0=tI[:], scalar1=sm["th8"][:, j:j + 1], scalar2=addc, op0=ALU.mult, op1=ALU.add), [tI], [turN])
                        V(lambda e: e.tensor_copy(out=turNi[:], in_=turN[:]), [turN], [turNi])
                        V(lambda e: e.tensor_tensor(out=turN[:], in0=turN[:], in1=turNi[:], op=ALU.subtract), [turN, turNi], [turN])
                        A(lambda e: e.activation(out=tab[:], in_=turN[:], func=AF.Sin, scale=TWO_PI), [turN], [tab])
                    V(lambda e: e.tensor_scalar(out=rho8T[st][k][:], in0=tI[:], scalar1=0.0, scalar2=sm["rho8"][:, j:j + 1], op0=ALU.mult, op1=ALU.add), [tI], [rho8T[st][k]])

            Ssb = Ring([sb(f"Ssb{i}", [128, 4, NCH], F32) for i in range(2)])

            def geom(gp):
                q = gp // 4
                qq = gp % 4
                if qq < 3:
                    return q, slice(32 * qq, 32 * qq + 32), slice(32, 64), slice(32 * qq, 32 * qq + 32), slice(32 * qq, 32 * qq + 32)
                return q, slice(64, 128), slice(0, 64), slice(64, 128), slice(32 * qq, 32 * qq + 32)

            def stageA(gp, st, sq):
                q = gp // 4
                uT = uTr.next()
                p.dma(lambda e: e.dma_start(out=uT[:], in_=uT_s[sq, q * 128:(q + 1) * 128, :]), uT, True)
                uD = uDr.next()
                A(lambda e: e.copy(out=uD[:], in_=uT[:].rearrange("p (n s) -> p s n", s=8)), [uT], [uD])
                ss = Ssb.next()
                for k in range(2):
                    for ri in range(2):
                        pb_ = psSt[k][ri]
                        if k == 0:
                            fns = [lambda e, s_=s_: e.matmul(pb_[:], lhsT=XT[st][k][:, s_, ri, :], rhs=uD[:, s_, :], start=(s_ == 0), stop=(s_ == 7)) for s_ in range(8)]
                        else:
                            fns = [lambda e, s_=s_: e.matmul(pb_[:], lhsT=XT[st][k][:, s_, ri, :], rhs=uD[:, 7 - s_, ::-1], start=(s_ == 0), stop=(s_ == 7)) for s_ in range(8)]
                        p.ops("pe", fns, reads=[XT[st][k], uD], writes=[pb_])
                        A(lambda e: e.copy(out=ss[:, 2 * k + ri, :], in_=pb_[:]), [pb_], [ss])
                return (gp, st, sq, uD, ss)

            def stageB(ctx):
                gp, st, sq, uD, ss = ctx
                T = [[tmpr.next() for _ in range(4)] for k in range(2)]
                for (ti, si, tab) in ((0, 0, cosN), (1, 1, sinN), (2, 1, cosN), (3, 0, sinN)):
                    for k in range(2):
                        V(lambda e, k=k: e.tensor_tensor(out=T[k][ti][:], in0=ss[:, 2 * k + si, :], in1=tab[st][k][:], op=ALU.mult), [ss, tab[st][k]], [T[k][ti]])
                W = [[wrr.next(), wrr.next()] for k in range(2)]
                for k in range(2):
                    V(lambda e, k=k: e.tensor_tensor(out=W[k][0][:], in0=T[k][0][:], in1=T[k][1][:], op=ALU.add), [T[k][0], T[k][1]], [W[k][0]])
                for k in range(2):
                    V(lambda e, k=k: e.tensor_tensor(out=W[k][1][:], in0=T[k][2][:], in1=T[k][3][:], op=ALU.subtract), [T[k][2], T[k][3]], [W[k][1]])
                R = [[Rrr.next(), Rrr.next()] for k in range(2)]
                for ri in range(2):
                    for k in range(2):
                        V(lambda e, k=k, ri=ri: e.tensor_tensor_scan(out=R[k][ri][:], data0=rho8T[st][k][:], data1=W[k][ri][:], initial=0.0, op0=ALU.mult, op1=ALU.add),
                          [rho8T[st][k], W[k][ri]], [R[k][ri]])
                T = [[tmpr.next() for _ in range(4)] for k in range(2)]
                for (ti, si, tab) in ((0, 0, cosN), (1, 1, sinN), (2, 1, cosN), (3, 0, sinN)):
                    for k in range(2):
                        V(lambda e, k=k: e.tensor_tensor(out=T[k][ti][:], in0=R[k][si][:], in1=tab[st][k][:], op=ALU.mult), [R[k][si], tab[st][k]], [T[k][ti]])
                Vv = [[Vrr.next(), Vrr.next()] for k in range(2)]
                for k in range(2):
                    V(lambda e, k=k: e.tensor_tensor(out=Vv[k][0][:], in0=T[k][0][:], in1=T[k][1][:], op=ALU.subtract), [T[k][0], T[k][1]], [Vv[k][0]])
                for k in range(2):
                    V(lambda e, k=k: e.tensor_tensor(out=Vv[k][1][:], in0=T[k][2][:], in1=T[k][3][:], op=ALU.add), [T[k][2], T[k][3]], [Vv[k][1]])
                Z = [Zr_[k].next() for k in range(2)]
                for ri in range(2):
                    for k in range(2):
                        V(lambda e, k=k, ri=ri: e.tensor_tensor(out=Z[k][:, ri, :], in0=Vv[k][ri][:], in1=ss[:, 2 * k + ri, :], op=ALU.subtract), [Vv[k][ri], ss], [Z[k]])
                return ctx + (Z,)

            def stageC(ctx):
                gp, st, sq, uD, ss, Z = ctx
                q, rows, lcs, dds, orow = geom(gp)
                zo = zor.next()
                for tau in range(8):
                    py = psT.next()
                    fns = [
                        lambda e: e.matmul(py[rows, :], lhsT=LY[st][0][:, tau, 0, lcs], rhs=Z[0][:, 0, :], start=True, stop=False),
                        lambda e: e.matmul(py[rows, :], lhsT=LY[st][0][:, tau, 1, lcs], rhs=Z[0][:, 1, :], start=False, stop=False),
                        lambda e: e.matmul(py[rows, :], lhsT=LY[st][1][:, 7 - tau, 0, lcs], rhs=Z[1][:, 0, ::-1], start=False, stop=False),
                        lambda e: e.matmul(py[rows, :], lhsT=LY[st][1][:, 7 - tau, 1, lcs], rhs=Z[1][:, 1, ::-1], start=False, stop=False),
                    ]
                    for s_ in range(0, tau + 1):
                        fns.append(lambda e, s_=s_: e.matmul(py[rows, :], lhsT=KT[st][0][:, tau - s_, lcs], rhs=uD[:, s_, :], start=False, stop=False))
                    for s_ in range(tau, 8):
                        fns.append(lambda e, s_=s_: e.matmul(py[rows, :], lhsT=KT[st][1][:, s_ - tau, lcs], rhs=uD[:, s_, :], start=False, stop=False))
                    fns.append(lambda e: e.matmul(py[rows, :], lhsT=Dd[:, q, dds], rhs=uD[:, tau, :], start=False, stop=True))
                    p.ops("pe", fns, reads=[LY[st][0], LY[st][1], KT[st][0], KT[st][1], Dd, Z[0], Z[1], uD], writes=[py])
                    A(lambda e: e.activation(out=zo[rows, tau:S:8], in_=py[rows, :], func=AF.Gelu), [py], [zo])
                p.dma(lambda e: e.dma_start(out=zT_s[sq, gp * 32:(gp + 1) * 32, :], in_=zo[orow, :]), zo, False)

            runs = [(gp, gp % NSET, sq) for gp in range(16) for sq in range(nseq)]
            prep(0, 0)
            ctxA = stageA(*runs[0])
            for i, (gp, st, sq) in enumerate(runs):
                if sq == 0 and gp + 1 < 16:
                    prep(gp + 1, (gp + 1) % NSET)
                nxt = stageA(*runs[i + 1]) if i + 1 < len(runs) else None
                ctxB = stageB(ctxA)
                stageC(ctxB)
                ctxA = nxt
                if sq == nseq - 1:
                    stop(f'S_gp{gp}')
        new_phase()
        stop('S')

        with ExitStack() as es:
            def sb(name, shape, dt, dma=False, const=False):
                return p.buf(es.enter_context(nc.sbuf_tensor(name, list(shape), dt)), dma=dma, const=const)

            mstage = sb("mstage", [128, 256], F32, dma=True)
            ostage = sb("ostage", [128, 3, 64], F32, dma=True)
            maskB = sb("maskB", [128, 256], BF16); ones3 = sb("ones3", [128, 3, 64], BF16); identb = sb("identb", [128, 128], BF16)
            p.dma(lambda e: e.dma_start(out=mstage[:], in_=md["maskb"]), mstage, True)
            p.dma(lambda e: e.dma_start(out=ostage[:], in_=md["ones3"]), ostage, True)
            p.op("dve", lambda e: e.tensor_copy(out=maskB[:], in_=mstage[:]), reads=[mstage], writes=[maskB])
            p.op("dve", lambda e: e.tensor_copy(out=ones3[:], in_=ostage[:]), reads=[ostage], writes=[ones3])
            p.op("dve", lambda e: e.tensor_copy(out=identb[:], in_=ident[:]), reads=[ident], writes=[identb])
            maskB.const = True; ones3.const = True; identb.const = True
            qTr = Ring([sb(f"qTa{i}", [128, S], BF16, dma=True) for i in range(2)])
            kSr = Ring([sb(f"kSa{i}", [128, S], BF16, dma=True) for i in range(2)])
            qDr = Ring([sb(f"qDa{i}", [128, S], BF16) for i in range(2)])
            kTr = Ring([sb(f"kTa{i}", [128, S + 2 * KPAD], BF16) for i in range(2)])
            vTr = Ring([sb(f"vTa{i}", [128, 48, 128], BF16, dma=True) for i in range(2)])
            acc = sb("acc", [128, 2, S], F32)
            rden = sb("rden", [128, S], F32)
            aTo = sb("aTo", [128, S], BF16, dma=True)
            PTr = Ring([sb(f"PT{i}", [128, 256], BF16) for i in range(6)])
            psS = Ring(psum[0:4]); psO = Ring(psum[4:8])
            SCALE = 64.0 ** -0.5
            for sq in range(nseq):
                for c in range(2):
                    for g in range(3):
                        d = DIL[g]; L = S // d; nb = L // 128 + 1
                        qN = qTr.next(); kS = kSr.next(); qT = qDr.next(); kT = kTr.next(); vT = vTr.next()
                        ch = 2 * g + c
                        LP = L + 128
                        p.dma(lambda e: e.dma_start(out=qN[:], in_=qT_s[sq, ch * 128:(ch + 1) * 128, :]), qN, True)
                        p.dma(lambda e: e.dma_start(out=kS[:], in_=kT_s[sq, ch * 128:(ch + 1) * 128, :]), kS, True)
                        p.op("pool", lambda e: e.memset(kT[:, 0:d * LP], 0.0), writes=[kT])
                        p.op("act", lambda e: e.copy(out=qT[:].rearrange("p (r i) -> p r i", r=d), in_=qN[:].rearrange("p (i r) -> p r i", r=d)), reads=[qN], writes=[qT])
                        p.op("dve", lambda e: e.tensor_copy(out=kT[:, 0:d * LP].rearrange("p (r i) -> p r i", r=d)[:, :, 64:64 + L], in_=kS[:].rearrange("p (i r) -> p r i", r=d)),
                             reads=[kS], writes=[kT])
                        p.dma(lambda e: e.dma_start(out=vT[:, 0:NBLK[g], :], in_=v_s[g][sq, :, :, c * 128:(c + 1) * 128]), vT, True)
                        def stS(r, a, qT=qT, kT=kT, L=L, LP=LP):
                            qcs = slice(r * L + 128 * a, r * L + 128 * a + 128)
                            pts = []
                            for hp in range(2):
                                pb = 64 * hp
                                pS = psS.next()
                                ks = [slice(r * LP + 128 * m, r * LP + 128 * m + 128) for m in (a, a + 1)]
                                p.ops("pe", [
                                    lambda e: e.matmul(pS[:, 0:128], lhsT=kT[pb:pb + 64, ks[0]], rhs=qT[pb:pb + 64, qcs], start=True, stop=False),
                                    lambda e: e.matmul(pS[:, 128:256], lhsT=kT[pb:pb + 64, ks[1]], rhs=qT[pb:pb + 64, qcs], start=False, stop=False),
                                    lambda e: e.matmul(pS[:, 0:256], lhsT=identb[:], rhs=maskB[:], start=False, stop=True),
                                ], reads=[kT, qT, identb, maskB], writes=[pS])
                                PT = PTr.next()
                                p.op("act", lambda e: e.activation(out=PT[:], in_=pS[:, 0:256], func=AF.Exp, scale=SCALE), reads=[pS], writes=[PT])
                                pts.append(PT)
                            return pts

                        def stPV(r, a, pts, vT=vT, d=d, nb=nb, g=g):
                            qsl = slice(r + d * 128 * a, r + d * 128 * a + d * 127 + 1, d)
                            pO = psO.next()
                            fns = []
                            for hp in range(2):
                                pb = 64 * hp
                                PT = pts[hp]
                                o1 = 1 if a == 0 else 0
                                o2 = 2 if a + 1 == nb - 1 else 0
                                b1 = r * nb + a; b2 = r * nb + a + 1
                                fns += [
                                    lambda e, PT=PT, pb=pb, b1=b1, hp=hp: e.matmul(pO[pb:pb + 64, 0:128], lhsT=vT[:, b1, hp * 64:(hp + 1) * 64], rhs=PT[:, 0:128], start=True, stop=False),
                                    lambda e, PT=PT, pb=pb, b2=b2, hp=hp: e.matmul(pO[pb:pb + 64, 0:128], lhsT=vT[:, b2, hp * 64:(hp + 1) * 64], rhs=PT[:, 128:256], start=False, stop=False),
                                    lambda e, PT=PT, pb=pb, o1=o1: e.matmul(pO[pb:pb + 64, 128:256], lhsT=ones3[:, o1, :], rhs=PT[:, 0:128], start=False, stop=False),
                                    lambda e, PT=PT, pb=pb, o2=o2: e.matmul(pO[pb:pb + 64, 128:256], lhsT=ones3[:, o2, :], rhs=PT[:, 128:256], start=False, stop=True),
                                ]
                            p.ops("pe", fns, reads=pts + [vT, ones3], writes=[pO])
                            pov = pO[:, 0:256].rearrange("p (n i) -> p n i", n=2)
                            if g == 0:
                                p.op("dve", lambda e: e.tensor_copy(out=acc[:, :, qsl], in_=pov), reads=[pO], writes=[acc])
                            else:
                                p.op("dve", lambda e: e.tensor_tensor(out=acc[:, :, qsl], in0=pov, in1=acc[:, :, qsl], op=ALU.add), reads=[pO, acc], writes=[acc])

                        units = [(r, a) for r in range(d) for a in range(L // 128)]
                        cur = stS(*units[0])
                        for ui, (r, a) in enumerate(units):
                            nxt = stS(*units[ui + 1]) if ui + 1 < len(units) else None
                            stPV(r, a, cur)
                            cur = nxt
                    p.op("dve", lambda e: e.reciprocal(out=rden[:], in_=acc[:, 1, :]), reads=[acc], writes=[rden])
                    p.op("dve", lambda e: e.tensor_tensor(out=aTo[:], in0=acc[:, 0, :], in1=rden[:], op=ALU.mult), reads=[acc, rden], writes=[aTo])
                    p.dma(lambda e: e.dma_start(out=aT_s[sq, c * 128:(c + 1) * 128, :], in_=aTo[:]), aTo, False)
        new_phase()
        stop('T')

        def load_w_bf16(sbf, wdst, src, K, N, tag, stg=None):
            piece = 1024 if N >= 1024 else N
            if stg is None:
                stg = Ring([sbf(f"wl_{tag}{i}", [128, piece], F32, dma=True) for i in range(2)])
            engs = ("dve", "pool", "act")
            n = 0
            for k in range(K):
                for c0 in range(0, N, piece):
                    w = min(piece, N - c0)
                    st = stg.next()
                    p.dma(lambda e, st=st, k=k, c0=c0, w=w: e.dma_start(out=st[:, 0:w], in_=src[k * 128:(k + 1) * 128, c0:c0 + w]), st, True)
                    eng = engs[n % 3]; n += 1
                    if eng == "act":
                        p.op("act", lambda e, st=st, k=k, c0=c0, w=w: e.copy(out=wdst[:, k, c0:c0 + w], in_=st[:, 0:w]), reads=[st], writes=[wdst])
                    else:
                        p.op(eng, lambda e, st=st, k=k, c0=c0, w=w: e.tensor_copy(out=wdst[:, k, c0:c0 + w], in_=st[:, 0:w]), reads=[st], writes=[wdst])
            wdst.const = True

        def layer_norm_multi(items, gB, bB):
            for (stats, mv, rstd, nmr, hn), hpre, outt in items:
                for n in range(2):
                    p.op("dve", lambda e, n=n: e.bn_stats(out=stats[:, n, :], in_=hpre[:, n * 512:(n + 1) * 512]), reads=[hpre], writes=[stats])
            for (stats, mv, rstd, nmr, hn), hpre, outt in items:
                p.op("dve", lambda e: e.bn_aggr(out=mv[:], in_=stats[:].rearrange("p n s -> p (n s)")), reads=[stats], writes=[mv])
            for (stats, mv, rstd, nmr, hn), hpre, outt in items:
                p.op("act", lambda e: e.activation(out=rstd[:], in_=mv[:, 1:2], func=AF.Sqrt, bias=epsb[:, 0:1]), reads=[mv, epsb], writes=[rstd])
            for (stats, mv, rstd, nmr, hn), hpre, outt in items:
                p.op("dve", lambda e: e.reciprocal(out=rstd[:], in_=rstd[:]), reads=[rstd], writes=[rstd])
            for (stats, mv, rstd, nmr, hn), hpre, outt in items:
                p.op("dve", lambda e: e.scalar_tensor_tensor(out=nmr[:], in0=mv[:, 0:1], scalar=-1.0, in1=rstd[:], op0=ALU.mult, op1=ALU.mult), reads=[mv, rstd], writes=[nmr])
            for (stats, mv, rstd, nmr, hn), hpre, outt in items:
                p.op("act", lambda e: e.activation(out=hn[:], in_=hpre[:], func=AF.Identity, scale=rstd[:, 0:1], bias=nmr[:, 0:1]), reads=[hpre, rstd, nmr], writes=[hn])
            for (stats, mv, rstd, nmr, hn), hpre, outt in items:
                p.op("dve", lambda e: e.tensor_tensor(out=hn[:], in0=hn[:], in1=gB[:], op=ALU.mult), reads=[hn, gB], writes=[hn])
            for (stats, mv, rstd, nmr, hn), hpre, outt in items:
                p.op("dve", lambda e: e.tensor_tensor(out=outt[:], in0=hn[:], in1=bB[:], op=ALU.add), reads=[hn, bB], writes=[outt])

        with ExitStack() as es:
            def sb(name, shape, dt, dma=False, const=False):
                return p.buf(es.enter_context(nc.sbuf_tensor(name, list(shape), dt)), dma=dma, const=const)
            wgv = sb("wgv", [128, 4, D], BF16, dma=True); wgg = sb("wgg", [128, 4, D], BF16, dma=True); wab = sb("wab", [128, 2, D], BF16, dma=True); wo = sb("wo", [128, 8, D], BF16, dma=True)
            load_wbf(wgv, "wgv", 4); load_wbf(wgg, "wgg", 4); load_wbf(wab, "wab", 2); load_wbf(wo, "wo", 8)
            gB = sb("ln1gB", [128, D], F32, dma=True, const=True); bB = sb("ln1bB", [128, D], F32, dma=True, const=True)
            p.dma(lambda e: e.dma_start(out=gB[:], in_=md["ln1g"].partition_broadcast(128)), gB, True)
            p.dma(lambda e: e.dma_start(out=bB[:], in_=md["ln1b"].partition_broadcast(128)), bB, True)
            epsb = sb("epsb", [128, 1], F32)
            p.op("pool", lambda e: e.memset(epsb[:], LN_EPS), writes=[epsb])
            zTr_ = Ring([sb(f"zTm{i}", [128, 4, 512], BF16, dma=True) for i in range(2)]); aTr_ = Ring([sb(f"aTm{i}", [128, 2, 512], BF16, dma=True) for i in range(2)])
            gTr_ = Ring([sb(f"gTm{i}", [128, 16, 512], BF16, dma=True) for i in range(2)])
            xs = Ring([sb(f"xm{i}", [128, D], F32, dma=True) for i in range(2)])
            mixT = sb("mixT", [128, 8, 512], BF16)
            sg = Ring([sb(f"sg{i}", [128, 512], F32) for i in range(2)])
            t1r = Ring([sb(f"t1m{i}", [128, 512], F32) for i in range(2)])
            t2r = Ring([sb(f"t2m{i}", [128, 512], F32) for i in range(2)])
            hpre = Ring([sb(f"hpre{i}", [128, D], F32) for i in range(2)])
            hout = Ring([sb(f"hout{i}", [128, D], F32, dma=True) for i in range(2)])
            hTt = Ring([sb(f"hTt{i}", [128, 8, 128], BF16, dma=True) for i in range(2)])
            lnbs = [(sb(f"st1_{i}", [128, 2, 6], F32), sb(f"mv1_{i}", [128, 2], F32), sb(f"rstd1_{i}", [128, 1], F32), sb(f"nmr1_{i}", [128, 1], F32), sb(f"hn1_{i}", [128, D], F32)) for i in range(2)]
            psr = Ring(psum)
            def load_m1(sq_, tb_):
                ts_ = slice(tb_ * 512, (tb_ + 1) * 512)
                zT_ = zTr_.next(); aT_ = aTr_.next(); gT_ = gTr_.next()
                p.dma(lambda e: e.dma_start(out=zT_[:], in_=zT_s[sq_].rearrange("(k q) t -> q k t", q=128)[:, :, ts_]), zT_, True)
                p.dma(lambda e: e.dma_start(out=aT_[:], in_=aT_s[sq_].rearrange("(k q) t -> q k t", q=128)[:, :, ts_]), aT_, True)
                p.dma(lambda e: e.dma_start(out=gT_[:], in_=gT_s[sq_].rearrange("(k q) t -> q k t", q=128)[:, :, ts_]), gT_, True)
                return zT_, aT_, gT_
            blocks_m1 = [(sq_, tb_) for sq_ in range(nseq) for tb_ in range(S // 512)]
            nxt_in = load_m1(*blocks_m1[0])
            for bi, (sq, tb) in enumerate(blocks_m1):
                if True:
                    ts = slice(tb * 512, (tb + 1) * 512)
                    zT, aT, gT = nxt_in
                    if bi + 1 < len(blocks_m1):
                        nxt_in = load_m1(*blocks_m1[bi + 1])
                    for do in range(8):
                        ds_ = slice(do * 128, (do + 1) * 128)
                        pA = psr.next(); pG = psr.next(); pB = psr.next()
                        p.ops("pe", [lambda e, k=k: e.matmul(pA[:], lhsT=wgv[:, k, ds_], rhs=zT[:, k, :], start=(k == 0), stop=(k == 3)) for k in range(4)], reads=[wgv, zT], writes=[pA])
                        p.ops("pe", [lambda e, k=k: e.matmul(pG[:], lhsT=wgg[:, k, ds_], rhs=zT[:, k, :], start=(k == 0), stop=(k == 3)) for k in range(4)], reads=[wgg, zT], writes=[pG])
                        p.ops("pe", [lambda e, k=k: e.matmul(pB[:], lhsT=wab[:, k, ds_], rhs=aT[:, k, :], start=(k == 0), stop=(k == 1)) for k in range(2)], reads=[wab, aT], writes=[pB])
                        sgt = sg.next(); t1 = t1r.next(); t2 = t2r.next()
                        p.op("act", lambda e: e.activation(out=sgt[:], in_=pG[:], func=AF.Sigmoid), reads=[pG], writes=[sgt])
                        p.op("dve", lambda e: e.tensor_tensor(out=t1[:], in0=pA[:], in1=sgt[:], op=ALU.mult), reads=[pA, sgt], writes=[t1])
                        p.op("dve", lambda e: e.tensor_tensor(out=t2[:], in0=pB[:], in1=gT[:, 8 + do, :], op=ALU.mult), reads=[pB, gT], writes=[t2])
                        p.op("dve", lambda e: e.tensor_tensor(out=t1[:], in0=t1[:], in1=gT[:, do, :], op=ALU.mult), reads=[t1, gT], writes=[t1])
                        p.op("dve", lambda e: e.tensor_tensor(out=mixT[:, do, :], in0=t1[:], in1=t2[:], op=ALU.add), reads=[t1, t2], writes=[mixT])
                    for ip in (0, 2):
                        items = []
                        for i in (ip, ip + 1):
                            tok = slice(tb * 512 + i * 128, tb * 512 + (i + 1) * 128)
                            xt = xs.next()
                            p.dma(lambda e: e.dma_start(out=xt[:], in_=x_d[sq, tok, :]), xt, True)
                            hp_ = hpre.next()
                            for n in range(2):
                                ns = slice(n * 512, (n + 1) * 512)
                                po = psr.next()
                                p.ops("pe", [lambda e, k=k: e.matmul(po[:], lhsT=mixT[:, k, i * 128:(i + 1) * 128], rhs=wo[:, k, ns], start=(k == 0), stop=(k == 7)) for k in range(8)],
                                      reads=[mixT, wo], writes=[po])
                                p.op("dve", lambda e: e.scalar_tensor_tensor(out=hp_[:, ns], in0=xt[:, ns], scalar=ALPHA, in1=po[:], op0=ALU.mult, op1=ALU.add),
                                     reads=[xt, po], writes=[hp_])
                            ho = hout.next()
                            items.append((lnbs[i - ip], hp_, ho, tok))
                        layer_norm_multi([(a_, b_, c_) for (a_, b_, c_, _) in items], gB, bB)
                        for (_, _, ho, tok) in items:
                            p.dma(lambda e: e.dma_start(out=h_s[sq, tok, :], in_=ho[:]), ho, False)
                            hT = hTt.next()
                            for kk in range(2):
                                pt = psr.next()
                                p.ops("pe", [lambda e, k4=k4: e.transpose(out=pt[:, k4 * 128:(k4 + 1) * 128], in_=ho[:, (kk * 4 + k4) * 128:(kk * 4 + k4 + 1) * 128], identity=ident[:])
                                             for k4 in range(4)], reads=[ho, ident], writes=[pt])
                                p.op("act", lambda e: e.copy(out=hT[:, kk * 4:(kk + 1) * 4, :], in_=pt[:].rearrange("p (k t) -> p k t", k=4)), reads=[pt], writes=[hT])
                            p.dma(lambda e: e.dma_start(out=hT_s[sq].rearrange("(k q) t -> q k t", q=128)[:, :, tok], in_=hT[:]), hT, False)
        new_phase()
        stop('M1')

        with ExitStack() as es:
            def sb(name, shape, dt, dma=False, const=False):
                return p.buf(es.enter_context(nc.sbuf_tensor(name, list(shape), dt)), dma=dma, const=const)
            wup = sb("wup", [128, 8, 2 * DFF], BF16, dma=True); wdn = sb("wdn", [128, 22, D], BF16, dma=True)
            load_wbf(wup, "wup", 8); load_wbf(wdn, "wdn", 22)
            gB = sb("ln2gB", [128, D], F32, dma=True, const=True); bB = sb("ln2bB", [128, D], F32, dma=True, const=True)
            p.dma(lambda e: e.dma_start(out=gB[:], in_=md["ln2g"].partition_broadcast(128)), gB, True)
            p.dma(lambda e: e.dma_start(out=bB[:], in_=md["ln2b"].partition_broadcast(128)), bB, True)
            cw = sb("cw", [128, 44, 3], F32, dma=True, const=True); cbias = sb("cbias", [128, 44], F32, dma=True, const=True)
            p.dma(lambda e: e.dma_start(out=cw[:], in_=md["cw"]), cw, True)
            p.dma(lambda e: e.dma_start(out=cbias[:], in_=md["cbias"]), cbias, True)
            epsb = sb("epsb2", [128, 1], F32)
            p.op("pool", lambda e: e.memset(epsb[:], LN_EPS), writes=[epsb])
            hT = sb("hTf", [128, 8, 514], BF16, dma=True)
            hres = Ring([sb(f"hres{i}", [128, D], F32, dma=True) for i in range(1)])
            cvr = Ring([sb(f"cv{i}", [128, 512], F32) for i in range(8)])
            actT = sb("actT", [128, 22, 512], BF16)
            opre = sb("opre", [128, D], F32)
            oout = Ring([sb(f"oout{i}", [128, D], F32, dma=True) for i in range(1)])
            lnb = (sb("st2", [128, 2, 6], F32), sb("mv2", [128, 2], F32), sb("rstd2", [128, 1], F32), sb("nmr2", [128, 1], F32), sb("hn2", [128, D], F32))
            psm = Ring(psum[0:5])
            psh = Ring([(psum[5], 0), (psum[6], 0), (psum[7], 0)])
            for sq in range(nseq):
                for tb in range(S // 512):
                    t0 = tb * 512
                    lo = max(t0 - 1, 0); hi = min(t0 + 513, S)
                    if t0 == 0:
                        p.op("pool", lambda e: e.memset(hT[:, :, 0:1], 0.0), writes=[hT])
                    if t0 + 512 == S:
                        p.op("pool", lambda e: e.memset(hT[:, :, 513:514], 0.0), writes=[hT])
                    p.dma(lambda e: e.dma_start(out=hT[:, :, lo - (t0 - 1):hi - (t0 - 1)], in_=hT_s[sq].rearrange("(k q) t -> q k t", q=128)[:, :, lo:hi]), hT, True)
                    for c in range(22):
                        cvs = []
                        for ch in (c, 22 + c):
                            cs_ = slice(ch * 128, (ch + 1) * 128)
                            pm = psm.next(); ph, hc = psh.next()
                            p.ops("pe", [lambda e, k=k: e.matmul(pm[:], lhsT=wup[:, k, cs_], rhs=hT[:, k, 1:513], start=(k == 0), stop=(k == 7)) for k in range(8)]
                                  + [lambda e, k=k: e.matmul(ph[:, hc:hc + 2], lhsT=wup[:, k, cs_], rhs=hT[:, k, 0:514:513], start=(k == 0), stop=(k == 7)) for k in range(8)],
                                  reads=[wup, hT], writes=[pm, ph])
                            cv = cvr.next()
                            p.op("act", lambda e: e.activation(out=cv[:], in_=pm[:], func=AF.Identity, scale=cw[:, ch, 1:2], bias=cbias[:, ch:ch + 1]),
                                 reads=[pm, cw, cbias], writes=[cv])
                            p.op("dve", lambda e: e.scalar_tensor_tensor(out=cv[:, 1:512], in0=pm[:, 0:511], scalar=cw[:, ch, 0:1], in1=cv[:, 1:512], op0=ALU.mult, op1=ALU.add), reads=[pm, cw, cv], writes=[cv])
                            p.op("dve", lambda e: e.scalar_tensor_tensor(out=cv[:, 0:511], in0=pm[:, 1:512], scalar=cw[:, ch, 2:3], in1=cv[:, 0:511], op0=ALU.mult, op1=ALU.add), reads=[pm, cw, cv], writes=[cv])
                            p.op("dve", lambda e: e.scalar_tensor_tensor(out=cv[:, 0:1], in0=ph[:, hc:hc + 1], scalar=cw[:, ch, 0:1], in1=cv[:, 0:1], op0=ALU.mult, op1=ALU.add), reads=[ph, cw, cv], writes=[cv])
                            p.op("dve", lambda e: e.scalar_tensor_tensor(out=cv[:, 511:512], in0=ph[:, hc + 1:hc + 2], scalar=cw[:, ch, 2:3], in1=cv[:, 511:512], op0=ALU.mult, op1=ALU.add), reads=[ph, cw, cv], writes=[cv])
                            cvs.append(cv)
                        p.op("act", lambda e: e.activation(out=cvs[0][:], in_=cvs[0][:], func=AF.Gelu), reads=[cvs[0]], writes=[cvs[0]])
                        p.op("pool", lambda e: e.tensor_tensor(out=actT[:, c, :], in0=cvs[0][:], in1=cvs[1][:], op=ALU.mult), reads=cvs, writes=[actT])
                    for i in range(4):
                        tok = slice(t0 + i * 128, t0 + (i + 1) * 128)
                        hr = hres.next()
                        p.dma(lambda e: e.dma_start(out=hr[:], in_=h_s[sq, tok, :]), hr, True)
                        for n in range(2):
                            ns = slice(n * 512, (n + 1) * 512)
                            po = psm.next()
                            p.ops("pe", [lambda e, k=k: e.matmul(po[:], lhsT=actT[:, k, i * 128:(i + 1) * 128], rhs=wdn[:, k, ns], start=(k == 0), stop=(k == 21)) for k in range(22)],
                                  reads=[actT, wdn], writes=[po])
                            p.op("dve", lambda e: e.scalar_tensor_tensor(out=opre[:, ns], in0=hr[:, ns], scalar=ALPHA, in1=po[:], op0=ALU.mult, op1=ALU.add),
                                 reads=[hr, po], writes=[opre])
                        oo = oout.next()
                        layer_norm_multi([(lnb, opre, oo)], gB, bB)
                        p.dma(lambda e: e.dma_start(out=out_d[sq, tok, :], in_=oo[:]), oo, False)
        new_phase()


def _host_inputs(inputs, core, nseq=NSEQ):
    f32 = np.float32
    x = np.ascontiguousarray(inputs["x"][core * nseq:(core + 1) * nseq]).astype(f32)
    pos = np.ascontiguousarray(inputs["positions"][core * nseq:(core + 1) * nseq]).astype(np.int32)
    w_in = np.asarray(inputs["w_in"][0], f32)
    b_in = np.asarray(inputs["b_in"][0], f32)
    sw = np.arange(AW).reshape(-1, 2, 32)[:, ::-1, :].reshape(-1)
    q0, k0, v0, g0 = SSMW, SSMW + AW, SSMW + 2 * AW, SSMW + 3 * AW
    cols = np.concatenate([np.arange(0, SSMW), np.arange(q0, q0 + AW), np.arange(k0, k0 + AW),
                           q0 + sw, k0 + sw, np.arange(g0, g0 + 2 * D)])
    w_fm = np.ascontiguousarray(w_in[:, cols])
    b_fm = np.ascontiguousarray(b_in[cols].reshape(NFM // 128, 128).T)
    w_v = np.ascontiguousarray(w_in[:, v0:v0 + AW])
    b_v = np.ascontiguousarray(b_in[v0:v0 + AW].reshape(1, AW))
    half = 32
    inv_freq = (10000.0 ** (-np.arange(half, dtype=np.float64) * 2.0 / 64)).astype(f32)
    invf = np.zeros((128, 2), f32)
    for pp in range(128):
        invf[pp, 0] = inv_freq[pp % 32] / TWO_PI
        invf[pp, 1] = -TWO_PI if (pp % 64) < 32 else TWO_PI
    def tile_layout(a):
        return np.ascontiguousarray(a.reshape(2, 16, 2, 64).transpose(2, 3, 0, 1).reshape(128, 32)).astype(f32)
    lre_h = tile_layout(np.asarray(inputs["ssm_lam_re"][0], f32))
    lim_h = tile_layout(np.asarray(inputs["ssm_lam_im"][0], f32))
    ldt_h = tile_layout(np.broadcast_to(np.asarray(inputs["ssm_log_dt"][0], f32)[:, :, None], (2, 32, 64)).copy())

    def bz(b):
        o = np.zeros((128, 32, 128), f32)
        b = np.asarray(b, f32)
        for dr in range(2):
            for gp in range(16):
                for gl in range(2):
                    c0 = (gp % 4) * 32 + gl * 16
                    o[gl * 64:(gl + 1) * 64, dr * 16 + gp, c0:c0 + 16] = b[dr, 2 * gp + gl]
        return o

    def cb(c):
        o = np.zeros((32, 32, 128), f32)
        c = np.asarray(c, f32)
        for dr in range(2):
            for gp in range(16):
                for gl in range(2):
                    o[gl * 16:(gl + 1) * 16, dr * 16 + gp, gl * 64:(gl + 1) * 64] = c[dr, 2 * gp + gl]
        return o
    ssm = {"lre_h": lre_h, "lim_h": lim_h, "ldt_h": ldt_h,
           "bzr_h": bz(inputs["ssm_b_re"][0]), "bzi_h": bz(inputs["ssm_b_im"][0]),
           "cbr_h": cb(inputs["ssm_c_re"][0]), "cbi_h": cb(inputs["ssm_c_im"][0]),
           "dsk_h": np.ascontiguousarray(np.asarray(inputs["ssm_d"][0], f32).reshape(4, 128).T),
           "iota_h": np.arange(S, dtype=f32).reshape(1, S)}
    d = {"x": x, "pos": pos, "ident": np.eye(128, dtype=f32), "invf": invf,
         "w_in_fm": w_fm, "b_fm": b_fm, "w_v": w_v, "b_v": b_v}
    d.update(ssm)
    ii = np.arange(128)[:, None]; jj = np.arange(128)[None, :]
    maskb = np.concatenate([np.where(ii >= jj, 0.0, -30000.0), np.where(ii <= jj, 0.0, -30000.0)], axis=1).astype(f32)
    ones3 = np.zeros((128, 3, 64), f32)
    ones3[:, 0, :] = 1.0; ones3[64:, 1, :] = 1.0; ones3[:64, 2, :] = 1.0
    g = lambda n: np.ascontiguousarray(np.asarray(inputs[n][0], f32))
    cwh = np.ascontiguousarray(g("conv_w").reshape(3, 44, 128).transpose(2, 1, 0))
    cbh = np.ascontiguousarray(g("conv_b").reshape(44, 128).T)
    d.update({"maskb_h": maskb, "ones3_h": ones3, "wgv_h": g("w_glu_v"), "wgg_h": g("w_glu_g"), "wab_h": g("w_attn_br"), "wo_h": g("w_out"),
              "ln1g_h": g("ln1_g").reshape(1, D), "ln1b_h": g("ln1_b").reshape(1, D), "ln2g_h": g("ln2_g").reshape(1, D), "ln2b_h": g("ln2_b").reshape(1, D),
              "wup_h": g("w_up"), "wdn_h": g("w_down"), "cw_h": cwh, "cb_h": cbh})
    return d


def kernel(**inputs):
    nc = build()
    in_maps = [_host_inputs(inputs, c) for c in range(NCORES)]
    res = run_bass_kernel_spmd(nc, in_maps, core_ids=list(range(NCORES)))
    out = np.concatenate([r["out"] for r in res.results], axis=0)
    return out.astype(np.float32)
```

```python
import math
from contextlib import ExitStack

import numpy as np
import concourse.bass as bass
import concourse.mybir as mybir
from concourse.bass_utils import run_bass_kernel_spmd

F32 = mybir.dt.float32
BF16 = mybir.dt.bfloat16
I32 = mybir.dt.int32
AF = mybir.ActivationFunctionType
ALU = mybir.AluOpType
AX = mybir.AxisListType

S = 4096
D = 1024
NCORES = 8
NSEQ = 2
SSMW = 512
AW = 768
DFF = 2816
NFM = 4096
ALPHA = 2.0 ** 0.25
LN_EPS = 1e-5
TWO_PI = 2.0 * math.pi
DIL = (1, 4, 16)
KPAD = 1024


class Buf:
    __slots__ = ("t", "w", "r", "dsem", "const")

    def __init__(self, t, dsem=None, const=False):
        self.t = t
        self.w = None
        self.r = {}
        self.dsem = dsem
        self.const = const

    def __getitem__(self, k):
        return self.t[k]


class Prog:
    ENG = ("pe", "act", "dve", "pool", "sp")

    def __init__(self, nc, es, n_dsem=72):
        self.nc = nc
        self.engobj = {'pe': nc.tensor, 'act': nc.scalar, 'dve': nc.vector, 'pool': nc.gpsimd, 'sp': nc.sync}
        self.ninst = 0
        self.stopped = False
        self.esem = {e: es.enter_context(nc.semaphore("es_" + e)) for e in ("pe", "act", "dve", "pool")}
        self.ecount = {e: 0 for e in self.esem}
        self.dsems = [es.enter_context(nc.semaphore(f"ds{i}")) for i in range(n_dsem)]
        self.dcount = {id(s): 0 for s in self.dsems}
        self.dnext = 0
        self.waited = {e: {} for e in self.ENG}
        self.semobj = {}
        for s in list(self.esem.values()) + self.dsems:
            self.semobj[id(s)] = s

    def buf(self, t, dma=False, const=False):
        ds = None
        if dma:
            assert self.dnext < len(self.dsems), "out of DMA semaphores in this phase"
            ds = self.dsems[self.dnext]
            self.dnext += 1
        return Buf(t, ds, const)

    def _deps(self, reads, writes):
        deps = {}

        def add(ev):
            if ev is None:
                return
            k, v = ev
            if deps.get(k, 0) < v:
                deps[k] = v
        for b in reads:
            add(b.w)
        for b in writes:
            add(b.w)
            for k, v in b.r.items():
                add((k, v))
        return deps

    def _record(self, ev, reads, writes):
        for b in writes:
            b.w = ev
            b.r = {}
        for b in reads:
            if b.const:
                continue
            if b.r.get(ev[0], 0) < ev[1]:
                b.r[ev[0]] = ev[1]

    def _emit(self, eng, deps, fn, inc):
        e = self.engobj[eng]
        wd = self.waited[eng]
        own = id(self.esem[eng]) if eng in self.esem else None
        for k, v in deps.items():
            if eng == "pe" and k == own:
                continue
            if wd.get(k, 0) >= v:
                continue
            wd[k] = v
            e.wait_ge(self.semobj[k], v)
        if fn is None:
            return
        ins = fn(e)
        if inc is not None:
            ins.then_inc(inc[0], inc[1])
        self.ninst += 1

    def op(self, eng, fn, reads=(), writes=()):
        if self.stopped:
            return None
        deps = self._deps(reads, writes)
        self.ecount[eng] += 1
        sem = self.esem[eng]
        ev = (id(sem), self.ecount[eng])
        self._emit(eng, deps, fn, (sem, 1))
        self._record(ev, reads, writes)
        return ev

    def ops(self, eng, fns, reads=(), writes=()):
        assert eng == "pe"
        if self.stopped:
            return None
        deps = self._deps(reads, writes)
        for fn in fns[:-1]:
            self._emit(eng, deps, fn, None)
            deps = {}
        self.ecount[eng] += 1
        sem = self.esem[eng]
        ev = (id(sem), self.ecount[eng])
        self._emit(eng, deps, fns[-1], (sem, 1))
        self._record(ev, reads, writes)
        return ev

    def dma(self, fn, sb, load, reads=(), writes=(), q="sp"):
        if self.stopped:
            return None
        reads = list(reads)
        writes = list(writes)
        if load:
            writes.append(sb)
        else:
            reads.append(sb)
        deps = self._deps(reads, writes)
        sem = sb.dsem
        assert sem is not None
        self.dcount[id(sem)] += 16
        ev = (id(sem), self.dcount[id(sem)])
        self._emit(q, deps, fn, (sem, 16))
        self._record(ev, reads, writes)
        return ev

    def dma_group(self, fns, sb, load, reads=(), writes=(), q="sp"):
        if self.stopped:
            return None
        reads = list(reads)
        writes = list(writes)
        if load:
            writes.append(sb)
        else:
            reads.append(sb)
        deps = self._deps(reads, writes)
        sem = sb.dsem
        ev = None
        for fn in fns:
            self.dcount[id(sem)] += 16
            ev = (id(sem), self.dcount[id(sem)])
            self._emit(q, deps, fn, (sem, 16))
            deps = {}
        self._record(ev, reads, writes)
        return ev

    def barrier(self):
        allev = {}
        for e, s in self.esem.items():
            if self.ecount[e]:
                allev[id(s)] = self.ecount[e]
        for s in self.dsems:
            if self.dcount[id(s)]:
                allev[id(s)] = self.dcount[id(s)]
        for eng in self.ENG:
            self._emit(eng, allev, None, None)
        self.dnext = 0

    def emit(self):
        pass


class StopBuild(Exception):
    pass


class Ring:
    def __init__(self, bufs):
        self.bufs = bufs
        self.i = 0

    def next(self):
        b = self.bufs[self.i % len(self.bufs)]
        self.i += 1
        return b


def build(nseq=NSEQ, debug=False, stop_after=None):
    nc = bass.Bass("TRN2", target_bir_lowering=False)

    def din(name, shape, dt=F32):
        return nc.dram_tensor(name, list(shape), dt, kind="ExternalInput").ap()

    dbg_kind = "ExternalOutput" if debug else "Internal"

    def dscr(name, shape, dt):
        return nc.dram_tensor(name, list(shape), dt, kind=dbg_kind).ap()

    x_d = din("x", [nseq, S, D])
    pos_d = din("pos", [nseq, S], I32)
    ident_d = din("ident", [128, 128])
    pswap_d = din("pswap", [128, 128])
    invf_d = din("invf", [128, 2])
    w_in_d = din("w_in_fm", [D, NFM])
    b_fm_d = din("b_fm", [128, NFM // 128])
    w_v_d = din("w_v", [D, AW])
    b_v_d = din("b_v", [1, AW])
    out_d = nc.dram_tensor("out", [nseq, S, D], F32, kind="ExternalOutput").ap()
    ssm_d = dict(
        lre=din("lre_h", [128, 32]), lim=din("lim_h", [128, 32]), ldt=din("ldt_h", [128, 32]),
        bzr=din("bzr_h", [128, 32, 128]), bzi=din("bzi_h", [128, 32, 128]),
        cbr=din("cbr_h", [32, 32, 128]), cbi=din("cbi_h", [32, 32, 128]),
        dsk=din("dsk_h", [128, 4]), iota=din("iota_h", [1, S]))
    zT_s = dscr("zT_s", [nseq, SSMW, S], BF16)
    aT_s = dscr("aT_s", [nseq, 256, S], BF16)
    h_s = dscr("h_s", [nseq, S, D], F32)
    hT_s = dscr("hT_s", [nseq, D, S], BF16)
    md = dict(maskb=din("maskb_h", [128, 256]), ones3=din("ones3_h", [128, 3, 64]),
              wgv=din("wgv_h", [512, D]), wgg=din("wgg_h", [512, D]), wab=din("wab_h", [256, D]), wo=din("wo_h", [D, D]),
              ln1g=din("ln1g_h", [1, D]), ln1b=din("ln1b_h", [1, D]), ln2g=din("ln2g_h", [1, D]), ln2b=din("ln2b_h", [1, D]),
              wup=din("wup_h", [D, 2 * DFF]), wdn=din("wdn_h", [DFF, D]), cw=din("cw_h", [128, 44, 3]), cbias=din("cb_h", [128, 44]),
              aT_s=aT_s, h_s=h_s, hT_s=hT_s)

    xT_s = dscr("xT_s", [nseq, D, S], BF16)
    uT_s = dscr("uT_s", [nseq, SSMW, S], BF16)
    qT_s = dscr("qT_s", [nseq, AW, S], BF16)
    kT_s = dscr("kT_s", [nseq, AW, S], BF16)
    gT_s = dscr("gT_s", [nseq, 2 * D, S], BF16)
    NBLK = [d * (S // d // 128 + 1) for d in DIL]
    v_s = [dscr(f"v_s{g}", [nseq, 128, NBLK[g], 256], BF16) for g in range(3)]

    with ExitStack() as es0:
        p = Prog(nc, es0)
        psum = [p.buf(es0.enter_context(nc.psum_tensor(f"ps{i}", [128, 512], F32))) for i in range(8)]
        ident = p.buf(es0.enter_context(nc.sbuf_tensor("ident_sb", [128, 128], F32)), dma=True, const=True)
        p.dma(lambda e: e.dma_start(out=ident[:], in_=ident_d), ident, True)
        p.dnext = 1
        wbf = {}

        def cast_w(key, src, R, C):
            dst = nc.dram_tensor(key + "_bf", [R, C], BF16, kind="Internal").ap()
            pb = p.buf(None, dma=True)
            p.dma_group([lambda e, r0=r0: e.dma_start(out=dst[r0:min(r0 + 128, R), :], in_=src[r0:min(r0 + 128, R), :], max_dma_last_dim=4096)
                         for r0 in range(0, R, 128)], pb, True, q="pool")
            wbf[key] = (dst, pb)
        cast_w("w_in", w_in_d, D, NFM)
        cast_w("w_v", w_v_d, D, AW)
        cast_w("wgv", md["wgv"], SSMW, D)
        cast_w("wgg", md["wgg"], SSMW, D)
        cast_w("wab", md["wab"], 256, D)
        cast_w("wo", md["wo"], D, D)
        cast_w("wup", md["wup"], D, 2 * DFF)
        cast_w("wdn", md["wdn"], DFF, D)
        NRES = p.dnext

        def new_phase():
            p.barrier()
            p.dnext = NRES

        def stop(tag):
            if stop_after == tag:
                p.stopped = True

        try:
            _phases(nc, p, psum, ident, nseq, locals_d=dict(pswap_d=pswap_d, x_d=x_d, pos_d=pos_d, invf_d=invf_d, w_in_d=w_in_d, b_fm_d=b_fm_d, w_v_d=w_v_d, b_v_d=b_v_d, out_d=out_d, xT_s=xT_s, uT_s=uT_s, qT_s=qT_s, kT_s=kT_s, gT_s=gT_s, v_s=v_s, NBLK=NBLK, ssm_d=ssm_d, zT_s=zT_s, md=md, wbf=wbf), new_phase=new_phase, stop=stop)
        except StopBuild:
            pass
        p.stopped = False
        p.barrier()
    print('instructions', p.ninst)
    return nc


def _phases(nc, p, psum, ident, nseq, locals_d, new_phase, stop):
    globals_ = locals_d
    pswap_d = globals_['pswap_d']; x_d = globals_['x_d']; pos_d = globals_['pos_d']; invf_d = globals_['invf_d']; w_in_d = globals_['w_in_d']; b_fm_d = globals_['b_fm_d']
    w_v_d = globals_['w_v_d']; b_v_d = globals_['b_v_d']; out_d = globals_['out_d']; xT_s = globals_['xT_s']; uT_s = globals_['uT_s']
    qT_s = globals_['qT_s']; kT_s = globals_['kT_s']; gT_s = globals_['gT_s']; v_s = globals_['v_s']; NBLK = globals_['NBLK']
    ssm_d = globals_['ssm_d']; zT_s = globals_['zT_s']; md = globals_['md']
    aT_s = md['aT_s']; h_s = md['h_s']; hT_s = md['hT_s']; wbf = globals_['wbf']

    def load_wbf(wdst, key, K):
        src, pb = wbf[key]
        p.dma_group([lambda e, k=k: e.dma_start(out=wdst[:, k, :], in_=src[k * 128:(k + 1) * 128, :]) for k in range(K)], wdst, True, reads=[pb])
        wdst.const = True
    if True:

        with ExitStack() as es:
            def sb(name, shape, dt, dma=False, const=False):
                return p.buf(es.enter_context(nc.sbuf_tensor(name, list(shape), dt)), dma=dma, const=const)

            wA = sb("wA", [128, 8, NFM], BF16, dma=True)
            bfm = sb("bfm", [128, NFM // 128], F32, dma=True)
            invf = sb("invf_sb", [128, 2], F32, dma=True)
            p.dma(lambda e: e.dma_start(out=bfm[:], in_=b_fm_d), bfm, True)
            p.dma(lambda e: e.dma_start(out=invf[:], in_=invf_d), invf, True)
            load_wbf(wA, 'w_in', 8)
            psw_st = sb("psw_st", [128, 128], F32, dma=True)
            psw = sb("psw", [128, 128], BF16)
            p.dma(lambda e: e.dma_start(out=psw_st[:], in_=pswap_d), psw_st, True)
            p.op("dve", lambda e: e.tensor_copy(out=psw[:], in_=psw_st[:]), reads=[psw_st], writes=[psw])
            psw.const = True
            qbr = Ring([sb(f"qb{j}", [128, 512], BF16) for j in range(4)])
            stop('A0')

            cosT = sb("cosT", [128, S], F32)
            sinT = sb("sinT", [128, S], F32)
            posi = sb("posi", [128, 1024], I32, dma=True)
            tur = sb("tur", [128, 1024], F32)
            turi = sb("turi", [128, 1024], I32)
            xs = [sb(f"xs{i}", [128, D], F32, dma=True) for i in range(4)]
            xT = Ring([sb(f"xT{j}", [128, 8, 512], BF16, dma=True) for j in range(2)])
            ev_bf = Ring([sb(f"evbf{j}", [128, 512], BF16, dma=True) for j in range(8)])
            rt = Ring([sb(f"rt{j}", [128, 512], F32) for j in range(4)])
            psr = Ring(psum)

            for sq in range(nseq):
                for c in range(S // 1024):
                    cs = slice(c * 1024, (c + 1) * 1024)
                    p.dma(lambda e, cs=cs: e.dma_start(out=posi[:], in_=pos_d[sq:sq + 1, cs].partition_broadcast(128)), posi, True)
                    for (tab, addc, scol) in ((sinT, 0.0, 1), (cosT, 0.25, None)):
                        p.op("dve", lambda e: e.tensor_copy(out=tur[:], in_=posi[:]), reads=[posi], writes=[tur])
                        p.op("dve", lambda e, addc=addc: e.tensor_scalar(out=tur[:], in0=tur[:], scalar1=invf[:, 0:1], scalar2=addc,
                                                                          op0=ALU.mult, op1=ALU.add), reads=[tur, invf], writes=[tur])
                        p.op("dve", lambda e: e.tensor_copy(out=turi[:], in_=tur[:]), reads=[tur], writes=[turi])
                        p.op("dve", lambda e: e.tensor_tensor(out=tur[:], in0=tur[:], in1=turi[:], op=ALU.subtract),
                             reads=[tur, turi], writes=[tur])
                        if scol is not None:
                            p.op("act", lambda e, tab=tab, cs=cs: e.activation(out=tab[:, cs], in_=tur[:], func=AF.Sin, scale=invf[:, 1:2]),
                                 reads=[tur, invf], writes=[tab])
                        else:
                            p.op("act", lambda e, tab=tab, cs=cs: e.activation(out=tab[:, cs], in_=tur[:], func=AF.Sin, scale=TWO_PI),
                                 reads=[tur], writes=[tab])
                stop('A1')
                def load_x(tb_):
                    for i in range(4):
                        p.dma(lambda e, i=i: e.dma_start(out=xs[i][:], in_=x_d[sq, tb_ * 512 + i * 128:tb_ * 512 + (i + 1) * 128, :]), xs[i], True)
                load_x(0)
                for tb in range(S // 512):
                    t0 = tb * 512
                    ts = slice(t0, t0 + 512)
                    xtile = xs
                    xTb = xT.next()
                    for k in range(8):
                        ps = psr.next()
                        p.ops("pe", [lambda e, ps=ps, i=i, k=k: e.transpose(out=ps[:, i * 128:(i + 1) * 128],
                                                                           in_=xtile[i][:, k * 128:(k + 1) * 128], identity=ident[:])
                                     for i in range(4)], reads=xtile + [ident], writes=[ps])
                        if k % 2 == 0:
                            p.op("act", lambda e, ps=ps, k=k: e.copy(out=xTb[:, k, :], in_=ps[:]), reads=[ps], writes=[xTb])
                        else:
                            p.op("dve", lambda e, ps=ps, k=k: e.tensor_copy(out=xTb[:, k, :], in_=ps[:]), reads=[ps], writes=[xTb])
                    if tb + 1 < S // 512:
                        load_x(tb + 1)
                    p.dma(lambda e: e.dma_start(out=xT_s[sq].rearrange("(k q) t -> q k t", q=128)[:, :, ts], in_=xTb[:]), xTb, False)

                    def proj(fo):
                        ps = psr.next()
                        p.ops("pe", [lambda e, ps=ps, k=k: e.matmul(ps[:], lhsT=wA[:, k, fo * 128:(fo + 1) * 128], rhs=xTb[:, k, :],
                                                                      start=(k == 0), stop=(k == 7)) for k in range(8)],
                              reads=[wA, xTb], writes=[ps])
                        return ps

                    for fo in range(4):
                        ps = proj(fo)
                        o = ev_bf.next()
                        p.op("act", lambda e, ps=ps, o=o, fo=fo: e.activation(out=o[:], in_=ps[:], func=AF.Identity, bias=bfm[:, fo:fo + 1]),
                             reads=[ps, bfm], writes=[o])
                        p.dma(lambda e, o=o, fo=fo: e.dma_start(out=uT_s[sq, fo * 128:(fo + 1) * 128, ts], in_=o[:]), o, False)
                    for which, dst in ((0, qT_s), (1, kT_s)):
                        for c in range(6):
                            fo = 4 + which * 6 + c
                            psa = proj(fo)
                            qb = qbr.next()
                            p.op("act", lambda e: e.activation(out=qb[:], in_=psa[:], func=AF.Identity, bias=bfm[:, fo:fo + 1]), reads=[psa, bfm], writes=[qb])
                            psb = psr.next()
                            p.ops("pe", [lambda e: e.matmul(psb[:], lhsT=psw[:], rhs=qb[:], start=True, stop=True)], reads=[psw, qb], writes=[psb])
                            t1 = rt.next()
                            t2 = rt.next()
                            p.op("dve", lambda e: e.tensor_tensor(out=t1[:], in0=qb[:], in1=cosT[:, ts], op=ALU.mult), reads=[qb, cosT], writes=[t1])
                            p.op("dve", lambda e: e.tensor_tensor(out=t2[:], in0=psb[:], in1=sinT[:, ts], op=ALU.mult), reads=[psb, sinT], writes=[t2])
                            o = ev_bf.next()
                            p.op("pool", lambda e, o=o, t1=t1, t2=t2: e.tensor_tensor(out=o[:], in0=t1[:], in1=t2[:], op=ALU.add),
                                 reads=[t1, t2], writes=[o])
                            p.dma(lambda e, o=o, c=c, dst=dst: e.dma_start(out=dst[sq, c * 128:(c + 1) * 128, ts], in_=o[:]), o, False)
                    for c in range(16):
                        fo = 16 + c
                        ps = proj(fo)
                        o = ev_bf.next()
                        p.op("act", lambda e, ps=ps, o=o, fo=fo: e.activation(out=o[:], in_=ps[:], func=AF.Sigmoid, bias=bfm[:, fo:fo + 1]),
                             reads=[ps, bfm], writes=[o])
                        p.dma(lambda e, o=o, c=c: e.dma_start(out=gT_s[sq, c * 128:(c + 1) * 128, ts], in_=o[:]), o, False)
                    stop(f'A2_{tb}')
        new_phase()
        stop('A')

        with ExitStack() as es:
            def sb(name, shape, dt, dma=False, const=False):
                return p.buf(es.enter_context(nc.sbuf_tensor(name, list(shape), dt)), dma=dma, const=const)

            wV = sb("wV", [128, 8, AW], BF16, dma=True)
            load_wbf(wV, 'w_v', 8)
            bv = sb("bv", [128, AW], F32, dma=True, const=True)
            p.dma(lambda e: e.dma_start(out=bv[:], in_=b_v_d.partition_broadcast(128)), bv, True)
            xTf = sb("xTf", [128, 8, S], BF16, dma=True)
            VCH = 12
            vring = Ring([sb(f"vstg{j}", [128, VCH, 256], BF16, dma=True) for j in range(2)])
            psr = Ring(psum)
            for sq in range(nseq):
                p.dma(lambda e: e.dma_start(out=xTf[:], in_=xT_s[sq].rearrange("(k q) t -> q k t", q=128)), xTf, True)
                for g in range(3):
                    d = DIL[g]
                    L = S // d
                    nb = L // 128 + 1
                    blocks = [(r, m) for r in range(d) for m in range(nb)]
                    for c0 in range(0, len(blocks), VCH):
                        chunk = blocks[c0:c0 + VCH]
                        stg = vring.next()
                        p.op("pool", lambda e, stg=stg: e.memset(stg[:], 0.0), writes=[stg])
                        for j, (r, m) in enumerate(chunk):
                            lo = 64 + 128 * (m - 1)
                            i0 = max(0, -lo)
                            i1 = min(128, L - lo)
                            M = i1 - i0
                            tok0 = r + d * (lo + i0)
                            ps = psr.next()
                            p.ops("pe", [lambda e, ps=ps, k=k, tok0=tok0, M=M, i0=i0, d=d, g=g: e.matmul(
                                ps[i0:i0 + M, 0:256], lhsT=xTf[:, k, tok0:tok0 + d * (M - 1) + 1:d], rhs=wV[:, k, g * 256:(g + 1) * 256],
                                start=(k == 0), stop=(k == 7)) for k in range(8)], reads=[xTf, wV], writes=[ps])
                            p.op("dve", lambda e, ps=ps, stg=stg, j=j, i0=i0, M=M, g=g: e.tensor_tensor(
                                out=stg[i0:i0 + M, j, :], in0=ps[i0:i0 + M, 0:256], in1=bv[i0:i0 + M, g * 256:(g + 1) * 256], op=ALU.add),
                                reads=[ps, bv], writes=[stg])
                        p.dma(lambda e, stg=stg, c0=c0, n=len(chunk), g=g: e.dma_start(out=v_s[g][sq, :, c0:c0 + n, :], in_=stg[:, 0:n, :]), stg, False)
                        stop(f'V{g}_{c0}')
                    stop(f'V{g}')
        new_phase()

        with ExitStack() as es:
            def sb(name, shape, dt, dma=False, const=False):
                return p.buf(es.enter_context(nc.sbuf_tensor(name, list(shape), dt)), dma=dma, const=const)

            NT = 32
            NCH = S // 8
            lre = sb("lre", [128, NT], F32, dma=True); lim = sb("lim", [128, NT], F32, dma=True); ldt = sb("ldt", [128, NT], F32, dma=True)
            p.dma(lambda e: e.dma_start(out=lre[:], in_=ssm_d["lre"]), lre, True)
            p.dma(lambda e: e.dma_start(out=lim[:], in_=ssm_d["lim"]), lim, True)
            p.dma(lambda e: e.dma_start(out=ldt[:], in_=ssm_d["ldt"]), ldt, True)
            dsk = sb("dsk", [128, 4], F32, dma=True)
            p.dma(lambda e: e.dma_start(out=dsk[:], in_=ssm_d["dsk"]), dsk, True)
            tI = sb("tI", [128, NCH], F32, dma=True, const=True)
            p.dma(lambda e: e.dma_start(out=tI[:], in_=ssm_d["iota"][:, 0:NCH].partition_broadcast(128)), tI, True)
            sm = {n: sb("sm_" + n, [128, NT], F32) for n in
                  ("dt", "xr", "xi", "rho", "th", "t0", "t1", "f", "sinx", "cosx", "sinh", "em1", "am1", "abi", "den", "kr", "ki", "u0", "u1",
                   "rho8", "th8", "pm", "pc", "ps")}
            smi = sb("smi", [128, NT], I32)
            pwr = sb("pwr", [128, 16, NT], F32); pwi = sb("pwi", [128, 16, NT], F32); npwi = sb("npwi", [128, 16, NT], F32)

            def V(fn, reads, writes):
                return p.op("dve", fn, reads=reads, writes=writes)

            def A(fn, reads, writes):
                return p.op("act", fn, reads=reads, writes=writes)

            def tt(o, a, b, op):
                V(lambda e: e.tensor_tensor(out=o[:], in0=a[:], in1=b[:], op=op), [a, b], [o])

            def tsc(o, a, s1, op0, s2=None, op1=None):
                if op1 is None:
                    V(lambda e: e.tensor_scalar(out=o[:], in0=a[:], scalar1=s1, scalar2=None, op0=op0), [a], [o])
                else:
                    V(lambda e: e.tensor_scalar(out=o[:], in0=a[:], scalar1=s1, scalar2=s2, op0=op0, op1=op1), [a], [o])

            def frac_sin(o, turns_src, mul, add):
                tsc(sm["t0"], turns_src, mul, ALU.mult, add, ALU.add)
                V(lambda e: e.tensor_copy(out=smi[:], in_=sm["t0"][:]), [sm["t0"]], [smi])
                tt(sm["f"], sm["t0"], smi, ALU.subtract)
                A(lambda e: e.activation(out=o[:], in_=sm["f"][:], func=AF.Sin, scale=TWO_PI), [sm["f"]], [o])

            A(lambda e: e.activation(out=sm["dt"][:], in_=ldt[:], func=AF.Exp), [ldt], [sm["dt"]])
            tt(sm["xr"], lre, sm["dt"], ALU.mult)
            tt(sm["xi"], lim, sm["dt"], ALU.mult)
            A(lambda e: e.activation(out=sm["rho"][:], in_=sm["xr"][:], func=AF.Exp), [sm["xr"]], [sm["rho"]])
            A(lambda e: e.activation(out=sm["rho8"][:], in_=sm["xr"][:], func=AF.Exp, scale=8.0), [sm["xr"]], [sm["rho8"]])
            tsc(sm["th"], sm["xi"], 1.0 / TWO_PI, ALU.mult)
            tsc(sm["th8"], sm["th"], 8.0, ALU.mult)
            frac_sin(sm["sinx"], sm["th"], 1.0, 0.0)
            frac_sin(sm["cosx"], sm["th"], 1.0, 0.25)
            frac_sin(sm["sinh"], sm["th"], 0.5, 0.0)
            tsc(sm["em1"], sm["xr"], 0.2, ALU.mult, 1.0, ALU.add)
            for cdiv in (0.25, 1.0 / 3.0, 0.5):
                tt(sm["em1"], sm["em1"], sm["xr"], ALU.mult)
                tsc(sm["em1"], sm["em1"], cdiv, ALU.mult, 1.0, ALU.add)
            tt(sm["em1"], sm["em1"], sm["xr"], ALU.mult)
            tt(sm["am1"], sm["em1"], sm["cosx"], ALU.mult)
            tt(sm["u0"], sm["sinh"], sm["sinh"], ALU.mult)
            V(lambda e: e.scalar_tensor_tensor(out=sm["am1"][:], in0=sm["u0"][:], scalar=-2.0, in1=sm["am1"][:], op0=ALU.mult, op1=ALU.add),
              [sm["u0"], sm["am1"]], [sm["am1"]])
            tt(sm["abi"], sm["rho"], sm["sinx"], ALU.mult)
            tt(sm["den"], lre, lre, ALU.mult)
            tt(sm["u0"], lim, lim, ALU.mult)
            tt(sm["den"], sm["den"], sm["u0"], ALU.add)
            V(lambda e: e.reciprocal(out=sm["den"][:], in_=sm["den"][:]), [sm["den"]], [sm["den"]])
            tt(sm["u0"], sm["am1"], lre, ALU.mult)
            tt(sm["u1"], sm["abi"], lim, ALU.mult)
            tt(sm["u0"], sm["u0"], sm["u1"], ALU.add)
            tt(sm["kr"], sm["u0"], sm["den"], ALU.mult)
            tt(sm["u0"], sm["abi"], lre, ALU.mult)
            tt(sm["u1"], sm["am1"], lim, ALU.mult)
            tt(sm["u0"], sm["u0"], sm["u1"], ALU.subtract)
            tt(sm["ki"], sm["u0"], sm["den"], ALU.mult)
            tsc(sm["t1"], sm["ki"], -1.0, ALU.mult)
            nki = sb("nki", [128, NT], F32)
            V(lambda e: e.tensor_copy(out=nki[:], in_=sm["t1"][:]), [sm["t1"]], [nki])
            for jj in range(16):
                jv = float(jj - 7)
                A(lambda e, jv=jv: e.activation(out=sm["pm"][:], in_=sm["xr"][:], func=AF.Exp, scale=jv), [sm["xr"]], [sm["pm"]])
                frac_sin(sm["ps"], sm["th"], jv, 0.0)
                frac_sin(sm["pc"], sm["th"], jv, 0.25)
                V(lambda e, jj=jj: e.tensor_tensor(out=pwr[:, jj, :], in0=sm["pm"][:], in1=sm["pc"][:], op=ALU.mult), [sm["pm"], sm["pc"]], [pwr])
                V(lambda e, jj=jj: e.tensor_tensor(out=pwi[:, jj, :], in0=sm["pm"][:], in1=sm["ps"][:], op=ALU.mult), [sm["pm"], sm["ps"]], [pwi])
            V(lambda e: e.tensor_scalar(out=npwi[:], in0=pwi[:], scalar1=-1.0, scalar2=None, op0=ALU.mult), [pwi], [npwi])
            for b_ in (pwr, pwi, npwi, sm["kr"], sm["ki"], nki, sm["rho8"], sm["th8"]):
                b_.const = True

            Dd = sb("Dd", [128, 4, 128], BF16)
            for q in range(4):
                V(lambda e, q=q: e.tensor_scalar(out=Dd[:, q, :], in0=ident[:], scalar1=dsk[:, q:q + 1], scalar2=None, op0=ALU.mult),
                  [ident, dsk], [Dd])
            Dd.const = True

            NSET = 2
            bz = [[sb(f"bz{i}_{k}", [128, 2, 128], F32, dma=True) for k in range(2)] for i in range(NSET)]
            cbt = [[sb(f"cbt{i}_{k}", [32, 2, 128], F32, dma=True) for k in range(2)] for i in range(NSET)]
            Bz = [[sb(f"Bz{i}_{k}", [128, 2, 128], F32) for k in range(2)] for i in range(NSET)]
            CT = [[sb(f"CT{i}_{k}", [128, 2, 64], F32) for k in range(2)] for i in range(NSET)]
            XT = [[sb(f"XT{i}_{k}", [128, 8, 2, 128], BF16) for k in range(2)] for i in range(NSET)]
            KT = [[sb(f"KT{i}_{k}", [128, 8, 64], BF16) for k in range(2)] for i in range(NSET)]
            LY = [[sb(f"LY{i}_{k}", [128, 8, 2, 64], BF16) for k in range(2)] for i in range(NSET)]
            cosN = [[sb(f"cosN{i}_{k}", [128, NCH], F32) for k in range(2)] for i in range(NSET)]
            sinN = [[sb(f"sinN{i}_{k}", [128, NCH], F32) for k in range(2)] for i in range(NSET)]
            rho8T = [[sb(f"rho8T{i}_{k}", [128, NCH], F32) for k in range(2)] for i in range(NSET)]
            for i in range(NSET):
                for k in range(2):
                    p.op("pool", lambda e, i=i, k=k: e.memset(CT[i][k][:], 0.0), writes=[CT[i][k]])
            xtmp = Ring([sb(f"xtmp{i}", [128, 2, 128], F32) for i in range(3)])
            lyf = Ring([sb(f"lyf{i}", [128, 2, 64], F32) for i in range(3)])
            turN = sb("turN", [128, NCH], F32); turNi = sb("turNi", [128, NCH], I32)
            uTr = Ring([sb(f"uTc{i}", [128, S], BF16, dma=True) for i in range(2)])
            uDr = Ring([sb(f"uD{i}", [128, 8, NCH], BF16) for i in range(2)])
            tmpr = Ring([sb(f"tmpS{i}", [128, NCH], F32) for i in range(8)])
            wrr = Ring([sb(f"wS{i}", [128, NCH], F32) for i in range(4)])
            Rrr = Ring([sb(f"RS{i}", [128, NCH], F32) for i in range(4)])
            Vrr = Ring([sb(f"VS{i}", [128, NCH], F32) for i in range(4)])
            Zr_ = [Ring([sb(f"ZS{k}_{i}", [128, 2, NCH], BF16) for i in range(2)]) for k in range(2)]
            zor = Ring([sb(f"zo{i}", [128, S], BF16, dma=True) for i in range(2)])
            psT = Ring(psum[4:8])
            psSt = [psum[0:2], psum[2:4]]

            def cmul(o, orow, oi_row, src, sr, si, nsi):
                V(lambda e: e.tensor_scalar(out=o[:, 0, :], in0=src[:, 0, :], scalar1=sr, scalar2=None, op0=ALU.mult), [src], [o])
                V(lambda e: e.scalar_tensor_tensor(out=o[:, 0, :], in0=src[:, 1, :], scalar=nsi, in1=o[:, 0, :], op0=ALU.mult, op1=ALU.add), [src, o], [o])
                V(lambda e: e.tensor_scalar(out=o[:, 1, :], in0=src[:, 1, :], scalar1=sr, scalar2=None, op0=ALU.mult), [src], [o])
                V(lambda e: e.scalar_tensor_tensor(out=o[:, 1, :], in0=src[:, 0, :], scalar=si, in1=o[:, 1, :], op0=ALU.mult, op1=ALU.add), [src, o], [o])

            def prep(gp, st):
                for k in range(2):
                    j = k * 16 + gp
                    p.dma(lambda e: e.dma_start(out=bz[st][k][:, 0, :], in_=ssm_d["bzr"][:, j, :]), bz[st][k], True)
                    p.dma(lambda e: e.dma_start(out=bz[st][k][:, 1, :], in_=ssm_d["bzi"][:, j, :]), bz[st][k], True)
                    p.dma(lambda e: e.dma_start(out=cbt[st][k][:, 0, :], in_=ssm_d["cbr"][:, j, :]), cbt[st][k], True)
                    p.dma(lambda e: e.dma_start(out=cbt[st][k][:, 1, :], in_=ssm_d["cbi"][:, j, :]), cbt[st][k], True)
                    cmul(Bz[st][k], None, None, bz[st][k], sm["kr"][:, j:j + 1], sm["ki"][:, j:j + 1], nki[:, j:j + 1])
                    ps = psT.next()
                    p.ops("pe", [lambda e: e.transpose(out=ps[:, 0:32], in_=cbt[st][k][:, 0, :], identity=ident[0:32, 0:32]),
                                 lambda e: e.transpose(out=ps[:, 32:64], in_=cbt[st][k][:, 1, :], identity=ident[0:32, 0:32])],
                          reads=[cbt[st][k], ident], writes=[ps])
                    A(lambda e: e.copy(out=CT[st][k][:, 0, 32:64], in_=ps[:, 0:32]), [ps], [CT[st][k]])
                    A(lambda e: e.mul(out=CT[st][k][:, 1, 32:64], in_=ps[:, 32:64], mul=-1.0), [ps], [CT[st][k]])
                    for s_ in range(8):
                        if s_ == 0:
                            xs_ = Bz[st][k]
                        else:
                            xs_ = xtmp.next()
                            jj = 7 - s_
                            cmul(xs_, None, None, Bz[st][k], pwr[:, jj, j:j + 1], pwi[:, jj, j:j + 1], npwi[:, jj, j:j + 1])
                        ps = psT.next()
                        p.ops("pe", [lambda e: e.transpose(out=ps[:, 0:128], in_=xs_[:, 0, :], identity=ident[:]),
                                     lambda e: e.transpose(out=ps[:, 128:256], in_=xs_[:, 1, :], identity=ident[:])],
                              reads=[xs_, ident], writes=[ps])
                        A(lambda e: e.copy(out=XT[st][k][:, s_, :, :], in_=ps[:, 0:256].rearrange("p (r c) -> p r c", r=2)), [ps], [XT[st][k]])
                    for tau in range(8):
                        ly = lyf.next()
                        jj = 7 + tau
                        ctr = CT[st][k]
                        V(lambda e: e.tensor_scalar(out=ly[:, 0, :], in0=ctr[:, 0, :], scalar1=pwr[:, jj, j:j + 1], scalar2=None, op0=ALU.mult), [ctr], [ly])
                        V(lambda e: e.scalar_tensor_tensor(out=ly[:, 0, :], in0=ctr[:, 1, :], scalar=pwi[:, jj, j:j + 1], in1=ly[:, 0, :], op0=ALU.mult, op1=ALU.add), [ctr, ly], [ly])
                        V(lambda e: e.tensor_scalar(out=ly[:, 1, :], in0=ctr[:, 1, :], scalar1=pwr[:, jj, j:j + 1], scalar2=None, op0=ALU.mult), [ctr], [ly])
                        V(lambda e: e.scalar_tensor_tensor(out=ly[:, 1, :], in0=ctr[:, 0, :], scalar=npwi[:, jj, j:j + 1], in1=ly[:, 1, :], op0=ALU.mult, op1=ALU.add), [ctr, ly], [ly])
                        A(lambda e: e.copy(out=LY[st][k][:, tau, :, :], in_=ly[:]), [ly], [LY[st][k]])
                        ps = psT.next()
                        p.ops("pe", [lambda e: e.matmul(ps[:, 0:64], lhsT=Bz[st][k][:, 0, :], rhs=ly[:, 0, :], start=True, stop=False),
                                     lambda e: e.matmul(ps[:, 0:64], lhsT=Bz[st][k][:, 1, :], rhs=ly[:, 1, :], start=False, stop=True)],
                              reads=[Bz[st][k], ly], writes=[ps])
                        A(lambda e: e.copy(out=KT[st][k][:, tau, :], in_=ps[:, 0:64]), [ps], [KT[st][k]])
                    for (tab, addc) in ((sinN[st][k], 0.0), (cosN[st][k], 0.25)):
                        V(lambda e: e.tensor_scalar(out=turN[:], in0=tI[:], scalar1=sm["th8"][:, j:j + 1], scalar2=addc, op0=ALU.mult, op1=ALU.add), [tI], [turN])
                        V(lambda e: e.tensor_copy(out=turNi[:], in_=turN[:]), [turN], [turNi])
                        V(lambda e: e.tensor_tensor(out=turN[:], in0=turN[:], in1=turNi[:], op=ALU.subtract), [turN, turNi], [turN])
                        A(lambda e: e.activation(out=tab[:], in_=turN[:], func=AF.Sin, scale=TWO_PI), [turN], [tab])
                    V(lambda e: e.tensor_scalar(out=rho8T[st][k][:], in0=tI[:], scalar1=0.0, scalar2=sm["rho8"][:, j:j + 1], op0=ALU.mult, op1=ALU.add), [tI], [rho8T[st][k]])

            Ssb = Ring([sb(f"Ssb{i}", [128, 4, NCH], F32) for i in range(2)])

            def geom(gp):
                q = gp // 4
                qq = gp % 4
                if qq < 3:
                    return q, slice(32 * qq, 32 * qq + 32), slice(32, 64), slice(32 * qq, 32 * qq + 32), slice(32 * qq, 32 * qq + 32)
                return q, slice(64, 128), slice(0, 64), slice(64, 128), slice(32 * qq, 32 * qq + 32)

            def stageA(gp, st, sq):
                q = gp // 4
                uT = uTr.next()
                p.dma(lambda e: e.dma_start(out=uT[:], in_=uT_s[sq, q * 128:(q + 1) * 128, :]), uT, True)
                uD = uDr.next()
                A(lambda e: e.copy(out=uD[:], in_=uT[:].rearrange("p (n s) -> p s n", s=8)), [uT], [uD])
                ss = Ssb.next()
                for k in range(2):
                    for ri in range(2):
                        pb_ = psSt[k][ri]
                        if k == 0:
                            fns = [lambda e, s_=s_: e.matmul(pb_[:], lhsT=XT[st][k][:, s_, ri, :], rhs=uD[:, s_, :], start=(s_ == 0), stop=(s_ == 7)) for s_ in range(8)]
                        else:
                            fns = [lambda e, s_=s_: e.matmul(pb_[:], lhsT=XT[st][k][:, s_, ri, :], rhs=uD[:, 7 - s_, ::-1], start=(s_ == 0), stop=(s_ == 7)) for s_ in range(8)]
                        p.ops("pe", fns, reads=[XT[st][k], uD], writes=[pb_])
                        A(lambda e: e.copy(out=ss[:, 2 * k + ri, :], in_=pb_[:]), [pb_], [ss])
                return (gp, st, sq, uD, ss)

            def stageB(ctx):
                gp, st, sq, uD, ss = ctx
                T = [[tmpr.next() for _ in range(4)] for k in range(2)]
                for (ti, si, tab) in ((0, 0, cosN), (1, 1, sinN), (2, 1, cosN), (3, 0, sinN)):
                    for k in range(2):
                        V(lambda e, k=k: e.tensor_tensor(out=T[k][ti][:], in0=ss[:, 2 * k + si, :], in1=tab[st][k][:], op=ALU.mult), [ss, tab[st][k]], [T[k][ti]])
                W = [[wrr.next(), wrr.next()] for k in range(2)]
                for k in range(2):
                    V(lambda e, k=k: e.tensor_tensor(out=W[k][0][:], in0=T[k][0][:], in1=T[k][1][:], op=ALU.add), [T[k][0], T[k][1]], [W[k][0]])
                for k in range(2):
                    V(lambda e, k=k: e.tensor_tensor(out=W[k][1][:], in0=T[k][2][:], in1=T[k][3][:], op=ALU.subtract), [T[k][2], T[k][3]], [W[k][1]])
                R = [[Rrr.next(), Rrr.next()] for k in range(2)]
                for ri in range(2):
                    for k in range(2):
                        V(lambda e, k=k, ri=ri: e.tensor_tensor_scan(out=R[k][ri][:], data0=rho8T[st][k][:], data1=W[k][ri][:], initial=0.0, op0=ALU.mult, op1=ALU.add),
                          [rho8T[st][k], W[k][ri]], [R[k][ri]])
                T = [[tmpr.next() for _ in range(4)] for k in range(2)]
                for (ti, si, tab) in ((0, 0, cosN), (1, 1, sinN), (2, 1, cosN), (3, 0, sinN)):
                    for k in range(2):
                        V(lambda e, k=k: e.tensor_tensor(out=T[k][ti][:], in0=R[k][si][:], in1=tab[st][k][:], op=ALU.mult), [R[k][si], tab[st][k]], [T[k][ti]])
                Vv = [[Vrr.next(), Vrr.next()] for k in range(2)]
                for k in range(2):
                    V(lambda e, k=k: e.tensor_tensor(out=Vv[k][0][:], in0=T[k][0][:], in1=T[k][1][:], op=ALU.subtract), [T[k][0], T[k][1]], [Vv[k][0]])
                for k in range(2):
                    V(lambda e, k=k: e.tensor_tensor(out=Vv[k][1][:], in0=T[k][2][:], in1=T[k][3][:], op=ALU.add), [T[k][2], T[k][3]], [Vv[k][1]])
                Z = [Zr_[k].next() for k in range(2)]
                for ri in range(2):
                    for k in range(2):
                        V(lambda e, k=k, ri=ri: e.tensor_tensor(out=Z[k][:, ri, :], in0=Vv[k][ri][:], in1=ss[:, 2 * k + ri, :], op=ALU.subtract), [Vv[k][ri], ss], [Z[k]])
                return ctx + (Z,)

            def stageC(ctx):
                gp, st, sq, uD, ss, Z = ctx
                q, rows, lcs, dds, orow = geom(gp)
                zo = zor.next()
                for tau in range(8):
                    py = psT.next()
                    fns = [
                        lambda e: e.matmul(py[rows, :], lhsT=LY[st][0][:, tau, 0, lcs], rhs=Z[0][:, 0, :], start=True, stop=False),
                        lambda e: e.matmul(py[rows, :], lhsT=LY[st][0][:, tau, 1, lcs], rhs=Z[0][:, 1, :], start=False, stop=False),
                        lambda e: e.matmul(py[rows, :], lhsT=LY[st][1][:, 7 - tau, 0, lcs], rhs=Z[1][:, 0, ::-1], start=False, stop=False),
                        lambda e: e.matmul(py[rows, :], lhsT=LY[st][1][:, 7 - tau, 1, lcs], rhs=Z[1][:, 1, ::-1], start=False, stop=False),
                    ]
                    for s_ in range(0, tau + 1):
                        fns.append(lambda e, s_=s_: e.matmul(py[rows, :], lhsT=KT[st][0][:, tau - s_, lcs], rhs=uD[:, s_, :], start=False, stop=False))
                    for s_ in range(tau, 8):
                        fns.append(lambda e, s_=s_: e.matmul(py[rows, :], lhsT=KT[st][1][:, s_ - tau, lcs], rhs=uD[:, s_, :], start=False, stop=False))
                    fns.append(lambda e: e.matmul(py[rows, :], lhsT=Dd[:, q, dds], rhs=uD[:, tau, :], start=False, stop=True))
                    p.ops("pe", fns, reads=[LY[st][0], LY[st][1], KT[st][0], KT[st][1], Dd, Z[0], Z[1], uD], writes=[py])
                    A(lambda e: e.activation(out=zo[rows, tau:S:8], in_=py[rows, :], func=AF.Gelu), [py], [zo])
                p.dma(lambda e: e.dma_start(out=zT_s[sq, gp * 32:(gp + 1) * 32, :], in_=zo[orow, :]), zo, False)

            runs = [(gp, gp % NSET, sq) for gp in range(16) for sq in range(nseq)]
            prep(0, 0)
            ctxA = stageA(*runs[0])
            for i, (gp, st, sq) in enumerate(runs):
                if sq == 0 and gp + 1 < 16:
                    prep(gp + 1, (gp + 1) % NSET)
                nxt = stageA(*runs[i + 1]) if i + 1 < len(runs) else None
                ctxB = stageB(ctxA)
                stageC(ctxB)
                ctxA = nxt
                if sq == nseq - 1:
                    stop(f'S_gp{gp}')
        new_phase()
        stop('S')

        with ExitStack() as es:
            def sb(name, shape, dt, dma=False, const=False):
                return p.buf(es.enter_context(nc.sbuf_tensor(name, list(shape), dt)), dma=dma, const=const)

            mstage = sb("mstage", [128, 256], F32, dma=True)
            ostage = sb("ostage", [128, 3, 64], F32, dma=True)
            maskB = sb("maskB", [128, 256], BF16); ones3 = sb("ones3", [128, 3, 64], BF16); identb = sb("identb", [128, 128], BF16)
            p.dma(lambda e: e.dma_start(out=mstage[:], in_=md["maskb"]), mstage, True)
            p.dma(lambda e: e.dma_start(out=ostage[:], in_=md["ones3"]), ostage, True)
            p.op("dve", lambda e: e.tensor_copy(out=maskB[:], in_=mstage[:]), reads=[mstage], writes=[maskB])
            p.op("dve", lambda e: e.tensor_copy(out=ones3[:], in_=ostage[:]), reads=[ostage], writes=[ones3])
            p.op("dve", lambda e: e.tensor_copy(out=identb[:], in_=ident[:]), reads=[ident], writes=[identb])
            maskB.const = True; ones3.const = True; identb.const = True
            qTr = Ring([sb(f"qTa{i}", [128, S], BF16, dma=True) for i in range(2)])
            kSr = Ring([sb(f"kSa{i}", [128, S], BF16, dma=True) for i in range(2)])
            qDr = Ring([sb(f"qDa{i}", [128, S], BF16) for i in range(2)])
            kTr = Ring([sb(f"kTa{i}", [128, S + 2 * KPAD], BF16) for i in range(2)])
            vTr = Ring([sb(f"vTa{i}", [128, 48, 128], BF16, dma=True) for i in range(2)])
            acc = sb("acc", [128, 2, S], F32)
            rden = sb("rden", [128, S], F32)
            aTo = sb("aTo", [128, S], BF16, dma=True)
            PTr = Ring([sb(f"PT{i}", [128, 256], BF16) for i in range(6)])
            psS = Ring(psum[0:4]); psO = Ring(psum[4:8])
            SCALE = 64.0 ** -0.5
            for sq in range(nseq):
                for c in range(2):
                    for g in range(3):
                        d = DIL[g]; L = S // d; nb = L // 128 + 1
                        qN = qTr.next(); kS = kSr.next(); qT = qDr.next(); kT = kTr.next(); vT = vTr.next()
                        ch = 2 * g + c
                        LP = L + 128
                        p.dma(lambda e: e.dma_start(out=qN[:], in_=qT_s[sq, ch * 128:(ch + 1) * 128, :]), qN, True)
                        p.dma(lambda e: e.dma_start(out=kS[:], in_=kT_s[sq, ch * 128:(ch + 1) * 128, :]), kS, True)
                        p.op("pool", lambda e: e.memset(kT[:, 0:d * LP], 0.0), writes=[kT])
                        p.op("act", lambda e: e.copy(out=qT[:].rearrange("p (r i) -> p r i", r=d), in_=qN[:].rearrange("p (i r) -> p r i", r=d)), reads=[qN], writes=[qT])
                        p.op("dve", lambda e: e.tensor_copy(out=kT[:, 0:d * LP].rearrange("p (r i) -> p r i", r=d)[:, :, 64:64 + L], in_=kS[:].rearrange("p (i r) -> p r i", r=d)),
                             reads=[kS], writes=[kT])
                        p.dma(lambda e: e.dma_start(out=vT[:, 0:NBLK[g], :], in_=v_s[g][sq, :, :, c * 128:(c + 1) * 128]), vT, True)
                        def stS(r, a, qT=qT, kT=kT, L=L, LP=LP):
                            qcs = slice(r * L + 128 * a, r * L + 128 * a + 128)
                            pts = []
                            for hp in range(2):
                                pb = 64 * hp
                                pS = psS.next()
                                ks = [slice(r * LP + 128 * m, r * LP + 128 * m + 128) for m in (a, a + 1)]
                                p.ops("pe", [
                                    lambda e: e.matmul(pS[:, 0:128], lhsT=kT[pb:pb + 64, ks[0]], rhs=qT[pb:pb + 64, qcs], start=True, stop=False),
                                    lambda e: e.matmul(pS[:, 128:256], lhsT=kT[pb:pb + 64, ks[1]], rhs=qT[pb:pb + 64, qcs], start=False, stop=False),
                                    lambda e: e.matmul(pS[:, 0:256], lhsT=identb[:], rhs=maskB[:], start=False, stop=True),
                                ], reads=[kT, qT, identb, maskB], writes=[pS])
                                PT = PTr.next()
                                p.op("act", lambda e: e.activation(out=PT[:], in_=pS[:, 0:256], func=AF.Exp, scale=SCALE), reads=[pS], writes=[PT])
                                pts.append(PT)
                            return pts

                        def stPV(r, a, pts, vT=vT, d=d, nb=nb, g=g):
                            qsl = slice(r + d * 128 * a, r + d * 128 * a + d * 127 + 1, d)
                            pO = psO.next()
                            fns = []
                            for hp in range(2):
                                pb = 64 * hp
                                PT = pts[hp]
                                o1 = 1 if a == 0 else 0
                                o2 = 2 if a + 1 == nb - 1 else 0
                                b1 = r * nb + a; b2 = r * nb + a + 1
                                fns += [
                                    lambda e, PT=PT, pb=pb, b1=b1, hp=hp: e.matmul(pO[pb:pb + 64, 0:128], lhsT=vT[:, b1, hp * 64:(hp + 1) * 64], rhs=PT[:, 0:128], start=True, stop=False),
                                    lambda e, PT=PT, pb=pb, b2=b2, hp=hp: e.matmul(pO[pb:pb + 64, 0:128], lhsT=vT[:, b2, hp * 64:(hp + 1) * 64], rhs=PT[:, 128:256], start=False, stop=False),
                                    lambda e, PT=PT, pb=pb, o1=o1: e.matmul(pO[pb:pb + 64, 128:256], lhsT=ones3[:, o1, :], rhs=PT[:, 0:128], start=False, stop=False),
                                    lambda e, PT=PT, pb=pb, o2=o2: e.matmul(pO[pb:pb + 64, 128:256], lhsT=ones3[:, o2, :], rhs=PT[:, 128:256], start=False, stop=True),
                                ]
                            p.ops("pe", fns, reads=pts + [vT, ones3], writes=[pO])
                            pov = pO[:, 0:256].rearrange("p (n i) -> p n i", n=2)
                            if g == 0:
                                p.op("dve", lambda e: e.tensor_copy(out=acc[:, :, qsl], in_=pov), reads=[pO], writes=[acc])
                            else:
                                p.op("dve", lambda e: e.tensor_tensor(out=acc[:, :, qsl], in0=pov, in1=acc[:, :, qsl], op=ALU.add), reads=[pO, acc], writes=[acc])

                        units = [(r, a) for r in range(d) for a in range(L // 128)]
                        cur = stS(*units[0])
                        for ui, (r, a) in enumerate(units):
                            nxt = stS(*units[ui + 1]) if ui + 1 < len(units) else None
                            stPV(r, a, cur)
                            cur = nxt
                    p.op("dve", lambda e: e.reciprocal(out=rden[:], in_=acc[:, 1, :]), reads=[acc], writes=[rden])
                    p.op("dve", lambda e: e.tensor_tensor(out=aTo[:], in0=acc[:, 0, :], in1=rden[:], op=ALU.mult), reads=[acc, rden], writes=[aTo])
                    p.dma(lambda e: e.dma_start(out=aT_s[sq, c * 128:(c + 1) * 128, :], in_=aTo[:]), aTo, False)
        new_phase()
        stop('T')

        def load_w_bf16(sbf, wdst, src, K, N, tag, stg=None):
            piece = 1024 if N >= 1024 else N
            if stg is None:
                stg = Ring([sbf(f"wl_{tag}{i}", [128, piece], F32, dma=True) for i in range(2)])
            engs = ("dve", "pool", "act")
            n = 0
            for k in range(K):
                for c0 in range(0, N, piece):
                    w = min(piece, N - c0)
                    st = stg.next()
                    p.dma(lambda e, st=st, k=k, c0=c0, w=w: e.dma_start(out=st[:, 0:w], in_=src[k * 128:(k + 1) * 128, c0:c0 + w]), st, True)
                    eng = engs[n % 3]; n += 1
                    if eng == "act":
                        p.op("act", lambda e, st=st, k=k, c0=c0, w=w: e.copy(out=wdst[:, k, c0:c0 + w], in_=st[:, 0:w]), reads=[st], writes=[wdst])
                    else:
                        p.op(eng, lambda e, st=st, k=k, c0=c0, w=w: e.tensor_copy(out=wdst[:, k, c0:c0 + w], in_=st[:, 0:w]), reads=[st], writes=[wdst])
            wdst.const = True

        def layer_norm_multi(items, gB, bB):
            for (stats, mv, rstd, nmr, hn), hpre, outt in items:
                for n in range(2):
                    p.op("dve", lambda e, n=n: e.bn_stats(out=stats[:, n, :], in_=hpre[:, n * 512:(n + 1) * 512]), reads=[hpre], writes=[stats])
            for (stats, mv, rstd, nmr, hn), hpre, outt in items:
                p.op("dve", lambda e: e.bn_aggr(out=mv[:], in_=stats[:].rearrange("p n s -> p (n s)")), reads=[stats], writes=[mv])
            for (stats, mv, rstd, nmr, hn), hpre, outt in items:
                p.op("act", lambda e: e.activation(out=rstd[:], in_=mv[:, 1:2], func=AF.Sqrt, bias=epsb[:, 0:1]), reads=[mv, epsb], writes=[rstd])
            for (stats, mv, rstd, nmr, hn), hpre, outt in items:
                p.op("dve", lambda e: e.reciprocal(out=rstd[:], in_=rstd[:]), reads=[rstd], writes=[rstd])
            for (stats, mv, rstd, nmr, hn), hpre, outt in items:
                p.op("dve", lambda e: e.scalar_tensor_tensor(out=nmr[:], in0=mv[:, 0:1], scalar=-1.0, in1=rstd[:], op0=ALU.mult, op1=ALU.mult), reads=[mv, rstd], writes=[nmr])
            for (stats, mv, rstd, nmr, hn), hpre, outt in items:
                p.op("act", lambda e: e.activation(out=hn[:], in_=hpre[:], func=AF.Identity, scale=rstd[:, 0:1], bias=nmr[:, 0:1]), reads=[hpre, rstd, nmr], writes=[hn])
            for (stats, mv, rstd, nmr, hn), hpre, outt in items:
                p.op("dve", lambda e: e.tensor_tensor(out=hn[:], in0=hn[:], in1=gB[:], op=ALU.mult), reads=[hn, gB], writes=[hn])
            for (stats, mv, rstd, nmr, hn), hpre, outt in items:
                p.op("dve", lambda e: e.tensor_tensor(out=outt[:], in0=hn[:], in1=bB[:], op=ALU.add), reads=[hn, bB], writes=[outt])

        with ExitStack() as es:
            def sb(name, shape, dt, dma=False, const=False):
                return p.buf(es.enter_context(nc.sbuf_tensor(name, list(shape), dt)), dma=dma, const=const)
            wgv = sb("wgv", [128, 4, D], BF16, dma=True); wgg = sb("wgg", [128, 4, D], BF16, dma=True); wab = sb("wab", [128, 2, D], BF16, dma=True); wo = sb("wo", [128, 8, D], BF16, dma=True)
            load_wbf(wgv, "wgv", 4); load_wbf(wgg, "wgg", 4); load_wbf(wab, "wab", 2); load_wbf(wo, "wo", 8)
            gB = sb("ln1gB", [128, D], F32, dma=True, const=True); bB = sb("ln1bB", [128, D], F32, dma=True, const=True)
            p.dma(lambda e: e.dma_start(out=gB[:], in_=md["ln1g"].partition_broadcast(128)), gB, True)
            p.dma(lambda e: e.dma_start(out=bB[:], in_=md["ln1b"].partition_broadcast(128)), bB, True)
            epsb = sb("epsb", [128, 1], F32)
            p.op("pool", lambda e: e.memset(epsb[:], LN_EPS), writes=[epsb])
            zTr_ = Ring([sb(f"zTm{i}", [128, 4, 512], BF16, dma=True) for i in range(2)]); aTr_ = Ring([sb(f"aTm{i}", [128, 2, 512], BF16, dma=True) for i in range(2)])
            gTr_ = Ring([sb(f"gTm{i}", [128, 16, 512], BF16, dma=True) for i in range(2)])
            xs = Ring([sb(f"xm{i}", [128, D], F32, dma=True) for i in range(2)])
            mixT = sb("mixT", [128, 8, 512], BF16)
            sg = Ring([sb(f"sg{i}", [128, 512], F32) for i in range(2)])
            t1r = Ring([sb(f"t1m{i}", [128, 512], F32) for i in range(2)])
            t2r = Ring([sb(f"t2m{i}", [128, 512], F32) for i in range(2)])
            hpre = Ring([sb(f"hpre{i}", [128, D], F32) for i in range(2)])
            hout = Ring([sb(f"hout{i}", [128, D], F32, dma=True) for i in range(2)])
            hTt = Ring([sb(f"hTt{i}", [128, 8, 128], BF16, dma=True) for i in range(2)])
            lnbs = [(sb(f"st1_{i}", [128, 2, 6], F32), sb(f"mv1_{i}", [128, 2], F32), sb(f"rstd1_{i}", [128, 1], F32), sb(f"nmr1_{i}", [128, 1], F32), sb(f"hn1_{i}", [128, D], F32)) for i in range(2)]
            psr = Ring(psum)
            def load_m1(sq_, tb_):
                ts_ = slice(tb_ * 512, (tb_ + 1) * 512)
                zT_ = zTr_.next(); aT_ = aTr_.next(); gT_ = gTr_.next()
                p.dma(lambda e: e.dma_start(out=zT_[:], in_=zT_s[sq_].rearrange("(k q) t -> q k t", q=128)[:, :, ts_]), zT_, True)
                p.dma(lambda e: e.dma_start(out=aT_[:], in_=aT_s[sq_].rearrange("(k q) t -> q k t", q=128)[:, :, ts_]), aT_, True)
                p.dma(lambda e: e.dma_start(out=gT_[:], in_=gT_s[sq_].rearrange("(k q) t -> q k t", q=128)[:, :, ts_]), gT_, True)
                return zT_, aT_, gT_
            blocks_m1 = [(sq_, tb_) for sq_ in range(nseq) for tb_ in range(S // 512)]
            nxt_in = load_m1(*blocks_m1[0])
            for bi, (sq, tb) in enumerate(blocks_m1):
                if True:
                    ts = slice(tb * 512, (tb + 1) * 512)
                    zT, aT, gT = nxt_in
                    if bi + 1 < len(blocks_m1):
                        nxt_in = load_m1(*blocks_m1[bi + 1])
                    for do in range(8):
                        ds_ = slice(do * 128, (do + 1) * 128)
                        pA = psr.next(); pG = psr.next(); pB = psr.next()
                        p.ops("pe", [lambda e, k=k: e.matmul(pA[:], lhsT=wgv[:, k, ds_], rhs=zT[:, k, :], start=(k == 0), stop=(k == 3)) for k in range(4)], reads=[wgv, zT], writes=[pA])
                        p.ops("pe", [lambda e, k=k: e.matmul(pG[:], lhsT=wgg[:, k, ds_], rhs=zT[:, k, :], start=(k == 0), stop=(k == 3)) for k in range(4)], reads=[wgg, zT], writes=[pG])
                        p.ops("pe", [lambda e, k=k: e.matmul(pB[:], lhsT=wab[:, k, ds_], rhs=aT[:, k, :], start=(k == 0), stop=(k == 1)) for k in range(2)], reads=[wab, aT], writes=[pB])
                        sgt = sg.next(); t1 = t1r.next(); t2 = t2r.next()
                        p.op("act", lambda e: e.activation(out=sgt[:], in_=pG[:], func=AF.Sigmoid), reads=[pG], writes=[sgt])
                        p.op("dve", lambda e: e.tensor_tensor(out=t1[:], in0=pA[:], in1=sgt[:], op=ALU.mult), reads=[pA, sgt], writes=[t1])
                        p.op("dve", lambda e: e.tensor_tensor(out=t2[:], in0=pB[:], in1=gT[:, 8 + do, :], op=ALU.mult), reads=[pB, gT], writes=[t2])
                        p.op("dve", lambda e: e.tensor_tensor(out=t1[:], in0=t1[:], in1=gT[:, do, :], op=ALU.mult), reads=[t1, gT], writes=[t1])
                        p.op("dve", lambda e: e.tensor_tensor(out=mixT[:, do, :], in0=t1[:], in1=t2[:], op=ALU.add), reads=[t1, t2], writes=[mixT])
                    for ip in (0, 2):
                        items = []
                        for i in (ip, ip + 1):
                            tok = slice(tb * 512 + i * 128, tb * 512 + (i + 1) * 128)
                            xt = xs.next()
                            p.dma(lambda e: e.dma_start(out=xt[:], in_=x_d[sq, tok, :]), xt, True)
                            hp_ = hpre.next()
                            for n in range(2):
                                ns = slice(n * 512, (n + 1) * 512)
                                po = psr.next()
                                p.ops("pe", [lambda e, k=k: e.matmul(po[:], lhsT=mixT[:, k, i * 128:(i + 1) * 128], rhs=wo[:, k, ns], start=(k == 0), stop=(k == 7)) for k in range(8)],
                                      reads=[mixT, wo], writes=[po])
                                p.op("dve", lambda e: e.scalar_tensor_tensor(out=hp_[:, ns], in0=xt[:, ns], scalar=ALPHA, in1=po[:], op0=ALU.mult, op1=ALU.add),
                                     reads=[xt, po], writes=[hp_])
                            ho = hout.next()
                            items.append((lnbs[i - ip], hp_, ho, tok))
                        layer_norm_multi([(a_, b_, c_) for (a_, b_, c_, _) in items], gB, bB)
                        for (_, _, ho, tok) in items:
                            p.dma(lambda e: e.dma_start(out=h_s[sq, tok, :], in_=ho[:]), ho, False)
                            hT = hTt.next()
                            for kk in range(2):
                                pt = psr.next()
                                p.ops("pe", [lambda e, k4=k4: e.transpose(out=pt[:, k4 * 128:(k4 + 1) * 128], in_=ho[:, (kk * 4 + k4) * 128:(kk * 4 + k4 + 1) * 128], identity=ident[:])
                                             for k4 in range(4)], reads=[ho, ident], writes=[pt])
                                p.op("act", lambda e: e.copy(out=hT[:, kk * 4:(kk + 1) * 4, :], in_=pt[:].rearrange("p (k t) -> p k t", k=4)), reads=[pt], writes=[hT])
                            p.dma(lambda e: e.dma_start(out=hT_s[sq].rearrange("(k q) t -> q k t", q=128)[:, :, tok], in_=hT[:]), hT, False)
        new_phase()
        stop('M1')

        with ExitStack() as es:
            def sb(name, shape, dt, dma=False, const=False):
                return p.buf(es.enter_context(nc.sbuf_tensor(name, list(shape), dt)), dma=dma, const=const)
            wup = sb("wup", [128, 8, 2 * DFF], BF16, dma=True); wdn = sb("wdn", [128, 22, D], BF16, dma=True)
            load_wbf(wup, "wup", 8); load_wbf(wdn, "wdn", 22)
            gB = sb("ln2gB", [128, D], F32, dma=True, const=True); bB = sb("ln2bB", [128, D], F32, dma=True, const=True)
            p.dma(lambda e: e.dma_start(out=gB[:], in_=md["ln2g"].partition_broadcast(128)), gB, True)
            p.dma(lambda e: e.dma_start(out=bB[:], in_=md["ln2b"].partition_broadcast(128)), bB, True)
            cw = sb("cw", [128, 44, 3], F32, dma=True, const=True); cbias = sb("cbias", [128, 44], F32, dma=True, const=True)
            p.dma(lambda e: e.dma_start(out=cw[:], in_=md["cw"]), cw, True)
            p.dma(lambda e: e.dma_start(out=cbias[:], in_=md["cbias"]), cbias, True)
            epsb = sb("epsb2", [128, 1], F32)
            p.op("pool", lambda e: e.memset(epsb[:], LN_EPS), writes=[epsb])
            hT = sb("hTf", [128, 8, 514], BF16, dma=True)
            hres = Ring([sb(f"hres{i}", [128, D], F32, dma=True) for i in range(1)])
            cvr = Ring([sb(f"cv{i}", [128, 512], F32) for i in range(8)])
            actT = sb("actT", [128, 22, 512], BF16)
            opre = sb("opre", [128, D], F32)
            oout = Ring([sb(f"oout{i}", [128, D], F32, dma=True) for i in range(1)])
            lnb = (sb("st2", [128, 2, 6], F32), sb("mv2", [128, 2], F32), sb("rstd2", [128, 1], F32), sb("nmr2", [128, 1], F32), sb("hn2", [128, D], F32))
            psm = Ring(psum[0:5])
            psh = Ring([(psum[5], 0), (psum[6], 0), (psum[7], 0)])
            for sq in range(nseq):
                for tb in range(S // 512):
                    t0 = tb * 512
                    lo = max(t0 - 1, 0); hi = min(t0 + 513, S)
                    if t0 == 0:
                        p.op("pool", lambda e: e.memset(hT[:, :, 0:1], 0.0), writes=[hT])
                    if t0 + 512 == S:
                        p.op("pool", lambda e: e.memset(hT[:, :, 513:514], 0.0), writes=[hT])
                    p.dma(lambda e: e.dma_start(out=hT[:, :, lo - (t0 - 1):hi - (t0 - 1)], in_=hT_s[sq].rearrange("(k q) t -> q k t", q=128)[:, :, lo:hi]), hT, True)
                    for c in range(22):
                        cvs = []
                        for ch in (c, 22 + c):
                            cs_ = slice(ch * 128, (ch + 1) * 128)
                            pm = psm.next(); ph, hc = psh.next()
                            p.ops("pe", [lambda e, k=k: e.matmul(pm[:], lhsT=wup[:, k, cs_], rhs=hT[:, k, 1:513], start=(k == 0), stop=(k == 7)) for k in range(8)]
                                  + [lambda e, k=k: e.matmul(ph[:, hc:hc + 2], lhsT=wup[:, k, cs_], rhs=hT[:, k, 0:514:513], start=(k == 0), stop=(k == 7)) for k in range(8)],
                                  reads=[wup, hT], writes=[pm, ph])
                            cv = cvr.next()
                            p.op("act", lambda e: e.activation(out=cv[:], in_=pm[:], func=AF.Identity, scale=cw[:, ch, 1:2], bias=cbias[:, ch:ch + 1]),
                                 reads=[pm, cw, cbias], writes=[cv])
                            p.op("dve", lambda e: e.scalar_tensor_tensor(out=cv[:, 1:512], in0=pm[:, 0:511], scalar=cw[:, ch, 0:1], in1=cv[:, 1:512], op0=ALU.mult, op1=ALU.add), reads=[pm, cw, cv], writes=[cv])
                            p.op("dve", lambda e: e.scalar_tensor_tensor(out=cv[:, 0:511], in0=pm[:, 1:512], scalar=cw[:, ch, 2:3], in1=cv[:, 0:511], op0=ALU.mult, op1=ALU.add), reads=[pm, cw, cv], writes=[cv])
                            p.op("dve", lambda e: e.scalar_tensor_tensor(out=cv[:, 0:1], in0=ph[:, hc:hc + 1], scalar=cw[:, ch, 0:1], in1=cv[:, 0:1], op0=ALU.mult, op1=ALU.add), reads=[ph, cw, cv], writes=[cv])
                            p.op("dve", lambda e: e.scalar_tensor_tensor(out=cv[:, 511:512], in0=ph[:, hc + 1:hc + 2], scalar=cw[:, ch, 2:3], in1=cv[:, 511:512], op0=ALU.mult, op1=ALU.add), reads=[ph, cw, cv], writes=[cv])
                            cvs.append(cv)
                        p.op("act", lambda e: e.activation(out=cvs[0][:], in_=cvs[0][:], func=AF.Gelu), reads=[cvs[0]], writes=[cvs[0]])
                        p.op("pool", lambda e: e.tensor_tensor(out=actT[:, c, :], in0=cvs[0][:], in1=cvs[1][:], op=ALU.mult), reads=cvs, writes=[actT])
                    for i in range(4):
                        tok = slice(t0 + i * 128, t0 + (i + 1) * 128)
                        hr = hres.next()
                        p.dma(lambda e: e.dma_start(out=hr[:], in_=h_s[sq, tok, :]), hr, True)
                        for n in range(2):
                            ns = slice(n * 512, (n + 1) * 512)
                            po = psm.next()
                            p.ops("pe", [lambda e, k=k: e.matmul(po[:], lhsT=actT[:, k, i * 128:(i + 1) * 128], rhs=wdn[:, k, ns], start=(k == 0), stop=(k == 21)) for k in range(22)],
                                  reads=[actT, wdn], writes=[po])
                            p.op("dve", lambda e: e.scalar_tensor_tensor(out=opre[:, ns], in0=hr[:, ns], scalar=ALPHA, in1=po[:], op0=ALU.mult, op1=ALU.add),
                                 reads=[hr, po], writes=[opre])
                        oo = oout.next()
                        layer_norm_multi([(lnb, opre, oo)], gB, bB)
                        p.dma(lambda e: e.dma_start(out=out_d[sq, tok, :], in_=oo[:]), oo, False)
        new_phase()


def _host_inputs(inputs, core, nseq=NSEQ):
    f32 = np.float32
    x = np.ascontiguousarray(inputs["x"][core * nseq:(core + 1) * nseq]).astype(f32)
    pos = np.ascontiguousarray(inputs["positions"][core * nseq:(core + 1) * nseq]).astype(np.int32)
    w_in = np.asarray(inputs["w_in"][0], f32)
    b_in = np.asarray(inputs["b_in"][0], f32)
    sw = np.arange(AW).reshape(-1, 2, 32)[:, ::-1, :].reshape(-1)
    q0, k0, v0, g0 = SSMW, SSMW + AW, SSMW + 2 * AW, SSMW + 3 * AW
    cols = np.concatenate([np.arange(0, SSMW), np.arange(q0, q0 + AW), np.arange(k0, k0 + AW), np.arange(g0, g0 + 2 * D)])
    pswap = np.zeros((128, 128), f32)
    for m_ in range(128):
        pswap[m_ + 32 if (m_ % 64) < 32 else m_ - 32, m_] = 1.0
    w_fm = np.ascontiguousarray(w_in[:, cols])
    b_fm = np.ascontiguousarray(b_in[cols].reshape(NFM // 128, 128).T)
    w_v = np.ascontiguousarray(w_in[:, v0:v0 + AW])
    b_v = np.ascontiguousarray(b_in[v0:v0 + AW].reshape(1, AW))
    half = 32
    inv_freq = (10000.0 ** (-np.arange(half, dtype=np.float64) * 2.0 / 64)).astype(f32)
    invf = np.zeros((128, 2), f32)
    for pp in range(128):
        invf[pp, 0] = inv_freq[pp % 32] / TWO_PI
        invf[pp, 1] = -TWO_PI if (pp % 64) < 32 else TWO_PI
    def tile_layout(a):
        return np.ascontiguousarray(a.reshape(2, 16, 2, 64).transpose(2, 3, 0, 1).reshape(128, 32)).astype(f32)
    lre_h = tile_layout(np.asarray(inputs["ssm_lam_re"][0], f32))
    lim_h = tile_layout(np.asarray(inputs["ssm_lam_im"][0], f32))
    ldt_h = tile_layout(np.broadcast_to(np.asarray(inputs["ssm_log_dt"][0], f32)[:, :, None], (2, 32, 64)).copy())

    def bz(b):
        o = np.zeros((128, 32, 128), f32)
        b = np.asarray(b, f32)
        for dr in range(2):
            for gp in range(16):
                for gl in range(2):
                    c0 = (gp % 4) * 32 + gl * 16
                    o[gl * 64:(gl + 1) * 64, dr * 16 + gp, c0:c0 + 16] = b[dr, 2 * gp + gl]
        return o

    def cb(c):
        o = np.zeros((32, 32, 128), f32)
        c = np.asarray(c, f32)
        for dr in range(2):
            for gp in range(16):
                for gl in range(2):
                    o[gl * 16:(gl + 1) * 16, dr * 16 + gp, gl * 64:(gl + 1) * 64] = c[dr, 2 * gp + gl]
        return o
    ssm = {"lre_h": lre_h, "lim_h": lim_h, "ldt_h": ldt_h,
           "bzr_h": bz(inputs["ssm_b_re"][0]), "bzi_h": bz(inputs["ssm_b_im"][0]),
           "cbr_h": cb(inputs["ssm_c_re"][0]), "cbi_h": cb(inputs["ssm_c_im"][0]),
           "dsk_h": np.ascontiguousarray(np.asarray(inputs["ssm_d"][0], f32).reshape(4, 128).T),
           "iota_h": np.arange(S, dtype=f32).reshape(1, S)}
    d = {"x": x, "pos": pos, "ident": np.eye(128, dtype=f32), "invf": invf, "pswap": pswap,
         "w_in_fm": w_fm, "b_fm": b_fm, "w_v": w_v, "b_v": b_v}
    d.update(ssm)
    ii = np.arange(128)[:, None]; jj = np.arange(128)[None, :]
    maskb = np.concatenate([np.where(ii >= jj, 0.0, -30000.0), np.where(ii <= jj, 0.0, -30000.0)], axis=1).astype(f32)
    ones3 = np.zeros((128, 3, 64), f32)
    ones3[:, 0, :] = 1.0; ones3[64:, 1, :] = 1.0; ones3[:64, 2, :] = 1.0
    g = lambda n: np.ascontiguousarray(np.asarray(inputs[n][0], f32))
    cwh = np.ascontiguousarray(g("conv_w").reshape(3, 44, 128).transpose(2, 1, 0))
    cbh = np.ascontiguousarray(g("conv_b").reshape(44, 128).T)
    d.update({"maskb_h": maskb, "ones3_h": ones3, "wgv_h": g("w_glu_v"), "wgg_h": g("w_glu_g"), "wab_h": g("w_attn_br"), "wo_h": g("w_out"),
              "ln1g_h": g("ln1_g").reshape(1, D), "ln1b_h": g("ln1_b").reshape(1, D), "ln2g_h": g("ln2_g").reshape(1, D), "ln2b_h": g("ln2_b").reshape(1, D),
              "wup_h": g("w_up"), "wdn_h": g("w_down"), "cw_h": cwh, "cb_h": cbh})
    return d


def kernel(**inputs):
    nc = build()
    in_maps = [_host_inputs(inputs, c) for c in range(NCORES)]
    res = run_bass_kernel_spmd(nc, in_maps, core_ids=list(range(NCORES)))
    out = np.concatenate([r["out"] for r in res.results], axis=0)
    return out.astype(np.float32)
```

```python
import math
from contextlib import ExitStack

import numpy as np
import concourse.bass as bass
import concourse.mybir as mybir
from concourse.bass_utils import run_bass_kernel_spmd

F32 = mybir.dt.float32
BF16 = mybir.dt.bfloat16
I32 = mybir.dt.int32
AF = mybir.ActivationFunctionType
ALU = mybir.AluOpType
AX = mybir.AxisListType

S = 4096
D = 1024
NCORES = 8
NSEQ = 2
SSMW = 512
AW = 768
DFF = 2816
NFM = 4096
ALPHA = 2.0 ** 0.25
LN_EPS = 1e-5
TWO_PI = 2.0 * math.pi
DIL = (1, 4, 16)
KPAD = 1024


class Buf:
    __slots__ = ("t", "w", "r", "dsem", "const")

    def __init__(self, t, dsem=None, const=False):
        self.t = t
        self.w = None
        self.r = {}
        self.dsem = dsem
        self.const = const

    def __getitem__(self, k):
        return self.t[k]


class Prog:
    ENG = ("pe", "act", "dve", "pool", "sp")

    def __init__(self, nc, es, n_dsem=72):
        self.nc = nc
        self.engobj = {'pe': nc.tensor, 'act': nc.scalar, 'dve': nc.vector, 'pool': nc.gpsimd, 'sp': nc.sync}
        self.ninst = 0
        self.stopped = False
        self.esem = {e: es.enter_context(nc.semaphore("es_" + e)) for e in ("pe", "act", "dve", "pool")}
        self.ecount = {e: 0 for e in self.esem}
        self.dsems = [es.enter_context(nc.semaphore(f"ds{i}")) for i in range(n_dsem)]
        self.dcount = {id(s): 0 for s in self.dsems}
        self.dnext = 0
        self.waited = {e: {} for e in self.ENG}
        self.semobj = {}
        for s in list(self.esem.values()) + self.dsems:
            self.semobj[id(s)] = s

    def buf(self, t, dma=False, const=False):
        ds = None
        if dma:
            assert self.dnext < len(self.dsems), "out of DMA semaphores in this phase"
            ds = self.dsems[self.dnext]
            self.dnext += 1
        return Buf(t, ds, const)

    def _deps(self, reads, writes):
        deps = {}

        def add(ev):
            if ev is None:
                return
            k, v = ev
            if deps.get(k, 0) < v:
                deps[k] = v
        for b in reads:
            add(b.w)
        for b in writes:
            add(b.w)
            for k, v in b.r.items():
                add((k, v))
        return deps

    def _record(self, ev, reads, writes):
        for b in writes:
            b.w = ev
            b.r = {}
        for b in reads:
            if b.const:
                continue
            if b.r.get(ev[0], 0) < ev[1]:
                b.r[ev[0]] = ev[1]

    def _emit(self, eng, deps, fn, inc):
        e = self.engobj[eng]
        wd = self.waited[eng]
        own = id(self.esem[eng]) if eng in self.esem else None
        for k, v in deps.items():
            if eng == "pe" and k == own:
                continue
            if wd.get(k, 0) >= v:
                continue
            wd[k] = v
            e.wait_ge(self.semobj[k], v)
        if fn is None:
            return
        ins = fn(e)
        if inc is not None:
            ins.then_inc(inc[0], inc[1])
        self.ninst += 1

    def op(self, eng, fn, reads=(), writes=()):
        if self.stopped:
            return None
        deps = self._deps(reads, writes)
        self.ecount[eng] += 1
        sem = self.esem[eng]
        ev = (id(sem), self.ecount[eng])
        self._emit(eng, deps, fn, (sem, 1))
        self._record(ev, reads, writes)
        return ev

    def ops(self, eng, fns, reads=(), writes=()):
        assert eng == "pe"
        if self.stopped:
            return None
        deps = self._deps(reads, writes)
        for fn in fns[:-1]:
            self._emit(eng, deps, fn, None)
            deps = {}
        self.ecount[eng] += 1
        sem = self.esem[eng]
        ev = (id(sem), self.ecount[eng])
        self._emit(eng, deps, fns[-1], (sem, 1))
        self._record(ev, reads, writes)
        return ev

    def dma(self, fn, sb, load, reads=(), writes=(), q="sp"):
        if self.stopped:
            return None
        reads = list(reads)
        writes = list(writes)
        if load:
            writes.append(sb)
        else:
            reads.append(sb)
        deps = self._deps(reads, writes)
        sem = sb.dsem
        assert sem is not None
        self.dcount[id(sem)] += 16
        ev = (id(sem), self.dcount[id(sem)])
        self._emit(q, deps, fn, (sem, 16))
        self._record(ev, reads, writes)
        return ev

    def dma_group(self, fns, sb, load, reads=(), writes=(), q="sp"):
        if self.stopped:
            return None
        reads = list(reads)
        writes = list(writes)
        if load:
            writes.append(sb)
        else:
            reads.append(sb)
        deps = self._deps(reads, writes)
        sem = sb.dsem
        ev = None
        for fn in fns:
            self.dcount[id(sem)] += 16
            ev = (id(sem), self.dcount[id(sem)])
            self._emit(q, deps, fn, (sem, 16))
            deps = {}
        self._record(ev, reads, writes)
        return ev

    def barrier(self):
        allev = {}
        for e, s in self.esem.items():
            if self.ecount[e]:
                allev[id(s)] = self.ecount[e]
        for s in self.dsems:
            if self.dcount[id(s)]:
                allev[id(s)] = self.dcount[id(s)]
        for eng in self.ENG:
            self._emit(eng, allev, None, None)
        self.dnext = 0

    def emit(self):
        pass


class StopBuild(Exception):
    pass


class Ring:
    def __init__(self, bufs):
        self.bufs = bufs
        self.i = 0

    def next(self):
        b = self.bufs[self.i % len(self.bufs)]
        self.i += 1
        return b


def build(nseq=NSEQ, debug=False, stop_after=None):
    nc = bass.Bass("TRN2", target_bir_lowering=False)

    def din(name, shape, dt=F32):
        return nc.dram_tensor(name, list(shape), dt, kind="ExternalInput").ap()

    dbg_kind = "ExternalOutput" if debug else "Internal"

    def dscr(name, shape, dt):
        return nc.dram_tensor(name, list(shape), dt, kind=dbg_kind).ap()

    x_d = din("x", [nseq, S, D])
    pos_d = din("pos", [nseq, S], I32)
    ident_d = din("ident", [128, 128])
    pswap_d = din("pswap", [128, 128])
    invf_d = din("invf", [128, 2])
    w_in_d = din("w_in_fm", [D, NFM])
    b_fm_d = din("b_fm", [128, NFM // 128])
    w_v_d = din("w_v", [D, AW])
    b_v_d = din("b_v", [1, AW])
    out_d = nc.dram_tensor("out", [nseq, S, D], F32, kind="ExternalOutput").ap()
    ssm_d = dict(
        lre=din("lre_h", [128, 32]), lim=din("lim_h", [128, 32]), ldt=din("ldt_h", [128, 32]),
        bzr=din("bzr_h", [128, 32, 128]), bzi=din("bzi_h", [128, 32, 128]),
        cbr=din("cbr_h", [32, 32, 128]), cbi=din("cbi_h", [32, 32, 128]),
        dsk=din("dsk_h", [128, 4]), iota=din("iota_h", [1, S]))
    zT_s = dscr("zT_s", [nseq, SSMW, S], BF16)
    aT_s = dscr("aT_s", [nseq, 256, S], BF16)
    h_s = dscr("h_s", [nseq, S, D], F32)
    hT_s = dscr("hT_s", [nseq, D, S], BF16)
    md = dict(maskb=din("maskb_h", [128, 256]), ones3=din("ones3_h", [128, 3, 64]),
              wgv=din("wgv_h", [512, D]), wgg=din("wgg_h", [512, D]), wab=din("wab_h", [256, D]), wo=din("wo_h", [D, D]),
              ln1g=din("ln1g_h", [1, D]), ln1b=din("ln1b_h", [1, D]), ln2g=din("ln2g_h", [1, D]), ln2b=din("ln2b_h", [1, D]),
              wup=din("wup_h", [D, 2 * DFF]), wdn=din("wdn_h", [DFF, D]), cw=din("cw_h", [128, 44, 3]), cbias=din("cb_h", [128, 44]),
              aT_s=aT_s, h_s=h_s, hT_s=hT_s)

    xT_s = dscr("xT_s", [nseq, D, S], BF16)
    uT_s = dscr("uT_s", [nseq, SSMW, S], BF16)
    qT_s = dscr("qT_s", [nseq, AW, S], BF16)
    kT_s = dscr("kT_s", [nseq, AW, S], BF16)
    gT_s = dscr("gT_s", [nseq, 2 * D, S], BF16)
    NBLK = [d * (S // d // 128 + 1) for d in DIL]
    v_s = [dscr(f"v_s{g}", [nseq, 128, NBLK[g], 256], BF16) for g in range(3)]

    with ExitStack() as es0:
        p = Prog(nc, es0)
        psum = [p.buf(es0.enter_context(nc.psum_tensor(f"ps{i}", [128, 512], F32))) for i in range(8)]
        ident = p.buf(es0.enter_context(nc.sbuf_tensor("ident_sb", [128, 128], F32)), dma=True, const=True)
        p.dma(lambda e: e.dma_start(out=ident[:], in_=ident_d), ident, True)
        p.dnext = 1
        wbf = {}

        def cast_w(key, src, R, C):
            dst = nc.dram_tensor(key + "_bf", [R, C], BF16, kind="Internal").ap()
            pb = p.buf(None, dma=True)
            p.dma_group([lambda e, r0=r0: e.dma_start(out=dst[r0:min(r0 + 128, R), :], in_=src[r0:min(r0 + 128, R), :], max_dma_last_dim=4096)
                         for r0 in range(0, R, 128)], pb, True, q="pool")
            wbf[key] = (dst, pb)
        cast_w("w_in", w_in_d, D, NFM)
        cast_w("w_v", w_v_d, D, AW)
        cast_w("wgv", md["wgv"], SSMW, D)
        cast_w("wgg", md["wgg"], SSMW, D)
        cast_w("wab", md["wab"], 256, D)
        cast_w("wo", md["wo"], D, D)
        cast_w("wup", md["wup"], D, 2 * DFF)
        cast_w("wdn", md["wdn"], DFF, D)
        NRES = p.dnext

        def new_phase():
            p.barrier()
            p.dnext = NRES

        def stop(tag):
            if stop_after == tag:
                p.stopped = True

        try:
            _phases(nc, p, psum, ident, nseq, locals_d=dict(pswap_d=pswap_d, x_d=x_d, pos_d=pos_d, invf_d=invf_d, w_in_d=w_in_d, b_fm_d=b_fm_d, w_v_d=w_v_d, b_v_d=b_v_d, out_d=out_d, xT_s=xT_s, uT_s=uT_s, qT_s=qT_s, kT_s=kT_s, gT_s=gT_s, v_s=v_s, NBLK=NBLK, ssm_d=ssm_d, zT_s=zT_s, md=md, wbf=wbf), new_phase=new_phase, stop=stop)
        except StopBuild:
            pass
        p.stopped = False
        p.barrier()
    print('instructions', p.ninst)
    return nc


def _phases(nc, p, psum, ident, nseq, locals_d, new_phase, stop):
    globals_ = locals_d
    pswap_d = globals_['pswap_d']; x_d = globals_['x_d']; pos_d = globals_['pos_d']; invf_d = globals_['invf_d']; w_in_d = globals_['w_in_d']; b_fm_d = globals_['b_fm_d']
    w_v_d = globals_['w_v_d']; b_v_d = globals_['b_v_d']; out_d = globals_['out_d']; xT_s = globals_['xT_s']; uT_s = globals_['uT_s']
    qT_s = globals_['qT_s']; kT_s = globals_['kT_s']; gT_s = globals_['gT_s']; v_s = globals_['v_s']; NBLK = globals_['NBLK']
    ssm_d = globals_['ssm_d']; zT_s = globals_['zT_s']; md = globals_['md']
    aT_s = md['aT_s']; h_s = md['h_s']; hT_s = md['hT_s']; wbf = globals_['wbf']

    def load_wbf(wdst, key, K):
        src, pb = wbf[key]
        p.dma_group([lambda e, k=k: e.dma_start(out=wdst[:, k, :], in_=src[k * 128:(k + 1) * 128, :]) for k in range(K)], wdst, True, reads=[pb])
        wdst.const = True
    if True:

        with ExitStack() as es:
            def sb(name, shape, dt, dma=False, const=False):
                return p.buf(es.enter_context(nc.sbuf_tensor(name, list(shape), dt)), dma=dma, const=const)

            wA = sb("wA", [128, 8, NFM], BF16, dma=True)
            bfm = sb("bfm", [128, NFM // 128], F32, dma=True)
            invf = sb("invf_sb", [128, 2], F32, dma=True)
            p.dma(lambda e: e.dma_start(out=bfm[:], in_=b_fm_d), bfm, True)
            p.dma(lambda e: e.dma_start(out=invf[:], in_=invf_d), invf, True)
            load_wbf(wA, 'w_in', 8)
            psw_st = sb("psw_st", [128, 128], F32, dma=True)
            psw = sb("psw", [128, 128], BF16)
            p.dma(lambda e: e.dma_start(out=psw_st[:], in_=pswap_d), psw_st, True)
            p.op("dve", lambda e: e.tensor_copy(out=psw[:], in_=psw_st[:]), reads=[psw_st], writes=[psw])
            psw.const = True
            qbr = Ring([sb(f"qb{j}", [128, 512], BF16) for j in range(4)])
            stop('A0')

            cosT = sb("cosT", [128, S], F32)
            sinT = sb("sinT", [128, S], F32)
            posi = sb("posi", [128, 1024], I32, dma=True)
            tur = sb("tur", [128, 1024], F32)
            turi = sb("turi", [128, 1024], I32)
            xs = [sb(f"xs{i}", [128, D], F32, dma=True) for i in range(4)]
            xT = Ring([sb(f"xT{j}", [128, 8, 512], BF16, dma=True) for j in range(2)])
            ev_bf = Ring([sb(f"evbf{j}", [128, 512], BF16, dma=True) for j in range(8)])
            rt = Ring([sb(f"rt{j}", [128, 512], F32) for j in range(4)])
            psr = Ring(psum)

            for sq in range(nseq):
                for c in range(S // 1024):
                    cs = slice(c * 1024, (c + 1) * 1024)
                    p.dma(lambda e, cs=cs: e.dma_start(out=posi[:], in_=pos_d[sq:sq + 1, cs].partition_broadcast(128)), posi, True)
                    for (tab, addc, scol) in ((sinT, 0.0, 1), (cosT, 0.25, None)):
                        p.op("dve", lambda e: e.tensor_copy(out=tur[:], in_=posi[:]), reads=[posi], writes=[tur])
                        p.op("dve", lambda e, addc=addc: e.tensor_scalar(out=tur[:], in0=tur[:], scalar1=invf[:, 0:1], scalar2=addc,
                                                                          op0=ALU.mult, op1=ALU.add), reads=[tur, invf], writes=[tur])
                        p.op("dve", lambda e: e.tensor_copy(out=turi[:], in_=tur[:]), reads=[tur], writes=[turi])
                        p.op("dve", lambda e: e.tensor_tensor(out=tur[:], in0=tur[:], in1=turi[:], op=ALU.subtract),
                             reads=[tur, turi], writes=[tur])
                        if scol is not None:
                            p.op("act", lambda e, tab=tab, cs=cs: e.activation(out=tab[:, cs], in_=tur[:], func=AF.Sin, scale=invf[:, 1:2]),
                                 reads=[tur, invf], writes=[tab])
                        else:
                            p.op("act", lambda e, tab=tab, cs=cs: e.activation(out=tab[:, cs], in_=tur[:], func=AF.Sin, scale=TWO_PI),
                                 reads=[tur], writes=[tab])
                stop('A1')
                def load_x(tb_):
                    for i in range(4):
                        p.dma(lambda e, i=i: e.dma_start(out=xs[i][:], in_=x_d[sq, tb_ * 512 + i * 128:tb_ * 512 + (i + 1) * 128, :]), xs[i], True)
                load_x(0)
                for tb in range(S // 512):
                    t0 = tb * 512
                    ts = slice(t0, t0 + 512)
                    xtile = xs
                    xTb = xT.next()
                    for k in range(8):
                        ps = psr.next()
                        p.ops("pe", [lambda e, ps=ps, i=i, k=k: e.transpose(out=ps[:, i * 128:(i + 1) * 128],
                                                                           in_=xtile[i][:, k * 128:(k + 1) * 128], identity=ident[:])
                                     for i in range(4)], reads=xtile + [ident], writes=[ps])
                        if k % 2 == 0:
                            p.op("act", lambda e, ps=ps, k=k: e.copy(out=xTb[:, k, :], in_=ps[:]), reads=[ps], writes=[xTb])
                        else:
                            p.op("dve", lambda e, ps=ps, k=k: e.tensor_copy(out=xTb[:, k, :], in_=ps[:]), reads=[ps], writes=[xTb])
                    if tb + 1 < S // 512:
                        load_x(tb + 1)
                    p.dma(lambda e: e.dma_start(out=xT_s[sq].rearrange("(k q) t -> q k t", q=128)[:, :, ts], in_=xTb[:]), xTb, False)

                    def proj(fo):
                        ps = psr.next()
                        p.ops("pe", [lambda e, ps=ps, k=k: e.matmul(ps[:], lhsT=wA[:, k, fo * 128:(fo + 1) * 128], rhs=xTb[:, k, :],
                                                                      start=(k == 0), stop=(k == 7)) for k in range(8)],
                              reads=[wA, xTb], writes=[ps])
                        return ps

                    for fo in range(4):
                        ps = proj(fo)
                        o = ev_bf.next()
                        p.op("act", lambda e, ps=ps, o=o, fo=fo: e.activation(out=o[:], in_=ps[:], func=AF.Identity, bias=bfm[:, fo:fo + 1]),
                             reads=[ps, bfm], writes=[o])
                        p.dma(lambda e, o=o, fo=fo: e.dma_start(out=uT_s[sq, fo * 128:(fo + 1) * 128, ts], in_=o[:]), o, False)
                    for which, dst in ((0, qT_s), (1, kT_s)):
                        for c in range(6):
                            fo = 4 + which * 6 + c
                            psa = proj(fo)
                            qb = qbr.next()
                            p.op("act", lambda e: e.activation(out=qb[:], in_=psa[:], func=AF.Identity, bias=bfm[:, fo:fo + 1]), reads=[psa, bfm], writes=[qb])
                            psb = psr.next()
                            p.ops("pe", [lambda e: e.matmul(psb[:], lhsT=psw[:], rhs=qb[:], start=True, stop=True)], reads=[psw, qb], writes=[psb])
                            t1 = rt.next()
                            t2 = rt.next()
                            p.op("dve", lambda e: e.tensor_tensor(out=t1[:], in0=qb[:], in1=cosT[:, ts], op=ALU.mult), reads=[qb, cosT], writes=[t1])
                            p.op("dve", lambda e: e.tensor_tensor(out=t2[:], in0=psb[:], in1=sinT[:, ts], op=ALU.mult), reads=[psb, sinT], writes=[t2])
                            o = ev_bf.next()
                            p.op("pool", lambda e, o=o, t1=t1, t2=t2: e.tensor_tensor(out=o[:], in0=t1[:], in1=t2[:], op=ALU.add),
                                 reads=[t1, t2], writes=[o])
                            p.dma(lambda e, o=o, c=c, dst=dst: e.dma_start(out=dst[sq, c * 128:(c + 1) * 128, ts], in_=o[:]), o, False)
                    for c in range(16):
                        fo = 16 + c
                        ps = proj(fo)
                        o = ev_bf.next()
                        p.op("act", lambda e, ps=ps, o=o, fo=fo: e.activation(out=o[:], in_=ps[:], func=AF.Sigmoid, bias=bfm[:, fo:fo + 1]),
                             reads=[ps, bfm], writes=[o])
                        p.dma(lambda e, o=o, c=c: e.dma_start(out=gT_s[sq, c * 128:(c + 1) * 128, ts], in_=o[:]), o, False)
                    stop(f'A2_{tb}')
        new_phase()
        stop('A')

        with ExitStack() as es:
            def sb(name, shape, dt, dma=False, const=False):
                return p.buf(es.enter_context(nc.sbuf_tensor(name, list(shape), dt)), dma=dma, const=const)

            wV = sb("wV", [128, 8, AW], BF16, dma=True)
            load_wbf(wV, 'w_v', 8)
            bv = sb("bv", [128, AW], F32, dma=True, const=True)
            p.dma(lambda e: e.dma_start(out=bv[:], in_=b_v_d.partition_broadcast(128)), bv, True)
            xTf = sb("xTf", [128, 8, S], BF16, dma=True)
            VCH = 12
            vring = Ring([sb(f"vstg{j}", [128, VCH, 256], BF16, dma=True) for j in range(2)])
            psr = Ring(psum)
            for sq in range(nseq):
                p.dma(lambda e: e.dma_start(out=xTf[:], in_=xT_s[sq].rearrange("(k q) t -> q k t", q=128)), xTf, True)
                for g in range(3):
                    d = DIL[g]
                    L = S // d
                    nb = L // 128 + 1
                    blocks = [(r, m) for r in range(d) for m in range(nb)]
                    for c0 in range(0, len(blocks), VCH):
                        chunk = blocks[c0:c0 + VCH]
                        stg = vring.next()
                        p.op("pool", lambda e, stg=stg: e.memset(stg[:], 0.0), writes=[stg])
                        for j, (r, m) in enumerate(chunk):
                            lo = 64 + 128 * (m - 1)
                            i0 = max(0, -lo)
                            i1 = min(128, L - lo)
                            M = i1 - i0
                            tok0 = r + d * (lo + i0)
                            ps = psr.next()
                            p.ops("pe", [lambda e, ps=ps, k=k, tok0=tok0, M=M, i0=i0, d=d, g=g: e.matmul(
                                ps[i0:i0 + M, 0:256], lhsT=xTf[:, k, tok0:tok0 + d * (M - 1) + 1:d], rhs=wV[:, k, g * 256:(g + 1) * 256],
                                start=(k == 0), stop=(k == 7)) for k in range(8)], reads=[xTf, wV], writes=[ps])
                            p.op("dve", lambda e, ps=ps, stg=stg, j=j, i0=i0, M=M, g=g: e.tensor_tensor(
                                out=stg[i0:i0 + M, j, :], in0=ps[i0:i0 + M, 0:256], in1=bv[i0:i0 + M, g * 256:(g + 1) * 256], op=ALU.add),
                                reads=[ps, bv], writes=[stg])
                        p.dma(lambda e, stg=stg, c0=c0, n=len(chunk), g=g: e.dma_start(out=v_s[g][sq, :, c0:c0 + n, :], in_=stg[:, 0:n, :]), stg, False)
                        stop(f'V{g}_{c0}')
                    stop(f'V{g}')
        new_phase()

        with ExitStack() as es:
            def sb(name, shape, dt, dma=False, const=False):
                return p.buf(es.enter_context(nc.sbuf_tensor(name, list(shape), dt)), dma=dma, const=const)

            NT = 32
            NCH = S // 8
            lre = sb("lre", [128, NT], F32, dma=True); lim = sb("lim", [128, NT], F32, dma=True); ldt = sb("ldt", [128, NT], F32, dma=True)
            p.dma(lambda e: e.dma_start(out=lre[:], in_=ssm_d["lre"]), lre, True)
            p.dma(lambda e: e.dma_start(out=lim[:], in_=ssm_d["lim"]), lim, True)
            p.dma(lambda e: e.dma_start(out=ldt[:], in_=ssm_d["ldt"]), ldt, True)
            dsk = sb("dsk", [128, 4], F32, dma=True)
            p.dma(lambda e: e.dma_start(out=dsk[:], in_=ssm_d["dsk"]), dsk, True)
            tI = sb("tI", [128, NCH], F32, dma=True, const=True)
            p.dma(lambda e: e.dma_start(out=tI[:], in_=ssm_d["iota"][:, 0:NCH].partition_broadcast(128)), tI, True)
            sm = {n: sb("sm_" + n, [128, NT], F32) for n in
                  ("dt", "xr", "xi", "rho", "th", "t0", "t1", "f", "sinx", "cosx", "sinh", "em1", "am1", "abi", "den", "kr", "ki", "u0", "u1",
                   "rho8", "th8", "pm", "pc", "ps")}
            smi = sb("smi", [128, NT], I32)
            pwr = sb("pwr", [128, 16, NT], F32); pwi = sb("pwi", [128, 16, NT], F32); npwi = sb("npwi", [128, 16, NT], F32)

            def V(fn, reads, writes):
                return p.op("dve", fn, reads=reads, writes=writes)

            def A(fn, reads, writes):
                return p.op("act", fn, reads=reads, writes=writes)

            def tt(o, a, b, op):
                V(lambda e: e.tensor_tensor(out=o[:], in0=a[:], in1=b[:], op=op), [a, b], [o])

            def tsc(o, a, s1, op0, s2=None, op1=None):
                if op1 is None:
                    V(lambda e: e.tensor_scalar(out=o[:], in0=a[:], scalar1=s1, scalar2=None, op0=op0), [a], [o])
                else:
                    V(lambda e: e.tensor_scalar(out=o[:], in0=a[:], scalar1=s1, scalar2=s2, op0=op0, op1=op1), [a], [o])

            def frac_sin(o, turns_src, mul, add):
                tsc(sm["t0"], turns_src, mul, ALU.mult, add, ALU.add)
                V(lambda e: e.tensor_copy(out=smi[:], in_=sm["t0"][:]), [sm["t0"]], [smi])
                tt(sm["f"], sm["t0"], smi, ALU.subtract)
                A(lambda e: e.activation(out=o[:], in_=sm["f"][:], func=AF.Sin, scale=TWO_PI), [sm["f"]], [o])

            A(lambda e: e.activation(out=sm["dt"][:], in_=ldt[:], func=AF.Exp), [ldt], [sm["dt"]])
            tt(sm["xr"], lre, sm["dt"], ALU.mult)
            tt(sm["xi"], lim, sm["dt"], ALU.mult)
            A(lambda e: e.activation(out=sm["rho"][:], in_=sm["xr"][:], func=AF.Exp), [sm["xr"]], [sm["rho"]])
            A(lambda e: e.activation(out=sm["rho8"][:], in_=sm["xr"][:], func=AF.Exp, scale=8.0), [sm["xr"]], [sm["rho8"]])
            tsc(sm["th"], sm["xi"], 1.0 / TWO_PI, ALU.mult)
            tsc(sm["th8"], sm["th"], 8.0, ALU.mult)
            frac_sin(sm["sinx"], sm["th"], 1.0, 0.0)
            frac_sin(sm["cosx"], sm["th"], 1.0, 0.25)
            frac_sin(sm["sinh"], sm["th"], 0.5, 0.0)
            tsc(sm["em1"], sm["xr"], 0.2, ALU.mult, 1.0, ALU.add)
            for cdiv in (0.25, 1.0 / 3.0, 0.5):
                tt(sm["em1"], sm["em1"], sm["xr"], ALU.mult)
                tsc(sm["em1"], sm["em1"], cdiv, ALU.mult, 1.0, ALU.add)
            tt(sm["em1"], sm["em1"], sm["xr"], ALU.mult)
            tt(sm["am1"], sm["em1"], sm["cosx"], ALU.mult)
            tt(sm["u0"], sm["sinh"], sm["sinh"], ALU.mult)
            V(lambda e: e.scalar_tensor_tensor(out=sm["am1"][:], in0=sm["u0"][:], scalar=-2.0, in1=sm["am1"][:], op0=ALU.mult, op1=ALU.add),
              [sm["u0"], sm["am1"]], [sm["am1"]])
            tt(sm["abi"], sm["rho"], sm["sinx"], ALU.mult)
            tt(sm["den"], lre, lre, ALU.mult)
            tt(sm["u0"], lim, lim, ALU.mult)
            tt(sm["den"], sm["den"], sm["u0"], ALU.add)
            V(lambda e: e.reciprocal(out=sm["den"][:], in_=sm["den"][:]), [sm["den"]], [sm["den"]])
            tt(sm["u0"], sm["am1"], lre, ALU.mult)
            tt(sm["u1"], sm["abi"], lim, ALU.mult)
            tt(sm["u0"], sm["u0"], sm["u1"], ALU.add)
            tt(sm["kr"], sm["u0"], sm["den"], ALU.mult)
            tt(sm["u0"], sm["abi"], lre, ALU.mult)
            tt(sm["u1"], sm["am1"], lim, ALU.mult)
            tt(sm["u0"], sm["u0"], sm["u1"], ALU.subtract)
            tt(sm["ki"], sm["u0"], sm["den"], ALU.mult)
            tsc(sm["t1"], sm["ki"], -1.0, ALU.mult)
            nki = sb("nki", [128, NT], F32)
            V(lambda e: e.tensor_copy(out=nki[:], in_=sm["t1"][:]), [sm["t1"]], [nki])
            for jj in range(16):
                jv = float(jj - 7)
                A(lambda e, jv=jv: e.activation(out=sm["pm"][:], in_=sm["xr"][:], func=AF.Exp, scale=jv), [sm["xr"]], [sm["pm"]])
                frac_sin(sm["ps"], sm["th"], jv, 0.0)
                frac_sin(sm["pc"], sm["th"], jv, 0.25)
                V(lambda e, jj=jj: e.tensor_tensor(out=pwr[:, jj, :], in0=sm["pm"][:], in1=sm["pc"][:], op=ALU.mult), [sm["pm"], sm["pc"]], [pwr])
                V(lambda e, jj=jj: e.tensor_tensor(out=pwi[:, jj, :], in0=sm["pm"][:], in1=sm["ps"][:], op=ALU.mult), [sm["pm"], sm["ps"]], [pwi])
            V(lambda e: e.tensor_scalar(out=npwi[:], in0=pwi[:], scalar1=-1.0, scalar2=None, op0=ALU.mult), [pwi], [npwi])
            for b_ in (pwr, pwi, npwi, sm["kr"], sm["ki"], nki, sm["rho8"], sm["th8"]):
                b_.const = True

            Dd = sb("Dd", [128, 4, 128], BF16)
            for q in range(4):
                V(lambda e, q=q: e.tensor_scalar(out=Dd[:, q, :], in0=ident[:], scalar1=dsk[:, q:q + 1], scalar2=None, op0=ALU.mult),
                  [ident, dsk], [Dd])
            Dd.const = True
            Ddf = sb("Ddf", [128, 4, 128], F32)
            for q in range(4):
                V(lambda e, q=q: e.tensor_scalar(out=Ddf[:, q, :], in0=ident[:], scalar1=dsk[:, q:q + 1], scalar2=None, op0=ALU.mult), [ident, dsk], [Ddf])
            Ddf.const = True
            identP = sb("identP", [128, 160], F32)
            p.op("pool", lambda e: e.memset(identP[:], 0.0), writes=[identP])
            V(lambda e: e.tensor_copy(out=identP[:, 32:160], in_=ident[:]), [ident, identP], [identP])
            identP.const = True
            K0c = [sb(f"K0c{i}", [128, 64], BF16) for i in range(2)]

            NSET = 2
            bz = [[sb(f"bz{i}_{k}", [128, 2, 128], F32, dma=True) for k in range(2)] for i in range(NSET)]
            cbt = [[sb(f"cbt{i}_{k}", [32, 2, 128], F32, dma=True) for k in range(2)] for i in range(NSET)]
            Bz = [[sb(f"Bz{i}_{k}", [128, 2, 128], F32) for k in range(2)] for i in range(NSET)]
            CT = [[sb(f"CT{i}_{k}", [128, 2, 64], F32) for k in range(2)] for i in range(NSET)]
            XT = [[sb(f"XT{i}_{k}", [128, 8, 2, 128], BF16) for k in range(2)] for i in range(NSET)]
            KT = [[sb(f"KT{i}_{k}", [128, 8, 64], BF16) for k in range(2)] for i in range(NSET)]
            LY = [[sb(f"LY{i}_{k}", [128, 8, 2, 64], BF16) for k in range(2)] for i in range(NSET)]
            cosN = [[sb(f"cosN{i}_{k}", [128, NCH], F32) for k in range(2)] for i in range(NSET)]
            sinN = [[sb(f"sinN{i}_{k}", [128, NCH], F32) for k in range(2)] for i in range(NSET)]
            rho8T = [[sb(f"rho8T{i}_{k}", [128, NCH], F32) for k in range(2)] for i in range(NSET)]
            for i in range(NSET):
                for k in range(2):
                    p.op("pool", lambda e, i=i, k=k: e.memset(CT[i][k][:], 0.0), writes=[CT[i][k]])
            xtmp = Ring([sb(f"xtmp{i}", [128, 2, 128], F32) for i in range(3)])
            lyf = Ring([sb(f"lyf{i}", [128, 2, 64], F32) for i in range(3)])
            turN = sb("turN", [128, NCH], F32); turNi = sb("turNi", [128, NCH], I32)
            uTr = Ring([sb(f"uTc{i}", [128, S], BF16, dma=True) for i in range(2)])
            uDr = Ring([sb(f"uD{i}", [128, 8, NCH], BF16) for i in range(2)])
            tmpr = Ring([sb(f"tmpS{i}", [128, NCH], F32) for i in range(8)])
            wrr = Ring([sb(f"wS{i}", [128, NCH], F32) for i in range(4)])
            Rrr = Ring([sb(f"RS{i}", [128, NCH], F32) for i in range(4)])
            Vrr = Ring([sb(f"VS{i}", [128, NCH], F32) for i in range(4)])
            Zr_ = [Ring([sb(f"ZS{k}_{i}", [128, 2, NCH], BF16) for i in range(2)]) for k in range(2)]
            zor = Ring([sb(f"zo{i}", [128, S], BF16, dma=True) for i in range(2)])
            psT = Ring(psum[4:8])
            psSt = [psum[0:2], psum[2:4]]

            def cmul(o, orow, oi_row, src, sr, si, nsi):
                V(lambda e: e.tensor_scalar(out=o[:, 0, :], in0=src[:, 0, :], scalar1=sr, scalar2=None, op0=ALU.mult), [src], [o])
                V(lambda e: e.scalar_tensor_tensor(out=o[:, 0, :], in0=src[:, 1, :], scalar=nsi, in1=o[:, 0, :], op0=ALU.mult, op1=ALU.add), [src, o], [o])
                V(lambda e: e.tensor_scalar(out=o[:, 1, :], in0=src[:, 1, :], scalar1=sr, scalar2=None, op0=ALU.mult), [src], [o])
                V(lambda e: e.scalar_tensor_tensor(out=o[:, 1, :], in0=src[:, 0, :], scalar=si, in1=o[:, 1, :], op0=ALU.mult, op1=ALU.add), [src, o], [o])

            def prep(gp, st):
                for k in range(2):
                    j = k * 16 + gp
                    p.dma(lambda e: e.dma_start(out=bz[st][k][:, 0, :], in_=ssm_d["bzr"][:, j, :]), bz[st][k], True)
                    p.dma(lambda e: e.dma_start(out=bz[st][k][:, 1, :], in_=ssm_d["bzi"][:, j, :]), bz[st][k], True)
                    p.dma(lambda e: e.dma_start(out=cbt[st][k][:, 0, :], in_=ssm_d["cbr"][:, j, :]), cbt[st][k], True)
                    p.dma(lambda e: e.dma_start(out=cbt[st][k][:, 1, :], in_=ssm_d["cbi"][:, j, :]), cbt[st][k], True)
                    cmul(Bz[st][k], None, None, bz[st][k], sm["kr"][:, j:j + 1], sm["ki"][:, j:j + 1], nki[:, j:j + 1])
                    ps = psT.next()
                    p.ops("pe", [lambda e: e.transpose(out=ps[:, 0:32], in_=cbt[st][k][:, 0, :], identity=ident[0:32, 0:32]),
                                 lambda e: e.transpose(out=ps[:, 32:64], in_=cbt[st][k][:, 1, :], identity=ident[0:32, 0:32])],
                          reads=[cbt[st][k], ident], writes=[ps])
                    A(lambda e: e.copy(out=CT[st][k][:, 0, 32:64], in_=ps[:, 0:32]), [ps], [CT[st][k]])
                    A(lambda e: e.mul(out=CT[st][k][:, 1, 32:64], in_=ps[:, 32:64], mul=-1.0), [ps], [CT[st][k]])
                    for s_ in range(8):
                        if s_ == 0:
                            xs_ = Bz[st][k]
                        else:
                            xs_ = xtmp.next()
                            jj = 7 - s_
                            cmul(xs_, None, None, Bz[st][k], pwr[:, jj, j:j + 1], pwi[:, jj, j:j + 1], npwi[:, jj, j:j + 1])
                        ps = psT.next()
                        p.ops("pe", [lambda e: e.transpose(out=ps[:, 0:128], in_=xs_[:, 0, :], identity=ident[:]),
                                     lambda e: e.transpose(out=ps[:, 128:256], in_=xs_[:, 1, :], identity=ident[:])],
                              reads=[xs_, ident], writes=[ps])
                        A(lambda e: e.copy(out=XT[st][k][:, s_, :, :], in_=ps[:, 0:256].rearrange("p (r c) -> p r c", r=2)), [ps], [XT[st][k]])
                    for tau in range(8):
                        ly = lyf.next()
                        jj = 7 + tau
                        ctr = CT[st][k]
                        V(lambda e: e.tensor_scalar(out=ly[:, 0, :], in0=ctr[:, 0, :], scalar1=pwr[:, jj, j:j + 1], scalar2=None, op0=ALU.mult), [ctr], [ly])
                        V(lambda e: e.scalar_tensor_tensor(out=ly[:, 0, :], in0=ctr[:, 1, :], scalar=pwi[:, jj, j:j + 1], in1=ly[:, 0, :], op0=ALU.mult, op1=ALU.add), [ctr, ly], [ly])
                        V(lambda e: e.tensor_scalar(out=ly[:, 1, :], in0=ctr[:, 1, :], scalar1=pwr[:, jj, j:j + 1], scalar2=None, op0=ALU.mult), [ctr], [ly])
                        V(lambda e: e.scalar_tensor_tensor(out=ly[:, 1, :], in0=ctr[:, 0, :], scalar=npwi[:, jj, j:j + 1], in1=ly[:, 1, :], op0=ALU.mult, op1=ALU.add), [ctr, ly], [ly])
                        A(lambda e: e.copy(out=LY[st][k][:, tau, :, :], in_=ly[:]), [ly], [LY[st][k]])
                        ps = psT.next()
                        p.ops("pe", [lambda e: e.matmul(ps[:, 0:64], lhsT=Bz[st][k][:, 0, :], rhs=ly[:, 0, :], start=True, stop=False),
                                     lambda e: e.matmul(ps[:, 0:64], lhsT=Bz[st][k][:, 1, :], rhs=ly[:, 1, :], start=False, stop=True)],
                              reads=[Bz[st][k], ly], writes=[ps])
                        A(lambda e: e.copy(out=KT[st][k][:, tau, :], in_=ps[:, 0:64]), [ps], [KT[st][k]])
                    for (tab, addc) in ((sinN[st][k], 0.0), (cosN[st][k], 0.25)):
                        V(lambda e: e.tensor_scalar(out=turN[:], in0=tI[:], scalar1=sm["th8"][:, j:j + 1], scalar2=addc, op0=ALU.mult, op1=ALU.add), [tI], [turN])
                        V(lambda e: e.tensor_copy(out=turNi[:], in_=turN[:]), [turN], [turNi])
                        V(lambda e: e.tensor_tensor(out=turN[:], in0=turN[:], in1=turNi[:], op=ALU.subtract), [turN, turNi], [turN])
                        A(lambda e: e.activation(out=tab[:], in_=turN[:], func=AF.Sin, scale=TWO_PI), [turN], [tab])
                    V(lambda e: e.tensor_scalar(out=rho8T[st][k][:], in0=tI[:], scalar1=0.0, scalar2=sm["rho8"][:, j:j + 1], op0=ALU.mult, op1=ALU.add), [tI], [rho8T[st][k]])
                q_ = gp // 4; qq_ = gp % 4
                ps = psT.next()
                p.ops("pe", [
                    lambda e: e.matmul(ps[:, 0:64], lhsT=Bz[st][0][:, 0, :], rhs=CT[st][0][:, 0, :], start=True, stop=False),
                    lambda e: e.matmul(ps[:, 0:64], lhsT=Bz[st][0][:, 1, :], rhs=CT[st][0][:, 1, :], start=False, stop=False),
                    lambda e: e.matmul(ps[:, 0:64], lhsT=Bz[st][1][:, 0, :], rhs=CT[st][1][:, 0, :], start=False, stop=False),
                    lambda e: e.matmul(ps[:, 0:64], lhsT=Bz[st][1][:, 1, :], rhs=CT[st][1][:, 1, :], start=False, stop=False),
                    lambda e: e.matmul(ps[:, 0:64], lhsT=Ddf[:, q_, :], rhs=identP[:, 32 * qq_:32 * qq_ + 64], start=False, stop=True),
                ], reads=[Bz[st][0], Bz[st][1], CT[st][0], CT[st][1], Ddf, identP], writes=[ps])
                A(lambda e: e.copy(out=K0c[st][:], in_=ps[:, 0:64]), [ps], [K0c[st]])

            Ssb = Ring([sb(f"Ssb{i}", [128, 4, NCH], F32) for i in range(2)])

            def geom(gp):
                q = gp // 4
                qq = gp % 4
                if qq < 3:
                    return q, slice(32 * qq, 32 * qq + 32), slice(32, 64), slice(32 * qq, 32 * qq + 32), slice(32 * qq, 32 * qq + 32)
                return q, slice(64, 128), slice(0, 64), slice(64, 128), slice(32 * qq, 32 * qq + 32)

            def stageA(gp, st, sq):
                q = gp // 4
                uT = uTr.next()
                p.dma(lambda e: e.dma_start(out=uT[:], in_=uT_s[sq, q * 128:(q + 1) * 128, :]), uT, True)
                uD = uDr.next()
                A(lambda e: e.copy(out=uD[:], in_=uT[:].rearrange("p (n s) -> p s n", s=8)), [uT], [uD])
                ss = Ssb.next()
                for k in range(2):
                    for ri in range(2):
                        pb_ = psSt[k][ri]
                        if k == 0:
                            fns = [lambda e, s_=s_: e.matmul(pb_[:], lhsT=XT[st][k][:, s_, ri, :], rhs=uD[:, s_, :], start=(s_ == 0), stop=(s_ == 7)) for s_ in range(8)]
                        else:
                            fns = [lambda e, s_=s_: e.matmul(pb_[:], lhsT=XT[st][k][:, s_, ri, :], rhs=uD[:, 7 - s_, ::-1], start=(s_ == 0), stop=(s_ == 7)) for s_ in range(8)]
                        p.ops("pe", fns, reads=[XT[st][k], uD], writes=[pb_])
                        A(lambda e: e.copy(out=ss[:, 2 * k + ri, :], in_=pb_[:]), [pb_], [ss])
                return (gp, st, sq, uD, ss)

            def stageB(ctx):
                gp, st, sq, uD, ss = ctx
                T = [[tmpr.next() for _ in range(4)] for k in range(2)]
                for (ti, si, tab) in ((0, 0, cosN), (1, 1, sinN), (2, 1, cosN), (3, 0, sinN)):
                    for k in range(2):
                        V(lambda e, k=k: e.tensor_tensor(out=T[k][ti][:], in0=ss[:, 2 * k + si, :], in1=tab[st][k][:], op=ALU.mult), [ss, tab[st][k]], [T[k][ti]])
                W = [[wrr.next(), wrr.next()] for k in range(2)]
                for k in range(2):
                    V(lambda e, k=k: e.tensor_tensor(out=W[k][0][:], in0=T[k][0][:], in1=T[k][1][:], op=ALU.add), [T[k][0], T[k][1]], [W[k][0]])
                for k in range(2):
                    V(lambda e, k=k: e.tensor_tensor(out=W[k][1][:], in0=T[k][2][:], in1=T[k][3][:], op=ALU.subtract), [T[k][2], T[k][3]], [W[k][1]])
                R = [[Rrr.next(), Rrr.next()] for k in range(2)]
                for ri in range(2):
                    for k in range(2):
                        V(lambda e, k=k, ri=ri: e.tensor_tensor_scan(out=R[k][ri][:], data0=rho8T[st][k][:], data1=W[k][ri][:], initial=0.0, op0=ALU.mult, op1=ALU.add),
                          [rho8T[st][k], W[k][ri]], [R[k][ri]])
                T = [[tmpr.next() for _ in range(4)] for k in range(2)]
                for (ti, si, tab) in ((0, 0, cosN), (1, 1, sinN), (2, 1, cosN), (3, 0, sinN)):
                    for k in range(2):
                        V(lambda e, k=k: e.tensor_tensor(out=T[k][ti][:], in0=R[k][si][:], in1=tab[st][k][:], op=ALU.mult), [R[k][si], tab[st][k]], [T[k][ti]])
                Vv = [[Vrr.next(), Vrr.next()] for k in range(2)]
                for k in range(2):
                    V(lambda e, k=k: e.tensor_tensor(out=Vv[k][0][:], in0=T[k][0][:], in1=T[k][1][:], op=ALU.subtract), [T[k][0], T[k][1]], [Vv[k][0]])
                for k in range(2):
                    V(lambda e, k=k: e.tensor_tensor(out=Vv[k][1][:], in0=T[k][2][:], in1=T[k][3][:], op=ALU.add), [T[k][2], T[k][3]], [Vv[k][1]])
                Z = [Zr_[k].next() for k in range(2)]
                for ri in range(2):
                    for k in range(2):
                        V(lambda e, k=k, ri=ri: e.tensor_tensor(out=Z[k][:, ri, :], in0=Vv[k][ri][:], in1=ss[:, 2 * k + ri, :], op=ALU.subtract), [Vv[k][ri], ss], [Z[k]])
                return ctx + (Z,)

            def stageC(ctx):
                gp, st, sq, uD, ss, Z = ctx
                q, rows, lcs, dds, orow = geom(gp)
                zo = zor.next()
                for tau in range(8):
                    py = psT.next()
                    fns = [
                        lambda e: e.matmul(py[rows, :], lhsT=LY[st][0][:, tau, 0, lcs], rhs=Z[0][:, 0, :], start=True, stop=False),
                        lambda e: e.matmul(py[rows, :], lhsT=LY[st][0][:, tau, 1, lcs], rhs=Z[0][:, 1, :], start=False, stop=False),
                        lambda e: e.matmul(py[rows, :], lhsT=LY[st][1][:, 7 - tau, 0, lcs], rhs=Z[1][:, 0, ::-1], start=False, stop=False),
                        lambda e: e.matmul(py[rows, :], lhsT=LY[st][1][:, 7 - tau, 1, lcs], rhs=Z[1][:, 1, ::-1], start=False, stop=False),
                    ]
                    for s_ in range(8):
                        if s_ < tau:
                            lw = KT[st][0][:, tau - s_, lcs]
                        elif s_ > tau:
                            lw = KT[st][1][:, s_ - tau, lcs]
                        else:
                            lw = K0c[st][:, lcs]
                        fns.append(lambda e, s_=s_, lw=lw: e.matmul(py[rows, :], lhsT=lw, rhs=uD[:, s_, :], start=False, stop=(s_ == 7)))
                    p.ops("pe", fns, reads=[LY[st][0], LY[st][1], KT[st][0], KT[st][1], K0c[st], Z[0], Z[1], uD], writes=[py])
                    A(lambda e: e.activation(out=zo[rows, tau:S:8], in_=py[rows, :], func=AF.Gelu), [py], [zo])
                p.dma(lambda e: e.dma_start(out=zT_s[sq, gp * 32:(gp + 1) * 32, :], in_=zo[orow, :]), zo, False)

            runs = [(gp, gp % NSET, sq) for gp in range(16) for sq in range(nseq)]
            prepped = set()

            def ensure_prep(idx):
                if idx < len(runs) and runs[idx][0] not in prepped:
                    prepped.add(runs[idx][0])
                    prep(runs[idx][0], runs[idx][1])
            ensure_prep(0)
            ctxA = stageA(*runs[0])
            for i, (gp, st, sq) in enumerate(runs):
                ensure_prep(i + 1)
                nxt = stageA(*runs[i + 1]) if i + 1 < len(runs) else None
                ctxB = stageB(ctxA)
                stageC(ctxB)
                ensure_prep(i + 2)
                ctxA = nxt
                if sq == nseq - 1:
                    stop(f'S_gp{gp}')
        new_phase()
        stop('S')

        with ExitStack() as es:
            def sb(name, shape, dt, dma=False, const=False):
                return p.buf(es.enter_context(nc.sbuf_tensor(name, list(shape), dt)), dma=dma, const=const)

            mstage = sb("mstage", [128, 256], F32, dma=True)
            ostage = sb("ostage", [128, 3, 64], F32, dma=True)
            maskB = sb("maskB", [128, 256], BF16); ones3 = sb("ones3", [128, 3, 64], BF16); identb = sb("identb", [128, 128], BF16)
            p.dma(lambda e: e.dma_start(out=mstage[:], in_=md["maskb"]), mstage, True)
            p.dma(lambda e: e.dma_start(out=ostage[:], in_=md["ones3"]), ostage, True)
            p.op("dve", lambda e: e.tensor_copy(out=maskB[:], in_=mstage[:]), reads=[mstage], writes=[maskB])
            p.op("dve", lambda e: e.tensor_copy(out=ones3[:], in_=ostage[:]), reads=[ostage], writes=[ones3])
            p.op("dve", lambda e: e.tensor_copy(out=identb[:], in_=ident[:]), reads=[ident], writes=[identb])
            maskB.const = True; ones3.const = True; identb.const = True
            qTr = Ring([sb(f"qTa{i}", [128, S], BF16, dma=True) for i in range(2)])
            kSr = Ring([sb(f"kSa{i}", [128, S], BF16, dma=True) for i in range(2)])
            qDr = Ring([sb(f"qDa{i}", [128, S], BF16) for i in range(2)])
            kTr = Ring([sb(f"kTa{i}", [128, S + 2 * KPAD], BF16) for i in range(2)])
            vTr = Ring([sb(f"vTa{i}", [128, 48, 128], BF16, dma=True) for i in range(2)])
            acc = sb("acc", [128, 2, S], F32)
            rden = sb("rden", [128, S], F32)
            aTo = sb("aTo", [128, S], BF16, dma=True)
            PTr = Ring([sb(f"PT{i}", [128, 256], BF16) for i in range(6)])
            psS = Ring(psum[0:4]); psO = Ring(psum[4:8])
            SCALE = 64.0 ** -0.5
            for sq in range(nseq):
                for c in range(2):
                    for g in range(3):
                        d = DIL[g]; L = S // d; nb = L // 128 + 1
                        qN = qTr.next(); kS = kSr.next(); qT = qDr.next(); kT = kTr.next(); vT = vTr.next()
                        ch = 2 * g + c
                        LP = L + 128
                        p.dma(lambda e: e.dma_start(out=qN[:], in_=qT_s[sq, ch * 128:(ch + 1) * 128, :]), qN, True)
                        p.dma(lambda e: e.dma_start(out=kS[:], in_=kT_s[sq, ch * 128:(ch + 1) * 128, :]), kS, True)
                        p.op("pool", lambda e: e.memset(kT[:, 0:d * LP], 0.0), writes=[kT])
                        p.op("act", lambda e: e.copy(out=qT[:].rearrange("p (r i) -> p r i", r=d), in_=qN[:].rearrange("p (i r) -> p r i", r=d)), reads=[qN], writes=[qT])
                        p.op("dve", lambda e: e.tensor_copy(out=kT[:, 0:d * LP].rearrange("p (r i) -> p r i", r=d)[:, :, 64:64 + L], in_=kS[:].rearrange("p (i r) -> p r i", r=d)),
                             reads=[kS], writes=[kT])
                        p.dma(lambda e: e.dma_start(out=vT[:, 0:NBLK[g], :], in_=v_s[g][sq, :, :, c * 128:(c + 1) * 128]), vT, True)
                        def stS(r, a, qT=qT, kT=kT, L=L, LP=LP):
                            qcs = slice(r * L + 128 * a, r * L + 128 * a + 128)
                            pts = []
                            for hp in range(2):
                                pb = 64 * hp
                                pS = psS.next()
                                ks = [slice(r * LP + 128 * m, r * LP + 128 * m + 128) for m in (a, a + 1)]
                                p.ops("pe", [
                                    lambda e: e.matmul(pS[:, 0:128], lhsT=kT[pb:pb + 64, ks[0]], rhs=qT[pb:pb + 64, qcs], start=True, stop=False),
                                    lambda e: e.matmul(pS[:, 128:256], lhsT=kT[pb:pb + 64, ks[1]], rhs=qT[pb:pb + 64, qcs], start=False, stop=False),
                                    lambda e: e.matmul(pS[:, 0:256], lhsT=identb[:], rhs=maskB[:], start=False, stop=True),
                                ], reads=[kT, qT, identb, maskB], writes=[pS])
                                PT = PTr.next()
                                p.op("act", lambda e: e.activation(out=PT[:], in_=pS[:, 0:256], func=AF.Exp, scale=SCALE), reads=[pS], writes=[PT])
                                pts.append(PT)
                            return pts

                        def stPV(r, a, pts, vT=vT, d=d, nb=nb, g=g):
                            qsl = slice(r + d * 128 * a, r + d * 128 * a + d * 127 + 1, d)
                            pO = psO.next()
                            fns = []
                            for hp in range(2):
                                pb = 64 * hp
                                PT = pts[hp]
                                o1 = 1 if a == 0 else 0
                                o2 = 2 if a + 1 == nb - 1 else 0
                                b1 = r * nb + a; b2 = r * nb + a + 1
                                fns += [
                                    lambda e, PT=PT, pb=pb, b1=b1, hp=hp: e.matmul(pO[pb:pb + 64, 0:128], lhsT=vT[:, b1, hp * 64:(hp + 1) * 64], rhs=PT[:, 0:128], start=True, stop=False),
                                    lambda e, PT=PT, pb=pb, b2=b2, hp=hp: e.matmul(pO[pb:pb + 64, 0:128], lhsT=vT[:, b2, hp * 64:(hp + 1) * 64], rhs=PT[:, 128:256], start=False, stop=False),
                                    lambda e, PT=PT, pb=pb, o1=o1: e.matmul(pO[pb:pb + 64, 128:256], lhsT=ones3[:, o1, :], rhs=PT[:, 0:128], start=False, stop=False),
                                    lambda e, PT=PT, pb=pb, o2=o2: e.matmul(pO[pb:pb + 64, 128:256], lhsT=ones3[:, o2, :], rhs=PT[:, 128:256], start=False, stop=True),
                                ]
                            p.ops("pe", fns, reads=pts + [vT, ones3], writes=[pO])
                            pov = pO[:, 0:256].rearrange("p (n i) -> p n i", n=2)
                            if g == 0:
                                p.op("dve", lambda e: e.tensor_copy(out=acc[:, :, qsl], in_=pov), reads=[pO], writes=[acc])
                            else:
                                p.op("dve", lambda e: e.tensor_tensor(out=acc[:, :, qsl], in0=pov, in1=acc[:, :, qsl], op=ALU.add), reads=[pO, acc], writes=[acc])

                        units = [(r, a) for r in range(d) for a in range(L // 128)]
                        cur = stS(*units[0])
                        for ui, (r, a) in enumerate(units):
                            nxt = stS(*units[ui + 1]) if ui + 1 < len(units) else None
                            stPV(r, a, cur)
                            cur = nxt
                    p.op("dve", lambda e: e.reciprocal(out=rden[:], in_=acc[:, 1, :]), reads=[acc], writes=[rden])
                    p.op("dve", lambda e: e.tensor_tensor(out=aTo[:], in0=acc[:, 0, :], in1=rden[:], op=ALU.mult), reads=[acc, rden], writes=[aTo])
                    p.dma(lambda e: e.dma_start(out=aT_s[sq, c * 128:(c + 1) * 128, :], in_=aTo[:]), aTo, False)
        new_phase()
        stop('T')

        def load_w_bf16(sbf, wdst, src, K, N, tag, stg=None):
            piece = 1024 if N >= 1024 else N
            if stg is None:
                stg = Ring([sbf(f"wl_{tag}{i}", [128, piece], F32, dma=True) for i in range(2)])
            engs = ("dve", "pool", "act")
            n = 0
            for k in range(K):
                for c0 in range(0, N, piece):
                    w = min(piece, N - c0)
                    st = stg.next()
                    p.dma(lambda e, st=st, k=k, c0=c0, w=w: e.dma_start(out=st[:, 0:w], in_=src[k * 128:(k + 1) * 128, c0:c0 + w]), st, True)
                    eng = engs[n % 3]; n += 1
                    if eng == "act":
                        p.op("act", lambda e, st=st, k=k, c0=c0, w=w: e.copy(out=wdst[:, k, c0:c0 + w], in_=st[:, 0:w]), reads=[st], writes=[wdst])
                    else:
                        p.op(eng, lambda e, st=st, k=k, c0=c0, w=w: e.tensor_copy(out=wdst[:, k, c0:c0 + w], in_=st[:, 0:w]), reads=[st], writes=[wdst])
            wdst.const = True

        def layer_norm_multi(items, gB, bB):
            for (stats, mv, rstd, nmr, hn), hpre, outt in items:
                for n in range(2):
                    p.op("dve", lambda e, n=n: e.bn_stats(out=stats[:, n, :], in_=hpre[:, n * 512:(n + 1) * 512]), reads=[hpre], writes=[stats])
            for (stats, mv, rstd, nmr, hn), hpre, outt in items:
                p.op("dve", lambda e: e.bn_aggr(out=mv[:], in_=stats[:].rearrange("p n s -> p (n s)")), reads=[stats], writes=[mv])
            for (stats, mv, rstd, nmr, hn), hpre, outt in items:
                p.op("act", lambda e: e.activation(out=rstd[:], in_=mv[:, 1:2], func=AF.Sqrt, bias=epsb[:, 0:1]), reads=[mv, epsb], writes=[rstd])
            for (stats, mv, rstd, nmr, hn), hpre, outt in items:
                p.op("dve", lambda e: e.reciprocal(out=rstd[:], in_=rstd[:]), reads=[rstd], writes=[rstd])
            for (stats, mv, rstd, nmr, hn), hpre, outt in items:
                p.op("dve", lambda e: e.scalar_tensor_tensor(out=nmr[:], in0=mv[:, 0:1], scalar=-1.0, in1=rstd[:], op0=ALU.mult, op1=ALU.mult), reads=[mv, rstd], writes=[nmr])
            for (stats, mv, rstd, nmr, hn), hpre, outt in items:
                p.op("act", lambda e: e.activation(out=hn[:], in_=hpre[:], func=AF.Identity, scale=rstd[:, 0:1], bias=nmr[:, 0:1]), reads=[hpre, rstd, nmr], writes=[hn])
            for (stats, mv, rstd, nmr, hn), hpre, outt in items:
                p.op("dve", lambda e: e.tensor_tensor(out=hn[:], in0=hn[:], in1=gB[:], op=ALU.mult), reads=[hn, gB], writes=[hn])
            for (stats, mv, rstd, nmr, hn), hpre, outt in items:
                p.op("dve", lambda e: e.tensor_tensor(out=outt[:], in0=hn[:], in1=bB[:], op=ALU.add), reads=[hn, bB], writes=[outt])

        with ExitStack() as es:
            def sb(name, shape, dt, dma=False, const=False):
                return p.buf(es.enter_context(nc.sbuf_tensor(name, list(shape), dt)), dma=dma, const=const)
            wgv = sb("wgv", [128, 4, D], BF16, dma=True); wgg = sb("wgg", [128, 4, D], BF16, dma=True); wab = sb("wab", [128, 2, D], BF16, dma=True); wo = sb("wo", [128, 8, D], BF16, dma=True)
            load_wbf(wgv, "wgv", 4); load_wbf(wgg, "wgg", 4); load_wbf(wab, "wab", 2); load_wbf(wo, "wo", 8)
            gB = sb("ln1gB", [128, D], F32, dma=True, const=True); bB = sb("ln1bB", [128, D], F32, dma=True, const=True)
            p.dma(lambda e: e.dma_start(out=gB[:], in_=md["ln1g"].partition_broadcast(128)), gB, True)
            p.dma(lambda e: e.dma_start(out=bB[:], in_=md["ln1b"].partition_broadcast(128)), bB, True)
            epsb = sb("epsb", [128, 1], F32)
            p.op("pool", lambda e: e.memset(epsb[:], LN_EPS), writes=[epsb])
            zTr_ = Ring([sb(f"zTm{i}", [128, 4, 512], BF16, dma=True) for i in range(2)]); aTr_ = Ring([sb(f"aTm{i}", [128, 2, 512], BF16, dma=True) for i in range(2)])
            gTr_ = Ring([sb(f"gTm{i}", [128, 16, 512], BF16, dma=True) for i in range(2)])
            xs = Ring([sb(f"xm{i}", [128, D], F32, dma=True) for i in range(2)])
            mixT = sb("mixT", [128, 8, 512], BF16)
            sg = Ring([sb(f"sg{i}", [128, 512], F32) for i in range(2)])
            t1r = Ring([sb(f"t1m{i}", [128, 512], F32) for i in range(2)])
            t2r = Ring([sb(f"t2m{i}", [128, 512], F32) for i in range(2)])
            hpre = Ring([sb(f"hpre{i}", [128, D], F32) for i in range(2)])
            hout = Ring([sb(f"hout{i}", [128, D], F32, dma=True) for i in range(2)])
            hTt = Ring([sb(f"hTt{i}", [128, 8, 128], BF16, dma=True) for i in range(2)])
            lnbs = [(sb(f"st1_{i}", [128, 2, 6], F32), sb(f"mv1_{i}", [128, 2], F32), sb(f"rstd1_{i}", [128, 1], F32), sb(f"nmr1_{i}", [128, 1], F32), sb(f"hn1_{i}", [128, D], F32)) for i in range(2)]
            psr = Ring(psum)
            def load_m1(sq_, tb_):
                ts_ = slice(tb_ * 512, (tb_ + 1) * 512)
                zT_ = zTr_.next(); aT_ = aTr_.next(); gT_ = gTr_.next()
                p.dma(lambda e: e.dma_start(out=zT_[:], in_=zT_s[sq_].rearrange("(k q) t -> q k t", q=128)[:, :, ts_]), zT_, True)
                p.dma(lambda e: e.dma_start(out=aT_[:], in_=aT_s[sq_].rearrange("(k q) t -> q k t", q=128)[:, :, ts_]), aT_, True)
                p.dma(lambda e: e.dma_start(out=gT_[:], in_=gT_s[sq_].rearrange("(k q) t -> q k t", q=128)[:, :, ts_]), gT_, True)
                return zT_, aT_, gT_
            blocks_m1 = [(sq_, tb_) for sq_ in range(nseq) for tb_ in range(S // 512)]
            nxt_in = load_m1(*blocks_m1[0])
            for bi, (sq, tb) in enumerate(blocks_m1):
                if True:
                    ts = slice(tb * 512, (tb + 1) * 512)
                    zT, aT, gT = nxt_in
                    if bi + 1 < len(blocks_m1):
                        nxt_in = load_m1(*blocks_m1[bi + 1])
                    for do in range(8):
                        ds_ = slice(do * 128, (do + 1) * 128)
                        pA = psr.next(); pG = psr.next(); pB = psr.next()
                        p.ops("pe", [lambda e, k=k: e.matmul(pA[:], lhsT=wgv[:, k, ds_], rhs=zT[:, k, :], start=(k == 0), stop=(k == 3)) for k in range(4)], reads=[wgv, zT], writes=[pA])
                        p.ops("pe", [lambda e, k=k: e.matmul(pG[:], lhsT=wgg[:, k, ds_], rhs=zT[:, k, :], start=(k == 0), stop=(k == 3)) for k in range(4)], reads=[wgg, zT], writes=[pG])
                        p.ops("pe", [lambda e, k=k: e.matmul(pB[:], lhsT=wab[:, k, ds_], rhs=aT[:, k, :], start=(k == 0), stop=(k == 1)) for k in range(2)], reads=[wab, aT], writes=[pB])
                        sgt = sg.next(); t1 = t1r.next(); t2 = t2r.next()
                        p.op("act", lambda e: e.activation(out=sgt[:], in_=pG[:], func=AF.Sigmoid), reads=[pG], writes=[sgt])
                        p.op("dve", lambda e: e.tensor_tensor(out=t1[:], in0=pA[:], in1=sgt[:], op=ALU.mult), reads=[pA, sgt], writes=[t1])
                        p.op("dve", lambda e: e.tensor_tensor(out=t2[:], in0=pB[:], in1=gT[:, 8 + do, :], op=ALU.mult), reads=[pB, gT], writes=[t2])
                        p.op("dve", lambda e: e.tensor_tensor(out=t1[:], in0=t1[:], in1=gT[:, do, :], op=ALU.mult), reads=[t1, gT], writes=[t1])
                        p.op("dve", lambda e: e.tensor_tensor(out=mixT[:, do, :], in0=t1[:], in1=t2[:], op=ALU.add), reads=[t1, t2], writes=[mixT])
                    for ip in (0, 2):
                        items = []
                        for i in (ip, ip + 1):
                            tok = slice(tb * 512 + i * 128, tb * 512 + (i + 1) * 128)
                            xt = xs.next()
                            p.dma(lambda e: e.dma_start(out=xt[:], in_=x_d[sq, tok, :]), xt, True)
                            hp_ = hpre.next()
                            for n in range(2):
                                ns = slice(n * 512, (n + 1) * 512)
                                po = psr.next()
                                p.ops("pe", [lambda e, k=k: e.matmul(po[:], lhsT=mixT[:, k, i * 128:(i + 1) * 128], rhs=wo[:, k, ns], start=(k == 0), stop=(k == 7)) for k in range(8)],
                                      reads=[mixT, wo], writes=[po])
                                p.op("dve", lambda e: e.scalar_tensor_tensor(out=hp_[:, ns], in0=xt[:, ns], scalar=ALPHA, in1=po[:], op0=ALU.mult, op1=ALU.add),
                                     reads=[xt, po], writes=[hp_])
                            ho = hout.next()
                            items.append((lnbs[i - ip], hp_, ho, tok))
                        layer_norm_multi([(a_, b_, c_) for (a_, b_, c_, _) in items], gB, bB)
                        for (_, _, ho, tok) in items:
                            p.dma(lambda e: e.dma_start(out=h_s[sq, tok, :], in_=ho[:]), ho, False)
                            hT = hTt.next()
                            for kk in range(2):
                                pt = psr.next()
                                p.ops("pe", [lambda e, k4=k4: e.transpose(out=pt[:, k4 * 128:(k4 + 1) * 128], in_=ho[:, (kk * 4 + k4) * 128:(kk * 4 + k4 + 1) * 128], identity=ident[:])
                                             for k4 in range(4)], reads=[ho, ident], writes=[pt])
                                p.op("act", lambda e: e.copy(out=hT[:, kk * 4:(kk + 1) * 4, :], in_=pt[:].rearrange("p (k t) -> p k t", k=4)), reads=[pt], writes=[hT])
                            p.dma(lambda e: e.dma_start(out=hT_s[sq].rearrange("(k q) t -> q k t", q=128)[:, :, tok], in_=hT[:]), hT, False)
        new_phase()
        stop('M1')

        with ExitStack() as es:
            def sb(name, shape, dt, dma=False, const=False):
                return p.buf(es.enter_context(nc.sbuf_tensor(name, list(shape), dt)), dma=dma, const=const)
            wup = sb("wup", [128, 8, 2 * DFF], BF16, dma=True); wdn = sb("wdn", [128, 22, D], BF16, dma=True)
            load_wbf(wup, "wup", 8); load_wbf(wdn, "wdn", 22)
            gB = sb("ln2gB", [128, D], F32, dma=True, const=True); bB = sb("ln2bB", [128, D], F32, dma=True, const=True)
            p.dma(lambda e: e.dma_start(out=gB[:], in_=md["ln2g"].partition_broadcast(128)), gB, True)
            p.dma(lambda e: e.dma_start(out=bB[:], in_=md["ln2b"].partition_broadcast(128)), bB, True)
            cw = sb("cw", [128, 44, 3], F32, dma=True, const=True); cbias = sb("cbias", [128, 44], F32, dma=True, const=True)
            p.dma(lambda e: e.dma_start(out=cw[:], in_=md["cw"]), cw, True)
            p.dma(lambda e: e.dma_start(out=cbias[:], in_=md["cbias"]), cbias, True)
            epsb = sb("epsb2", [128, 1], F32)
            p.op("pool", lambda e: e.memset(epsb[:], LN_EPS), writes=[epsb])
            hT = sb("hTf", [128, 8, 514], BF16, dma=True)
            hres = Ring([sb(f"hres{i}", [128, D], F32, dma=True) for i in range(1)])
            cvr = Ring([sb(f"cv{i}", [128, 512], F32) for i in range(8)])
            actT = sb("actT", [128, 22, 512], BF16)
            opre = sb("opre", [128, D], F32)
            oout = Ring([sb(f"oout{i}", [128, D], F32, dma=True) for i in range(1)])
            lnb = (sb("st2", [128, 2, 6], F32), sb("mv2", [128, 2], F32), sb("rstd2", [128, 1], F32), sb("nmr2", [128, 1], F32), sb("hn2", [128, D], F32))
            psm = Ring(psum[0:5])
            psh = Ring([(psum[5], 0), (psum[6], 0), (psum[7], 0)])
            for sq in range(nseq):
                for tb in range(S // 512):
                    t0 = tb * 512
                    lo = max(t0 - 1, 0); hi = min(t0 + 513, S)
                    if t0 == 0:
                        p.op("pool", lambda e: e.memset(hT[:, :, 0:1], 0.0), writes=[hT])
                    if t0 + 512 == S:
                        p.op("pool", lambda e: e.memset(hT[:, :, 513:514], 0.0), writes=[hT])
                    p.dma(lambda e: e.dma_start(out=hT[:, :, lo - (t0 - 1):hi - (t0 - 1)], in_=hT_s[sq].rearrange("(k q) t -> q k t", q=128)[:, :, lo:hi]), hT, True)
                    for c in range(22):
                        cvs = []
                        for ch in (c, 22 + c):
                            cs_ = slice(ch * 128, (ch + 1) * 128)
                            pm = psm.next(); ph, hc = psh.next()
                            p.ops("pe", [lambda e, k=k: e.matmul(pm[:], lhsT=wup[:, k, cs_], rhs=hT[:, k, 1:513], start=(k == 0), stop=(k == 7)) for k in range(8)]
                                  + [lambda e, k=k: e.matmul(ph[:, hc:hc + 2], lhsT=wup[:, k, cs_], rhs=hT[:, k, 0:514:513], start=(k == 0), stop=(k == 7)) for k in range(8)],
                                  reads=[wup, hT], writes=[pm, ph])
                            cv = cvr.next()
                            p.op("act", lambda e: e.activation(out=cv[:], in_=pm[:], func=AF.Identity, scale=cw[:, ch, 1:2], bias=cbias[:, ch:ch + 1]),
                                 reads=[pm, cw, cbias], writes=[cv])
                            p.op("dve", lambda e: e.scalar_tensor_tensor(out=cv[:, 1:512], in0=pm[:, 0:511], scalar=cw[:, ch, 0:1], in1=cv[:, 1:512], op0=ALU.mult, op1=ALU.add), reads=[pm, cw, cv], writes=[cv])
                            p.op("dve", lambda e: e.scalar_tensor_tensor(out=cv[:, 0:511], in0=pm[:, 1:512], scalar=cw[:, ch, 2:3], in1=cv[:, 0:511], op0=ALU.mult, op1=ALU.add), reads=[pm, cw, cv], writes=[cv])
                            p.op("dve", lambda e: e.scalar_tensor_tensor(out=cv[:, 0:1], in0=ph[:, hc:hc + 1], scalar=cw[:, ch, 0:1], in1=cv[:, 0:1], op0=ALU.mult, op1=ALU.add), reads=[ph, cw, cv], writes=[cv])
                            p.op("dve", lambda e: e.scalar_tensor_tensor(out=cv[:, 511:512], in0=ph[:, hc + 1:hc + 2], scalar=cw[:, ch, 2:3], in1=cv[:, 511:512], op0=ALU.mult, op1=ALU.add), reads=[ph, cw, cv], writes=[cv])
                            cvs.append(cv)
                        p.op("act", lambda e: e.activation(out=cvs[0][:], in_=cvs[0][:], func=AF.Gelu), reads=[cvs[0]], writes=[cvs[0]])
                        p.op("pool", lambda e: e.tensor_tensor(out=actT[:, c, :], in0=cvs[0][:], in1=cvs[1][:], op=ALU.mult), reads=cvs, writes=[actT])
                    for i in range(4):
                        tok = slice(t0 + i * 128, t0 + (i + 1) * 128)
                        hr = hres.next()
                        p.dma(lambda e: e.dma_start(out=hr[:], in_=h_s[sq, tok, :]), hr, True)
                        for n in range(2):
                            ns = slice(n * 512, (n + 1) * 512)
                            po = psm.next()
                            p.ops("pe", [lambda e, k=k: e.matmul(po[:], lhsT=actT[:, k, i * 128:(i + 1) * 128], rhs=wdn[:, k, ns], start=(k == 0), stop=(k == 21)) for k in range(22)],
                                  reads=[actT, wdn], writes=[po])
                            p.op("dve", lambda e: e.scalar_tensor_tensor(out=opre[:, ns], in0=hr[:, ns], scalar=ALPHA, in1=po[:], op0=ALU.mult, op1=ALU.add),
                                 reads=[hr, po], writes=[opre])
                        oo = oout.next()
                        layer_norm_multi([(lnb, opre, oo)], gB, bB)
                        p.dma(lambda e: e.dma_start(out=out_d[sq, tok, :], in_=oo[:]), oo, False)
        new_phase()


def _host_inputs(inputs, core, nseq=NSEQ):
    f32 = np.float32
    x = np.ascontiguousarray(inputs["x"][core * nseq:(core + 1) * nseq]).astype(f32)
    pos = np.ascontiguousarray(inputs["positions"][core * nseq:(core + 1) * nseq]).astype(np.int32)
    w_in = np.asarray(inputs["w_in"][0], f32)
    b_in = np.asarray(inputs["b_in"][0], f32)
    sw = np.arange(AW).reshape(-1, 2, 32)[:, ::-1, :].reshape(-1)
    q0, k0, v0, g0 = SSMW, SSMW + AW, SSMW + 2 * AW, SSMW + 3 * AW
    cols = np.concatenate([np.arange(0, SSMW), np.arange(q0, q0 + AW), np.arange(k0, k0 + AW), np.arange(g0, g0 + 2 * D)])
    pswap = np.zeros((128, 128), f32)
    for m_ in range(128):
        pswap[m_ + 32 if (m_ % 64) < 32 else m_ - 32, m_] = 1.0
    w_fm = np.ascontiguousarray(w_in[:, cols])
    b_fm = np.ascontiguousarray(b_in[cols].reshape(NFM // 128, 128).T)
    w_v = np.ascontiguousarray(w_in[:, v0:v0 + AW])
    b_v = np.ascontiguousarray(b_in[v0:v0 + AW].reshape(1, AW))
    half = 32
    inv_freq = (10000.0 ** (-np.arange(half, dtype=np.float64) * 2.0 / 64)).astype(f32)
    invf = np.zeros((128, 2), f32)
    for pp in range(128):
        invf[pp, 0] = inv_freq[pp % 32] / TWO_PI
        invf[pp, 1] = -TWO_PI if (pp % 64) < 32 else TWO_PI
    def tile_layout(a):
        return np.ascontiguousarray(a.reshape(2, 16, 2, 64).transpose(2, 3, 0, 1).reshape(128, 32)).astype(f32)
    lre_h = tile_layout(np.asarray(inputs["ssm_lam_re"][0], f32))
    lim_h = tile_layout(np.asarray(inputs["ssm_lam_im"][0], f32))
    ldt_h = tile_layout(np.broadcast_to(np.asarray(inputs["ssm_log_dt"][0], f32)[:, :, None], (2, 32, 64)).copy())

    def bz(b):
        o = np.zeros((128, 32, 128), f32)
        b = np.asarray(b, f32)
        for dr in range(2):
            for gp in range(16):
                for gl in range(2):
                    c0 = (gp % 4) * 32 + gl * 16
                    o[gl * 64:(gl + 1) * 64, dr * 16 + gp, c0:c0 + 16] = b[dr, 2 * gp + gl]
        return o

    def cb(c):
        o = np.zeros((32, 32, 128), f32)
        c = np.asarray(c, f32)
        for dr in range(2):
            for gp in range(16):
                for gl in range(2):
                    o[gl * 16:(gl + 1) * 16, dr * 16 + gp, gl * 64:(gl + 1) * 64] = c[dr, 2 * gp + gl]
        return o
    ssm = {"lre_h": lre_h, "lim_h": lim_h, "ldt_h": ldt_h,
           "bzr_h": bz(inputs["ssm_b_re"][0]), "bzi_h": bz(inputs["ssm_b_im"][0]),
           "cbr_h": cb(inputs["ssm_c_re"][0]), "cbi_h": cb(inputs["ssm_c_im"][0]),
           "dsk_h": np.ascontiguousarray(np.asarray(inputs["ssm_d"][0], f32).reshape(4, 128).T),
           "iota_h": np.arange(S, dtype=f32).reshape(1, S)}
    d = {"x": x, "pos": pos, "ident": np.eye(128, dtype=f32), "invf": invf, "pswap": pswap,
         "w_in_fm": w_fm, "b_fm": b_fm, "w_v": w_v, "b_v": b_v}
    d.update(ssm)
    ii = np.arange(128)[:, None]; jj = np.arange(128)[None, :]
    maskb = np.concatenate([np.where(ii >= jj, 0.0, -30000.0), np.where(ii <= jj, 0.0, -30000.0)], axis=1).astype(f32)
    ones3 = np.zeros((128, 3, 64), f32)
    ones3[:, 0, :] = 1.0; ones3[64:, 1, :] = 1.0; ones3[:64, 2, :] = 1.0
    g = lambda n: np.ascontiguousarray(np.asarray(inputs[n][0], f32))
    cwh = np.ascontiguousarray(g("conv_w").reshape(3, 44, 128).transpose(2, 1, 0))
    cbh = np.ascontiguousarray(g("conv_b").reshape(44, 128).T)
    d.update({"maskb_h": maskb, "ones3_h": ones3, "wgv_h": g("w_glu_v"), "wgg_h": g("w_glu_g"), "wab_h": g("w_attn_br"), "wo_h": g("w_out"),
              "ln1g_h": g("ln1_g").reshape(1, D), "ln1b_h": g("ln1_b").reshape(1, D), "ln2g_h": g("ln2_g").reshape(1, D), "ln2b_h": g("ln2_b").reshape(1, D),
              "wup_h": g("w_up"), "wdn_h": g("w_down"), "cw_h": cwh, "cb_h": cbh})
    return d


def kernel(**inputs):
    nc = build()
    in_maps = [_host_inputs(inputs, c) for c in range(NCORES)]
    res = run_bass_kernel_spmd(nc, in_maps, core_ids=list(range(NCORES)))
    out = np.concatenate([r["out"] for r in res.results], axis=0)
    return out.astype(np.float32)
```
